# Optimizing a Trainium2 kernel written in Bass

```python
import math
import jax, jax.numpy as jnp
from jax import lax
import numpy as np

D_MODEL = 1024
BATCH = 2
SEQ = 8192
DEPTH = 2
DEC_BATCH = 32
DEC_SEQ = 8
PAST_LEN = 16384
PAGE_SIZE = 128

N_AB_LAYERS = (DEPTH + 1) // 2
N_C_LAYERS = DEPTH // 2

H_A = 4
DK_A = 64
DV_A = 128
RET_CHUNK = 128
H_B = 8
DH_B = 64
SWA_PAIRS = ((128, 1), (512, 4), (2048, 16))
SWA_MAX_WINDOW = 2048
S5_GROUP = 16
S5_GROUPS = D_MODEL // S5_GROUP
S5_STATE = 64
D_FF = -(-8 * D_MODEL // (3 * 256)) * 256

AB_WIDTHS = (H_A * DK_A, H_A * DK_A, H_A * DV_A, H_A * DV_A, H_B * DH_B, H_B * DH_B, H_B * DH_B)
AB_IN_WIDTH = sum(AB_WIDTHS)
AB_SPLITS = tuple(int(c) for c in np.cumsum(AB_WIDTHS)[:-1])
AB_MIX_WIDTH = H_A * DV_A + H_B * DH_B

EPS = 1e-6
NEG_INF = -1e30

kernel_name = "retnet_longnet_s5_hybrid_step"


def rms_norm(x, g):
    xf = x.astype(jnp.float32)
    y = xf * lax.rsqrt(jnp.mean(xf * xf, axis=-1, keepdims=True) + EPS)
    return (y * g.astype(jnp.float32)).astype(x.dtype)


def swiglu(h, w_in, w_out):
    gate, up = jnp.split(h @ w_in, 2, axis=-1)
    return (jax.nn.silu(gate) * up) @ w_out


def swa_buffer_len():
    return min(SWA_MAX_WINDOW, PAST_LEN)


def alibi_slopes():
    return jnp.exp2(-8.0 * jnp.arange(1, H_B + 1, dtype=jnp.float32) / H_B)


def retention_log_decay():
    return jnp.log1p(-jnp.exp2(-5.0 - jnp.arange(H_A, dtype=jnp.float32)))


def window_rows(x, n):
    seq = x.shape[1]
    if seq >= n:
        return x[:, seq - n:]
    return jnp.pad(x, ((0, 0), (n - seq, 0), (0, 0), (0, 0)))


def retention_chunk(q, k, v, state, log_g):
    c = q.shape[1]
    idx = jnp.arange(c, dtype=jnp.float32)
    rel = idx[:, None] - idx[None, :]
    decay = jnp.where(rel >= 0, jnp.exp(jnp.maximum(rel, 0.0)[None] * log_g[:, None, None]), 0.0)
    scores = jnp.einsum('bnhd,bmhd->bhnm', q, k) * decay[None]
    inner = jnp.einsum('bhnm,bmhe->bnhe', scores, v)
    q_decay = jnp.exp((idx + 1.0)[:, None] * log_g[None, :])
    cross = jnp.einsum('bnhd,bhde->bnhe', q, state) * q_decay[None, :, :, None]
    k_decay = jnp.exp((c - 1.0 - idx)[:, None] * log_g[None, :])
    new_state = (jnp.exp(c * log_g)[None, :, None, None] * state
                 + jnp.einsum('bmhd,bmhe->bhde', k * k_decay[None, :, :, None], v))
    return inner + cross, new_state


def retention(q, k, v, state, chunk):
    bsz, seq, nh, dk = q.shape
    n_chunks = seq // chunk
    log_g = retention_log_decay()

    def to_chunks(x):
        return jnp.swapaxes(x.astype(jnp.float32).reshape(bsz, n_chunks, chunk, nh, x.shape[-1]), 0, 1)

    def step(st, qkv):
        o, st = retention_chunk(qkv[0], qkv[1], qkv[2], st, log_g)
        return st, o

    state, o = lax.scan(step, state.astype(jnp.float32),
                        (to_chunks(q), to_chunks(k) * dk ** -0.5, to_chunks(v)))
    return jnp.swapaxes(o, 0, 1).reshape(bsz, seq, nh, -1), state


def head_group_norm(o, gain):
    mu = jnp.mean(o, axis=-1, keepdims=True)
    var = jnp.mean(jnp.square(o - mu), axis=-1, keepdims=True)
    y = (o - mu) * lax.rsqrt(var + EPS)
    return y.reshape(o.shape[0], o.shape[1], -1) * gain.astype(jnp.float32)


def dilated_branch_prompt(q, k, v, window, dil, slopes):
    bsz, seq, nh, dh = q.shape
    span = window // dil
    sub = seq // dil
    pad = (-sub) % span

    def blocks(x):
        x = x.reshape(bsz, sub, dil, nh, dh).transpose(0, 2, 1, 3, 4)
        x = jnp.pad(x, ((0, 0), (0, 0), (0, pad), (0, 0), (0, 0)))
        return x.reshape(bsz, dil, -1, span, nh, dh)

    qb, kb, vb = blocks(q), blocks(k), blocks(v)
    nblk = qb.shape[2]

    def with_prev(x):
        prev = jnp.pad(x[:, :, :-1], ((0, 0), (0, 0), (1, 0), (0, 0), (0, 0), (0, 0)))
        return jnp.concatenate([prev, x], axis=3)

    kk, vv = with_prev(kb), with_prev(vb)
    qi = jnp.arange(span)[:, None]
    kj = jnp.arange(2 * span)[None, :]
    dist = span + qi - kj
    key_pos = (jnp.arange(nblk)[:, None, None] - 1) * span + kj[None]
    valid = (dist >= 0) & (dist <= span) & (key_pos >= 0)
    bias = -slopes[:, None, None] * (dil * dist).astype(jnp.float32)[None]
    s = jnp.einsum('brnqhd,brnkhd->brnhqk', qb, kk) * dh ** -0.5 + bias
    s = jnp.where(valid[:, None], s, NEG_INF)
    m = jnp.max(s, axis=-1, keepdims=True)
    p = jnp.exp(s - m)
    den = jnp.sum(p, axis=-1, keepdims=True)
    o = jnp.einsum('brnhqk,brnkhd->brnqhd', p / den, vv)
    lse = (m + jnp.log(den))[..., 0].transpose(0, 1, 2, 4, 3)

    def unblock(x):
        x = x.reshape((bsz, dil, nblk * span) + x.shape[4:])[:, :, :sub]
        x = jnp.swapaxes(x, 1, 2)
        return x.reshape((bsz, seq) + x.shape[3:])

    return unblock(o), unblock(lse)


def dilated_branch_sample(q, k_all, v_all, window, dil, slopes, p0):
    n_new, dh = q.shape[1], q.shape[-1]
    steps = jnp.arange(window // dil + 1)
    pos = PAST_LEN + jnp.arange(n_new)[:, None] - dil * steps[None, :]
    valid = pos >= 0
    idx = jnp.clip(pos - p0, 0, k_all.shape[1] - 1)
    kg = k_all[:, idx]
    vg = v_all[:, idx]
    bias = -slopes[:, None] * (dil * steps).astype(jnp.float32)[None, :]
    s = jnp.einsum('bthd,btkhd->bthk', q, kg) * dh ** -0.5 + bias
    s = jnp.where(valid[:, None, :], s, NEG_INF)
    m = jnp.max(s, axis=-1, keepdims=True)
    p = jnp.exp(s - m)
    den = jnp.sum(p, axis=-1, keepdims=True)
    o = jnp.einsum('bthk,btkhd->bthd', p / den, vg)
    return o, (m + jnp.log(den))[..., 0]


def merge_dilation_branches(branches):
    outs = jnp.stack([o for o, _ in branches])
    lses = jnp.stack([l for _, l in branches])
    wts = jax.nn.softmax(lses, axis=0)
    return jnp.einsum('gbsh,gbshd->bshd', wts, outs)


def ab_project(h, w_in):
    bsz, seq, _ = h.shape
    q_a, k_a, v_a, g_a, q_b, k_b, v_b = jnp.split(h @ w_in, AB_SPLITS, axis=-1)
    hd = lambda t, n: t.reshape(bsz, seq, n, -1)
    return (hd(q_a, H_A), hd(k_a, H_A), hd(v_a, H_A), g_a,
            hd(q_b, H_B), hd(k_b, H_B), hd(v_b, H_B))


def ab_merge(h, o_a, g_a, o_b, gn_gain, w_out):
    bsz, seq, _ = h.shape
    a = jax.nn.silu(g_a.astype(jnp.float32)) * head_group_norm(o_a, gn_gain)
    mixed = jnp.concatenate([a, o_b.reshape(bsz, seq, -1)], axis=-1)
    return mixed.astype(h.dtype) @ w_out


def ab_mixer_prompt(h, w_in, gn_gain, w_out, ret_state0):
    q_a, k_a, v_a, g_a, q_b, k_b, v_b = ab_project(h, w_in)
    o_a, ret_state = retention(q_a, k_a, v_a, ret_state0, RET_CHUNK)
    slopes = alibi_slopes()
    qf, kf, vf = q_b.astype(jnp.float32), k_b.astype(jnp.float32), v_b.astype(jnp.float32)
    o_b = merge_dilation_branches([dilated_branch_prompt(qf, kf, vf, w, d, slopes) for (w, d) in SWA_PAIRS])
    out = ab_merge(h, o_a, g_a, o_b, gn_gain, w_out)
    buf = swa_buffer_len()
    return out, ret_state, window_rows(k_b, buf), window_rows(v_b, buf)


def ab_mixer_sample(h, w_in, gn_gain, w_out, ret_state0, k_past, v_past):
    q_a, k_a, v_a, g_a, q_b, k_b, v_b = ab_project(h, w_in)
    o_a, ret_state = retention(q_a, k_a, v_a, ret_state0, q_a.shape[1])
    k_all = jnp.concatenate([k_past.astype(k_b.dtype), k_b], axis=1)
    v_all = jnp.concatenate([v_past.astype(v_b.dtype), v_b], axis=1)
    buf = k_past.shape[1]
    p0 = PAST_LEN - buf
    slopes = alibi_slopes()
    qf, kf, vf = q_b.astype(jnp.float32), k_all.astype(jnp.float32), v_all.astype(jnp.float32)
    o_b = merge_dilation_branches([dilated_branch_sample(qf, kf, vf, w, d, slopes, p0) for (w, d) in SWA_PAIRS])
    out = ab_merge(h, o_a, g_a, o_b, gn_gain, w_out)
    return out, ret_state, k_all[:, -buf:], v_all[:, -buf:]


def complex_affine_combine(e1, e2):
    a1r, a1i, b1r, b1i = e1
    a2r, a2i, b2r, b2i = e2
    return (a1r * a2r - a1i * a2i, a1r * a2i + a1i * a2r,
            a2r * b1r - a2i * b1i + b2r, a2r * b1i + a2i * b1r + b2i)


def s5_mixer(h, lam_re, lam_im, log_step, b_re, b_im, c_re, c_im, d_skip, w_glu, x0_re, x0_im):
    bsz, seq, _ = h.shape
    f32 = jnp.float32
    lam_re, lam_im = lam_re.astype(f32), lam_im.astype(f32)
    dt = jnp.exp(log_step.astype(f32))[:, None]
    mag = jnp.exp(lam_re * dt)
    ab_re, ab_im = mag * jnp.cos(lam_im * dt), mag * jnp.sin(lam_im * dt)
    den = lam_re * lam_re + lam_im * lam_im
    f_re = ((ab_re - 1.0) * lam_re + ab_im * lam_im) / den
    f_im = (ab_im * lam_re - (ab_re - 1.0) * lam_im) / den
    u = h.astype(f32).reshape(bsz, seq, S5_GROUPS, S5_GROUP)
    bu_re = jnp.einsum('bsgp,gnp->bsgn', u, b_re.astype(f32))
    bu_im = jnp.einsum('bsgp,gnp->bsgn', u, b_im.astype(f32))
    drive_re = f_re * bu_re - f_im * bu_im
    drive_im = f_re * bu_im + f_im * bu_re
    x0_re, x0_im = x0_re.astype(f32), x0_im.astype(f32)
    drive_re = drive_re.at[:, 0].add(ab_re * x0_re - ab_im * x0_im)
    drive_im = drive_im.at[:, 0].add(ab_re * x0_im + ab_im * x0_re)
    a_re = jnp.broadcast_to(ab_re, (1, seq) + ab_re.shape)
    a_im = jnp.broadcast_to(ab_im, (1, seq) + ab_im.shape)
    _, _, xs_re, xs_im = lax.associative_scan(complex_affine_combine, (a_re, a_im, drive_re, drive_im), axis=1)
    y = (jnp.einsum('bsgn,gpn->bsgp', xs_re, c_re.astype(f32))
         - jnp.einsum('bsgn,gpn->bsgp', xs_im, c_im.astype(f32))
         + d_skip.astype(f32).reshape(S5_GROUPS, S5_GROUP) * u).reshape(bsz, seq, D_MODEL)
    z = jax.nn.gelu(y).astype(h.dtype)
    val, gate = jnp.split(z @ w_glu, 2, axis=-1)
    return val * jax.nn.sigmoid(gate), xs_re[:, -1], xs_im[:, -1]


def setup_inputs(seed: int = 0) -> dict:
    key = jax.random.key(seed)
    ks = jax.random.split(key, 24)
    f32 = jnp.float32
    nrm = lambda k, shape, scale: scale * jax.random.normal(k, shape, f32)
    buf = swa_buffer_len()
    n_idx = jnp.arange(S5_STATE, dtype=f32)
    ssm_shape = (N_C_LAYERS, S5_GROUPS, S5_STATE)
    return {
        'x_prompt': nrm(ks[0], (BATCH, SEQ, D_MODEL), 1.0),
        'x_sample': nrm(ks[1], (DEC_BATCH, DEC_SEQ, D_MODEL), 1.0),
        'state_ret': nrm(ks[2], (N_AB_LAYERS, DEC_BATCH, H_A, DK_A, DV_A), 1.0),
        'state_swa_k': nrm(ks[3], (N_AB_LAYERS, DEC_BATCH, buf, H_B, DH_B), 1.0),
        'state_swa_v': nrm(ks[4], (N_AB_LAYERS, DEC_BATCH, buf, H_B, DH_B), 1.0),
        'state_ssm_re': nrm(ks[5], (N_C_LAYERS, DEC_BATCH, S5_GROUPS, S5_STATE), 0.5),
        'state_ssm_im': nrm(ks[6], (N_C_LAYERS, DEC_BATCH, S5_GROUPS, S5_STATE), 0.5),
        'norm_mix': 1.0 + nrm(ks[7], (DEPTH, D_MODEL), 0.02),
        'norm_ffn': 1.0 + nrm(ks[8], (DEPTH, D_MODEL), 0.02),
        'norm_final': 1.0 + nrm(ks[9], (D_MODEL,), 0.02),
        'w_in_ab': nrm(ks[10], (N_AB_LAYERS, D_MODEL, AB_IN_WIDTH), D_MODEL ** -0.5),
        'ret_gn': 1.0 + nrm(ks[11], (N_AB_LAYERS, H_A * DV_A), 0.02),
        'w_out_ab': nrm(ks[12], (N_AB_LAYERS, AB_MIX_WIDTH, D_MODEL), AB_MIX_WIDTH ** -0.5),
        'ssm_lam_re': -0.5 + nrm(ks[13], ssm_shape, 0.01),
        'ssm_lam_im': math.pi * n_idx + nrm(ks[14], ssm_shape, 0.01),
        'ssm_log_step': jax.random.uniform(ks[15], (N_C_LAYERS, S5_GROUPS), f32, math.log(1e-3), math.log(1e-1)),
        'ssm_b_re': nrm(ks[16], (N_C_LAYERS, S5_GROUPS, S5_STATE, S5_GROUP), (2 * S5_GROUP) ** -0.5),
        'ssm_b_im': nrm(ks[17], (N_C_LAYERS, S5_GROUPS, S5_STATE, S5_GROUP), (2 * S5_GROUP) ** -0.5),
        'ssm_c_re': nrm(ks[18], (N_C_LAYERS, S5_GROUPS, S5_GROUP, S5_STATE), S5_STATE ** -0.5),
        'ssm_c_im': nrm(ks[19], (N_C_LAYERS, S5_GROUPS, S5_GROUP, S5_STATE), S5_STATE ** -0.5),
        'ssm_d': nrm(ks[20], (N_C_LAYERS, D_MODEL), 1.0),
        'w_glu': nrm(ks[21], (N_C_LAYERS, D_MODEL, 2 * D_MODEL), D_MODEL ** -0.5),
        'w_ffn_in': nrm(ks[22], (DEPTH, D_MODEL, 2 * D_FF), D_MODEL ** -0.5),
        'w_ffn_out': nrm(ks[23], (DEPTH, D_FF, D_MODEL), D_FF ** -0.5),
    }


def reference(x_prompt, x_sample, state_ret, state_swa_k, state_swa_v, state_ssm_re, state_ssm_im,
              norm_mix, norm_ffn, norm_final, w_in_ab, ret_gn, w_out_ab,
              ssm_lam_re, ssm_lam_im, ssm_log_step, ssm_b_re, ssm_b_im, ssm_c_re, ssm_c_im, ssm_d, w_glu,
              w_ffn_in, w_ffn_out):
    yp, ys = x_prompt, x_sample
    ret_p, ret_s = [], []
    swk_p, swv_p, swk_s, swv_s = [], [], [], []
    sr_p, si_p, sr_s, si_s = [], [], [], []
    for layer in range(DEPTH):
        i = layer // 2
        hp = rms_norm(yp, norm_mix[layer])
        hs = rms_norm(ys, norm_mix[layer])
        if layer % 2 == 0:
            zero_ret = jnp.zeros((yp.shape[0], H_A, DK_A, DV_A), jnp.float32)
            mp, r_p, k_p, v_p = ab_mixer_prompt(hp, w_in_ab[i], ret_gn[i], w_out_ab[i], zero_ret)
            ms, r_s, k_s, v_s = ab_mixer_sample(hs, w_in_ab[i], ret_gn[i], w_out_ab[i],
                                                state_ret[i], state_swa_k[i], state_swa_v[i])
            ret_p.append(r_p); ret_s.append(r_s)
            swk_p.append(k_p); swv_p.append(v_p); swk_s.append(k_s); swv_s.append(v_s)
        else:
            s5w = (ssm_lam_re[i], ssm_lam_im[i], ssm_log_step[i], ssm_b_re[i], ssm_b_im[i],
                   ssm_c_re[i], ssm_c_im[i], ssm_d[i], w_glu[i])
            zero_ssm = jnp.zeros((yp.shape[0], S5_GROUPS, S5_STATE), jnp.float32)
            mp, xr_p, xi_p = s5_mixer(hp, *s5w, zero_ssm, zero_ssm)
            ms, xr_s, xi_s = s5_mixer(hs, *s5w, state_ssm_re[i], state_ssm_im[i])
            sr_p.append(xr_p); si_p.append(xi_p); sr_s.append(xr_s); si_s.append(xi_s)
        yp = yp + mp
        ys = ys + ms
        yp = yp + swiglu(rms_norm(yp, norm_ffn[layer]), w_ffn_in[layer], w_ffn_out[layer])
        ys = ys + swiglu(rms_norm(ys, norm_ffn[layer]), w_ffn_in[layer], w_ffn_out[layer])
    yp = rms_norm(yp, norm_final)
    ys = rms_norm(ys, norm_final)
    return (yp, ys,
            jnp.stack(ret_p), jnp.stack(ret_s),
            jnp.stack(swk_p), jnp.stack(swv_p), jnp.stack(swk_s), jnp.stack(swv_s),
            jnp.stack(sr_p), jnp.stack(si_p), jnp.stack(sr_s), jnp.stack(si_s))
```

```python
import numpy as np
import concourse.bass as bass
import concourse.mybir as mybir
from concourse.bass_utils import run_bass_kernel_spmd

F32 = mybir.dt.float32
BF16 = mybir.dt.bfloat16
ALU = mybir.AluOpType
AF = mybir.ActivationFunctionType

D = 1024
KC = D // 128
NCORES = 8
TP = 2048
TS = 32
SB = 4
WB = 2048
H_A, DK_A, DV_A = 4, 64, 128
H_B, DH_B = 8, 64
AB_IN = 3072
EPS = 1e-6


class Buf:
    __slots__ = ("name", "last_w", "readers")

    def __init__(self, name):
        self.name = name
        self.last_w = None
        self.readers = []


class Sched:
    ENGS = ("pe", "act", "dve", "pool", "sp")

    def __init__(self, nc):
        self.nc = nc
        self.ops = {e: [] for e in self.ENGS}
        self.dma_sems = []
        self.buf_dma = {}

    def _deps(self, reads, writes):
        deps = []
        for b in reads:
            if b.last_w is not None:
                deps.append(b.last_w)
        for b in writes:
            if b.last_w is not None:
                deps.append(b.last_w)
            deps.extend(b.readers)
        return deps

    def _commit(self, tok, reads, writes):
        for b in reads:
            b.readers = [r for r in b.readers if not (r[0] == tok[0] and r[1] == tok[1])]
            b.readers.append(tok)
        for b in writes:
            b.last_w = tok
            b.readers = []

    def op(self, eng, fn, reads=(), writes=()):
        deps = self._deps(reads, writes)
        idx = len(self.ops[eng])
        if eng == "pe":
            deps = [d for d in deps if not (d[0] == "e" and d[1] == "pe")]
        self.ops[eng].append({"fn": fn, "deps": deps, "sig": False, "dma": None})
        tok = ("e", eng, idx)
        self._commit(tok, reads, writes)
        return tok

    def dma(self, eng, fn, key, reads=(), writes=()):
        deps = self._deps(reads, writes)
        if key not in self.buf_dma:
            self.buf_dma[key] = [len(self.buf_dma), 0]
        ent = self.buf_dma[key]
        ent[1] += 16
        tok = ("d", ent[0], ent[1])
        self.ops[eng].append({"fn": fn, "deps": deps, "sig": False, "dma": ent[0]})
        self._commit(tok, reads, writes)
        return tok

    def finish(self, final_waits_eng="sp"):
        nc = self.nc
        for e in self.ENGS:
            for o in self.ops[e]:
                for d in o["deps"]:
                    if d[0] == "e":
                        self.ops[d[1]][d[2]]["sig"] = True
        cnt = {}
        for e in self.ENGS:
            c = 0
            for o in self.ops[e]:
                if o["sig"]:
                    c += 1
                o["cnt"] = c
            cnt[e] = c
        n_dma = len(self.buf_dma)
        from contextlib import ExitStack
        with ExitStack() as st:
            esem = {e: st.enter_context(nc.semaphore("es_" + e)) for e in self.ENGS}
            dsem = [st.enter_context(nc.semaphore("ds_%d" % i)) for i in range(n_dma)]
            block = st.enter_context(nc.Block())
            ops = self.ops
            finals = [(ent[0], ent[1]) for ent in self.buf_dma.values()]

            def emit(e, eng):
                waited_e = {}
                waited_d = {}
                for o in ops[e]:
                    need_e, need_d = {}, {}
                    for d in o["deps"]:
                        if d[0] == "e":
                            v = ops[d[1]][d[2]]["cnt"]
                            if v > need_e.get(d[1], 0):
                                need_e[d[1]] = v
                        else:
                            if d[2] > need_d.get(d[1], 0):
                                need_d[d[1]] = d[2]
                    for pe_, v in need_e.items():
                        if v > waited_e.get(pe_, 0):
                            eng.wait_ge(esem[pe_], v)
                            waited_e[pe_] = v
                    for si, v in need_d.items():
                        if v > waited_d.get(si, 0):
                            eng.wait_ge(dsem[si], v)
                            waited_d[si] = v
                    ins = o["fn"](eng)
                    if o["dma"] is not None:
                        ins.then_inc(dsem[o["dma"]], 16)
                    elif o["sig"]:
                        ins.then_inc(esem[e], 1)
                if e == final_waits_eng:
                    for si, v in finals:
                        eng.wait_ge(dsem[si], v)

            @block.tensor
            def _(eng):
                emit("pe", eng)

            @block.scalar
            def _(eng):
                emit("act", eng)

            @block.vector
            def _(eng):
                emit("dve", eng)

            @block.gpsimd
            def _(eng):
                emit("pool", eng)

            @block.sync
            def _(eng):
                emit("sp", eng)


import math
import os
import ml_dtypes
from contextlib import ExitStack

SEQ = 8192
NBLK = SEQ // 512
D_FF = 2816
FC = D_FF // 128
GAM = [1.0 - 2.0 ** (-5 - h) for h in range(4)]
SLOPES = [2.0 ** (-8.0 * (h + 1) / 8) for h in range(8)]
TW = 2944
RING = 20
TWO_PI = 2.0 * math.pi


def host_consts():
    f32 = np.float32
    c = {}
    c["ident_in"] = np.eye(128, dtype=f32)
    m = np.arange(128)[:, None]
    n = np.arange(128)[None, :]
    decT = np.zeros((128, 4, 128), np.float64)
    for h in range(4):
        decT[:, h, :] = np.where(n >= m, GAM[h] ** np.maximum(n - m, 0), 0.0)
    c["decT"] = decT.astype(f32)
    qdec = np.zeros((128, 2, 128), np.float64)
    for p in range(128):
        for pr in range(2):
            h = 2 * pr + p // 64
            qdec[p, pr, :] = GAM[h] ** (np.arange(128) + 1.0)
    c["qdec"] = qdec.astype(f32)
    kdec = np.zeros((128, 256), np.float64)
    for h in range(4):
        kdec[:, h * 64:(h + 1) * 64] = (GAM[h] ** (127.0 - np.arange(128)))[:, None] * 0.125
    c["kdec"] = kdec.astype(f32)
    jl = np.arange(128)[:, None]
    x = np.arange(TW)[None, :]
    dl = x - jl - 384
    cnt = ((dl <= 128).astype(np.float64) + ((dl % 4 == 0) & (dl <= 512)) + ((dl % 16 == 0) & (dl <= 2048)))
    valid = (dl >= 0) & (dl <= 2048)
    tab = np.zeros((8, 128, TW), np.float64)
    for h in range(8):
        tab[h] = np.where(valid, cnt * np.exp(-SLOPES[h] * np.maximum(dl, 0)), 0.0)
    c["swa_tab"] = tab.astype(ml_dtypes.bfloat16)
    sgn = np.ones((128, 1), f32); sgn[64:] = -1.0
    c["sgn"] = sgn
    c["tau"] = np.tile(np.arange(128, dtype=f32)[None, :], (128, 1))
    sw = np.zeros((128, 128), f32)
    for p in range(64):
        sw[p, p + 64] = 1.0; sw[p + 64, p] = 1.0
    c["swapm"] = sw
    rm = np.zeros((128, 4), f32)
    for p in range(128):
        rm[p, (p % 64) // 16] = 1.0
    c["rowmask"] = rm
    hm = np.zeros((128, 2), f32); hm[:64, 0] = 1.0; hm[64:, 1] = 1.0
    c["hmask"] = hm
    p32 = np.arange(32)
    kdS = np.zeros((32, 256), np.float64)
    for h in range(4):
        kdS[:, h * 64:(h + 1) * 64] = (GAM[h] ** (7.0 - (p32 % 8)))[:, None] * 0.125
    c["kdecS"] = kdS.astype(f32)
    qdS = np.zeros((128, 2, 32), np.float64)
    for p in range(128):
        for pr in range(2):
            qdS[p, pr, :] = GAM[2 * pr + p // 64] ** ((p32 % 8) + 1.0)
    c["qdecS"] = qdS.astype(f32)
    dS = np.zeros((32, 4, 32), np.float64)
    mm, nn = p32[:, None], p32[None, :]
    for h in range(4):
        dS[:, h, :] = np.where((mm // 8 == nn // 8) & (nn >= mm), GAM[h] ** np.maximum(nn - mm, 0), 0.0)
    c["decTS"] = dS.astype(f32)
    bm = np.zeros((32, 4), f32)
    bm[p32, p32 // 8] = 1.0
    c["bmask"] = bm
    r64 = np.arange(64)
    hh, tt = r64 // 8, r64 % 8
    jj = np.arange(2056)[None, :]
    dls = 2048 + tt[:, None] - jj
    cnts = ((dls <= 128).astype(np.float64) + ((dls % 4 == 0) & (dls <= 512)) + ((dls % 16 == 0) & (dls <= 2048)))
    ws = np.where((dls >= 0) & (dls <= 2048), cnts * np.exp(-np.array(SLOPES)[hh][:, None] * np.maximum(dls, 0)), 0.0)
    wsp = np.zeros((64, TW), np.float64); wsp[:, :2056] = ws
    c["WS"] = wsp.astype(ml_dtypes.bfloat16)
    hs = np.zeros((64, 512), f32)
    for r in range(64):
        hs[r, (r // 8) * 64:(r // 8) * 64 + 64] = 1.0
    c["hselB"] = hs
    eo = np.zeros((64, 2), f32); eo[:, 0] = (hh % 2 == 0); eo[:, 1] = (hh % 2 == 1)
    c["eomask"] = eo
    return c


CONST_SHAPES = {"ident_in": ([128, 128], F32), "decT": ([128, 4, 128], F32), "qdec": ([128, 2, 128], F32),
                "kdec": ([128, 256], F32), "swa_tab": ([8, 128, TW], BF16), "sgn": ([128, 1], F32),
                "tau": ([128, 128], F32), "swapm": ([128, 128], F32), "rowmask": ([128, 4], F32), "hmask": ([128, 2], F32),
                "kdecS": ([32, 256], F32), "qdecS": ([128, 2, 32], F32), "decTS": ([32, 4, 32], F32), "bmask": ([32, 4], F32),
                "WS": ([64, TW], BF16), "hselB": ([64, 512], F32), "eomask": ([64, 2], F32)}

WEIGHT_SHAPES = {"norm_mix": [2, D], "norm_ffn": [2, D], "norm_final": [D], "w_in_ab": [D, AB_IN], "ret_gn": [512],
                 "w_out_ab": [D, D], "ssm_lam_re": [64, 64], "ssm_lam_im": [64, 64], "ssm_log_step": [64],
                 "ssm_b_re": [64, 64, 16], "ssm_b_im": [64, 64, 16], "ssm_c_re": [64, 16, 64], "ssm_c_im": [64, 16, 64],
                 "ssm_d": [D], "w_glu": [D, 2 * D], "w_ffn_in": [2, D, 2 * D_FF], "w_ffn_out": [2, D_FF, D]}


def build_program(nblk_run=NBLK, PH=9, sim=False, SUB=9, RUN_SAMPLE=True, KV_FROM=NBLK - 4, blk_lo=0):
    nc = bass.Bass("TRN2", target_bir_lowering=False)
    S = Sched(nc)
    es = ExitStack()

    def din(name, shape, dt=F32):
        return nc.dram_tensor(name, list(shape), dt, kind="ExternalInput").ap()

    def dout(name, shape):
        return nc.dram_tensor(name, list(shape), F32, kind="ExternalOutput").ap()

    xp = din("xp", [SEQ, D])
    W = {k: din(k, v) for k, v in WEIGHT_SHAPES.items()}
    C = {k: din("c_" + k, v[0], v[1]) for k, v in CONST_SHAPES.items()}
    o_yp = dout("o_yp", [SEQ, D])
    o_kp = dout("o_kp", [WB, 512])
    o_vp = dout("o_vp", [WB, 512])
    o_retp = dout("o_retp", [128, 2, 128])
    o_ssp = dout("o_ssp", [128, 64])
    i_Sst = din("i_Sst", [128, 2, 128]); i_Wc = din("i_Wc", [128, 64]); i_Zend = din("i_Zend", [128, 64])
    i_KBT = din("i_KBT", [128, 4, RING * 128], BF16); i_VB = din("i_VB", [128, RING, 512], BF16)
    o_Wc = dout("o_Wc", [128, 64]); o_Zend = dout("o_Zend", [128, 64])
    o_KBT = nc.dram_tensor("o_KBT", [128, 4, RING * 128], BF16, kind="ExternalOutput").ap()
    o_VB = nc.dram_tensor("o_VB", [128, RING, 512], BF16, kind="ExternalOutput").ap()
    xs = din("xs", [TS, D])
    st_ret = din("st_ret", [SB, 4, 64, 128])
    st_k = din("st_k", [SB, WB, 512]); st_v = din("st_v", [SB, WB, 512])
    st_sr = din("st_sr", [SB, 64, 64]); st_si = din("st_si", [SB, 64, 64])
    o_ys = dout("o_ys", [TS, D])
    o_rets = dout("o_rets", [SB, 128, 2, 128])
    o_ks = dout("o_ks", [SB, WB, 512]); o_vs = dout("o_vs", [SB, WB, 512])
    o_sss = dout("o_sss", [SB, 128, 64])

    def dscr(name, shape):
        if sim:
            return din(name, shape, BF16), Buf(name)
        t = nc.dram_tensor(name, list(shape), BF16)
        return t.ap(), Buf(name)
    wb_in, b_wb_in = dscr("wb_in", [6, 128, KC, 512])
    wb_out, b_wb_out = dscr("wb_out", [2, 128, KC, 512])
    wb_glu, b_wb_glu = dscr("wb_glu", [4, 128, KC, 512])
    wb_ffi, b_wb_ffi = dscr("wb_ffi", [2, FC // 2, 128, KC, 512])
    wb_ffo, b_wb_ffo = dscr("wb_ffo", [2, KC, 128, FC, 128])

    def cast_piece(dst_ap, src_ap, b_dst, kcn):
        if sim:
            return
        S.dma("pool", lambda e, a=dst_ap, b=src_ap.rearrange("(k p) n -> p k n", p=128): e.dma_start(out=a, in_=b), b_dst, writes=[b_dst])
    for pi in range(6):
        cast_piece(wb_in[pi], W["w_in_ab"][:, pi * 512:(pi + 1) * 512], b_wb_in, KC)
    for pi in range(2):
        cast_piece(wb_out[pi], W["w_out_ab"][:, pi * 512:(pi + 1) * 512], b_wb_out, KC)
    for l in range(2):
        for p in range(FC // 2):
            cast_piece(wb_ffi[l, p][:, :, 0:256], W["w_ffn_in"][l][:, p * 256:(p + 1) * 256], b_wb_ffi, KC)
            cast_piece(wb_ffi[l, p][:, :, 256:512], W["w_ffn_in"][l][:, D_FF + p * 256:D_FF + (p + 1) * 256], b_wb_ffi, KC)
        for oc in range(KC):
            cast_piece(wb_ffo[l, oc], W["w_ffn_out"][l][:, oc * 128:(oc + 1) * 128], b_wb_ffo, FC)
    for p in range(4):
        cast_piece(wb_glu[p][:, :, 0:256], W["w_glu"][:, p * 256:(p + 1) * 256], b_wb_glu, KC)
        cast_piece(wb_glu[p][:, :, 256:512], W["w_glu"][:, D + p * 256:D + (p + 1) * 256], b_wb_glu, KC)

    def sb(name, shape, dt=F32):
        t = es.enter_context(nc.sbuf_tensor(name, list(shape), dt))
        return t, Buf(name)

    def op(eng, method, reads, writes, *a, **kw):
        return S.op(eng, lambda e, m=method, a=a, kw=kw: getattr(e, m)(*a, **kw), reads=reads, writes=writes)

    def ld(dst_ap, src_ap, b_dst, eng="sp", **kw):
        return S.dma(eng, lambda e, a=dst_ap, b=src_ap, kw=kw: e.dma_start(out=a, in_=b, **kw), b_dst, writes=[b_dst])

    def st(dst_ap, src_ap, b_src, eng="sp", extra_reads=()):
        return S.dma(eng, lambda e, a=dst_ap, b=src_ap: e.dma_start(out=a, in_=b), b_src, reads=[b_src] + list(extra_reads))

    def bcast(t, free_total, off, dims):
        return bass.AP(t, off, [[free_total, 128]] + [[s_, c_] for s_, c_ in dims])

    ident, b_ident = sb("ident", [128, 128])
    ld(ident[:], C["ident_in"], b_ident)
    ones_d, b_ones_d = sb("ones_d", [128, 128], BF16)
    ones_g, b_ones_g = sb("ones_g", [128, 128], BF16)
    ones_1, b_ones_1 = sb("ones_1", [128, 128], BF16)
    op("dve", "memset", [], [b_ones_d], ones_d[:], 1.0 / D)
    op("dve", "memset", [], [b_ones_g], ones_g[:], 1.0 / 128)
    op("dve", "memset", [], [b_ones_1], ones_1[:], 1.0)
    gvec, b_gvec = sb("gvec", [128, 5, KC])
    for i, (nm, l) in enumerate([("norm_mix", 0), ("norm_ffn", 0), ("norm_mix", 1), ("norm_ffn", 1)]):
        ld(gvec[:, i, :], W[nm][l].rearrange("(k p) -> p k", p=128), b_gvec, allow_slow_non_contiguous=True)
    ld(gvec[:, 4, :], W["norm_final"].rearrange("(k p) -> p k", p=128), b_gvec, allow_slow_non_contiguous=True)
    gn, b_gn = sb("gn", [128, 4])
    ld(gn[:], W["ret_gn"].rearrange("(h p) -> p h", p=128), b_gn, allow_slow_non_contiguous=True)
    dvec, b_dvec = sb("dvec", [128, KC])
    ld(dvec[:], W["ssm_d"].rearrange("(k p) -> p k", p=128), b_dvec, allow_slow_non_contiguous=True)
    decT, b_decT = sb("decT", [128, 4, 128]); ld(decT[:], C["decT"], b_decT)
    qdec, b_qdec = sb("qdec", [128, 2, 128]); ld(qdec[:], C["qdec"], b_qdec)
    kdec, b_kdec = sb("kdec", [128, 256]); ld(kdec[:], C["kdec"], b_kdec)
    sgn, b_sgn = sb("sgn", [128, 1]); ld(sgn[:], C["sgn"], b_sgn)
    tau, b_tau = sb("tau", [128, 128]); ld(tau[:], C["tau"], b_tau)
    swapm, b_swapm = sb("swapm", [128, 128]); ld(swapm[:], C["swapm"], b_swapm)
    rowmask, b_rowmask = sb("rowmask", [128, 4]); ld(rowmask[:], C["rowmask"], b_rowmask)
    hmask, b_hmask = sb("hmask", [128, 2]); ld(hmask[:], C["hmask"], b_hmask)

    b_d2d = Buf("d2d")
    if RUN_SAMPLE:
        for b in range(SB):
            S.dma("sp", lambda e, a=o_ks[b, 0:WB - 8, :], c_=st_k[b, 8:WB, :]: e.dma_start(out=a, in_=c_), b_d2d)
            S.dma("sp", lambda e, a=o_vs[b, 0:WB - 8, :], c_=st_v[b, 8:WB, :]: e.dma_start(out=a, in_=c_), b_d2d)
    psb = []
    for i in range(8):
        t = es.enter_context(nc.psum_tensor("ps%d" % i, [128, 512], F32))
        psb.append((t, Buf("ps%d" % i)))
    rr = [0]

    def next_ps():
        i = rr[0] % 5
        rr[0] += 1
        return psb[i]
    PS_A, PS_B, PS_C = psb[5], psb[6], psb[7]

    PANEL_EL = 4096
    panels = [sb("panel%d" % i, [128, PANEL_EL], BF16) for i in range(2)]
    prr = [0]

    def load_panel(src_ap, kcn, w, b_src):
        t, b = panels[prr[0] % 2]
        prr[0] += 1
        view = t[:, 0:kcn * w].rearrange("p (k n) -> p k n", k=kcn)
        q = "sp" if (prr[0] % 2) == 0 else "pool"
        S.dma(q, lambda e, a=t[:, 0:kcn * w], s_=src_ap.rearrange("p k n -> p (k n)"): e.dma_start(out=a, in_=s_), b, reads=[b_src], writes=[b])
        return view, b

    def fm_chunk(view, b_pan, kcn, c0, rhs_t, b_rhs, ntok, extra=None):
        pt, b_pt = next_ps()
        for kc in range(kcn):
            op("pe", "matmul", [b_pan, b_rhs], [b_pt], pt[:, 0:ntok], lhsT=view[:, kc, c0:c0 + 128],
               rhs=rhs_t[:, kc, 0:ntok], start=(kc == 0), stop=(kc == kcn - 1))
        return pt, b_pt

    def tm_tile(view, b_pan, kcn, c0, w, lhs_t, b_lhs, t0, rows):
        pt, b_pt = next_ps()
        for kc in range(kcn):
            op("pe", "matmul", [b_pan, b_lhs], [b_pt], pt[0:rows, 0:w], lhsT=lhs_t[:, kc, t0:t0 + rows],
               rhs=view[:, kc, c0:c0 + w], start=(kc == 0), stop=(kc == kcn - 1))
        return pt, b_pt

    xtok, b_xtok = sb("xtok", [128, D])
    xT, b_xT = sb("xT", [128, KC, 512])
    rstd, b_rstd = sb("rstd", [128, 512])
    hT, b_hT = sb("hT", [128, KC, 512], BF16)
    sq, b_sq = hT, b_hT
    mixT, b_mixT = sb("mixT", [128, KC, 512], BF16)
    act, b_act = sb("act", [128, FC, 512], BF16)
    qaT, b_qaT = act[:, 0:2, :], b_act
    kaT, b_kaT = act[:, 2:4, :], b_act
    vatok, b_vatok = act[:, 4:8, :], b_act
    sgT, b_sgT = act[:, 8:12, :], b_act
    qbT, b_qbT = act[:, 12:16, :], b_act
    katok, b_katok = act[:, 16:18, :].rearrange("p a (b c) -> p (a b) c", c=256), b_act
    KBT, b_KBT = sb("KBT", [128, 4, RING * 128], BF16)
    VB, b_VB = sb("VB", [128, RING, 512], BF16)
    kvo = [(xtok[:, 0:512], b_xtok), (xtok[:, 512:1024], b_xtok)]
    tmpA, b_tmpA = sb("tmpA", [128, 512])
    tmpB, b_tmpB = sb("tmpB", [128, 512])
    tmpC, b_tmpC = sb("tmpC", [128, 512])
    tmpD, b_tmpD = sb("tmpD", [128, 512])
    pT = [sb("pT%d" % i, [128, 512], BF16) for i in range(3)]
    eT = [sb("eT%d" % i, [128, 512], BF16) for i in range(3)]
    thtab = [sb("thtab%d" % i, [128, TW], BF16) for i in range(1)]
    Sst, b_Sst = sb("Sst", [128, 2, 128])
    Sbf, b_Sbf = sb("Sbf", [128, 2, 128], BF16)
    op("dve", "memset", [], [b_Sst], Sst[:], 0.0)
    op("dve", "memset", [], [b_Sbf], Sbf[:], 0.0)
    qz, b_qz = sb("qz", [128, 4, 128], BF16)
    qd, b_qd = sb("qd", [128, 4, 128], BF16)
    osb, b_osb = tmpC, b_tmpC
    obf, b_obf = eT[0]
    osq, b_osq = eT[1]

    lam_abs, b_lam_abs = sb("lam_abs", [128, 64])
    th, b_th = sb("th", [128, 64])
    thS, b_thS = sb("thS", [128, 64])
    CT, b_CT = sb("CT", [128, 64, 128], BF16)
    ST, b_ST = sb("ST", [128, 64, 128], BF16)
    LT1, b_LT1 = sb("LT1", [128, KC, 4, 128], BF16)
    LT2, b_LT2 = sb("LT2", [128, KC, 4, 128], BF16)
    C1, b_C1 = sb("C1", [128, 64, 16], BF16)
    C2, b_C2 = sb("C2", [128, 64, 16], BF16)
    cL, b_cL = sb("cL", [128, 64]); sL, b_sL = sb("sL", [128, 64])
    cE, b_cE = sb("cE", [128, 64]); sE, b_sE = sb("sE", [128, 64])
    gd, b_gd = sb("gd", [128, KC])
    Wc, b_Wc = sb("Wc", [128, 64])
    Zend, b_Zend = sb("Zend", [128, 64])
    op("dve", "memset", [], [b_Wc], Wc[:], 0.0)
    op("dve", "memset", [], [b_Zend], Zend[:], 0.0)
    setup_es = ExitStack()

    def sbt(name, shape, dt=F32):
        t = setup_es.enter_context(nc.sbuf_tensor(name, list(shape), dt))
        return t, Buf(name)

    lamT, b_lamT = sb("lamT", [64, 256])
    ld(lamT[:, 0:64], W["ssm_lam_re"], b_lamT); ld(lamT[:, 64:128], W["ssm_lam_re"], b_lamT)
    ld(lamT[:, 128:192], W["ssm_lam_im"], b_lamT); ld(lamT[:, 192:256], W["ssm_lam_im"], b_lamT)
    lre, b_lre = sb("lre", [128, 64]); lim, b_lim = sb("lim", [128, 64])
    for src0, dst, b_dst in ((0, lre, b_lre), (128, lim, b_lim)):
        pt, b_pt = next_ps()
        op("pe", "transpose", [b_lamT, b_ident], [b_pt], out=pt[:, 0:64], in_=lamT[:, src0:src0 + 128], identity=ident[0:64, 0:64])
        op("dve", "tensor_copy", [b_pt], [b_dst], out=dst[:], in_=pt[:, 0:64])
    dtt, b_dtt = sb("dtt", [128, 64])
    ld(dtt[:], W["ssm_log_step"].partition_broadcast(128), b_dtt)
    op("act", "activation", [b_dtt], [b_dtt], out=dtt[:], in_=dtt[:], func=AF.Exp)
    op("dve", "tensor_mul", [b_lim, b_dtt], [b_th], out=th[:], in0=lim[:], in1=dtt[:])
    op("dve", "tensor_scalar_mul", [b_th, b_sgn], [b_thS], out=thS[:], in0=th[:], scalar1=sgn[:, 0:1])
    rho, b_rho = sb("rho", [128, 64])
    op("dve", "tensor_mul", [b_lre, b_dtt], [b_rho], out=rho[:], in0=lre[:], in1=dtt[:])
    op("act", "activation", [b_rho], [b_lam_abs], out=lam_abs[:], in_=rho[:], func=AF.Exp)

    s5a, b_s5a = tmpA, b_tmpA
    s5b, b_s5b = tmpB, b_tmpB
    s5i, b_s5i = sb("s5i", [128, 512], mybir.dt.int32)

    def sin_of(dst_ap, ang_ap, n, b_dst, reads):
        shp = ang_ap.shape
        kb = s5b[:, 0:n] if len(shp) == 2 else s5b[:, 0:n].rearrange("p (a b) -> p a b", a=shp[1])
        ki = s5i[:, 0:n] if len(shp) == 2 else s5i[:, 0:n].rearrange("p (a b) -> p a b", a=shp[1])
        op("dve", "tensor_scalar_mul", reads, [b_s5b], out=kb, in0=ang_ap, scalar1=1.0 / TWO_PI)
        op("dve", "tensor_copy", [b_s5b], [b_s5i], out=ki, in_=kb)
        op("dve", "tensor_copy", [b_s5i], [b_s5b], out=kb, in_=ki)
        op("dve", "scalar_tensor_tensor", [b_s5b] + reads, [b_s5b], out=kb, in0=kb, scalar=-TWO_PI, in1=ang_ap,
           op0=ALU.mult, op1=ALU.add)
        op("dve", "tensor_scalar", [b_s5b], [b_s5b], out=kb, in0=kb, scalar1=-3.14159, scalar2=3.14159, op0=ALU.max, op1=ALU.min)
        op("act", "activation", [b_s5b], [b_dst], out=dst_ap, in_=kb, func=AF.Sin)

    for g0 in range(0, 64, 4):
        angv = s5a[:, 0:512].rearrange("p (a b) -> p a b", a=4)
        for (thsrc, b_thsrc, dst, b_dst, shift) in ((th, b_th, CT, b_CT, math.pi / 2), (thS, b_thS, ST, b_ST, 0.0)):
            op("dve", "tensor_tensor", [b_thsrc, b_tau], [b_s5a], out=angv, in0=bcast(thsrc, 64, g0, [(1, 4), (0, 128)]),
               in1=bcast(tau, 128, 0, [(0, 4), (1, 128)]), op=ALU.mult)
            if shift:
                op("dve", "tensor_scalar_add", [b_s5a], [b_s5a], out=angv, in0=angv, scalar1=shift)
            sin_of(dst[:, g0:g0 + 4, :], angv, 512, b_dst, [b_s5a])

    def cs_small(mult, cdst, b_c, sdst, b_s, neg_sin):
        a = s5a[:, 0:64]
        op("dve", "tensor_scalar", [b_th], [b_s5a], out=a, in0=th[:], scalar1=float(mult), scalar2=math.pi / 2,
           op0=ALU.mult, op1=ALU.add)
        sin_of(cdst[:], a, 64, b_c, [b_s5a])
        op("dve", "tensor_scalar_mul", [b_thS], [b_s5a], out=a, in0=thS[:], scalar1=(-float(mult) if neg_sin else float(mult)))
        sin_of(sdst[:], a, 64, b_s, [b_s5a])
    cs_small(128.0, cL, b_cL, sL, b_sL, True)
    cs_small(127.0, cE, b_cE, sE, b_sE, True)
    c1t, b_c1t = sb("c1t", [128, 64]); s1t, b_s1t = sb("s1t", [128, 64])
    cs_small(1.0, c1t, b_c1t, s1t, b_s1t, False)
    fre, b_fre = sb("fre", [128, 64]); fim, b_fim = sb("fim", [128, 64])
    abr, b_abr = sb("abr", [128, 64]); abi, b_abi = sb("abi", [128, 64]); den, b_den = sb("den", [128, 64])
    t64, b_t64 = sb("t64", [128, 64])
    op("dve", "tensor_mul", [b_lam_abs, b_c1t], [b_abr], out=abr[:], in0=lam_abs[:], in1=c1t[:])
    op("dve", "tensor_scalar_add", [b_abr], [b_abr], out=abr[:], in0=abr[:], scalar1=-1.0)
    op("dve", "tensor_mul", [b_lam_abs, b_s1t], [b_abi], out=abi[:], in0=lam_abs[:], in1=s1t[:])
    op("dve", "tensor_scalar_mul", [b_abi, b_sgn], [b_abi], out=abi[:], in0=abi[:], scalar1=sgn[:, 0:1])
    op("dve", "tensor_mul", [b_lre], [b_den], out=den[:], in0=lre[:], in1=lre[:])
    op("dve", "tensor_mul", [b_lim], [b_t64], out=t64[:], in0=lim[:], in1=lim[:])
    op("dve", "tensor_add", [b_den, b_t64], [b_den], out=den[:], in0=den[:], in1=t64[:])
    op("dve", "reciprocal", [b_den], [b_den], out=den[:], in_=den[:])
    op("dve", "tensor_mul", [b_abr, b_lre], [b_fre], out=fre[:], in0=abr[:], in1=lre[:])
    op("dve", "tensor_mul", [b_abi, b_lim], [b_t64], out=t64[:], in0=abi[:], in1=lim[:])
    op("dve", "tensor_add", [b_fre, b_t64], [b_fre], out=fre[:], in0=fre[:], in1=t64[:])
    op("dve", "tensor_mul", [b_fre, b_den], [b_fre], out=fre[:], in0=fre[:], in1=den[:])
    op("dve", "tensor_mul", [b_abi, b_lre], [b_fim], out=fim[:], in0=abi[:], in1=lre[:])
    op("dve", "tensor_mul", [b_abr, b_lim], [b_t64], out=t64[:], in0=abr[:], in1=lim[:])
    op("dve", "tensor_sub", [b_fim, b_t64], [b_fim], out=fim[:], in0=fim[:], in1=t64[:])
    op("dve", "tensor_mul", [b_fim, b_den], [b_fim], out=fim[:], in0=fim[:], in1=den[:])
    fiS, b_fiS = sb("fiS", [128, 64])
    op("dve", "tensor_scalar_mul", [b_fim, b_sgn], [b_fiS], out=fiS[:], in0=fim[:], scalar1=sgn[:, 0:1])
    bre = W["ssm_b_re"].rearrange("g n q -> n g q"); bim = W["ssm_b_im"].rearrange("g n q -> n g q")
    v3 = lambda t, c0: t[:, c0:c0 + 128].rearrange("p (a b) -> p a b", a=8)
    for kc in range(KC):
        B1k, B2k = v3(tmpC, 0), v3(tmpD, 0)
        FBak, FBbk, tFk = v3(tmpA, 0), v3(tmpB, 0), v3(tmpA, 128)
        gsl = slice(kc * 8, (kc + 1) * 8)
        ld(B1k[0:64], bre[:, gsl, :], b_tmpC); ld(B1k[64:128], bim[:, gsl, :], b_tmpC)
        ld(B2k[0:64], bim[:, gsl, :], b_tmpD); ld(B2k[64:128], bre[:, gsl, :], b_tmpD)
        frb = bcast(fre, 64, kc * 8, [(1, 8), (0, 16)]); fib = bcast(fiS, 64, kc * 8, [(1, 8), (0, 16)])
        op("dve", "tensor_tensor", [b_tmpC, b_fre], [b_tmpA], out=FBak, in0=B1k, in1=frb, op=ALU.mult)
        op("dve", "tensor_tensor", [b_tmpD, b_fiS], [b_tmpA], out=tFk, in0=B2k, in1=fib, op=ALU.mult)
        op("dve", "tensor_sub", [b_tmpA], [b_tmpA], out=FBak, in0=FBak, in1=tFk)
        op("dve", "tensor_tensor", [b_tmpD, b_fre], [b_tmpB], out=FBbk, in0=B2k, in1=frb, op=ALU.mult)
        op("dve", "tensor_tensor", [b_tmpC, b_fiS], [b_tmpA], out=tFk, in0=B1k, in1=fib, op=ALU.mult)
        op("dve", "tensor_add", [b_tmpB, b_tmpA], [b_tmpB], out=FBbk, in0=FBbk, in1=tFk)
        for (FBt, b_FB, LT, b_LT) in ((tmpA, b_tmpA, LT1, b_LT1), (tmpB, b_tmpB, LT2, b_LT2)):
            pt, b_pt = next_ps()
            op("pe", "transpose", [b_FB, b_ident], [b_pt], out=pt[:, 0:128], in_=FBt[:, 0:128], identity=ident[:])
            for gi in range(4):
                op("dve", "tensor_scalar_mul", [b_pt, b_rowmask], [b_LT], out=LT[:, kc, gi, :], in0=pt[:, 0:128],
                   scalar1=rowmask[:, gi:gi + 1])
    cre = W["ssm_c_re"].rearrange("(k a) p n -> k (a p) n", k=KC); cim = W["ssm_c_im"].rearrange("(k a) p n -> k (a p) n", k=KC)
    for kc in range(KC):
        ld(tmpC[:, 0:64], cre[kc], b_tmpC); ld(tmpC[:, 64:128], cim[kc], b_tmpC)
        for (Cd, b_Cd, first_im) in ((C1, b_C1, False), (C2, b_C2, True)):
            cst = tmpD
            if not first_im:
                op("dve", "tensor_copy", [b_tmpC], [b_tmpD], out=cst[:, 0:64], in_=tmpC[:, 0:64])
                op("dve", "tensor_scalar_mul", [b_tmpC], [b_tmpD], out=cst[:, 64:128], in0=tmpC[:, 64:128], scalar1=-1.0)
            else:
                op("dve", "tensor_scalar_mul", [b_tmpC], [b_tmpD], out=cst[:, 0:64], in0=tmpC[:, 64:128], scalar1=-1.0)
                op("dve", "tensor_copy", [b_tmpC], [b_tmpD], out=cst[:, 64:128], in_=tmpC[:, 0:64])
            pt, b_pt = next_ps()
            op("pe", "transpose", [b_tmpD, b_ident], [b_pt], out=pt[:, 0:128], in_=cst[:, 0:128], identity=ident[:])
            op("dve", "tensor_copy", [b_pt], [b_Cd], out=Cd[:, kc * 8:(kc + 1) * 8, :].rearrange("p a b -> p (a b)"), in_=pt[:, 0:128])
    op("dve", "tensor_mul", [b_gvec, b_dvec], [b_gd], out=gd[:], in0=gvec[:, 2, :], in1=dvec[:])

    def evac(i, pt, b_pt, dst_ap, b_dst, n, scale=None, func=None, rows=128):
        if func is not None:
            kw = {"scale": scale} if scale is not None else {}
            op("act", "activation", [b_pt], [b_dst], out=dst_ap, in_=pt[0:rows, 0:n], func=func, **kw)
        elif scale is not None:
            op("act", "activation", [b_pt], [b_dst], out=dst_ap, in_=pt[0:rows, 0:n], func=AF.Copy, scale=scale)
        elif i % 2 == 0:
            op("act", "copy", [b_pt], [b_dst], out=dst_ap, in_=pt[0:rows, 0:n])
        else:
            op("dve", "tensor_copy", [b_pt], [b_dst], out=dst_ap, in_=pt[0:rows, 0:n])

    def rmsnorm(gi, ntok, dst=None, b_dst=None):
        dst = hT if dst is None else dst
        b_dst = b_hT if b_dst is None else b_dst
        op("act", "activation", [b_xT], [b_sq], out=sq[:, :, 0:ntok], in_=xT[:, :, 0:ntok], func=AF.Square)
        pt, b_pt = next_ps()
        for kc in range(KC):
            op("pe", "matmul", [b_ones_d, b_sq], [b_pt], pt[:, 0:ntok], lhsT=ones_d[:], rhs=sq[:, kc, 0:ntok],
               start=(kc == 0), stop=(kc == KC - 1))
        op("dve", "tensor_scalar_add", [b_pt], [b_rstd], out=rstd[:, 0:ntok], in0=pt[:, 0:ntok], scalar1=EPS)
        op("act", "activation", [b_rstd], [b_rstd], out=rstd[:, 0:ntok], in_=rstd[:, 0:ntok], func=AF.Sqrt)
        op("dve", "reciprocal", [b_rstd], [b_rstd], out=rstd[:, 0:ntok], in_=rstd[:, 0:ntok])
        for kc in range(KC):
            op("dve", "scalar_tensor_tensor", [b_xT, b_rstd, b_gvec], [b_dst], out=dst[:, kc, 0:ntok],
               in0=xT[:, kc, 0:ntok], scalar=gvec[:, gi, kc:kc + 1], in1=rstd[:, 0:ntok], op0=ALU.mult, op1=ALU.mult)

    def ffn(l, ntok):
        rmsnorm(1 + 2 * l, ntok)
        for p in range(FC // 2):
            view, b_pan = load_panel(wb_ffi[l, p], KC, 512, b_wb_ffi)
            for j in range(2):
                pg, b_pg = fm_chunk(view, b_pan, KC, j * 128, hT, b_hT, ntok)
                pu, b_pu = fm_chunk(view, b_pan, KC, 256 + j * 128, hT, b_hT, ntok)
                op("act", "activation", [b_pg], [b_tmpA], out=tmpA[:, 0:ntok], in_=pg[:, 0:ntok], func=AF.Silu)
                op("dve", "tensor_tensor", [b_tmpA, b_pu], [b_act], out=act[:, 2 * p + j, 0:ntok], in0=tmpA[:, 0:ntok],
                   in1=pu[:, 0:ntok], op=ALU.mult)
        for oc in range(KC):
            view, b_pan = load_panel(wb_ffo[l, oc], FC, 128, b_wb_ffo)
            pt, b_pt = fm_chunk(view, b_pan, FC, 0, act, b_act, ntok)
            op("dve", "tensor_tensor", [b_pt, b_xT], [b_xT], out=xT[:, oc, 0:ntok], in0=pt[:, 0:ntok],
               in1=xT[:, oc, 0:ntok], op=ALU.add)

    def load_x(src_ap, ntok):
        ntile = (ntok + 127) // 128
        for t in range(ntile):
            rows = min(128, ntok - t * 128)
            ld(xtok[0:rows, :], src_ap[t * 128:t * 128 + rows, :], b_xtok)
            for half in range(2):
                pt, b_pt = next_ps()
                for k4 in range(4):
                    kc = half * 4 + k4
                    op("pe", "transpose", [b_xtok, b_ident], [b_pt], out=pt[:, k4 * 128:k4 * 128 + rows],
                       in_=xtok[0:rows, kc * 128:(kc + 1) * 128], identity=ident[0:rows, 0:rows])
                src = pt[:, 0:512].rearrange("p (a b) -> p a b", a=4)[:, :, 0:rows]
                dst = xT[:, half * 4:half * 4 + 4, t * 128:t * 128 + rows]
                if half == 0:
                    op("act", "copy", [b_pt], [b_xT], out=dst, in_=src)
                else:
                    op("dve", "tensor_copy", [b_pt], [b_xT], out=dst, in_=src)

    def final_out(dst_ap, ntok):
        rmsnorm(4, ntok, dst=xT, b_dst=b_xT)
        ntile = (ntok + 127) // 128
        for t in range(ntile):
            rows = min(128, ntok - t * 128)
            for half in range(2):
                pt, b_pt = next_ps()
                for k4 in range(4):
                    kc = half * 4 + k4
                    op("pe", "transpose", [b_xT, b_ident], [b_pt], out=pt[0:rows, k4 * 128:(k4 + 1) * 128],
                       in_=xT[:, kc, t * 128:t * 128 + rows], identity=ident[:])
                evac(half, pt, b_pt, xtok[0:rows, half * 512:(half + 1) * 512], b_xtok, 512, rows=rows)
            st(dst_ap[t * 128:t * 128 + rows, :], xtok[0:rows, :], b_xtok)

    if blk_lo > 0:
        ld(Sst[:], i_Sst, b_Sst)
        op("dve", "tensor_copy", [b_Sst], [b_Sbf], out=Sbf[:], in_=Sst[:])
        ld(Wc[:], i_Wc, b_Wc); ld(Zend[:], i_Zend, b_Zend)
        ld(KBT[:], i_KBT, b_KBT); ld(VB[:], i_VB, b_VB)
    for blk in range(blk_lo, nblk_run):
        t0 = blk * 512
        load_x(xp[t0:t0 + 512, :], 512)
        rmsnorm(0, 512)
        last_kv = blk >= KV_FROM
        for pi in range(6):
            view, b_pan = load_panel(wb_in[pi], KC, 512, b_wb_in)
            if pi == 0:
                for j in range(2):
                    pt, b_pt = fm_chunk(view, b_pan, KC, j * 128, hT, b_hT, 512)
                    evac(j, pt, b_pt, qaT[:, j, :], b_qaT, 512)
                for j in range(2):
                    pt, b_pt = fm_chunk(view, b_pan, KC, 256 + j * 128, hT, b_hT, 512)
                    evac(j, pt, b_pt, kaT[:, j, :], b_kaT, 512, scale=0.125)
                for t in range(4):
                    pt, b_pt = tm_tile(view, b_pan, KC, 256, 256, hT, b_hT, t * 128, 128)
                    op("dve", "tensor_tensor", [b_pt, b_kdec], [b_katok], out=katok[:, t, :], in0=pt[:, 0:256], in1=kdec[:], op=ALU.mult)
            elif pi == 1:
                for t in range(4):
                    pt, b_pt = tm_tile(view, b_pan, KC, 0, 512, hT, b_hT, t * 128, 128)
                    evac(t, pt, b_pt, vatok[:, t, :], b_vatok, 512)
            elif pi == 2:
                for j in range(4):
                    pt, b_pt = fm_chunk(view, b_pan, KC, j * 128, hT, b_hT, 512)
                    evac(j, pt, b_pt, sgT[:, j, :], b_sgT, 512, func=AF.Silu)
            elif pi == 3:
                for j in range(4):
                    pt, b_pt = fm_chunk(view, b_pan, KC, j * 128, hT, b_hT, 512)
                    evac(j, pt, b_pt, qbT[:, j, :], b_qbT, 512)
            elif pi == 4:
                rs0 = ((blk * 4) % RING) * 128
                for j in range(4):
                    pt, b_pt = fm_chunk(view, b_pan, KC, j * 128, hT, b_hT, 512)
                    evac(j, pt, b_pt, KBT[:, j, rs0:rs0 + 512], b_KBT, 512)
                if last_kv and os.environ.get("NOKVK") is None:
                    for t in range(4):
                        pt, b_pt = tm_tile(view, b_pan, KC, 0, 512, hT, b_hT, t * 128, 128)
                        ko, b_ko = (tmpC, b_tmpC) if t % 2 == 0 else (tmpD, b_tmpD)
                        evac(t, pt, b_pt, ko[:], b_ko, 512)
                        r0 = (blk - KV_FROM) * 512 + t * 128
                        st(o_kp[r0:r0 + 128, :], ko[:], b_ko)
            else:
                for t in range(4):
                    pt, b_pt = tm_tile(view, b_pan, KC, 0, 512, hT, b_hT, t * 128, 128)
                    slot = (blk * 4 + t) % RING
                    if last_kv and os.environ.get("NOKVV") is None:
                        ko, b_ko = (tmpC, b_tmpC) if t % 2 == 0 else (tmpD, b_tmpD)
                        evac(t, pt, b_pt, ko[:], b_ko, 512)
                        op("pool", "tensor_copy", [b_ko], [b_VB], out=VB[:, slot, :], in_=ko[:])
                        r0 = (blk - KV_FROM) * 512 + t * 128
                        st(o_vp[r0:r0 + 128, :], ko[:], b_ko)
                    else:
                        evac(t, pt, b_pt, VB[:, slot, :], b_VB, 512)
        for c in range(4 if PH >= 2 else 0):
            cs = c * 128
            for h in range(4):
                pr = h // 2
                op("dve", "tensor_scalar_mul", [b_qaT, b_hmask], [b_qz], out=qz[:, h, :], in0=qaT[:, pr, cs:cs + 128], scalar1=hmask[:, (h % 2):(h % 2) + 1])
                op("dve", "tensor_tensor", [b_qz, b_qdec], [b_qd], out=qd[:, h, :], in0=qz[:, h, :], in1=qdec[:, pr, :], op=ALU.mult)
            ps_s, b_ps_s = next_ps()
            for h in range(4):
                pr = h // 2
                op("pe", "matmul", [b_kaT, b_qz], [b_ps_s], ps_s[:, h * 128:(h + 1) * 128], lhsT=kaT[:, pr, cs:cs + 128],
                   rhs=qz[:, h, :], start=True, stop=True)
            pTt, b_pTt = pT[c % 3]
            op("dve", "tensor_tensor", [b_ps_s, b_decT], [b_pTt], out=pTt[:], in0=ps_s[:], in1=decT[:].rearrange("p a b -> p (a b)"), op=ALU.mult)
            if SUB < 2:
                continue
            ps_o, b_ps_o = next_ps()
            for h in range(4):
                pr = h // 2
                op("pe", "matmul", [b_vatok, b_pTt], [b_ps_o], ps_o[:, h * 128:(h + 1) * 128], lhsT=vatok[:, c, h * 128:(h + 1) * 128],
                   rhs=pTt[:, h * 128:(h + 1) * 128], start=True, stop=False)
                op("pe", "matmul", [b_Sbf, b_qd], [b_ps_o], ps_o[:, h * 128:(h + 1) * 128], lhsT=Sbf[:, pr, :],
                   rhs=qd[:, h, :], start=False, stop=True)
            if SUB < 3:
                continue
            ps_d, b_ps_d = next_ps()
            for h in range(4):
                pr = h // 2
                op("pe", "matmul", [b_katok, b_vatok], [b_ps_d], ps_d[:, h * 128:(h + 1) * 128], lhsT=katok[:, c, pr * 128:(pr + 1) * 128],
                   rhs=vatok[:, c, h * 128:(h + 1) * 128], start=True, stop=True)
            for h in range(4):
                hp, pr = (h % 2) * 64, h // 2
                op("dve", "scalar_tensor_tensor", [b_Sst, b_ps_d, b_Sbf, b_ps_o], [b_Sst], out=Sst[hp:hp + 64, pr, :], in0=Sst[hp:hp + 64, pr, :],
                   scalar=float(GAM[h] ** 128), in1=ps_d[hp:hp + 64, h * 128:(h + 1) * 128], op0=ALU.mult, op1=ALU.add)
            op("dve", "tensor_copy", [b_Sst], [b_Sbf], out=Sbf[:], in_=Sst[:])
            if SUB < 4:
                continue
            op("act", "copy", [b_ps_o], [b_osb], out=osb[:], in_=ps_o[:])
            op("dve", "tensor_copy", [b_osb], [b_obf], out=obf[:], in_=osb[:])
            ps_m, b_ps_m = next_ps()
            op("pe", "matmul", [b_ones_g, b_obf], [b_ps_m], ps_m[:], lhsT=ones_g[:], rhs=obf[:], start=True, stop=True)
            op("dve", "tensor_tensor", [b_osb, b_ps_m], [b_osb], out=osb[:], in0=osb[:], in1=ps_m[:], op=ALU.subtract)
            op("act", "activation", [b_osb], [b_osq], out=osq[:], in_=osb[:], func=AF.Square)
            ps_q, b_ps_q = next_ps()
            op("pe", "matmul", [b_ones_g, b_osq], [b_ps_q], ps_q[:], lhsT=ones_g[:], rhs=osq[:], start=True, stop=True)
            if SUB < 5:
                continue
            op("dve", "tensor_scalar_add", [b_ps_q], [b_tmpA], out=tmpA[:], in0=ps_q[:], scalar1=EPS)
            op("act", "activation", [b_tmpA], [b_tmpA], out=tmpA[:], in_=tmpA[:], func=AF.Sqrt)
            op("dve", "reciprocal", [b_tmpA], [b_tmpA], out=tmpA[:], in_=tmpA[:])
            op("dve", "tensor_tensor", [b_osb, b_tmpA], [b_osb], out=osb[:], in0=osb[:], in1=tmpA[:], op=ALU.mult)
            if SUB < 6:
                continue
            for h in range(4):
                op("dve", "scalar_tensor_tensor", [b_osb, b_gn, b_sgT], [b_mixT], out=mixT[:, h, cs:cs + 128], in0=osb[:, h * 128:(h + 1) * 128],
                   scalar=gn[:, h:h + 1], in1=sgT[:, h, cs:cs + 128], op0=ALU.mult, op1=ALU.mult)
        kt_hi = blk * 4 + 3
        kt_lo = max(0, blk * 4 - 16)
        for h in range(8 if PH >= 3 else 0):
            hp, pr = (h % 2) * 64, h // 2
            tht, b_tht = thtab[0]
            ld(tht[:], C["swa_tab"][h], b_tht)
            ps_o, b_ps_o = PS_A if h % 2 == 0 else PS_B
            ps_dn, b_ps_dn = PS_C
            nk = kt_hi - kt_lo + 1
            for i, kt in enumerate(range(kt_lo, kt_hi + 1)):
                o = t0 - kt * 128
                slot = kt % RING
                ps_s, b_ps_s = next_ps()
                op("pe", "matmul", [b_KBT, b_qbT], [b_ps_s], ps_s[:], lhsT=KBT[hp:hp + 64, pr, slot * 128:(slot + 1) * 128],
                   rhs=qbT[hp:hp + 64, pr, :], start=True, stop=True)
                et, b_et = eT[i % 3]
                op("act", "activation", [b_ps_s], [b_et], out=et[:], in_=ps_s[:], func=AF.Exp, scale=0.125)
                pt_, b_pt_ = pT[i % 3]
                op("pool" if i % 2 else "dve", "tensor_tensor", [b_et, b_tht], [b_pt_], out=pt_[:], in0=et[:], in1=tht[:, o + 384:o + 384 + 512], op=ALU.mult)
                op("pe", "matmul", [b_VB, b_pt_], [b_ps_o], ps_o[:], lhsT=VB[:, slot, pr * 128:(pr + 1) * 128], rhs=pt_[:],
                   start=(i == 0), stop=(i == nk - 1))
                op("pe", "matmul", [b_ones_1, b_pt_], [b_ps_dn], ps_dn[:], lhsT=ones_1[:], rhs=pt_[:], start=(i == 0), stop=(i == nk - 1))
            op("dve", "reciprocal", [b_ps_dn], [b_tmpB], out=tmpB[hp:hp + 64, :], in_=ps_dn[hp:hp + 64, :])
            op("dve", "tensor_tensor", [b_ps_o, b_tmpB], [b_mixT], out=mixT[hp:hp + 64, 4 + pr, :], in0=ps_o[hp:hp + 64, :], in1=tmpB[hp:hp + 64, :], op=ALU.mult)
        for p in range(2 if PH >= 4 else 0):
            view, b_pan = load_panel(wb_out[p], KC, 512, b_wb_out)
            for j in range(4):
                oc = p * 4 + j
                pt, b_pt = fm_chunk(view, b_pan, KC, j * 128, mixT, b_mixT, 512)
                op("dve", "tensor_tensor", [b_pt, b_xT], [b_xT], out=xT[:, oc, :], in0=pt[:], in1=xT[:, oc, :], op=ALU.add)
        if PH >= 4:
            ffn(0, 512)
        rmsnorm(2, 512)
        for c in range(4 if PH >= 5 else 0):
            cs = c * 128
            ps_y0, b_ps_y0 = PS_A
            ps_y1, b_ps_y1 = PS_B
            for qd_i in range(16):
                g0 = qd_i * 4
                kc, half = g0 // 8, (g0 % 8) // 4
                hp = half * 64
                pd1, b_pd1 = next_ps()
                pd2, b_pd2 = next_ps()
                for gi in range(4):
                    op("pe", "matmul", [b_LT1, b_hT], [b_pd1], pd1[:, gi * 128:(gi + 1) * 128], lhsT=LT1[hp:hp + 64, kc, gi, :],
                       rhs=hT[hp:hp + 64, kc, cs:cs + 128], start=True, stop=True)
                for gi in range(4):
                    op("pe", "matmul", [b_LT2, b_hT], [b_pd2], pd2[:, gi * 128:(gi + 1) * 128], lhsT=LT2[hp:hp + 64, kc, gi, :],
                       rhs=hT[hp:hp + 64, kc, cs:cs + 128], start=True, stop=True)
                ctq = CT[:, g0:g0 + 4, :].rearrange("p a b -> p (a b)")
                stq = ST[:, g0:g0 + 4, :].rearrange("p a b -> p (a b)")
                op("dve", "tensor_tensor", [b_pd1, b_CT], [b_tmpA], out=tmpA[:], in0=pd1[:], in1=ctq, op=ALU.mult)
                op("dve", "tensor_tensor", [b_pd2, b_ST], [b_tmpB], out=tmpB[:], in0=pd2[:], in1=stq, op=ALU.mult)
                op("pool", "tensor_tensor", [b_tmpA, b_tmpB], [b_tmpC], out=tmpC[:], in0=tmpA[:], in1=tmpB[:], op=ALU.add)
                for gi in range(4):
                    g = g0 + gi
                    op("dve", "tensor_tensor_scan", [b_tmpC, b_lam_abs, b_Wc], [b_tmpD], out=tmpD[:, gi * 128:(gi + 1) * 128],
                       data0=lam_abs[:, g:g + 1].to_broadcast([128, 128]), data1=tmpC[:, gi * 128:(gi + 1) * 128],
                       initial=Wc[:, g:g + 1], op0=ALU.mult, op1=ALU.add)
                at, b_at = eT[qd_i % 3]
                bt, b_bt = pT[qd_i % 3]
                op("dve", "tensor_tensor", [b_tmpD, b_CT], [b_at], out=at[:], in0=tmpD[:], in1=ctq, op=ALU.mult)
                op("pool", "tensor_tensor", [b_tmpD, b_ST], [b_bt], out=bt[:], in0=tmpD[:], in1=stq, op=ALU.mult)
                op("pool", "tensor_copy", [b_tmpD], [b_Zend], out=Zend[:, g0:g0 + 4], in_=bcast(tmpD, 512, 127, [(128, 4)]))
                for gi in range(4):
                    g = g0 + gi
                    py, b_py = (ps_y0, b_ps_y0) if g < 32 else (ps_y1, b_ps_y1)
                    col = (g % 32) * 16
                    op("pe", "matmul", [b_at, b_C1], [b_py], py[:, col:col + 16], lhsT=at[:, gi * 128:(gi + 1) * 128], rhs=C1[:, g, :], start=True, stop=False)
                    op("pe", "matmul", [b_bt, b_C2], [b_py], py[:, col:col + 16], lhsT=bt[:, gi * 128:(gi + 1) * 128], rhs=C2[:, g, :], start=False, stop=True)
            pw, b_pw = next_ps()
            op("pe", "matmul", [b_swapm, b_Zend], [b_pw], pw[:, 0:64], lhsT=swapm[:], rhs=Zend[:], start=True, stop=True)
            op("dve", "tensor_tensor", [b_pw, b_sL], [b_t64], out=t64[:], in0=pw[:, 0:64], in1=sL[:], op=ALU.mult)
            op("dve", "tensor_tensor", [b_Zend, b_cL], [b_Wc], out=Wc[:], in0=Zend[:], in1=cL[:], op=ALU.mult)
            op("dve", "tensor_add", [b_Wc, b_t64], [b_Wc], out=Wc[:], in0=Wc[:], in1=t64[:])
            ysb, b_ysb = xtok[:, 0:1024].rearrange("p (a b) -> p a b", a=2), b_xtok
            op("act", "copy", [b_ps_y0], [b_ysb], out=ysb[:, 0, :], in_=ps_y0[:])
            op("dve", "tensor_copy", [b_ps_y1], [b_ysb], out=ysb[:, 1, :], in_=ps_y1[:])
            for half in range(2):
                pt, b_pt = next_ps()
                for k4 in range(4):
                    op("pe", "transpose", [b_ysb, b_ident], [b_pt], out=pt[:, k4 * 128:(k4 + 1) * 128],
                       in_=ysb[:, half, k4 * 128:(k4 + 1) * 128], identity=ident[:])
                for k4 in range(4):
                    kc = half * 4 + k4
                    op("dve", "scalar_tensor_tensor", [b_xT, b_gd, b_rstd], [b_tmpA], out=tmpA[:, k4 * 128:(k4 + 1) * 128], in0=xT[:, kc, cs:cs + 128],
                       scalar=gd[:, kc:kc + 1], in1=rstd[:, cs:cs + 128], op0=ALU.mult, op1=ALU.mult)
                op("dve", "tensor_tensor", [b_tmpA, b_pt], [b_tmpA], out=tmpA[:], in0=tmpA[:], in1=pt[:], op=ALU.add)
                op("act", "activation", [b_tmpA], [b_tmpB], out=tmpB[:], in_=tmpA[:], func=AF.Square)
                op("dve", "tensor_scalar", [b_tmpB], [b_tmpB], out=tmpB[:], in0=tmpB[:], scalar1=0.044715, scalar2=1.0, op0=ALU.mult, op1=ALU.add)
                op("dve", "tensor_tensor", [b_tmpB, b_tmpA], [b_tmpB], out=tmpB[:], in0=tmpB[:], in1=tmpA[:], op=ALU.mult)
                op("act", "activation", [b_tmpB], [b_tmpB], out=tmpB[:], in_=tmpB[:], func=AF.Sigmoid, scale=1.5957691216)
                for k4 in range(4):
                    kc = half * 4 + k4
                    op("dve", "tensor_tensor", [b_tmpA, b_tmpB], [b_mixT], out=mixT[:, kc, cs:cs + 128], in0=tmpA[:, k4 * 128:(k4 + 1) * 128],
                       in1=tmpB[:, k4 * 128:(k4 + 1) * 128], op=ALU.mult)
        for p in range(4 if PH >= 6 else 0):
            view, b_pan = load_panel(wb_glu[p], KC, 512, b_wb_glu)
            for j in range(2):
                oc = 2 * p + j
                pv, b_pv = fm_chunk(view, b_pan, KC, j * 128, mixT, b_mixT, 512)
                pg, b_pg = fm_chunk(view, b_pan, KC, 256 + j * 128, mixT, b_mixT, 512)
                op("act", "activation", [b_pg], [b_tmpA], out=tmpA[:], in_=pg[:], func=AF.Sigmoid)
                op("dve", "tensor_tensor", [b_tmpA, b_pv], [b_tmpA], out=tmpA[:], in0=tmpA[:], in1=pv[:], op=ALU.mult)
                op("dve", "tensor_tensor", [b_tmpA, b_xT], [b_xT], out=xT[:, oc, :], in0=tmpA[:], in1=xT[:, oc, :], op=ALU.add)
        if PH >= 6:
            ffn(1, 512)
        final_out(o_yp[t0:t0 + 512, :], 512)

    st(o_retp, Sst[:], b_Sst)
    st(o_Wc, Wc[:], b_Wc); st(o_Zend, Zend[:], b_Zend)
    st(o_KBT, KBT[:], b_KBT); st(o_VB, VB[:], b_VB)
    pw, b_pw = next_ps()
    op("pe", "matmul", [b_swapm, b_Zend], [b_pw], pw[:, 0:64], lhsT=swapm[:], rhs=Zend[:], start=True, stop=True)
    op("dve", "tensor_tensor", [b_pw, b_sE], [b_t64], out=t64[:], in0=pw[:, 0:64], in1=sE[:], op=ALU.mult)
    xfin, b_xfin = sb("xfin", [128, 64])
    op("dve", "tensor_tensor", [b_Zend, b_cE], [b_xfin], out=xfin[:], in0=Zend[:], in1=cE[:], op=ALU.mult)
    op("dve", "tensor_add", [b_xfin, b_t64], [b_xfin], out=xfin[:], in0=xfin[:], in1=t64[:])
    st(o_ssp, xfin[:], b_xfin)

    if RUN_SAMPLE:
        NS = TS
        AX = mybir.AxisListType.X
        bmask, b_bmask = sb("bmask", [32, 4]); ld(bmask[:], C["bmask"], b_bmask)
        eomask, b_eomask = sb("eomask", [64, 2]); ld(eomask[:], C["eomask"], b_eomask)
        kdecS = kdec[0:32, :]
        ld(kdecS, C["kdecS"], b_kdec)
        qdecS = qdec[:, 0, 0:64].rearrange("p (a b) -> p a b", a=2)
        ld(qdecS, C["qdecS"], b_qdec)
        decTS = decT[0:32, 0, :]
        ld(decTS.rearrange("p (a b) -> p a b", a=4), C["decTS"], b_decT)
        load_x(xs, NS)
        rmsnorm(0, NS)
        kvo0, kvo1 = kvo[0][0], kvo[1][0]
        for pi in range(6):
            view, b_pan = load_panel(wb_in[pi], KC, 512, b_wb_in)
            if pi == 0:
                for j in range(2):
                    pt, b_pt = fm_chunk(view, b_pan, KC, j * 128, hT, b_hT, NS)
                    evac(j, pt, b_pt, qaT[:, j, 0:NS], b_qaT, NS)
                for j in range(2):
                    pt, b_pt = fm_chunk(view, b_pan, KC, 256 + j * 128, hT, b_hT, NS)
                    evac(j, pt, b_pt, kaT[:, j, 0:NS], b_kaT, NS, scale=0.125)
                pt, b_pt = tm_tile(view, b_pan, KC, 256, 256, hT, b_hT, 0, NS)
                op("dve", "tensor_tensor", [b_pt, b_kdec], [b_katok], out=katok[0:NS, 0, :], in0=pt[0:NS, 0:256], in1=kdecS, op=ALU.mult)
            elif pi == 1:
                pt, b_pt = tm_tile(view, b_pan, KC, 0, 512, hT, b_hT, 0, NS)
                evac(0, pt, b_pt, vatok[0:NS, 0, :], b_vatok, 512, rows=NS)
            elif pi == 2:
                for j in range(4):
                    pt, b_pt = fm_chunk(view, b_pan, KC, j * 128, hT, b_hT, NS)
                    evac(j, pt, b_pt, sgT[:, j, 0:NS], b_sgT, NS, func=AF.Silu)
            elif pi == 3:
                for j in range(4):
                    pt, b_pt = fm_chunk(view, b_pan, KC, j * 128, hT, b_hT, NS)
                    evac(j, pt, b_pt, qbT[:, j, 0:NS], b_qbT, NS)
            elif pi == 4:
                for j in range(4):
                    pt, b_pt = fm_chunk(view, b_pan, KC, j * 128, hT, b_hT, NS)
                    evac(j, pt, b_pt, KBT[:, j, 2056:2056 + NS], b_KBT, NS)
                pt, b_pt = tm_tile(view, b_pan, KC, 0, 512, hT, b_hT, 0, NS)
                evac(1, pt, b_pt, kvo0[0:NS, :], b_xtok, 512, rows=NS)
                for b in range(SB):
                    st(o_ks[b, WB - 8:WB, :], kvo0[b * 8:(b + 1) * 8, :], b_xtok)
            else:
                pt, b_pt = tm_tile(view, b_pan, KC, 0, 512, hT, b_hT, 0, NS)
                evac(1, pt, b_pt, kvo1[0:NS, :], b_xtok, 512, rows=NS)
                for b in range(SB):
                    st(o_vs[b, WB - 8:WB, :], kvo1[b * 8:(b + 1) * 8, :], b_xtok)
                    S.dma("pool", lambda e, a=VB[0:8, 16 + b, :], c_=kvo1[b * 8:(b + 1) * 8, :]: e.dma_start(out=a, in_=c_),
                          b_VB, reads=[b_xtok], writes=[b_VB])
        def SstS(b):
            t_ = tmpB if b < 2 else tmpD
            return t_[:, (b % 2) * 256:(b % 2) * 256 + 256].rearrange("p (a c) -> p a c", a=2), (b_tmpB if b < 2 else b_tmpD)

        def SbfS(b):
            t_, bb_ = pT[1] if b < 2 else pT[2]
            return t_[:, (b % 2) * 256:(b % 2) * 256 + 256].rearrange("p (a c) -> p a c", a=2), bb_
        for b in range(SB):
            sv, sbuf_ = SstS(b)
            for h in range(4):
                hp, pr = (h % 2) * 64, h // 2
                ld(sv[hp:hp + 64, pr, :], st_ret[b, h], sbuf_)
        for b in range(SB):
            sv, sbuf_ = SstS(b); bv, bbuf_ = SbfS(b)
            op("dve", "tensor_copy", [sbuf_], [bbuf_], out=bv, in_=sv)
        for h in range(4):
            pr = h // 2
            op("dve", "tensor_scalar_mul", [b_qaT, b_hmask], [b_qz], out=qz[:, h, 0:NS], in0=qaT[:, pr, 0:NS], scalar1=hmask[:, (h % 2):(h % 2) + 1])
            op("dve", "tensor_tensor", [b_qz, b_qdec], [b_qd], out=qd[:, h, 0:NS], in0=qz[:, h, 0:NS], in1=qdecS[:, pr, :], op=ALU.mult)
        ps_s, b_ps_s = next_ps()
        for h in range(4):
            pr = h // 2
            op("pe", "matmul", [b_kaT, b_qz], [b_ps_s], ps_s[0:NS, h * NS:(h + 1) * NS], lhsT=kaT[:, pr, 0:NS], rhs=qz[:, h, 0:NS], start=True, stop=True)
        pTt, b_pTt = pT[0]
        op("dve", "tensor_tensor", [b_ps_s, b_decT], [b_pTt], out=pTt[0:NS, 0:4 * NS], in0=ps_s[0:NS, 0:4 * NS], in1=decTS, op=ALU.mult)
        ps_o, b_ps_o = next_ps()
        for h in range(4):
            pr = h // 2
            op("pe", "matmul", [b_vatok, b_pTt], [b_ps_o], ps_o[:, h * NS:(h + 1) * NS], lhsT=vatok[0:NS, 0, h * 128:(h + 1) * 128],
               rhs=pTt[0:NS, h * NS:(h + 1) * NS], start=True, stop=False)
            for b in range(SB):
                bv, bbuf_ = SbfS(b)
                op("pe", "matmul", [bbuf_, b_qd], [b_ps_o], ps_o[:, h * NS + 8 * b:h * NS + 8 * b + 8], lhsT=bv[:, pr, :],
                   rhs=qd[:, h, 8 * b:8 * b + 8], start=False, stop=(b == SB - 1))
        kdm, b_kdm = eT[2]
        for b in range(SB):
            sv, sbuf_ = SstS(b)
            op("dve", "tensor_scalar_mul", [b_katok, b_bmask], [b_kdm], out=kdm[0:NS, 0:256], in0=katok[0:NS, 0, :], scalar1=bmask[0:NS, b:b + 1])
            ps_d, b_ps_d = next_ps()
            for h in range(4):
                pr = h // 2
                op("pe", "matmul", [b_kdm, b_vatok], [b_ps_d], ps_d[:, h * 128:(h + 1) * 128], lhsT=kdm[0:NS, pr * 128:(pr + 1) * 128],
                   rhs=vatok[0:NS, 0, h * 128:(h + 1) * 128], start=True, stop=True)
            for h in range(4):
                hp, pr = (h % 2) * 64, h // 2
                op("dve", "scalar_tensor_tensor", [sbuf_, b_ps_d, b_ps_o], [sbuf_], out=sv[hp:hp + 64, pr, :], in0=sv[hp:hp + 64, pr, :],
                   scalar=float(GAM[h] ** 8), in1=ps_d[hp:hp + 64, h * 128:(h + 1) * 128], op0=ALU.mult, op1=ALU.add)
            st(o_rets[b], sv, sbuf_)
        W4 = 4 * NS
        op("act", "copy", [b_ps_o], [b_osb], out=osb[:, 0:W4], in_=ps_o[:, 0:W4])
        op("dve", "tensor_copy", [b_osb], [b_obf], out=obf[:, 0:W4], in_=osb[:, 0:W4])
        ps_m, b_ps_m = next_ps()
        op("pe", "matmul", [b_ones_g, b_obf], [b_ps_m], ps_m[:, 0:W4], lhsT=ones_g[:], rhs=obf[:, 0:W4], start=True, stop=True)
        op("dve", "tensor_tensor", [b_osb, b_ps_m], [b_osb], out=osb[:, 0:W4], in0=osb[:, 0:W4], in1=ps_m[:, 0:W4], op=ALU.subtract)
        op("act", "activation", [b_osb], [b_osq], out=osq[:, 0:W4], in_=osb[:, 0:W4], func=AF.Square)
        ps_q, b_ps_q = next_ps()
        op("pe", "matmul", [b_ones_g, b_osq], [b_ps_q], ps_q[:, 0:W4], lhsT=ones_g[:], rhs=osq[:, 0:W4], start=True, stop=True)
        op("dve", "tensor_scalar_add", [b_ps_q], [b_tmpA], out=tmpA[:, 0:W4], in0=ps_q[:, 0:W4], scalar1=EPS)
        op("act", "activation", [b_tmpA], [b_tmpA], out=tmpA[:, 0:W4], in_=tmpA[:, 0:W4], func=AF.Sqrt)
        op("dve", "reciprocal", [b_tmpA], [b_tmpA], out=tmpA[:, 0:W4], in_=tmpA[:, 0:W4])
        op("dve", "tensor_tensor", [b_osb, b_tmpA], [b_osb], out=osb[:, 0:W4], in0=osb[:, 0:W4], in1=tmpA[:, 0:W4], op=ALU.mult)
        for h in range(4):
            op("dve", "scalar_tensor_tensor", [b_osb, b_gn, b_sgT], [b_mixT], out=mixT[:, h, 0:NS], in0=osb[:, h * NS:(h + 1) * NS],
               scalar=gn[:, h:h + 1], in1=sgT[:, h, 0:NS], op0=ALU.mult, op1=ALU.mult)
        tht, b_tht = thtab[0]
        ld(tht[0:64, :], C["WS"], b_tht)
        ld(tmpD[0:64, :], C["hselB"], b_tmpD)
        qpad = act[:, 18, 0:256].rearrange("p (a c) -> p a c", a=4)
        PsT = act[:, 19:22, :].rearrange("p a b -> p (a b)")[:, 0:17 * 64].rearrange("p (k c) -> p k c", c=64)
        op("dve", "memset", [], [b_act], qpad, 0.0)
        o64, o2, dparts, dtot = tmpB[0:64, 0:64], tmpB[0:64, 64:192], tmpB[0:64, 192:197], tmpB[0:64, 200:201]
        for b in range(SB):
            for kt in range(16):
                ld(xtok[:, 0:512], st_k[b, kt * 128:(kt + 1) * 128, :], b_xtok)
                pt, b_pt = next_ps()
                for pr in range(4):
                    op("pe", "transpose", [b_xtok, b_ident], [b_pt], out=pt[:, pr * 128:(pr + 1) * 128], in_=xtok[:, pr * 128:(pr + 1) * 128], identity=ident[:])
                op("act" if kt % 2 else "dve", "copy" if kt % 2 else "tensor_copy", [b_pt], [b_KBT], out=KBT[:, :, kt * 128:(kt + 1) * 128],
                   in_=pt[:, 0:512].rearrange("p (a c) -> p a c", a=4))
            op("dve", "tensor_copy", [b_KBT], [b_KBT], out=KBT[:, :, 2048:2056], in_=KBT[:, :, 2056 + 8 * b:2056 + 8 * b + 8])
            S.dma("pool", lambda e, a=VB[:, 0:16, :], c_=st_v[b].rearrange("(t p) f -> p t f", p=128): e.dma_start(out=a, in_=c_),
                  b_VB, writes=[b_VB])
            for pr in range(4):
                op("dve", "tensor_copy", [b_qbT], [b_act], out=qpad[0:64, pr, (2 * pr) * 8:(2 * pr) * 8 + 8], in_=qbT[0:64, pr, 8 * b:8 * b + 8])
                op("dve", "tensor_copy", [b_qbT], [b_act], out=qpad[64:128, pr, (2 * pr + 1) * 8:(2 * pr + 1) * 8 + 8], in_=qbT[64:128, pr, 8 * b:8 * b + 8])
            for grp in range(5):
                k0 = grp * 512
                kw = 512 if grp < 4 else 8
                ps_sc, b_ps_sc = next_ps()
                for pr in range(4):
                    op("pe", "matmul", [b_act, b_KBT], [b_ps_sc], ps_sc[0:64, 0:kw], lhsT=qpad[:, pr, :], rhs=KBT[:, pr, k0:k0 + kw],
                       start=(pr == 0), stop=(pr == 3))
                op("act", "activation", [b_ps_sc], [b_tmpA], out=tmpA[0:64, 0:kw], in_=ps_sc[0:64, 0:kw], func=AF.Exp, scale=0.125)
                op("dve", "tensor_tensor", [b_tmpA, b_tht], [b_tmpA], out=tmpA[0:64, 0:kw], in0=tmpA[0:64, 0:kw], in1=tht[0:64, k0:k0 + kw], op=ALU.mult)
                op("dve", "reduce_sum", [b_tmpA], [b_tmpB], out=dparts[:, grp:grp + 1], in_=tmpA[0:64, 0:kw], axis=AX)
                pt, b_pt = next_ps()
                if grp < 4:
                    for t4 in range(4):
                        op("pe", "transpose", [b_tmpA, b_ident], [b_pt], out=pt[:, t4 * 64:(t4 + 1) * 64], in_=tmpA[0:64, t4 * 128:(t4 + 1) * 128], identity=ident[0:64, 0:64])
                    op("act", "copy", [b_pt], [b_act], out=PsT[:, grp * 4:grp * 4 + 4, :], in_=pt[:, 0:256].rearrange("p (a c) -> p a c", a=4))
                else:
                    op("pe", "transpose", [b_tmpA, b_ident], [b_pt], out=pt[0:8, 0:64], in_=tmpA[0:64, 0:8], identity=ident[0:64, 0:64])
                    op("act", "copy", [b_pt], [b_act], out=PsT[0:8, 16, :], in_=pt[0:8, 0:64])
            ps_pv, b_ps_pv = next_ps()
            for kt in range(16):
                op("pe", "matmul", [b_act, b_VB], [b_ps_pv], ps_pv[0:64, :], lhsT=PsT[:, kt, :], rhs=VB[:, kt, :], start=(kt == 0), stop=False)
            op("pe", "matmul", [b_act, b_VB], [b_ps_pv], ps_pv[0:64, :], lhsT=PsT[0:8, 16, :], rhs=VB[0:8, 16 + b, :], start=False, stop=True)
            op("dve", "reduce_sum", [b_tmpB], [b_tmpB], out=dtot, in_=dparts, axis=AX)
            op("dve", "reciprocal", [b_tmpB], [b_tmpB], out=dtot, in_=dtot)
            op("dve", "tensor_tensor", [b_ps_pv, b_tmpD], [b_tmpA], out=tmpA[0:64, :], in0=ps_pv[0:64, :], in1=tmpD[0:64, :], op=ALU.mult)
            op("dve", "tensor_reduce", [b_tmpA], [b_tmpB], out=o64, in_=tmpA[0:64, :].rearrange("p (h d) -> p d h", h=8), axis=AX, op=ALU.add)
            for e2 in range(2):
                op("dve", "tensor_scalar", [b_tmpB, b_eomask], [b_tmpB], out=o2[:, e2 * 64:(e2 + 1) * 64], in0=o64, scalar1=dtot, scalar2=eomask[:, e2:e2 + 1],
                   op0=ALU.mult, op1=ALU.mult)
            pt, b_pt = next_ps()
            op("pe", "transpose", [b_tmpB, b_ident], [b_pt], out=pt[:, 0:64], in_=o2, identity=ident[0:64, 0:64])
            for pr in range(4):
                op("dve", "tensor_copy", [b_pt], [b_mixT], out=mixT[0:64, 4 + pr, 8 * b:8 * b + 8], in_=pt[0:64, (2 * pr) * 8:(2 * pr) * 8 + 8])
                op("dve", "tensor_copy", [b_pt], [b_mixT], out=mixT[64:128, 4 + pr, 8 * b:8 * b + 8], in_=pt[64:128, (2 * pr + 1) * 8:(2 * pr + 1) * 8 + 8])
        for p in range(2):
            view, b_pan = load_panel(wb_out[p], KC, 512, b_wb_out)
            for j in range(4):
                oc = p * 4 + j
                pt, b_pt = fm_chunk(view, b_pan, KC, j * 128, mixT, b_mixT, NS)
                op("dve", "tensor_tensor", [b_pt, b_xT], [b_xT], out=xT[:, oc, 0:NS], in0=pt[:, 0:NS], in1=xT[:, oc, 0:NS], op=ALU.add)
        ffn(0, NS)
        rmsnorm(2, NS)
        s5f = s5i[:].bitcast(F32)
        WS5 = s5f[:, 0:256].rearrange("p (a c) -> p a c", a=4)
        ZendS = s5f[:, 256:512].rearrange("p (a c) -> p a c", a=4)
        c7, b_c7, s7, b_s7 = den, b_den, abr, b_abr
        cs_small(7.0, c7, b_c7, s7, b_s7, True)
        for b in range(SB):
            ld(tmpC[0:64, 0:64], st_sr[b], b_tmpC); ld(tmpC[0:64, 64:128], st_si[b], b_tmpC)
            ld(tmpC[0:64, 128:192], st_si[b], b_tmpC); ld(tmpC[0:64, 192:256], st_sr[b], b_tmpC)
            pt, b_pt = next_ps()
            op("pe", "transpose", [b_tmpC, b_ident], [b_pt], out=pt[:, 0:64], in_=tmpC[0:64, 0:128], identity=ident[0:64, 0:64])
            op("pe", "transpose", [b_tmpC, b_ident], [b_pt], out=pt[:, 64:128], in_=tmpC[0:64, 128:256], identity=ident[0:64, 0:64])
            op("dve", "tensor_tensor", [b_pt, b_s1t], [b_t64], out=t64[:], in0=pt[:, 64:128], in1=s1t[:], op=ALU.mult)
            op("dve", "tensor_tensor", [b_pt, b_c1t], [b_s5i], out=WS5[:, b, :], in0=pt[:, 0:64], in1=c1t[:], op=ALU.mult)
            op("dve", "tensor_sub", [b_s5i, b_t64], [b_s5i], out=WS5[:, b, :], in0=WS5[:, b, :], in1=t64[:])
        ps_y0, b_ps_y0 = PS_A
        ps_y1, b_ps_y1 = PS_B
        v4 = lambda ap: ap.rearrange("p (a b c) -> p a b c", a=4, b=4)
        for qd_i in range(16):
            g0 = qd_i * 4
            kc, half = g0 // 8, (g0 % 8) // 4
            hp = half * 64
            pd1, b_pd1 = next_ps()
            pd2, b_pd2 = next_ps()
            for gi in range(4):
                op("pe", "matmul", [b_LT1, b_hT], [b_pd1], pd1[:, gi * NS:(gi + 1) * NS], lhsT=LT1[hp:hp + 64, kc, gi, :], rhs=hT[hp:hp + 64, kc, 0:NS], start=True, stop=True)
            for gi in range(4):
                op("pe", "matmul", [b_LT2, b_hT], [b_pd2], pd2[:, gi * NS:(gi + 1) * NS], lhsT=LT2[hp:hp + 64, kc, gi, :], rhs=hT[hp:hp + 64, kc, 0:NS], start=True, stop=True)
            ctq = bcast(CT, 64 * 128, g0 * 128, [(128, 4), (0, 4), (1, 8)])
            stq = bcast(ST, 64 * 128, g0 * 128, [(128, 4), (0, 4), (1, 8)])
            op("dve", "tensor_tensor", [b_pd1, b_CT], [b_tmpA], out=v4(tmpA[:, 0:128]), in0=v4(pd1[:, 0:128]), in1=ctq, op=ALU.mult)
            op("dve", "tensor_tensor", [b_pd2, b_ST], [b_tmpB], out=v4(tmpB[:, 0:128]), in0=v4(pd2[:, 0:128]), in1=stq, op=ALU.mult)
            op("pool", "tensor_tensor", [b_tmpA, b_tmpB], [b_tmpC], out=tmpC[:, 0:128], in0=tmpA[:, 0:128], in1=tmpB[:, 0:128], op=ALU.add)
            for gi in range(4):
                g = g0 + gi
                for b in range(SB):
                    c0 = gi * NS + 8 * b
                    op("dve", "tensor_tensor_scan", [b_tmpC, b_lam_abs, b_s5i], [b_tmpD], out=tmpD[:, c0:c0 + 8],
                       data0=lam_abs[:, g:g + 1].to_broadcast([128, 8]), data1=tmpC[:, c0:c0 + 8], initial=WS5[:, b, g:g + 1], op0=ALU.mult, op1=ALU.add)
            at, b_at = eT[qd_i % 2]
            bt, b_bt = pT[qd_i % 2]
            op("dve", "tensor_tensor", [b_tmpD, b_CT], [b_at], out=v4(at[:, 0:128]), in0=v4(tmpD[:, 0:128]), in1=ctq, op=ALU.mult)
            op("pool", "tensor_tensor", [b_tmpD, b_ST], [b_bt], out=v4(bt[:, 0:128]), in0=v4(tmpD[:, 0:128]), in1=stq, op=ALU.mult)
            for b in range(SB):
                op("pool", "tensor_copy", [b_tmpD], [b_s5i], out=ZendS[:, b, g0:g0 + 4], in_=bcast(tmpD, 512, 8 * b + 7, [(NS, 4)]))
            for gi in range(4):
                g = g0 + gi
                py, b_py = (ps_y0, b_ps_y0) if g < 32 else (ps_y1, b_ps_y1)
                col = (g % 32) * 16
                op("pe", "matmul", [b_at, b_C1], [b_py], py[0:NS, col:col + 16], lhsT=at[:, gi * NS:(gi + 1) * NS], rhs=C1[:, g, :], start=True, stop=False)
                op("pe", "matmul", [b_bt, b_C2], [b_py], py[0:NS, col:col + 16], lhsT=bt[:, gi * NS:(gi + 1) * NS], rhs=C2[:, g, :], start=False, stop=True)
        for b in range(SB):
            pw, b_pw = next_ps()
            op("pe", "matmul", [b_swapm, b_s5i], [b_pw], pw[:, 0:64], lhsT=swapm[:], rhs=ZendS[:, b, :], start=True, stop=True)
            op("dve", "tensor_tensor", [b_pw, b_s7], [b_t64], out=t64[:], in0=pw[:, 0:64], in1=s7[:], op=ALU.mult)
            op("dve", "tensor_tensor", [b_s5i, b_c7], [b_fre], out=fre[:], in0=ZendS[:, b, :], in1=c7[:], op=ALU.mult)
            op("dve", "tensor_add", [b_fre, b_t64], [b_fre], out=fre[:], in0=fre[:], in1=t64[:])
            st(o_sss[b], fre[:], b_fre)
        ysb = xtok[:, 0:1024].rearrange("p (a b) -> p a b", a=2)
        op("act", "copy", [b_ps_y0], [b_xtok], out=ysb[0:NS, 0, :], in_=ps_y0[0:NS, :])
        op("dve", "tensor_copy", [b_ps_y1], [b_xtok], out=ysb[0:NS, 1, :], in_=ps_y1[0:NS, :])
        for half in range(2):
            pt, b_pt = next_ps()
            for k4 in range(4):
                op("pe", "transpose", [b_xtok, b_ident], [b_pt], out=pt[:, k4 * NS:(k4 + 1) * NS], in_=ysb[0:NS, half, k4 * 128:(k4 + 1) * 128], identity=ident[0:NS, 0:NS])
            for k4 in range(4):
                kc = half * 4 + k4
                op("dve", "scalar_tensor_tensor", [b_xT, b_gd, b_rstd], [b_tmpA], out=tmpA[:, k4 * NS:(k4 + 1) * NS], in0=xT[:, kc, 0:NS],
                   scalar=gd[:, kc:kc + 1], in1=rstd[:, 0:NS], op0=ALU.mult, op1=ALU.mult)
            op("dve", "tensor_tensor", [b_tmpA, b_pt], [b_tmpA], out=tmpA[:, 0:W4], in0=tmpA[:, 0:W4], in1=pt[:, 0:W4], op=ALU.add)
            op("act", "activation", [b_tmpA], [b_tmpB], out=tmpB[:, 0:W4], in_=tmpA[:, 0:W4], func=AF.Square)
            op("dve", "tensor_scalar", [b_tmpB], [b_tmpB], out=tmpB[:, 0:W4], in0=tmpB[:, 0:W4], scalar1=0.044715, scalar2=1.0, op0=ALU.mult, op1=ALU.add)
            op("dve", "tensor_tensor", [b_tmpB, b_tmpA], [b_tmpB], out=tmpB[:, 0:W4], in0=tmpB[:, 0:W4], in1=tmpA[:, 0:W4], op=ALU.mult)
            op("act", "activation", [b_tmpB], [b_tmpB], out=tmpB[:, 0:W4], in_=tmpB[:, 0:W4], func=AF.Sigmoid, scale=1.5957691216)
            for k4 in range(4):
                kc = half * 4 + k4
                op("dve", "tensor_tensor", [b_tmpA, b_tmpB], [b_mixT], out=mixT[:, kc, 0:NS], in0=tmpA[:, k4 * NS:(k4 + 1) * NS],
                   in1=tmpB[:, k4 * NS:(k4 + 1) * NS], op=ALU.mult)
        for p in range(4):
            view, b_pan = load_panel(wb_glu[p], KC, 512, b_wb_glu)
            for j in range(2):
                oc = 2 * p + j
                pv, b_pv = fm_chunk(view, b_pan, KC, j * 128, mixT, b_mixT, NS)
                pg, b_pg = fm_chunk(view, b_pan, KC, 256 + j * 128, mixT, b_mixT, NS)
                op("act", "activation", [b_pg], [b_tmpA], out=tmpA[:, 0:NS], in_=pg[:, 0:NS], func=AF.Sigmoid)
                op("dve", "tensor_tensor", [b_tmpA, b_pv], [b_tmpA], out=tmpA[:, 0:NS], in0=tmpA[:, 0:NS], in1=pv[:, 0:NS], op=ALU.mult)
                op("dve", "tensor_tensor", [b_tmpA, b_xT], [b_xT], out=xT[:, oc, 0:NS], in0=tmpA[:, 0:NS], in1=xT[:, oc, 0:NS], op=ALU.add)
        ffn(1, NS)
        final_out(o_ys, NS)

    S.finish()
    es.close()
    return nc


_NC_CACHE = {}
LAUNCH_RANGES = [(0, 16)]


def kernel(**inputs):
    f32 = np.float32
    x_prompt = np.asarray(inputs["x_prompt"], f32)
    consts = host_consts()
    wmap = {}
    for k, shp in WEIGHT_SHAPES.items():
        a = np.asarray(inputs[k], f32)
        if k in ("norm_mix", "norm_ffn", "norm_final", "w_ffn_in", "w_ffn_out"):
            wmap[k] = np.ascontiguousarray(a).reshape(shp)
        else:
            wmap[k] = np.ascontiguousarray(a[0]).reshape(shp)
    bf = ml_dtypes.bfloat16
    state = [{"i_Sst": np.zeros((128, 2, 128), f32), "i_Wc": np.zeros((128, 64), f32), "i_Zend": np.zeros((128, 64), f32),
              "i_KBT": np.zeros((128, 4, RING * 128), bf), "i_VB": np.zeros((128, RING, 512), bf)} for _ in range(NCORES)]
    y_prompt = np.zeros((2, SEQ, D), f32)
    r = None
    for li, (lo, hi) in enumerate(LAUNCH_RANGES):
        last = li == len(LAUNCH_RANGES) - 1
        key = ("nc", lo, hi, last)
        if key not in _NC_CACHE:
            _NC_CACHE[key] = build_program(hi, RUN_SAMPLE=last, blk_lo=lo)
        nc = _NC_CACHE[key]
        in_maps = []
        for c in range(NCORES):
            m = {"xp": np.ascontiguousarray(x_prompt[c % 2])}
            bs = slice(c * SB, (c + 1) * SB)
            m["xs"] = np.ascontiguousarray(np.asarray(inputs["x_sample"], f32)[bs].reshape(TS, D))
            m["st_ret"] = np.ascontiguousarray(np.asarray(inputs["state_ret"], f32)[0, bs])
            m["st_k"] = np.ascontiguousarray(np.asarray(inputs["state_swa_k"], f32)[0, bs].reshape(SB, WB, 512))
            m["st_v"] = np.ascontiguousarray(np.asarray(inputs["state_swa_v"], f32)[0, bs].reshape(SB, WB, 512))
            m["st_sr"] = np.ascontiguousarray(np.asarray(inputs["state_ssm_re"], f32)[0, bs])
            m["st_si"] = np.ascontiguousarray(np.asarray(inputs["state_ssm_im"], f32)[0, bs])
            m.update(state[c])
            m.update(wmap)
            m.update({"c_" + k: v for k, v in consts.items()})
            in_maps.append(m)
        res = run_bass_kernel_spmd(nc, in_maps, core_ids=list(range(NCORES)))
        r = res.results
        for sq_ in range(2):
            y_prompt[sq_, lo * 512:hi * 512] = r[sq_]["o_yp"][lo * 512:hi * 512]
        for c in range(NCORES):
            state[c] = {"i_Sst": np.asarray(r[c]["o_retp"], f32).reshape(128, 2, 128), "i_Wc": np.asarray(r[c]["o_Wc"], f32),
                        "i_Zend": np.asarray(r[c]["o_Zend"], f32), "i_KBT": np.asarray(r[c]["o_KBT"]).reshape(128, 4, RING * 128),
                        "i_VB": np.asarray(r[c]["o_VB"]).reshape(128, RING, 512)}
    B = 2
    y_sample = np.concatenate([r[c]["o_ys"] for c in range(NCORES)], 0).reshape(32, 8, D)

    def unret(a):
        a = np.asarray(a).reshape(128, 2, 128)
        out = np.zeros((4, 64, 128), f32)
        for h in range(4):
            out[h] = a[(h % 2) * 64:(h % 2) * 64 + 64, h // 2, :]
        return out
    ret_p = np.stack([unret(r[0]["o_retp"]), unret(r[1]["o_retp"])])[None]
    ret_s = np.stack([unret(np.asarray(r[c]["o_rets"]).reshape(SB, 128, 2, 128)[b]) for c in range(NCORES) for b in range(SB)])[None]
    swk_p = np.stack([r[0]["o_kp"], r[1]["o_kp"]]).reshape(1, B, WB, H_B, DH_B)
    swv_p = np.stack([r[0]["o_vp"], r[1]["o_vp"]]).reshape(1, B, WB, H_B, DH_B)
    swk_s = np.concatenate([r[c]["o_ks"] for c in range(NCORES)], 0).reshape(1, 32, WB, H_B, DH_B)
    swv_s = np.concatenate([r[c]["o_vs"] for c in range(NCORES)], 0).reshape(1, 32, WB, H_B, DH_B)
    sr_p = np.stack([r[0]["o_ssp"][0:64].T, r[1]["o_ssp"][0:64].T])[None]
    si_p = np.stack([r[0]["o_ssp"][64:128].T, r[1]["o_ssp"][64:128].T])[None]
    sss = lambda c, b: np.asarray(r[c]["o_sss"]).reshape(SB, 128, 64)[b]
    sr_s = np.stack([sss(c, b)[0:64].T for c in range(NCORES) for b in range(SB)])[None]
    si_s = np.stack([sss(c, b)[64:128].T for c in range(NCORES) for b in range(SB)])[None]
    return (y_prompt, y_sample, ret_p, ret_s, swk_p, swv_p, swk_s, swv_s, sr_p, si_p, sr_s, si_s)
```

```python
import numpy as np
import concourse.bass as bass
import concourse.mybir as mybir
from concourse.bass_utils import run_bass_kernel_spmd

F32 = mybir.dt.float32
BF16 = mybir.dt.bfloat16
ALU = mybir.AluOpType
AF = mybir.ActivationFunctionType

D = 1024
KC = D // 128
NCORES = 8
TP = 2048
TS = 32
SB = 4
WB = 2048
H_A, DK_A, DV_A = 4, 64, 128
H_B, DH_B = 8, 64
AB_IN = 3072
EPS = 1e-6


class Buf:
    __slots__ = ("name", "last_w", "readers")

    def __init__(self, name):
        self.name = name
        self.last_w = None
        self.readers = []


class Sched:
    ENGS = ("pe", "act", "dve", "pool", "sp")

    def __init__(self, nc):
        self.nc = nc
        self.ops = {e: [] for e in self.ENGS}
        self.dma_sems = []
        self.buf_dma = {}

    def _deps(self, reads, writes):
        deps = []
        for b in reads:
            if b.last_w is not None:
                deps.append(b.last_w)
        for b in writes:
            if b.last_w is not None:
                deps.append(b.last_w)
            deps.extend(b.readers)
        return deps

    def _commit(self, tok, reads, writes):
        for b in reads:
            b.readers = [r for r in b.readers if not (r[0] == tok[0] and r[1] == tok[1])]
            b.readers.append(tok)
        for b in writes:
            b.last_w = tok
            b.readers = []

    def op(self, eng, fn, reads=(), writes=()):
        deps = self._deps(reads, writes)
        idx = len(self.ops[eng])
        if eng == "pe":
            deps = [d for d in deps if not (d[0] == "e" and d[1] == "pe")]
        self.ops[eng].append({"fn": fn, "deps": deps, "sig": False, "dma": None})
        tok = ("e", eng, idx)
        self._commit(tok, reads, writes)
        return tok

    def dma(self, eng, fn, key, reads=(), writes=()):
        deps = self._deps(reads, writes)
        if key not in self.buf_dma:
            self.buf_dma[key] = [len(self.buf_dma), 0]
        ent = self.buf_dma[key]
        ent[1] += 16
        tok = ("d", ent[0], ent[1])
        self.ops[eng].append({"fn": fn, "deps": deps, "sig": False, "dma": ent[0]})
        self._commit(tok, reads, writes)
        return tok

    def finish(self, final_waits_eng="sp"):
        nc = self.nc
        for e in self.ENGS:
            for o in self.ops[e]:
                for d in o["deps"]:
                    if d[0] == "e":
                        self.ops[d[1]][d[2]]["sig"] = True
        cnt = {}
        for e in self.ENGS:
            c = 0
            for o in self.ops[e]:
                if o["sig"]:
                    c += 1
                o["cnt"] = c
            cnt[e] = c
        n_dma = len(self.buf_dma)
        from contextlib import ExitStack
        with ExitStack() as st:
            esem = {e: st.enter_context(nc.semaphore("es_" + e)) for e in self.ENGS}
            dsem = [st.enter_context(nc.semaphore("ds_%d" % i)) for i in range(n_dma)]
            block = st.enter_context(nc.Block())
            ops = self.ops
            finals = [(ent[0], ent[1]) for ent in self.buf_dma.values()]

            def emit(e, eng):
                waited_e = {}
                waited_d = {}
                for o in ops[e]:
                    need_e, need_d = {}, {}
                    for d in o["deps"]:
                        if d[0] == "e":
                            v = ops[d[1]][d[2]]["cnt"]
                            if v > need_e.get(d[1], 0):
                                need_e[d[1]] = v
                        else:
                            if d[2] > need_d.get(d[1], 0):
                                need_d[d[1]] = d[2]
                    for pe_, v in need_e.items():
                        if v > waited_e.get(pe_, 0):
                            eng.wait_ge(esem[pe_], v)
                            waited_e[pe_] = v
                    for si, v in need_d.items():
                        if v > waited_d.get(si, 0):
                            eng.wait_ge(dsem[si], v)
                            waited_d[si] = v
                    ins = o["fn"](eng)
                    if o["dma"] is not None:
                        ins.then_inc(dsem[o["dma"]], 16)
                    elif o["sig"]:
                        ins.then_inc(esem[e], 1)
                if e == final_waits_eng:
                    for si, v in finals:
                        eng.wait_ge(dsem[si], v)

            @block.tensor
            def _(eng):
                emit("pe", eng)

            @block.scalar
            def _(eng):
                emit("act", eng)

            @block.vector
            def _(eng):
                emit("dve", eng)

            @block.gpsimd
            def _(eng):
                emit("pool", eng)

            @block.sync
            def _(eng):
                emit("sp", eng)


import math
import os
import ml_dtypes
from contextlib import ExitStack

SEQ = 8192
NBLK = SEQ // 512
D_FF = 2816
FC = D_FF // 128
GAM = [1.0 - 2.0 ** (-5 - h) for h in range(4)]
SLOPES = [2.0 ** (-8.0 * (h + 1) / 8) for h in range(8)]
TW = 2944
RING = 20
TWO_PI = 2.0 * math.pi


def host_consts():
    f32 = np.float32
    c = {}
    c["ident_in"] = np.eye(128, dtype=f32)
    m = np.arange(128)[:, None]
    n = np.arange(128)[None, :]
    decT = np.zeros((128, 4, 128), np.float64)
    for h in range(4):
        decT[:, h, :] = np.where(n >= m, GAM[h] ** np.maximum(n - m, 0), 0.0)
    c["decT"] = decT.astype(f32)
    qdec = np.zeros((128, 2, 128), np.float64)
    for p in range(128):
        for pr in range(2):
            h = 2 * pr + p // 64
            qdec[p, pr, :] = GAM[h] ** (np.arange(128) + 1.0)
    c["qdec"] = qdec.astype(f32)
    kdec = np.zeros((128, 256), np.float64)
    for h in range(4):
        kdec[:, h * 64:(h + 1) * 64] = (GAM[h] ** (127.0 - np.arange(128)))[:, None] * 0.125
    c["kdec"] = kdec.astype(f32)
    jl = np.arange(128)[:, None]
    x = np.arange(TW)[None, :]
    dl = x - jl - 384
    cnt = ((dl <= 128).astype(np.float64) + ((dl % 4 == 0) & (dl <= 512)) + ((dl % 16 == 0) & (dl <= 2048)))
    valid = (dl >= 0) & (dl <= 2048)
    tab = np.zeros((8, 128, TW), np.float64)
    for h in range(8):
        tab[h] = np.where(valid, cnt * np.exp(-SLOPES[h] * np.maximum(dl, 0)), 0.0)
    c["swa_tab"] = tab.astype(ml_dtypes.bfloat16)
    sgn = np.ones((128, 1), f32); sgn[64:] = -1.0
    c["sgn"] = sgn
    c["tau"] = np.tile(np.arange(128, dtype=f32)[None, :], (128, 1))
    sw = np.zeros((128, 128), f32)
    for p in range(64):
        sw[p, p + 64] = 1.0; sw[p + 64, p] = 1.0
    c["swapm"] = sw
    rm = np.zeros((128, 4), f32)
    for p in range(128):
        rm[p, (p % 64) // 16] = 1.0
    c["rowmask"] = rm
    hm = np.zeros((128, 2), f32); hm[:64, 0] = 1.0; hm[64:, 1] = 1.0
    c["hmask"] = hm
    p32 = np.arange(32)
    kdS = np.zeros((32, 256), np.float64)
    for h in range(4):
        kdS[:, h * 64:(h + 1) * 64] = (GAM[h] ** (7.0 - (p32 % 8)))[:, None] * 0.125
    c["kdecS"] = kdS.astype(f32)
    qdS = np.zeros((128, 2, 32), np.float64)
    for p in range(128):
        for pr in range(2):
            qdS[p, pr, :] = GAM[2 * pr + p // 64] ** ((p32 % 8) + 1.0)
    c["qdecS"] = qdS.astype(f32)
    dS = np.zeros((32, 4, 32), np.float64)
    mm, nn = p32[:, None], p32[None, :]
    for h in range(4):
        dS[:, h, :] = np.where((mm // 8 == nn // 8) & (nn >= mm), GAM[h] ** np.maximum(nn - mm, 0), 0.0)
    c["decTS"] = dS.astype(f32)
    bm = np.zeros((32, 4), f32)
    bm[p32, p32 // 8] = 1.0
    c["bmask"] = bm
    r64 = np.arange(64)
    hh, tt = r64 // 8, r64 % 8
    jj = np.arange(2056)[None, :]
    dls = 2048 + tt[:, None] - jj
    cnts = ((dls <= 128).astype(np.float64) + ((dls % 4 == 0) & (dls <= 512)) + ((dls % 16 == 0) & (dls <= 2048)))
    ws = np.where((dls >= 0) & (dls <= 2048), cnts * np.exp(-np.array(SLOPES)[hh][:, None] * np.maximum(dls, 0)), 0.0)
    wsp = np.zeros((64, TW), np.float64); wsp[:, :2056] = ws
    c["WS"] = wsp.astype(ml_dtypes.bfloat16)
    hs = np.zeros((64, 512), f32)
    for r in range(64):
        hs[r, (r // 8) * 64:(r // 8) * 64 + 64] = 1.0
    c["hselB"] = hs
    eo = np.zeros((64, 2), f32); eo[:, 0] = (hh % 2 == 0); eo[:, 1] = (hh % 2 == 1)
    c["eomask"] = eo
    return c


CONST_SHAPES = {"ident_in": ([128, 128], F32), "decT": ([128, 4, 128], F32), "qdec": ([128, 2, 128], F32),
                "kdec": ([128, 256], F32), "swa_tab": ([8, 128, TW], BF16), "sgn": ([128, 1], F32),
                "tau": ([128, 128], F32), "swapm": ([128, 128], F32), "rowmask": ([128, 4], F32), "hmask": ([128, 2], F32),
                "kdecS": ([32, 256], F32), "qdecS": ([128, 2, 32], F32), "decTS": ([32, 4, 32], F32), "bmask": ([32, 4], F32),
                "WS": ([64, TW], BF16), "hselB": ([64, 512], F32), "eomask": ([64, 2], F32)}

WEIGHT_SHAPES = {"norm_mix": [2, D], "norm_ffn": [2, D], "norm_final": [D], "w_in_ab": [D, AB_IN], "ret_gn": [512],
                 "w_out_ab": [D, D], "ssm_lam_re": [64, 64], "ssm_lam_im": [64, 64], "ssm_log_step": [64],
                 "ssm_b_re": [64, 64, 16], "ssm_b_im": [64, 64, 16], "ssm_c_re": [64, 16, 64], "ssm_c_im": [64, 16, 64],
                 "ssm_d": [D], "w_glu": [D, 2 * D], "w_ffn_in": [2, D, 2 * D_FF], "w_ffn_out": [2, D_FF, D]}


def build_program(nblk_run=NBLK, PH=9, sim=False, SUB=9, RUN_SAMPLE=True, KV_FROM=NBLK - 4, blk_lo=0):
    nc = bass.Bass("TRN2", target_bir_lowering=False)
    S = Sched(nc)
    es = ExitStack()

    def din(name, shape, dt=F32):
        return nc.dram_tensor(name, list(shape), dt, kind="ExternalInput").ap()

    def dout(name, shape):
        return nc.dram_tensor(name, list(shape), F32, kind="ExternalOutput").ap()

    xp = din("xp", [SEQ, D])
    W = {k: din(k, v) for k, v in WEIGHT_SHAPES.items()}
    C = {k: din("c_" + k, v[0], v[1]) for k, v in CONST_SHAPES.items()}
    o_yp = dout("o_yp", [SEQ, D])
    o_kp = dout("o_kp", [WB, 512])
    o_vp = dout("o_vp", [WB, 512])
    o_retp = dout("o_retp", [128, 2, 128])
    o_ssp = dout("o_ssp", [128, 64])
    i_Sst = din("i_Sst", [128, 2, 128]); i_Wc = din("i_Wc", [128, 64]); i_Zend = din("i_Zend", [128, 64])
    i_KBT = din("i_KBT", [128, 4, RING * 128], BF16); i_VB = din("i_VB", [128, RING, 512], BF16)
    if len(LAUNCH_RANGES) > 1:
        o_Wc = dout("o_Wc", [128, 64]); o_Zend = dout("o_Zend", [128, 64])
        o_KBT = nc.dram_tensor("o_KBT", [128, 4, RING * 128], BF16, kind="ExternalOutput").ap()
        o_VB = nc.dram_tensor("o_VB", [128, RING, 512], BF16, kind="ExternalOutput").ap()
    xs = din("xs", [TS, D])
    st_ret = din("st_ret", [SB, 4, 64, 128])
    st_k = din("st_k", [SB, WB, 512]); st_v = din("st_v", [SB, WB, 512])
    st_sr = din("st_sr", [SB, 64, 64]); st_si = din("st_si", [SB, 64, 64])
    o_ys = dout("o_ys", [TS, D])
    o_rets = dout("o_rets", [SB, 128, 2, 128])
    o_ks = dout("o_ks", [SB, WB, 512]); o_vs = dout("o_vs", [SB, WB, 512])
    o_sss = dout("o_sss", [SB, 128, 64])

    def dscr(name, shape):
        if sim:
            return din(name, shape, BF16), Buf(name)
        t = nc.dram_tensor(name, list(shape), BF16)
        return t.ap(), Buf(name)
    wb_in, b_wb_in = dscr("wb_in", [6, 128, KC, 512])
    wb_out, b_wb_out = dscr("wb_out", [2, 128, KC, 512])
    wb_glu, b_wb_glu = dscr("wb_glu", [4, 128, KC, 512])
    wb_ffi, b_wb_ffi = dscr("wb_ffi", [2, FC // 2, 128, KC, 512])
    wb_ffo, b_wb_ffo = dscr("wb_ffo", [2, KC, 128, FC, 128])

    def cast_piece(dst_ap, src_ap, b_dst, kcn):
        if sim:
            return
        S.dma("pool", lambda e, a=dst_ap, b=src_ap.rearrange("(k p) n -> p k n", p=128): e.dma_start(out=a, in_=b), b_dst, writes=[b_dst])
    for pi in range(6):
        cast_piece(wb_in[pi], W["w_in_ab"][:, pi * 512:(pi + 1) * 512], b_wb_in, KC)
    for pi in range(2):
        cast_piece(wb_out[pi], W["w_out_ab"][:, pi * 512:(pi + 1) * 512], b_wb_out, KC)
    for l in range(2):
        for p in range(FC // 2):
            cast_piece(wb_ffi[l, p][:, :, 0:256], W["w_ffn_in"][l][:, p * 256:(p + 1) * 256], b_wb_ffi, KC)
            cast_piece(wb_ffi[l, p][:, :, 256:512], W["w_ffn_in"][l][:, D_FF + p * 256:D_FF + (p + 1) * 256], b_wb_ffi, KC)
        for oc in range(KC):
            cast_piece(wb_ffo[l, oc], W["w_ffn_out"][l][:, oc * 128:(oc + 1) * 128], b_wb_ffo, FC)
    for p in range(4):
        cast_piece(wb_glu[p][:, :, 0:256], W["w_glu"][:, p * 256:(p + 1) * 256], b_wb_glu, KC)
        cast_piece(wb_glu[p][:, :, 256:512], W["w_glu"][:, D + p * 256:D + (p + 1) * 256], b_wb_glu, KC)

    def sb(name, shape, dt=F32):
        t = es.enter_context(nc.sbuf_tensor(name, list(shape), dt))
        return t, Buf(name)

    def op(eng, method, reads, writes, *a, **kw):
        return S.op(eng, lambda e, m=method, a=a, kw=kw: getattr(e, m)(*a, **kw), reads=reads, writes=writes)

    def ld(dst_ap, src_ap, b_dst, eng="sp", **kw):
        return S.dma(eng, lambda e, a=dst_ap, b=src_ap, kw=kw: e.dma_start(out=a, in_=b, **kw), b_dst, writes=[b_dst])

    def st(dst_ap, src_ap, b_src, eng="sp", extra_reads=()):
        return S.dma(eng, lambda e, a=dst_ap, b=src_ap: e.dma_start(out=a, in_=b), b_src, reads=[b_src] + list(extra_reads))

    def bcast(t, free_total, off, dims):
        return bass.AP(t, off, [[free_total, 128]] + [[s_, c_] for s_, c_ in dims])

    ident, b_ident = sb("ident", [128, 128])
    ld(ident[:], C["ident_in"], b_ident)
    ones_d, b_ones_d = sb("ones_d", [128, 128], BF16)
    ones_g, b_ones_g = sb("ones_g", [128, 128], BF16)
    ones_1, b_ones_1 = sb("ones_1", [128, 128], BF16)
    op("dve", "memset", [], [b_ones_d], ones_d[:], 1.0 / D)
    op("dve", "memset", [], [b_ones_g], ones_g[:], 1.0 / 128)
    op("dve", "memset", [], [b_ones_1], ones_1[:], 1.0)
    gvec, b_gvec = sb("gvec", [128, 5, KC])
    for i, (nm, l) in enumerate([("norm_mix", 0), ("norm_ffn", 0), ("norm_mix", 1), ("norm_ffn", 1)]):
        ld(gvec[:, i, :], W[nm][l].rearrange("(k p) -> p k", p=128), b_gvec, allow_slow_non_contiguous=True)
    ld(gvec[:, 4, :], W["norm_final"].rearrange("(k p) -> p k", p=128), b_gvec, allow_slow_non_contiguous=True)
    gn, b_gn = sb("gn", [128, 4])
    ld(gn[:], W["ret_gn"].rearrange("(h p) -> p h", p=128), b_gn, allow_slow_non_contiguous=True)
    dvec, b_dvec = sb("dvec", [128, KC])
    ld(dvec[:], W["ssm_d"].rearrange("(k p) -> p k", p=128), b_dvec, allow_slow_non_contiguous=True)
    decT, b_decT = sb("decT", [128, 4, 128]); ld(decT[:], C["decT"], b_decT)
    qdec, b_qdec = sb("qdec", [128, 2, 128]); ld(qdec[:], C["qdec"], b_qdec)
    kdec, b_kdec = sb("kdec", [128, 256]); ld(kdec[:], C["kdec"], b_kdec)
    sgn, b_sgn = sb("sgn", [128, 1]); ld(sgn[:], C["sgn"], b_sgn)
    tau, b_tau = sb("tau", [128, 128]); ld(tau[:], C["tau"], b_tau)
    swapm, b_swapm = sb("swapm", [128, 128]); ld(swapm[:], C["swapm"], b_swapm)
    rowmask, b_rowmask = sb("rowmask", [128, 4]); ld(rowmask[:], C["rowmask"], b_rowmask)
    hmask, b_hmask = sb("hmask", [128, 2]); ld(hmask[:], C["hmask"], b_hmask)

    b_d2d = Buf("d2d")
    if RUN_SAMPLE:
        for b in range(SB):
            S.dma("sp", lambda e, a=o_ks[b, 0:WB - 8, :], c_=st_k[b, 8:WB, :]: e.dma_start(out=a, in_=c_), b_d2d)
            S.dma("sp", lambda e, a=o_vs[b, 0:WB - 8, :], c_=st_v[b, 8:WB, :]: e.dma_start(out=a, in_=c_), b_d2d)
    psb = []
    for i in range(8):
        t = es.enter_context(nc.psum_tensor("ps%d" % i, [128, 512], F32))
        psb.append((t, Buf("ps%d" % i)))
    rr = [0]

    def next_ps():
        i = rr[0] % 5
        rr[0] += 1
        return psb[i]
    PS_A, PS_B, PS_C = psb[5], psb[6], psb[7]

    PANEL_EL = 4096
    panels = [sb("panel%d" % i, [128, PANEL_EL], BF16) for i in range(2)]
    prr = [0]

    def load_panel(src_ap, kcn, w, b_src):
        t, b = panels[prr[0] % 2]
        prr[0] += 1
        view = t[:, 0:kcn * w].rearrange("p (k n) -> p k n", k=kcn)
        q = "sp" if (prr[0] % 2) == 0 else "pool"
        S.dma(q, lambda e, a=t[:, 0:kcn * w], s_=src_ap.rearrange("p k n -> p (k n)"): e.dma_start(out=a, in_=s_), b, reads=[b_src], writes=[b])
        return view, b

    def fm_chunk(view, b_pan, kcn, c0, rhs_t, b_rhs, ntok, extra=None):
        pt, b_pt = next_ps()
        for kc in range(kcn):
            op("pe", "matmul", [b_pan, b_rhs], [b_pt], pt[:, 0:ntok], lhsT=view[:, kc, c0:c0 + 128],
               rhs=rhs_t[:, kc, 0:ntok], start=(kc == 0), stop=(kc == kcn - 1))
        return pt, b_pt

    def tm_tile(view, b_pan, kcn, c0, w, lhs_t, b_lhs, t0, rows):
        pt, b_pt = next_ps()
        for kc in range(kcn):
            op("pe", "matmul", [b_pan, b_lhs], [b_pt], pt[0:rows, 0:w], lhsT=lhs_t[:, kc, t0:t0 + rows],
               rhs=view[:, kc, c0:c0 + w], start=(kc == 0), stop=(kc == kcn - 1))
        return pt, b_pt

    xtok, b_xtok = sb("xtok", [128, D])
    xT, b_xT = sb("xT", [128, KC, 512])
    rstd, b_rstd = sb("rstd", [128, 512])
    hT, b_hT = sb("hT", [128, KC, 512], BF16)
    sq, b_sq = hT, b_hT
    mixT, b_mixT = sb("mixT", [128, KC, 512], BF16)
    act, b_act = sb("act", [128, FC, 512], BF16)
    qaT, b_qaT = act[:, 0:2, :], b_act
    kaT, b_kaT = act[:, 2:4, :], b_act
    vatok, b_vatok = act[:, 4:8, :], b_act
    sgT, b_sgT = act[:, 8:12, :], b_act
    qbT, b_qbT = act[:, 12:16, :], b_act
    katok, b_katok = act[:, 16:18, :].rearrange("p a (b c) -> p (a b) c", c=256), b_act
    KBT, b_KBT = sb("KBT", [128, 4, RING * 128], BF16)
    VB, b_VB = sb("VB", [128, RING, 512], BF16)
    kvo = [(xtok[:, 0:512], b_xtok), (xtok[:, 512:1024], b_xtok)]
    tmpA, b_tmpA = sb("tmpA", [128, 512])
    tmpB, b_tmpB = sb("tmpB", [128, 512])
    tmpC, b_tmpC = sb("tmpC", [128, 512])
    tmpD, b_tmpD = sb("tmpD", [128, 512])
    pT = [sb("pT%d" % i, [128, 512], BF16) for i in range(3)]
    eT = [sb("eT%d" % i, [128, 512], BF16) for i in range(3)]
    thtab = [sb("thtab%d" % i, [128, TW], BF16) for i in range(1)]
    Sst, b_Sst = sb("Sst", [128, 2, 128])
    Sbf, b_Sbf = sb("Sbf", [128, 2, 128], BF16)
    op("dve", "memset", [], [b_Sst], Sst[:], 0.0)
    op("dve", "memset", [], [b_Sbf], Sbf[:], 0.0)
    qz, b_qz = sb("qz", [128, 4, 128], BF16)
    qd, b_qd = sb("qd", [128, 4, 128], BF16)
    osb, b_osb = tmpC, b_tmpC
    obf, b_obf = eT[0]
    osq, b_osq = eT[1]

    lam_abs, b_lam_abs = sb("lam_abs", [128, 64])
    th, b_th = sb("th", [128, 64])
    thS, b_thS = sb("thS", [128, 64])
    CT, b_CT = sb("CT", [128, 64, 128], BF16)
    ST, b_ST = sb("ST", [128, 64, 128], BF16)
    LT1, b_LT1 = sb("LT1", [128, KC, 4, 128], BF16)
    LT2, b_LT2 = sb("LT2", [128, KC, 4, 128], BF16)
    C1, b_C1 = sb("C1", [128, 64, 16], BF16)
    C2, b_C2 = sb("C2", [128, 64, 16], BF16)
    cL, b_cL = sb("cL", [128, 64]); sL, b_sL = sb("sL", [128, 64])
    cE, b_cE = sb("cE", [128, 64]); sE, b_sE = sb("sE", [128, 64])
    gd, b_gd = sb("gd", [128, KC])
    Wc, b_Wc = sb("Wc", [128, 64])
    Zend, b_Zend = sb("Zend", [128, 64])
    op("dve", "memset", [], [b_Wc], Wc[:], 0.0)
    op("dve", "memset", [], [b_Zend], Zend[:], 0.0)
    setup_es = ExitStack()

    def sbt(name, shape, dt=F32):
        t = setup_es.enter_context(nc.sbuf_tensor(name, list(shape), dt))
        return t, Buf(name)

    lamT, b_lamT = sb("lamT", [64, 256])
    ld(lamT[:, 0:64], W["ssm_lam_re"], b_lamT); ld(lamT[:, 64:128], W["ssm_lam_re"], b_lamT)
    ld(lamT[:, 128:192], W["ssm_lam_im"], b_lamT); ld(lamT[:, 192:256], W["ssm_lam_im"], b_lamT)
    lre, b_lre = sb("lre", [128, 64]); lim, b_lim = sb("lim", [128, 64])
    for src0, dst, b_dst in ((0, lre, b_lre), (128, lim, b_lim)):
        pt, b_pt = next_ps()
        op("pe", "transpose", [b_lamT, b_ident], [b_pt], out=pt[:, 0:64], in_=lamT[:, src0:src0 + 128], identity=ident[0:64, 0:64])
        op("dve", "tensor_copy", [b_pt], [b_dst], out=dst[:], in_=pt[:, 0:64])
    dtt, b_dtt = sb("dtt", [128, 64])
    ld(dtt[:], W["ssm_log_step"].partition_broadcast(128), b_dtt)
    op("act", "activation", [b_dtt], [b_dtt], out=dtt[:], in_=dtt[:], func=AF.Exp)
    op("dve", "tensor_mul", [b_lim, b_dtt], [b_th], out=th[:], in0=lim[:], in1=dtt[:])
    op("dve", "tensor_scalar_mul", [b_th, b_sgn], [b_thS], out=thS[:], in0=th[:], scalar1=sgn[:, 0:1])
    rho, b_rho = sb("rho", [128, 64])
    op("dve", "tensor_mul", [b_lre, b_dtt], [b_rho], out=rho[:], in0=lre[:], in1=dtt[:])
    op("act", "activation", [b_rho], [b_lam_abs], out=lam_abs[:], in_=rho[:], func=AF.Exp)

    s5a, b_s5a = tmpA, b_tmpA
    s5b, b_s5b = tmpB, b_tmpB
    s5i, b_s5i = sb("s5i", [128, 512], mybir.dt.int32)

    def sin_of(dst_ap, ang_ap, n, b_dst, reads):
        shp = ang_ap.shape
        kb = s5b[:, 0:n] if len(shp) == 2 else s5b[:, 0:n].rearrange("p (a b) -> p a b", a=shp[1])
        ki = s5i[:, 0:n] if len(shp) == 2 else s5i[:, 0:n].rearrange("p (a b) -> p a b", a=shp[1])
        op("dve", "tensor_scalar_mul", reads, [b_s5b], out=kb, in0=ang_ap, scalar1=1.0 / TWO_PI)
        op("dve", "tensor_copy", [b_s5b], [b_s5i], out=ki, in_=kb)
        op("dve", "tensor_copy", [b_s5i], [b_s5b], out=kb, in_=ki)
        op("dve", "scalar_tensor_tensor", [b_s5b] + reads, [b_s5b], out=kb, in0=kb, scalar=-TWO_PI, in1=ang_ap,
           op0=ALU.mult, op1=ALU.add)
        op("dve", "tensor_scalar", [b_s5b], [b_s5b], out=kb, in0=kb, scalar1=-3.14159, scalar2=3.14159, op0=ALU.max, op1=ALU.min)
        op("act", "activation", [b_s5b], [b_dst], out=dst_ap, in_=kb, func=AF.Sin)

    for g0 in range(0, 64, 4):
        angv = s5a[:, 0:512].rearrange("p (a b) -> p a b", a=4)
        for (thsrc, b_thsrc, dst, b_dst, shift) in ((th, b_th, CT, b_CT, math.pi / 2), (thS, b_thS, ST, b_ST, 0.0)):
            op("dve", "tensor_tensor", [b_thsrc, b_tau], [b_s5a], out=angv, in0=bcast(thsrc, 64, g0, [(1, 4), (0, 128)]),
               in1=bcast(tau, 128, 0, [(0, 4), (1, 128)]), op=ALU.mult)
            if shift:
                op("dve", "tensor_scalar_add", [b_s5a], [b_s5a], out=angv, in0=angv, scalar1=shift)
            sin_of(dst[:, g0:g0 + 4, :], angv, 512, b_dst, [b_s5a])

    def cs_small(mult, cdst, b_c, sdst, b_s, neg_sin):
        a = s5a[:, 0:64]
        op("dve", "tensor_scalar", [b_th], [b_s5a], out=a, in0=th[:], scalar1=float(mult), scalar2=math.pi / 2,
           op0=ALU.mult, op1=ALU.add)
        sin_of(cdst[:], a, 64, b_c, [b_s5a])
        op("dve", "tensor_scalar_mul", [b_thS], [b_s5a], out=a, in0=thS[:], scalar1=(-float(mult) if neg_sin else float(mult)))
        sin_of(sdst[:], a, 64, b_s, [b_s5a])
    cs_small(128.0, cL, b_cL, sL, b_sL, True)
    cs_small(127.0, cE, b_cE, sE, b_sE, True)
    c1t, b_c1t = sb("c1t", [128, 64]); s1t, b_s1t = sb("s1t", [128, 64])
    cs_small(1.0, c1t, b_c1t, s1t, b_s1t, False)
    fre, b_fre = sb("fre", [128, 64]); fim, b_fim = sb("fim", [128, 64])
    abr, b_abr = sb("abr", [128, 64]); abi, b_abi = sb("abi", [128, 64]); den, b_den = sb("den", [128, 64])
    t64, b_t64 = sb("t64", [128, 64])
    op("dve", "tensor_mul", [b_lam_abs, b_c1t], [b_abr], out=abr[:], in0=lam_abs[:], in1=c1t[:])
    op("dve", "tensor_scalar_add", [b_abr], [b_abr], out=abr[:], in0=abr[:], scalar1=-1.0)
    op("dve", "tensor_mul", [b_lam_abs, b_s1t], [b_abi], out=abi[:], in0=lam_abs[:], in1=s1t[:])
    op("dve", "tensor_scalar_mul", [b_abi, b_sgn], [b_abi], out=abi[:], in0=abi[:], scalar1=sgn[:, 0:1])
    op("dve", "tensor_mul", [b_lre], [b_den], out=den[:], in0=lre[:], in1=lre[:])
    op("dve", "tensor_mul", [b_lim], [b_t64], out=t64[:], in0=lim[:], in1=lim[:])
    op("dve", "tensor_add", [b_den, b_t64], [b_den], out=den[:], in0=den[:], in1=t64[:])
    op("dve", "reciprocal", [b_den], [b_den], out=den[:], in_=den[:])
    op("dve", "tensor_mul", [b_abr, b_lre], [b_fre], out=fre[:], in0=abr[:], in1=lre[:])
    op("dve", "tensor_mul", [b_abi, b_lim], [b_t64], out=t64[:], in0=abi[:], in1=lim[:])
    op("dve", "tensor_add", [b_fre, b_t64], [b_fre], out=fre[:], in0=fre[:], in1=t64[:])
    op("dve", "tensor_mul", [b_fre, b_den], [b_fre], out=fre[:], in0=fre[:], in1=den[:])
    op("dve", "tensor_mul", [b_abi, b_lre], [b_fim], out=fim[:], in0=abi[:], in1=lre[:])
    op("dve", "tensor_mul", [b_abr, b_lim], [b_t64], out=t64[:], in0=abr[:], in1=lim[:])
    op("dve", "tensor_sub", [b_fim, b_t64], [b_fim], out=fim[:], in0=fim[:], in1=t64[:])
    op("dve", "tensor_mul", [b_fim, b_den], [b_fim], out=fim[:], in0=fim[:], in1=den[:])
    fiS, b_fiS = sb("fiS", [128, 64])
    op("dve", "tensor_scalar_mul", [b_fim, b_sgn], [b_fiS], out=fiS[:], in0=fim[:], scalar1=sgn[:, 0:1])
    bre = W["ssm_b_re"].rearrange("g n q -> n g q"); bim = W["ssm_b_im"].rearrange("g n q -> n g q")
    v3 = lambda t, c0: t[:, c0:c0 + 128].rearrange("p (a b) -> p a b", a=8)
    for kc in range(KC):
        B1k, B2k = v3(tmpC, 0), v3(tmpD, 0)
        FBak, FBbk, tFk = v3(tmpA, 0), v3(tmpB, 0), v3(tmpA, 128)
        gsl = slice(kc * 8, (kc + 1) * 8)
        ld(B1k[0:64], bre[:, gsl, :], b_tmpC); ld(B1k[64:128], bim[:, gsl, :], b_tmpC)
        ld(B2k[0:64], bim[:, gsl, :], b_tmpD); ld(B2k[64:128], bre[:, gsl, :], b_tmpD)
        frb = bcast(fre, 64, kc * 8, [(1, 8), (0, 16)]); fib = bcast(fiS, 64, kc * 8, [(1, 8), (0, 16)])
        op("dve", "tensor_tensor", [b_tmpC, b_fre], [b_tmpA], out=FBak, in0=B1k, in1=frb, op=ALU.mult)
        op("dve", "tensor_tensor", [b_tmpD, b_fiS], [b_tmpA], out=tFk, in0=B2k, in1=fib, op=ALU.mult)
        op("dve", "tensor_sub", [b_tmpA], [b_tmpA], out=FBak, in0=FBak, in1=tFk)
        op("dve", "tensor_tensor", [b_tmpD, b_fre], [b_tmpB], out=FBbk, in0=B2k, in1=frb, op=ALU.mult)
        op("dve", "tensor_tensor", [b_tmpC, b_fiS], [b_tmpA], out=tFk, in0=B1k, in1=fib, op=ALU.mult)
        op("dve", "tensor_add", [b_tmpB, b_tmpA], [b_tmpB], out=FBbk, in0=FBbk, in1=tFk)
        for (FBt, b_FB, LT, b_LT) in ((tmpA, b_tmpA, LT1, b_LT1), (tmpB, b_tmpB, LT2, b_LT2)):
            pt, b_pt = next_ps()
            op("pe", "transpose", [b_FB, b_ident], [b_pt], out=pt[:, 0:128], in_=FBt[:, 0:128], identity=ident[:])
            for gi in range(4):
                op("dve", "tensor_scalar_mul", [b_pt, b_rowmask], [b_LT], out=LT[:, kc, gi, :], in0=pt[:, 0:128],
                   scalar1=rowmask[:, gi:gi + 1])
    cre = W["ssm_c_re"].rearrange("(k a) p n -> k (a p) n", k=KC); cim = W["ssm_c_im"].rearrange("(k a) p n -> k (a p) n", k=KC)
    for kc in range(KC):
        ld(tmpC[:, 0:64], cre[kc], b_tmpC); ld(tmpC[:, 64:128], cim[kc], b_tmpC)
        for (Cd, b_Cd, first_im) in ((C1, b_C1, False), (C2, b_C2, True)):
            cst = tmpD
            if not first_im:
                op("dve", "tensor_copy", [b_tmpC], [b_tmpD], out=cst[:, 0:64], in_=tmpC[:, 0:64])
                op("dve", "tensor_scalar_mul", [b_tmpC], [b_tmpD], out=cst[:, 64:128], in0=tmpC[:, 64:128], scalar1=-1.0)
            else:
                op("dve", "tensor_scalar_mul", [b_tmpC], [b_tmpD], out=cst[:, 0:64], in0=tmpC[:, 64:128], scalar1=-1.0)
                op("dve", "tensor_copy", [b_tmpC], [b_tmpD], out=cst[:, 64:128], in_=tmpC[:, 0:64])
            pt, b_pt = next_ps()
            op("pe", "transpose", [b_tmpD, b_ident], [b_pt], out=pt[:, 0:128], in_=cst[:, 0:128], identity=ident[:])
            op("dve", "tensor_copy", [b_pt], [b_Cd], out=Cd[:, kc * 8:(kc + 1) * 8, :].rearrange("p a b -> p (a b)"), in_=pt[:, 0:128])
    op("dve", "tensor_mul", [b_gvec, b_dvec], [b_gd], out=gd[:], in0=gvec[:, 2, :], in1=dvec[:])

    def evac(i, pt, b_pt, dst_ap, b_dst, n, scale=None, func=None, rows=128):
        if func is not None:
            kw = {"scale": scale} if scale is not None else {}
            op("act", "activation", [b_pt], [b_dst], out=dst_ap, in_=pt[0:rows, 0:n], func=func, **kw)
        elif scale is not None:
            op("act", "activation", [b_pt], [b_dst], out=dst_ap, in_=pt[0:rows, 0:n], func=AF.Copy, scale=scale)
        elif i % 2 == 0:
            op("act", "copy", [b_pt], [b_dst], out=dst_ap, in_=pt[0:rows, 0:n])
        else:
            op("dve", "tensor_copy", [b_pt], [b_dst], out=dst_ap, in_=pt[0:rows, 0:n])

    def rmsnorm(gi, ntok, dst=None, b_dst=None):
        dst = hT if dst is None else dst
        b_dst = b_hT if b_dst is None else b_dst
        op("act", "activation", [b_xT], [b_sq], out=sq[:, :, 0:ntok], in_=xT[:, :, 0:ntok], func=AF.Square)
        pt, b_pt = next_ps()
        for kc in range(KC):
            op("pe", "matmul", [b_ones_d, b_sq], [b_pt], pt[:, 0:ntok], lhsT=ones_d[:], rhs=sq[:, kc, 0:ntok],
               start=(kc == 0), stop=(kc == KC - 1))
        op("dve", "tensor_scalar_add", [b_pt], [b_rstd], out=rstd[:, 0:ntok], in0=pt[:, 0:ntok], scalar1=EPS)
        op("act", "activation", [b_rstd], [b_rstd], out=rstd[:, 0:ntok], in_=rstd[:, 0:ntok], func=AF.Sqrt)
        op("dve", "reciprocal", [b_rstd], [b_rstd], out=rstd[:, 0:ntok], in_=rstd[:, 0:ntok])
        for kc in range(KC):
            op("dve", "scalar_tensor_tensor", [b_xT, b_rstd, b_gvec], [b_dst], out=dst[:, kc, 0:ntok],
               in0=xT[:, kc, 0:ntok], scalar=gvec[:, gi, kc:kc + 1], in1=rstd[:, 0:ntok], op0=ALU.mult, op1=ALU.mult)

    def ffn(l, ntok):
        rmsnorm(1 + 2 * l, ntok)
        for p in range(FC // 2):
            view, b_pan = load_panel(wb_ffi[l, p], KC, 512, b_wb_ffi)
            for j in range(2):
                pg, b_pg = fm_chunk(view, b_pan, KC, j * 128, hT, b_hT, ntok)
                pu, b_pu = fm_chunk(view, b_pan, KC, 256 + j * 128, hT, b_hT, ntok)
                op("act", "activation", [b_pg], [b_tmpA], out=tmpA[:, 0:ntok], in_=pg[:, 0:ntok], func=AF.Silu)
                op("dve", "tensor_tensor", [b_tmpA, b_pu], [b_act], out=act[:, 2 * p + j, 0:ntok], in0=tmpA[:, 0:ntok],
                   in1=pu[:, 0:ntok], op=ALU.mult)
        for oc in range(KC):
            view, b_pan = load_panel(wb_ffo[l, oc], FC, 128, b_wb_ffo)
            pt, b_pt = fm_chunk(view, b_pan, FC, 0, act, b_act, ntok)
            op("dve", "tensor_tensor", [b_pt, b_xT], [b_xT], out=xT[:, oc, 0:ntok], in0=pt[:, 0:ntok],
               in1=xT[:, oc, 0:ntok], op=ALU.add)

    def load_x(src_ap, ntok):
        ntile = (ntok + 127) // 128
        for t in range(ntile):
            rows = min(128, ntok - t * 128)
            ld(xtok[0:rows, :], src_ap[t * 128:t * 128 + rows, :], b_xtok)
            for half in range(2):
                pt, b_pt = next_ps()
                for k4 in range(4):
                    kc = half * 4 + k4
                    op("pe", "transpose", [b_xtok, b_ident], [b_pt], out=pt[:, k4 * 128:k4 * 128 + rows],
                       in_=xtok[0:rows, kc * 128:(kc + 1) * 128], identity=ident[0:rows, 0:rows])
                src = pt[:, 0:512].rearrange("p (a b) -> p a b", a=4)[:, :, 0:rows]
                dst = xT[:, half * 4:half * 4 + 4, t * 128:t * 128 + rows]
                if half == 0:
                    op("act", "copy", [b_pt], [b_xT], out=dst, in_=src)
                else:
                    op("dve", "tensor_copy", [b_pt], [b_xT], out=dst, in_=src)

    def final_out(dst_ap, ntok):
        rmsnorm(4, ntok, dst=xT, b_dst=b_xT)
        ntile = (ntok + 127) // 128
        for t in range(ntile):
            rows = min(128, ntok - t * 128)
            for half in range(2):
                pt, b_pt = next_ps()
                for k4 in range(4):
                    kc = half * 4 + k4
                    op("pe", "transpose", [b_xT, b_ident], [b_pt], out=pt[0:rows, k4 * 128:(k4 + 1) * 128],
                       in_=xT[:, kc, t * 128:t * 128 + rows], identity=ident[:])
                evac(half, pt, b_pt, xtok[0:rows, half * 512:(half + 1) * 512], b_xtok, 512, rows=rows)
            st(dst_ap[t * 128:t * 128 + rows, :], xtok[0:rows, :], b_xtok)

    S5BUFS = []
    if blk_lo > 0:
        ld(Sst[:], i_Sst, b_Sst)
        op("dve", "tensor_copy", [b_Sst], [b_Sbf], out=Sbf[:], in_=Sst[:])
        ld(Wc[:], i_Wc, b_Wc); ld(Zend[:], i_Zend, b_Zend)
        ld(KBT[:], i_KBT, b_KBT); ld(VB[:], i_VB, b_VB)
    for blk in range(blk_lo, nblk_run):
        t0 = blk * 512
        load_x(xp[t0:t0 + 512, :], 512)
        rmsnorm(0, 512)
        last_kv = blk >= KV_FROM
        for pi in range(6):
            view, b_pan = load_panel(wb_in[pi], KC, 512, b_wb_in)
            if pi == 0:
                for j in range(2):
                    pt, b_pt = fm_chunk(view, b_pan, KC, j * 128, hT, b_hT, 512)
                    evac(j, pt, b_pt, qaT[:, j, :], b_qaT, 512)
                for j in range(2):
                    pt, b_pt = fm_chunk(view, b_pan, KC, 256 + j * 128, hT, b_hT, 512)
                    evac(j, pt, b_pt, kaT[:, j, :], b_kaT, 512, scale=0.125)
                for t in range(4):
                    pt, b_pt = tm_tile(view, b_pan, KC, 256, 256, hT, b_hT, t * 128, 128)
                    op("dve", "tensor_tensor", [b_pt, b_kdec], [b_katok], out=katok[:, t, :], in0=pt[:, 0:256], in1=kdec[:], op=ALU.mult)
            elif pi == 1:
                for t in range(4):
                    pt, b_pt = tm_tile(view, b_pan, KC, 0, 512, hT, b_hT, t * 128, 128)
                    evac(t, pt, b_pt, vatok[:, t, :], b_vatok, 512)
            elif pi == 2:
                for j in range(4):
                    pt, b_pt = fm_chunk(view, b_pan, KC, j * 128, hT, b_hT, 512)
                    evac(j, pt, b_pt, sgT[:, j, :], b_sgT, 512, func=AF.Silu)
            elif pi == 3:
                for j in range(4):
                    pt, b_pt = fm_chunk(view, b_pan, KC, j * 128, hT, b_hT, 512)
                    evac(j, pt, b_pt, qbT[:, j, :], b_qbT, 512)
            elif pi == 4:
                rs0 = ((blk * 4) % RING) * 128
                for j in range(4):
                    pt, b_pt = fm_chunk(view, b_pan, KC, j * 128, hT, b_hT, 512)
                    evac(j, pt, b_pt, KBT[:, j, rs0:rs0 + 512], b_KBT, 512)
                if last_kv and os.environ.get("NOKVK") is None:
                    for t in range(4):
                        pt, b_pt = tm_tile(view, b_pan, KC, 0, 512, hT, b_hT, t * 128, 128)
                        ko, b_ko = (tmpC, b_tmpC) if t % 2 == 0 else (tmpD, b_tmpD)
                        evac(t, pt, b_pt, ko[:], b_ko, 512)
                        r0 = (blk - KV_FROM) * 512 + t * 128
                        st(o_kp[r0:r0 + 128, :], ko[:], b_ko)
            else:
                for t in range(4):
                    pt, b_pt = tm_tile(view, b_pan, KC, 0, 512, hT, b_hT, t * 128, 128)
                    slot = (blk * 4 + t) % RING
                    if last_kv and os.environ.get("NOKVV") is None:
                        ko, b_ko = (tmpC, b_tmpC) if t % 2 == 0 else (tmpD, b_tmpD)
                        evac(t, pt, b_pt, ko[:], b_ko, 512)
                        op("pool", "tensor_copy", [b_ko], [b_VB], out=VB[:, slot, :], in_=ko[:])
                        r0 = (blk - KV_FROM) * 512 + t * 128
                        st(o_vp[r0:r0 + 128, :], ko[:], b_ko)
                    else:
                        evac(t, pt, b_pt, VB[:, slot, :], b_VB, 512)
        for c in range(4 if PH >= 2 else 0):
            cs = c * 128
            for h in range(4):
                pr = h // 2
                op("dve", "tensor_scalar_mul", [b_qaT, b_hmask], [b_qz], out=qz[:, h, :], in0=qaT[:, pr, cs:cs + 128], scalar1=hmask[:, (h % 2):(h % 2) + 1])
                op("dve", "tensor_tensor", [b_qz, b_qdec], [b_qd], out=qd[:, h, :], in0=qz[:, h, :], in1=qdec[:, pr, :], op=ALU.mult)
            ps_s, b_ps_s = next_ps()
            for h in range(4):
                pr = h // 2
                op("pe", "matmul", [b_kaT, b_qz], [b_ps_s], ps_s[:, h * 128:(h + 1) * 128], lhsT=kaT[:, pr, cs:cs + 128],
                   rhs=qz[:, h, :], start=True, stop=True)
            pTt, b_pTt = pT[c % 3]
            op("dve", "tensor_tensor", [b_ps_s, b_decT], [b_pTt], out=pTt[:], in0=ps_s[:], in1=decT[:].rearrange("p a b -> p (a b)"), op=ALU.mult)
            if SUB < 2:
                continue
            ps_o, b_ps_o = next_ps()
            for h in range(4):
                pr = h // 2
                op("pe", "matmul", [b_vatok, b_pTt], [b_ps_o], ps_o[:, h * 128:(h + 1) * 128], lhsT=vatok[:, c, h * 128:(h + 1) * 128],
                   rhs=pTt[:, h * 128:(h + 1) * 128], start=True, stop=False)
                op("pe", "matmul", [b_Sbf, b_qd], [b_ps_o], ps_o[:, h * 128:(h + 1) * 128], lhsT=Sbf[:, pr, :],
                   rhs=qd[:, h, :], start=False, stop=True)
            if SUB < 3:
                continue
            ps_d, b_ps_d = next_ps()
            for h in range(4):
                pr = h // 2
                op("pe", "matmul", [b_katok, b_vatok], [b_ps_d], ps_d[:, h * 128:(h + 1) * 128], lhsT=katok[:, c, pr * 128:(pr + 1) * 128],
                   rhs=vatok[:, c, h * 128:(h + 1) * 128], start=True, stop=True)
            for h in range(4):
                hp, pr = (h % 2) * 64, h // 2
                op("dve", "scalar_tensor_tensor", [b_Sst, b_ps_d, b_Sbf, b_ps_o], [b_Sst], out=Sst[hp:hp + 64, pr, :], in0=Sst[hp:hp + 64, pr, :],
                   scalar=float(GAM[h] ** 128), in1=ps_d[hp:hp + 64, h * 128:(h + 1) * 128], op0=ALU.mult, op1=ALU.add)
            op("dve", "tensor_copy", [b_Sst], [b_Sbf], out=Sbf[:], in_=Sst[:])
            if SUB < 4:
                continue
            op("act", "copy", [b_ps_o], [b_osb], out=osb[:], in_=ps_o[:])
            op("dve", "tensor_copy", [b_osb], [b_obf], out=obf[:], in_=osb[:])
            ps_m, b_ps_m = next_ps()
            op("pe", "matmul", [b_ones_g, b_obf], [b_ps_m], ps_m[:], lhsT=ones_g[:], rhs=obf[:], start=True, stop=True)
            op("dve", "tensor_tensor", [b_osb, b_ps_m], [b_osb], out=osb[:], in0=osb[:], in1=ps_m[:], op=ALU.subtract)
            op("act", "activation", [b_osb], [b_osq], out=osq[:], in_=osb[:], func=AF.Square)
            ps_q, b_ps_q = next_ps()
            op("pe", "matmul", [b_ones_g, b_osq], [b_ps_q], ps_q[:], lhsT=ones_g[:], rhs=osq[:], start=True, stop=True)
            if SUB < 5:
                continue
            op("dve", "tensor_scalar_add", [b_ps_q], [b_tmpA], out=tmpA[:], in0=ps_q[:], scalar1=EPS)
            op("act", "activation", [b_tmpA], [b_tmpA], out=tmpA[:], in_=tmpA[:], func=AF.Sqrt)
            op("dve", "reciprocal", [b_tmpA], [b_tmpA], out=tmpA[:], in_=tmpA[:])
            op("dve", "tensor_tensor", [b_osb, b_tmpA], [b_osb], out=osb[:], in0=osb[:], in1=tmpA[:], op=ALU.mult)
            if SUB < 6:
                continue
            for h in range(4):
                op("dve", "scalar_tensor_tensor", [b_osb, b_gn, b_sgT], [b_mixT], out=mixT[:, h, cs:cs + 128], in0=osb[:, h * 128:(h + 1) * 128],
                   scalar=gn[:, h:h + 1], in1=sgT[:, h, cs:cs + 128], op0=ALU.mult, op1=ALU.mult)
        kt_hi = blk * 4 + 3
        kt_lo = max(0, blk * 4 - 16)
        for h in range(8 if PH >= 3 else 0):
            hp, pr = (h % 2) * 64, h // 2
            tht, b_tht = thtab[0]
            ld(tht[:], C["swa_tab"][h], b_tht)
            ps_o, b_ps_o = PS_A if h % 2 == 0 else PS_B
            ps_dn, b_ps_dn = PS_C
            nk = kt_hi - kt_lo + 1
            for i, kt in enumerate(range(kt_lo, kt_hi + 1)):
                o = t0 - kt * 128
                slot = kt % RING
                ps_s, b_ps_s = next_ps()
                op("pe", "matmul", [b_KBT, b_qbT], [b_ps_s], ps_s[:], lhsT=KBT[hp:hp + 64, pr, slot * 128:(slot + 1) * 128],
                   rhs=qbT[hp:hp + 64, pr, :], start=True, stop=True)
                et, b_et = eT[i % 3]
                op("act", "activation", [b_ps_s], [b_et], out=et[:], in_=ps_s[:], func=AF.Exp, scale=0.125)
                pt_, b_pt_ = pT[i % 3]
                op("pool" if i % 2 else "dve", "tensor_tensor", [b_et, b_tht], [b_pt_], out=pt_[:], in0=et[:], in1=tht[:, o + 384:o + 384 + 512], op=ALU.mult)
                op("pe", "matmul", [b_VB, b_pt_], [b_ps_o], ps_o[:], lhsT=VB[:, slot, pr * 128:(pr + 1) * 128], rhs=pt_[:],
                   start=(i == 0), stop=(i == nk - 1))
                op("pe", "matmul", [b_ones_1, b_pt_], [b_ps_dn], ps_dn[:], lhsT=ones_1[:], rhs=pt_[:], start=(i == 0), stop=(i == nk - 1))
            op("dve", "reciprocal", [b_ps_dn], [b_tmpB], out=tmpB[hp:hp + 64, :], in_=ps_dn[hp:hp + 64, :])
            op("dve", "tensor_tensor", [b_ps_o, b_tmpB], [b_mixT], out=mixT[hp:hp + 64, 4 + pr, :], in0=ps_o[hp:hp + 64, :], in1=tmpB[hp:hp + 64, :], op=ALU.mult)
        for p in range(2 if PH >= 4 else 0):
            view, b_pan = load_panel(wb_out[p], KC, 512, b_wb_out)
            for j in range(4):
                oc = p * 4 + j
                pt, b_pt = fm_chunk(view, b_pan, KC, j * 128, mixT, b_mixT, 512)
                op("dve", "tensor_tensor", [b_pt, b_xT], [b_xT], out=xT[:, oc, :], in0=pt[:], in1=xT[:, oc, :], op=ALU.add)
        if PH >= 4:
            ffn(0, 512)
        rmsnorm(2, 512)
        for c in range(4 if PH >= 5 else 0):
            cs = c * 128
            ps_y0, b_ps_y0 = PS_A
            ps_y1, b_ps_y1 = PS_B
            actf = act[:].rearrange("p a b -> p (a b)").bitcast(F32)
            if c == 0 and blk == blk_lo:
                s5bufs = [[(actf[:, (k * 4 + j) * 512:(k * 4 + j + 1) * 512], Buf("s5t%d_%d" % (k, j))) for j in range(4)] for k in range(2)]
                S5BUFS.append(s5bufs)
            s5bufs = S5BUFS[0]

            def stage1(qd_i):
                g0 = qd_i * 4
                kc, half = g0 // 8, (g0 % 8) // 4
                hp = half * 64
                (t1, b_t1), (t2, b_t2), (zd, b_zd), _ = s5bufs[qd_i % 2]
                pd1, b_pd1 = next_ps()
                pd2, b_pd2 = next_ps()
                for gi in range(4):
                    op("pe", "matmul", [b_LT1, b_hT], [b_pd1], pd1[:, gi * 128:(gi + 1) * 128], lhsT=LT1[hp:hp + 64, kc, gi, :],
                       rhs=hT[hp:hp + 64, kc, cs:cs + 128], start=True, stop=True)
                for gi in range(4):
                    op("pe", "matmul", [b_LT2, b_hT], [b_pd2], pd2[:, gi * 128:(gi + 1) * 128], lhsT=LT2[hp:hp + 64, kc, gi, :],
                       rhs=hT[hp:hp + 64, kc, cs:cs + 128], start=True, stop=True)
                ctq = CT[:, g0:g0 + 4, :].rearrange("p a b -> p (a b)")
                stq = ST[:, g0:g0 + 4, :].rearrange("p a b -> p (a b)")
                op("dve", "tensor_tensor", [b_pd1, b_CT], [b_t1], out=t1, in0=pd1[:], in1=ctq, op=ALU.mult)
                op("dve", "tensor_tensor", [b_pd2, b_ST], [b_t2], out=t2, in0=pd2[:], in1=stq, op=ALU.mult)
                op("pool", "tensor_tensor", [b_t1, b_t2], [b_zd], out=zd, in0=t1, in1=t2, op=ALU.add)

            def stage2(qd_i):
                g0 = qd_i * 4
                _, _, (zd, b_zd), (zz, b_zz) = s5bufs[qd_i % 2]
                ctq = CT[:, g0:g0 + 4, :].rearrange("p a b -> p (a b)")
                stq = ST[:, g0:g0 + 4, :].rearrange("p a b -> p (a b)")
                for gi in range(4):
                    g = g0 + gi
                    op("dve", "tensor_tensor_scan", [b_zd, b_lam_abs, b_Wc], [b_zz], out=zz[:, gi * 128:(gi + 1) * 128],
                       data0=lam_abs[:, g:g + 1].to_broadcast([128, 128]), data1=zd[:, gi * 128:(gi + 1) * 128],
                       initial=Wc[:, g:g + 1], op0=ALU.mult, op1=ALU.add)
                at, b_at = eT[qd_i % 3]
                bt, b_bt = pT[qd_i % 3]
                op("dve", "tensor_tensor", [b_zz, b_CT], [b_at], out=at[:], in0=zz, in1=ctq, op=ALU.mult)
                op("pool", "tensor_tensor", [b_zz, b_ST], [b_bt], out=bt[:], in0=zz, in1=stq, op=ALU.mult)
                op("pool", "tensor_copy", [b_zz], [b_Zend], out=Zend[:, g0:g0 + 4],
                   in_=zz[:, 0:512].rearrange("p (a b) -> p a b", a=4)[:, :, 127])
                for gi in range(4):
                    g = g0 + gi
                    py, b_py = (ps_y0, b_ps_y0) if g < 32 else (ps_y1, b_ps_y1)
                    col = (g % 32) * 16
                    op("pe", "matmul", [b_at, b_C1], [b_py], py[:, col:col + 16], lhsT=at[:, gi * 128:(gi + 1) * 128], rhs=C1[:, g, :], start=True, stop=False)
                    op("pe", "matmul", [b_bt, b_C2], [b_py], py[:, col:col + 16], lhsT=bt[:, gi * 128:(gi + 1) * 128], rhs=C2[:, g, :], start=False, stop=True)
            for qd_i in range(17):
                if qd_i < 16:
                    stage1(qd_i)
                if qd_i >= 1:
                    stage2(qd_i - 1)
            pw, b_pw = next_ps()
            op("pe", "matmul", [b_swapm, b_Zend], [b_pw], pw[:, 0:64], lhsT=swapm[:], rhs=Zend[:], start=True, stop=True)
            op("dve", "tensor_tensor", [b_pw, b_sL], [b_t64], out=t64[:], in0=pw[:, 0:64], in1=sL[:], op=ALU.mult)
            op("dve", "tensor_tensor", [b_Zend, b_cL], [b_Wc], out=Wc[:], in0=Zend[:], in1=cL[:], op=ALU.mult)
            op("dve", "tensor_add", [b_Wc, b_t64], [b_Wc], out=Wc[:], in0=Wc[:], in1=t64[:])
            ysb, b_ysb = xtok[:, 0:1024].rearrange("p (a b) -> p a b", a=2), b_xtok
            op("act", "copy", [b_ps_y0], [b_ysb], out=ysb[:, 0, :], in_=ps_y0[:])
            op("dve", "tensor_copy", [b_ps_y1], [b_ysb], out=ysb[:, 1, :], in_=ps_y1[:])
            for half in range(2):
                pt, b_pt = next_ps()
                for k4 in range(4):
                    op("pe", "transpose", [b_ysb, b_ident], [b_pt], out=pt[:, k4 * 128:(k4 + 1) * 128],
                       in_=ysb[:, half, k4 * 128:(k4 + 1) * 128], identity=ident[:])
                for k4 in range(4):
                    kc = half * 4 + k4
                    op("dve", "scalar_tensor_tensor", [b_xT, b_gd, b_rstd], [b_tmpA], out=tmpA[:, k4 * 128:(k4 + 1) * 128], in0=xT[:, kc, cs:cs + 128],
                       scalar=gd[:, kc:kc + 1], in1=rstd[:, cs:cs + 128], op0=ALU.mult, op1=ALU.mult)
                op("dve", "tensor_tensor", [b_tmpA, b_pt], [b_tmpA], out=tmpA[:], in0=tmpA[:], in1=pt[:], op=ALU.add)
                op("act", "activation", [b_tmpA], [b_tmpB], out=tmpB[:], in_=tmpA[:], func=AF.Square)
                op("dve", "tensor_scalar", [b_tmpB], [b_tmpB], out=tmpB[:], in0=tmpB[:], scalar1=0.044715, scalar2=1.0, op0=ALU.mult, op1=ALU.add)
                op("dve", "tensor_tensor", [b_tmpB, b_tmpA], [b_tmpB], out=tmpB[:], in0=tmpB[:], in1=tmpA[:], op=ALU.mult)
                op("act", "activation", [b_tmpB], [b_tmpB], out=tmpB[:], in_=tmpB[:], func=AF.Sigmoid, scale=1.5957691216)
                for k4 in range(4):
                    kc = half * 4 + k4
                    op("dve", "tensor_tensor", [b_tmpA, b_tmpB], [b_mixT], out=mixT[:, kc, cs:cs + 128], in0=tmpA[:, k4 * 128:(k4 + 1) * 128],
                       in1=tmpB[:, k4 * 128:(k4 + 1) * 128], op=ALU.mult)
        for p in range(4 if PH >= 6 else 0):
            view, b_pan = load_panel(wb_glu[p], KC, 512, b_wb_glu)
            for j in range(2):
                oc = 2 * p + j
                pv, b_pv = fm_chunk(view, b_pan, KC, j * 128, mixT, b_mixT, 512)
                pg, b_pg = fm_chunk(view, b_pan, KC, 256 + j * 128, mixT, b_mixT, 512)
                op("act", "activation", [b_pg], [b_tmpA], out=tmpA[:], in_=pg[:], func=AF.Sigmoid)
                op("dve", "tensor_tensor", [b_tmpA, b_pv], [b_tmpA], out=tmpA[:], in0=tmpA[:], in1=pv[:], op=ALU.mult)
                op("dve", "tensor_tensor", [b_tmpA, b_xT], [b_xT], out=xT[:, oc, :], in0=tmpA[:], in1=xT[:, oc, :], op=ALU.add)
        if PH >= 6:
            ffn(1, 512)
        final_out(o_yp[t0:t0 + 512, :], 512)

    st(o_retp, Sst[:], b_Sst)
    if len(LAUNCH_RANGES) > 1:
        st(o_Wc, Wc[:], b_Wc); st(o_Zend, Zend[:], b_Zend)
        st(o_KBT, KBT[:], b_KBT); st(o_VB, VB[:], b_VB)
    pw, b_pw = next_ps()
    op("pe", "matmul", [b_swapm, b_Zend], [b_pw], pw[:, 0:64], lhsT=swapm[:], rhs=Zend[:], start=True, stop=True)
    op("dve", "tensor_tensor", [b_pw, b_sE], [b_t64], out=t64[:], in0=pw[:, 0:64], in1=sE[:], op=ALU.mult)
    xfin, b_xfin = sb("xfin", [128, 64])
    op("dve", "tensor_tensor", [b_Zend, b_cE], [b_xfin], out=xfin[:], in0=Zend[:], in1=cE[:], op=ALU.mult)
    op("dve", "tensor_add", [b_xfin, b_t64], [b_xfin], out=xfin[:], in0=xfin[:], in1=t64[:])
    st(o_ssp, xfin[:], b_xfin)

    if RUN_SAMPLE:
        NS = TS
        AX = mybir.AxisListType.X
        bmask, b_bmask = sb("bmask", [32, 4]); ld(bmask[:], C["bmask"], b_bmask)
        eomask, b_eomask = sb("eomask", [64, 2]); ld(eomask[:], C["eomask"], b_eomask)
        kdecS = kdec[0:32, :]
        ld(kdecS, C["kdecS"], b_kdec)
        qdecS = qdec[:, 0, 0:64].rearrange("p (a b) -> p a b", a=2)
        ld(qdecS, C["qdecS"], b_qdec)
        decTS = decT[0:32, 0, :]
        ld(decTS.rearrange("p (a b) -> p a b", a=4), C["decTS"], b_decT)
        load_x(xs, NS)
        rmsnorm(0, NS)
        kvo0, kvo1 = kvo[0][0], kvo[1][0]
        for pi in range(6):
            view, b_pan = load_panel(wb_in[pi], KC, 512, b_wb_in)
            if pi == 0:
                for j in range(2):
                    pt, b_pt = fm_chunk(view, b_pan, KC, j * 128, hT, b_hT, NS)
                    evac(j, pt, b_pt, qaT[:, j, 0:NS], b_qaT, NS)
                for j in range(2):
                    pt, b_pt = fm_chunk(view, b_pan, KC, 256 + j * 128, hT, b_hT, NS)
                    evac(j, pt, b_pt, kaT[:, j, 0:NS], b_kaT, NS, scale=0.125)
                pt, b_pt = tm_tile(view, b_pan, KC, 256, 256, hT, b_hT, 0, NS)
                op("dve", "tensor_tensor", [b_pt, b_kdec], [b_katok], out=katok[0:NS, 0, :], in0=pt[0:NS, 0:256], in1=kdecS, op=ALU.mult)
            elif pi == 1:
                pt, b_pt = tm_tile(view, b_pan, KC, 0, 512, hT, b_hT, 0, NS)
                evac(0, pt, b_pt, vatok[0:NS, 0, :], b_vatok, 512, rows=NS)
            elif pi == 2:
                for j in range(4):
                    pt, b_pt = fm_chunk(view, b_pan, KC, j * 128, hT, b_hT, NS)
                    evac(j, pt, b_pt, sgT[:, j, 0:NS], b_sgT, NS, func=AF.Silu)
            elif pi == 3:
                for j in range(4):
                    pt, b_pt = fm_chunk(view, b_pan, KC, j * 128, hT, b_hT, NS)
                    evac(j, pt, b_pt, qbT[:, j, 0:NS], b_qbT, NS)
            elif pi == 4:
                for j in range(4):
                    pt, b_pt = fm_chunk(view, b_pan, KC, j * 128, hT, b_hT, NS)
                    evac(j, pt, b_pt, KBT[:, j, 2056:2056 + NS], b_KBT, NS)
                pt, b_pt = tm_tile(view, b_pan, KC, 0, 512, hT, b_hT, 0, NS)
                evac(1, pt, b_pt, kvo0[0:NS, :], b_xtok, 512, rows=NS)
                for b in range(SB):
                    st(o_ks[b, WB - 8:WB, :], kvo0[b * 8:(b + 1) * 8, :], b_xtok)
            else:
                pt, b_pt = tm_tile(view, b_pan, KC, 0, 512, hT, b_hT, 0, NS)
                evac(1, pt, b_pt, kvo1[0:NS, :], b_xtok, 512, rows=NS)
                for b in range(SB):
                    st(o_vs[b, WB - 8:WB, :], kvo1[b * 8:(b + 1) * 8, :], b_xtok)
                    S.dma("pool", lambda e, a=VB[0:8, 16 + b, :], c_=kvo1[b * 8:(b + 1) * 8, :]: e.dma_start(out=a, in_=c_),
                          b_VB, reads=[b_xtok], writes=[b_VB])
        def SstS(b):
            t_ = tmpB if b < 2 else tmpD
            return t_[:, (b % 2) * 256:(b % 2) * 256 + 256].rearrange("p (a c) -> p a c", a=2), (b_tmpB if b < 2 else b_tmpD)

        def SbfS(b):
            t_, bb_ = pT[1] if b < 2 else pT[2]
            return t_[:, (b % 2) * 256:(b % 2) * 256 + 256].rearrange("p (a c) -> p a c", a=2), bb_
        for b in range(SB):
            sv, sbuf_ = SstS(b)
            for h in range(4):
                hp, pr = (h % 2) * 64, h // 2
                ld(sv[hp:hp + 64, pr, :], st_ret[b, h], sbuf_)
        for b in range(SB):
            sv, sbuf_ = SstS(b); bv, bbuf_ = SbfS(b)
            op("dve", "tensor_copy", [sbuf_], [bbuf_], out=bv, in_=sv)
        for h in range(4):
            pr = h // 2
            op("dve", "tensor_scalar_mul", [b_qaT, b_hmask], [b_qz], out=qz[:, h, 0:NS], in0=qaT[:, pr, 0:NS], scalar1=hmask[:, (h % 2):(h % 2) + 1])
            op("dve", "tensor_tensor", [b_qz, b_qdec], [b_qd], out=qd[:, h, 0:NS], in0=qz[:, h, 0:NS], in1=qdecS[:, pr, :], op=ALU.mult)
        ps_s, b_ps_s = next_ps()
        for h in range(4):
            pr = h // 2
            op("pe", "matmul", [b_kaT, b_qz], [b_ps_s], ps_s[0:NS, h * NS:(h + 1) * NS], lhsT=kaT[:, pr, 0:NS], rhs=qz[:, h, 0:NS], start=True, stop=True)
        pTt, b_pTt = pT[0]
        op("dve", "tensor_tensor", [b_ps_s, b_decT], [b_pTt], out=pTt[0:NS, 0:4 * NS], in0=ps_s[0:NS, 0:4 * NS], in1=decTS, op=ALU.mult)
        ps_o, b_ps_o = next_ps()
        for h in range(4):
            pr = h // 2
            op("pe", "matmul", [b_vatok, b_pTt], [b_ps_o], ps_o[:, h * NS:(h + 1) * NS], lhsT=vatok[0:NS, 0, h * 128:(h + 1) * 128],
               rhs=pTt[0:NS, h * NS:(h + 1) * NS], start=True, stop=False)
            for b in range(SB):
                bv, bbuf_ = SbfS(b)
                op("pe", "matmul", [bbuf_, b_qd], [b_ps_o], ps_o[:, h * NS + 8 * b:h * NS + 8 * b + 8], lhsT=bv[:, pr, :],
                   rhs=qd[:, h, 8 * b:8 * b + 8], start=False, stop=(b == SB - 1))
        kdm, b_kdm = eT[2]
        for b in range(SB):
            sv, sbuf_ = SstS(b)
            op("dve", "tensor_scalar_mul", [b_katok, b_bmask], [b_kdm], out=kdm[0:NS, 0:256], in0=katok[0:NS, 0, :], scalar1=bmask[0:NS, b:b + 1])
            ps_d, b_ps_d = next_ps()
            for h in range(4):
                pr = h // 2
                op("pe", "matmul", [b_kdm, b_vatok], [b_ps_d], ps_d[:, h * 128:(h + 1) * 128], lhsT=kdm[0:NS, pr * 128:(pr + 1) * 128],
                   rhs=vatok[0:NS, 0, h * 128:(h + 1) * 128], start=True, stop=True)
            for h in range(4):
                hp, pr = (h % 2) * 64, h // 2
                op("dve", "scalar_tensor_tensor", [sbuf_, b_ps_d, b_ps_o], [sbuf_], out=sv[hp:hp + 64, pr, :], in0=sv[hp:hp + 64, pr, :],
                   scalar=float(GAM[h] ** 8), in1=ps_d[hp:hp + 64, h * 128:(h + 1) * 128], op0=ALU.mult, op1=ALU.add)
            st(o_rets[b], sv, sbuf_)
        W4 = 4 * NS
        op("act", "copy", [b_ps_o], [b_osb], out=osb[:, 0:W4], in_=ps_o[:, 0:W4])
        op("dve", "tensor_copy", [b_osb], [b_obf], out=obf[:, 0:W4], in_=osb[:, 0:W4])
        ps_m, b_ps_m = next_ps()
        op("pe", "matmul", [b_ones_g, b_obf], [b_ps_m], ps_m[:, 0:W4], lhsT=ones_g[:], rhs=obf[:, 0:W4], start=True, stop=True)
        op("dve", "tensor_tensor", [b_osb, b_ps_m], [b_osb], out=osb[:, 0:W4], in0=osb[:, 0:W4], in1=ps_m[:, 0:W4], op=ALU.subtract)
        op("act", "activation", [b_osb], [b_osq], out=osq[:, 0:W4], in_=osb[:, 0:W4], func=AF.Square)
        ps_q, b_ps_q = next_ps()
        op("pe", "matmul", [b_ones_g, b_osq], [b_ps_q], ps_q[:, 0:W4], lhsT=ones_g[:], rhs=osq[:, 0:W4], start=True, stop=True)
        op("dve", "tensor_scalar_add", [b_ps_q], [b_tmpA], out=tmpA[:, 0:W4], in0=ps_q[:, 0:W4], scalar1=EPS)
        op("act", "activation", [b_tmpA], [b_tmpA], out=tmpA[:, 0:W4], in_=tmpA[:, 0:W4], func=AF.Sqrt)
        op("dve", "reciprocal", [b_tmpA], [b_tmpA], out=tmpA[:, 0:W4], in_=tmpA[:, 0:W4])
        op("dve", "tensor_tensor", [b_osb, b_tmpA], [b_osb], out=osb[:, 0:W4], in0=osb[:, 0:W4], in1=tmpA[:, 0:W4], op=ALU.mult)
        for h in range(4):
            op("dve", "scalar_tensor_tensor", [b_osb, b_gn, b_sgT], [b_mixT], out=mixT[:, h, 0:NS], in0=osb[:, h * NS:(h + 1) * NS],
               scalar=gn[:, h:h + 1], in1=sgT[:, h, 0:NS], op0=ALU.mult, op1=ALU.mult)
        tht, b_tht = thtab[0]
        ld(tht[0:64, :], C["WS"], b_tht)
        ld(tmpD[0:64, :], C["hselB"], b_tmpD)
        qpad = act[:, 18, 0:256].rearrange("p (a c) -> p a c", a=4)
        PsT = act[:, 19:22, :].rearrange("p a b -> p (a b)")[:, 0:17 * 64].rearrange("p (k c) -> p k c", c=64)
        op("dve", "memset", [], [b_act], qpad, 0.0)
        o64, o2, dparts, dtot = tmpB[0:64, 0:64], tmpB[0:64, 64:192], tmpB[0:64, 192:197], tmpB[0:64, 200:201]
        for b in range(SB):
            for kt in range(16):
                ld(xtok[:, 0:512], st_k[b, kt * 128:(kt + 1) * 128, :], b_xtok)
                pt, b_pt = next_ps()
                for pr in range(4):
                    op("pe", "transpose", [b_xtok, b_ident], [b_pt], out=pt[:, pr * 128:(pr + 1) * 128], in_=xtok[:, pr * 128:(pr + 1) * 128], identity=ident[:])
                op("act" if kt % 2 else "dve", "copy" if kt % 2 else "tensor_copy", [b_pt], [b_KBT], out=KBT[:, :, kt * 128:(kt + 1) * 128],
                   in_=pt[:, 0:512].rearrange("p (a c) -> p a c", a=4))
            op("dve", "tensor_copy", [b_KBT], [b_KBT], out=KBT[:, :, 2048:2056], in_=KBT[:, :, 2056 + 8 * b:2056 + 8 * b + 8])
            S.dma("pool", lambda e, a=VB[:, 0:16, :], c_=st_v[b].rearrange("(t p) f -> p t f", p=128): e.dma_start(out=a, in_=c_),
                  b_VB, writes=[b_VB])
            for pr in range(4):
                op("dve", "tensor_copy", [b_qbT], [b_act], out=qpad[0:64, pr, (2 * pr) * 8:(2 * pr) * 8 + 8], in_=qbT[0:64, pr, 8 * b:8 * b + 8])
                op("dve", "tensor_copy", [b_qbT], [b_act], out=qpad[64:128, pr, (2 * pr + 1) * 8:(2 * pr + 1) * 8 + 8], in_=qbT[64:128, pr, 8 * b:8 * b + 8])
            for grp in range(5):
                k0 = grp * 512
                kw = 512 if grp < 4 else 8
                ps_sc, b_ps_sc = next_ps()
                for pr in range(4):
                    op("pe", "matmul", [b_act, b_KBT], [b_ps_sc], ps_sc[0:64, 0:kw], lhsT=qpad[:, pr, :], rhs=KBT[:, pr, k0:k0 + kw],
                       start=(pr == 0), stop=(pr == 3))
                op("act", "activation", [b_ps_sc], [b_tmpA], out=tmpA[0:64, 0:kw], in_=ps_sc[0:64, 0:kw], func=AF.Exp, scale=0.125)
                op("dve", "tensor_tensor", [b_tmpA, b_tht], [b_tmpA], out=tmpA[0:64, 0:kw], in0=tmpA[0:64, 0:kw], in1=tht[0:64, k0:k0 + kw], op=ALU.mult)
                op("dve", "reduce_sum", [b_tmpA], [b_tmpB], out=dparts[:, grp:grp + 1], in_=tmpA[0:64, 0:kw], axis=AX)
                pt, b_pt = next_ps()
                if grp < 4:
                    for t4 in range(4):
                        op("pe", "transpose", [b_tmpA, b_ident], [b_pt], out=pt[:, t4 * 64:(t4 + 1) * 64], in_=tmpA[0:64, t4 * 128:(t4 + 1) * 128], identity=ident[0:64, 0:64])
                    op("act", "copy", [b_pt], [b_act], out=PsT[:, grp * 4:grp * 4 + 4, :], in_=pt[:, 0:256].rearrange("p (a c) -> p a c", a=4))
                else:
                    op("pe", "transpose", [b_tmpA, b_ident], [b_pt], out=pt[0:8, 0:64], in_=tmpA[0:64, 0:8], identity=ident[0:64, 0:64])
                    op("act", "copy", [b_pt], [b_act], out=PsT[0:8, 16, :], in_=pt[0:8, 0:64])
            ps_pv, b_ps_pv = next_ps()
            for kt in range(16):
                op("pe", "matmul", [b_act, b_VB], [b_ps_pv], ps_pv[0:64, :], lhsT=PsT[:, kt, :], rhs=VB[:, kt, :], start=(kt == 0), stop=False)
            op("pe", "matmul", [b_act, b_VB], [b_ps_pv], ps_pv[0:64, :], lhsT=PsT[0:8, 16, :], rhs=VB[0:8, 16 + b, :], start=False, stop=True)
            op("dve", "reduce_sum", [b_tmpB], [b_tmpB], out=dtot, in_=dparts, axis=AX)
            op("dve", "reciprocal", [b_tmpB], [b_tmpB], out=dtot, in_=dtot)
            op("dve", "tensor_tensor", [b_ps_pv, b_tmpD], [b_tmpA], out=tmpA[0:64, :], in0=ps_pv[0:64, :], in1=tmpD[0:64, :], op=ALU.mult)
            op("dve", "tensor_reduce", [b_tmpA], [b_tmpB], out=o64, in_=tmpA[0:64, :].rearrange("p (h d) -> p d h", h=8), axis=AX, op=ALU.add)
            for e2 in range(2):
                op("dve", "tensor_scalar", [b_tmpB, b_eomask], [b_tmpB], out=o2[:, e2 * 64:(e2 + 1) * 64], in0=o64, scalar1=dtot, scalar2=eomask[:, e2:e2 + 1],
                   op0=ALU.mult, op1=ALU.mult)
            pt, b_pt = next_ps()
            op("pe", "transpose", [b_tmpB, b_ident], [b_pt], out=pt[:, 0:64], in_=o2, identity=ident[0:64, 0:64])
            for pr in range(4):
                op("dve", "tensor_copy", [b_pt], [b_mixT], out=mixT[0:64, 4 + pr, 8 * b:8 * b + 8], in_=pt[0:64, (2 * pr) * 8:(2 * pr) * 8 + 8])
                op("dve", "tensor_copy", [b_pt], [b_mixT], out=mixT[64:128, 4 + pr, 8 * b:8 * b + 8], in_=pt[64:128, (2 * pr + 1) * 8:(2 * pr + 1) * 8 + 8])
        for p in range(2):
            view, b_pan = load_panel(wb_out[p], KC, 512, b_wb_out)
            for j in range(4):
                oc = p * 4 + j
                pt, b_pt = fm_chunk(view, b_pan, KC, j * 128, mixT, b_mixT, NS)
                op("dve", "tensor_tensor", [b_pt, b_xT], [b_xT], out=xT[:, oc, 0:NS], in0=pt[:, 0:NS], in1=xT[:, oc, 0:NS], op=ALU.add)
        ffn(0, NS)
        rmsnorm(2, NS)
        s5f = s5i[:].bitcast(F32)
        WS5 = s5f[:, 0:256].rearrange("p (a c) -> p a c", a=4)
        ZendS = s5f[:, 256:512].rearrange("p (a c) -> p a c", a=4)
        c7, b_c7, s7, b_s7 = den, b_den, abr, b_abr
        cs_small(7.0, c7, b_c7, s7, b_s7, True)
        for b in range(SB):
            ld(tmpC[0:64, 0:64], st_sr[b], b_tmpC); ld(tmpC[0:64, 64:128], st_si[b], b_tmpC)
            ld(tmpC[0:64, 128:192], st_si[b], b_tmpC); ld(tmpC[0:64, 192:256], st_sr[b], b_tmpC)
            pt, b_pt = next_ps()
            op("pe", "transpose", [b_tmpC, b_ident], [b_pt], out=pt[:, 0:64], in_=tmpC[0:64, 0:128], identity=ident[0:64, 0:64])
            op("pe", "transpose", [b_tmpC, b_ident], [b_pt], out=pt[:, 64:128], in_=tmpC[0:64, 128:256], identity=ident[0:64, 0:64])
            op("dve", "tensor_tensor", [b_pt, b_s1t], [b_t64], out=t64[:], in0=pt[:, 64:128], in1=s1t[:], op=ALU.mult)
            op("dve", "tensor_tensor", [b_pt, b_c1t], [b_s5i], out=WS5[:, b, :], in0=pt[:, 0:64], in1=c1t[:], op=ALU.mult)
            op("dve", "tensor_sub", [b_s5i, b_t64], [b_s5i], out=WS5[:, b, :], in0=WS5[:, b, :], in1=t64[:])
        ps_y0, b_ps_y0 = PS_A
        ps_y1, b_ps_y1 = PS_B
        v4 = lambda ap: ap.rearrange("p (a b c) -> p a b c", a=4, b=4)
        for qd_i in range(16):
            g0 = qd_i * 4
            kc, half = g0 // 8, (g0 % 8) // 4
            hp = half * 64
            pd1, b_pd1 = next_ps()
            pd2, b_pd2 = next_ps()
            for gi in range(4):
                op("pe", "matmul", [b_LT1, b_hT], [b_pd1], pd1[:, gi * NS:(gi + 1) * NS], lhsT=LT1[hp:hp + 64, kc, gi, :], rhs=hT[hp:hp + 64, kc, 0:NS], start=True, stop=True)
            for gi in range(4):
                op("pe", "matmul", [b_LT2, b_hT], [b_pd2], pd2[:, gi * NS:(gi + 1) * NS], lhsT=LT2[hp:hp + 64, kc, gi, :], rhs=hT[hp:hp + 64, kc, 0:NS], start=True, stop=True)
            ctq = bcast(CT, 64 * 128, g0 * 128, [(128, 4), (0, 4), (1, 8)])
            stq = bcast(ST, 64 * 128, g0 * 128, [(128, 4), (0, 4), (1, 8)])
            op("dve", "tensor_tensor", [b_pd1, b_CT], [b_tmpA], out=v4(tmpA[:, 0:128]), in0=v4(pd1[:, 0:128]), in1=ctq, op=ALU.mult)
            op("dve", "tensor_tensor", [b_pd2, b_ST], [b_tmpB], out=v4(tmpB[:, 0:128]), in0=v4(pd2[:, 0:128]), in1=stq, op=ALU.mult)
            op("pool", "tensor_tensor", [b_tmpA, b_tmpB], [b_tmpC], out=tmpC[:, 0:128], in0=tmpA[:, 0:128], in1=tmpB[:, 0:128], op=ALU.add)
            for gi in range(4):
                g = g0 + gi
                for b in range(SB):
                    c0 = gi * NS + 8 * b
                    op("dve", "tensor_tensor_scan", [b_tmpC, b_lam_abs, b_s5i], [b_tmpD], out=tmpD[:, c0:c0 + 8],
                       data0=lam_abs[:, g:g + 1].to_broadcast([128, 8]), data1=tmpC[:, c0:c0 + 8], initial=WS5[:, b, g:g + 1], op0=ALU.mult, op1=ALU.add)
            at, b_at = eT[qd_i % 2]
            bt, b_bt = pT[qd_i % 2]
            op("dve", "tensor_tensor", [b_tmpD, b_CT], [b_at], out=v4(at[:, 0:128]), in0=v4(tmpD[:, 0:128]), in1=ctq, op=ALU.mult)
            op("pool", "tensor_tensor", [b_tmpD, b_ST], [b_bt], out=v4(bt[:, 0:128]), in0=v4(tmpD[:, 0:128]), in1=stq, op=ALU.mult)
            for b in range(SB):
                op("pool", "tensor_copy", [b_tmpD], [b_s5i], out=ZendS[:, b, g0:g0 + 4], in_=bcast(tmpD, 512, 8 * b + 7, [(NS, 4)]))
            for gi in range(4):
                g = g0 + gi
                py, b_py = (ps_y0, b_ps_y0) if g < 32 else (ps_y1, b_ps_y1)
                col = (g % 32) * 16
                op("pe", "matmul", [b_at, b_C1], [b_py], py[0:NS, col:col + 16], lhsT=at[:, gi * NS:(gi + 1) * NS], rhs=C1[:, g, :], start=True, stop=False)
                op("pe", "matmul", [b_bt, b_C2], [b_py], py[0:NS, col:col + 16], lhsT=bt[:, gi * NS:(gi + 1) * NS], rhs=C2[:, g, :], start=False, stop=True)
        for b in range(SB):
            pw, b_pw = next_ps()
            op("pe", "matmul", [b_swapm, b_s5i], [b_pw], pw[:, 0:64], lhsT=swapm[:], rhs=ZendS[:, b, :], start=True, stop=True)
            op("dve", "tensor_tensor", [b_pw, b_s7], [b_t64], out=t64[:], in0=pw[:, 0:64], in1=s7[:], op=ALU.mult)
            op("dve", "tensor_tensor", [b_s5i, b_c7], [b_fre], out=fre[:], in0=ZendS[:, b, :], in1=c7[:], op=ALU.mult)
            op("dve", "tensor_add", [b_fre, b_t64], [b_fre], out=fre[:], in0=fre[:], in1=t64[:])
            st(o_sss[b], fre[:], b_fre)
        ysb = xtok[:, 0:1024].rearrange("p (a b) -> p a b", a=2)
        op("act", "copy", [b_ps_y0], [b_xtok], out=ysb[0:NS, 0, :], in_=ps_y0[0:NS, :])
        op("dve", "tensor_copy", [b_ps_y1], [b_xtok], out=ysb[0:NS, 1, :], in_=ps_y1[0:NS, :])
        for half in range(2):
            pt, b_pt = next_ps()
            for k4 in range(4):
                op("pe", "transpose", [b_xtok, b_ident], [b_pt], out=pt[:, k4 * NS:(k4 + 1) * NS], in_=ysb[0:NS, half, k4 * 128:(k4 + 1) * 128], identity=ident[0:NS, 0:NS])
            for k4 in range(4):
                kc = half * 4 + k4
                op("dve", "scalar_tensor_tensor", [b_xT, b_gd, b_rstd], [b_tmpA], out=tmpA[:, k4 * NS:(k4 + 1) * NS], in0=xT[:, kc, 0:NS],
                   scalar=gd[:, kc:kc + 1], in1=rstd[:, 0:NS], op0=ALU.mult, op1=ALU.mult)
            op("dve", "tensor_tensor", [b_tmpA, b_pt], [b_tmpA], out=tmpA[:, 0:W4], in0=tmpA[:, 0:W4], in1=pt[:, 0:W4], op=ALU.add)
            op("act", "activation", [b_tmpA], [b_tmpB], out=tmpB[:, 0:W4], in_=tmpA[:, 0:W4], func=AF.Square)
            op("dve", "tensor_scalar", [b_tmpB], [b_tmpB], out=tmpB[:, 0:W4], in0=tmpB[:, 0:W4], scalar1=0.044715, scalar2=1.0, op0=ALU.mult, op1=ALU.add)
            op("dve", "tensor_tensor", [b_tmpB, b_tmpA], [b_tmpB], out=tmpB[:, 0:W4], in0=tmpB[:, 0:W4], in1=tmpA[:, 0:W4], op=ALU.mult)
            op("act", "activation", [b_tmpB], [b_tmpB], out=tmpB[:, 0:W4], in_=tmpB[:, 0:W4], func=AF.Sigmoid, scale=1.5957691216)
            for k4 in range(4):
                kc = half * 4 + k4
                op("dve", "tensor_tensor", [b_tmpA, b_tmpB], [b_mixT], out=mixT[:, kc, 0:NS], in0=tmpA[:, k4 * NS:(k4 + 1) * NS],
                   in1=tmpB[:, k4 * NS:(k4 + 1) * NS], op=ALU.mult)
        for p in range(4):
            view, b_pan = load_panel(wb_glu[p], KC, 512, b_wb_glu)
            for j in range(2):
                oc = 2 * p + j
                pv, b_pv = fm_chunk(view, b_pan, KC, j * 128, mixT, b_mixT, NS)
                pg, b_pg = fm_chunk(view, b_pan, KC, 256 + j * 128, mixT, b_mixT, NS)
                op("act", "activation", [b_pg], [b_tmpA], out=tmpA[:, 0:NS], in_=pg[:, 0:NS], func=AF.Sigmoid)
                op("dve", "tensor_tensor", [b_tmpA, b_pv], [b_tmpA], out=tmpA[:, 0:NS], in0=tmpA[:, 0:NS], in1=pv[:, 0:NS], op=ALU.mult)
                op("dve", "tensor_tensor", [b_tmpA, b_xT], [b_xT], out=xT[:, oc, 0:NS], in0=tmpA[:, 0:NS], in1=xT[:, oc, 0:NS], op=ALU.add)
        ffn(1, NS)
        final_out(o_ys, NS)

    S.finish()
    es.close()
    return nc


_NC_CACHE = {}
LAUNCH_RANGES = [(0, 16)]


def kernel(**inputs):
    f32 = np.float32
    x_prompt = np.asarray(inputs["x_prompt"], f32)
    consts = host_consts()
    wmap = {}
    for k, shp in WEIGHT_SHAPES.items():
        a = np.asarray(inputs[k], f32)
        if k in ("norm_mix", "norm_ffn", "norm_final", "w_ffn_in", "w_ffn_out"):
            wmap[k] = np.ascontiguousarray(a).reshape(shp)
        else:
            wmap[k] = np.ascontiguousarray(a[0]).reshape(shp)
    bf = ml_dtypes.bfloat16
    state = [{"i_Sst": np.zeros((128, 2, 128), f32), "i_Wc": np.zeros((128, 64), f32), "i_Zend": np.zeros((128, 64), f32),
              "i_KBT": np.zeros((128, 4, RING * 128), bf), "i_VB": np.zeros((128, RING, 512), bf)} for _ in range(NCORES)]
    y_prompt = np.zeros((2, SEQ, D), f32)
    r = None
    for li, (lo, hi) in enumerate(LAUNCH_RANGES):
        last = li == len(LAUNCH_RANGES) - 1
        key = ("nc", lo, hi, last)
        if key not in _NC_CACHE:
            _NC_CACHE[key] = build_program(hi, RUN_SAMPLE=last, blk_lo=lo)
        nc = _NC_CACHE[key]
        in_maps = []
        for c in range(NCORES):
            m = {"xp": np.ascontiguousarray(x_prompt[c % 2])}
            bs = slice(c * SB, (c + 1) * SB)
            m["xs"] = np.ascontiguousarray(np.asarray(inputs["x_sample"], f32)[bs].reshape(TS, D))
            m["st_ret"] = np.ascontiguousarray(np.asarray(inputs["state_ret"], f32)[0, bs])
            m["st_k"] = np.ascontiguousarray(np.asarray(inputs["state_swa_k"], f32)[0, bs].reshape(SB, WB, 512))
            m["st_v"] = np.ascontiguousarray(np.asarray(inputs["state_swa_v"], f32)[0, bs].reshape(SB, WB, 512))
            m["st_sr"] = np.ascontiguousarray(np.asarray(inputs["state_ssm_re"], f32)[0, bs])
            m["st_si"] = np.ascontiguousarray(np.asarray(inputs["state_ssm_im"], f32)[0, bs])
            m.update(state[c])
            m.update(wmap)
            m.update({"c_" + k: v for k, v in consts.items()})
            in_maps.append(m)
        res = run_bass_kernel_spmd(nc, in_maps, core_ids=list(range(NCORES)))
        r = res.results
        for sq_ in range(2):
            y_prompt[sq_, lo * 512:hi * 512] = r[sq_]["o_yp"][lo * 512:hi * 512]
        for c in range(NCORES if len(LAUNCH_RANGES) > 1 else 0):
            state[c] = {"i_Sst": np.asarray(r[c]["o_retp"], f32).reshape(128, 2, 128), "i_Wc": np.asarray(r[c]["o_Wc"], f32),
                        "i_Zend": np.asarray(r[c]["o_Zend"], f32), "i_KBT": np.asarray(r[c]["o_KBT"]).reshape(128, 4, RING * 128),
                        "i_VB": np.asarray(r[c]["o_VB"]).reshape(128, RING, 512)}
    B = 2
    y_sample = np.concatenate([r[c]["o_ys"] for c in range(NCORES)], 0).reshape(32, 8, D)

    def unret(a):
        a = np.asarray(a).reshape(128, 2, 128)
        out = np.zeros((4, 64, 128), f32)
        for h in range(4):
            out[h] = a[(h % 2) * 64:(h % 2) * 64 + 64, h // 2, :]
        return out
    ret_p = np.stack([unret(r[0]["o_retp"]), unret(r[1]["o_retp"])])[None]
    ret_s = np.stack([unret(np.asarray(r[c]["o_rets"]).reshape(SB, 128, 2, 128)[b]) for c in range(NCORES) for b in range(SB)])[None]
    swk_p = np.stack([r[0]["o_kp"], r[1]["o_kp"]]).reshape(1, B, WB, H_B, DH_B)
    swv_p = np.stack([r[0]["o_vp"], r[1]["o_vp"]]).reshape(1, B, WB, H_B, DH_B)
    swk_s = np.concatenate([r[c]["o_ks"] for c in range(NCORES)], 0).reshape(1, 32, WB, H_B, DH_B)
    swv_s = np.concatenate([r[c]["o_vs"] for c in range(NCORES)], 0).reshape(1, 32, WB, H_B, DH_B)
    sr_p = np.stack([r[0]["o_ssp"][0:64].T, r[1]["o_ssp"][0:64].T])[None]
    si_p = np.stack([r[0]["o_ssp"][64:128].T, r[1]["o_ssp"][64:128].T])[None]
    sss = lambda c, b: np.asarray(r[c]["o_sss"]).reshape(SB, 128, 64)[b]
    sr_s = np.stack([sss(c, b)[0:64].T for c in range(NCORES) for b in range(SB)])[None]
    si_s = np.stack([sss(c, b)[64:128].T for c in range(NCORES) for b in range(SB)])[None]
    return (y_prompt, y_sample, ret_p, ret_s, swk_p, swv_p, swk_s, swv_s, sr_p, si_p, sr_s, si_s)
```

```python
import numpy as np
import concourse.bass as bass
import concourse.mybir as mybir
from concourse.bass_utils import run_bass_kernel_spmd

F32 = mybir.dt.float32
BF16 = mybir.dt.bfloat16
ALU = mybir.AluOpType
AF = mybir.ActivationFunctionType

D = 1024
KC = D // 128
NCORES = 8
TP = 2048
TS = 32
SB = 4
WB = 2048
H_A, DK_A, DV_A = 4, 64, 128
H_B, DH_B = 8, 64
AB_IN = 3072
EPS = 1e-6


class Buf:
    __slots__ = ("name", "last_w", "readers")

    def __init__(self, name):
        self.name = name
        self.last_w = None
        self.readers = []


class Sched:
    ENGS = ("pe", "act", "dve", "pool", "sp")

    def __init__(self, nc):
        self.nc = nc
        self.ops = {e: [] for e in self.ENGS}
        self.dma_sems = []
        self.buf_dma = {}

    def _deps(self, reads, writes):
        deps = []
        for b in reads:
            if b.last_w is not None:
                deps.append(b.last_w)
        for b in writes:
            if b.last_w is not None:
                deps.append(b.last_w)
            deps.extend(b.readers)
        return deps

    def _commit(self, tok, reads, writes):
        for b in reads:
            b.readers = [r for r in b.readers if not (r[0] == tok[0] and r[1] == tok[1])]
            b.readers.append(tok)
        for b in writes:
            b.last_w = tok
            b.readers = []

    def op(self, eng, fn, reads=(), writes=()):
        deps = self._deps(reads, writes)
        idx = len(self.ops[eng])
        if eng == "pe":
            deps = [d for d in deps if not (d[0] == "e" and d[1] == "pe")]
        self.ops[eng].append({"fn": fn, "deps": deps, "sig": False, "dma": None})
        tok = ("e", eng, idx)
        self._commit(tok, reads, writes)
        return tok

    def dma(self, eng, fn, key, reads=(), writes=()):
        deps = self._deps(reads, writes)
        if key not in self.buf_dma:
            self.buf_dma[key] = [len(self.buf_dma), 0]
        ent = self.buf_dma[key]
        ent[1] += 16
        tok = ("d", ent[0], ent[1])
        self.ops[eng].append({"fn": fn, "deps": deps, "sig": False, "dma": ent[0]})
        self._commit(tok, reads, writes)
        return tok

    def finish(self, final_waits_eng="sp"):
        nc = self.nc
        for e in self.ENGS:
            for o in self.ops[e]:
                for d in o["deps"]:
                    if d[0] == "e":
                        self.ops[d[1]][d[2]]["sig"] = True
        cnt = {}
        for e in self.ENGS:
            c = 0
            for o in self.ops[e]:
                if o["sig"]:
                    c += 1
                o["cnt"] = c
            cnt[e] = c
        n_dma = len(self.buf_dma)
        from contextlib import ExitStack
        with ExitStack() as st:
            esem = {e: st.enter_context(nc.semaphore("es_" + e)) for e in self.ENGS}
            dsem = [st.enter_context(nc.semaphore("ds_%d" % i)) for i in range(n_dma)]
            block = st.enter_context(nc.Block())
            ops = self.ops
            finals = [(ent[0], ent[1]) for ent in self.buf_dma.values()]

            def emit(e, eng):
                waited_e = {}
                waited_d = {}
                for o in ops[e]:
                    need_e, need_d = {}, {}
                    for d in o["deps"]:
                        if d[0] == "e":
                            v = ops[d[1]][d[2]]["cnt"]
                            if v > need_e.get(d[1], 0):
                                need_e[d[1]] = v
                        else:
                            if d[2] > need_d.get(d[1], 0):
                                need_d[d[1]] = d[2]
                    for pe_, v in need_e.items():
                        if v > waited_e.get(pe_, 0):
                            eng.wait_ge(esem[pe_], v)
                            waited_e[pe_] = v
                    for si, v in need_d.items():
                        if v > waited_d.get(si, 0):
                            eng.wait_ge(dsem[si], v)
                            waited_d[si] = v
                    ins = o["fn"](eng)
                    if o["dma"] is not None:
                        ins.then_inc(dsem[o["dma"]], 16)
                    elif o["sig"]:
                        ins.then_inc(esem[e], 1)
                if e == final_waits_eng:
                    for si, v in finals:
                        eng.wait_ge(dsem[si], v)

            @block.tensor
            def _(eng):
                emit("pe", eng)

            @block.scalar
            def _(eng):
                emit("act", eng)

            @block.vector
            def _(eng):
                emit("dve", eng)

            @block.gpsimd
            def _(eng):
                emit("pool", eng)

            @block.sync
            def _(eng):
                emit("sp", eng)


import math
import os
import ml_dtypes
from contextlib import ExitStack

SEQ = 8192
NBLK = SEQ // 512
D_FF = 2816
FC = D_FF // 128
GAM = [1.0 - 2.0 ** (-5 - h) for h in range(4)]
SLOPES = [2.0 ** (-8.0 * (h + 1) / 8) for h in range(8)]
TW = 2944
RING = 20
TWO_PI = 2.0 * math.pi


def host_consts():
    f32 = np.float32
    c = {}
    c["ident_in"] = np.eye(128, dtype=f32)
    m = np.arange(128)[:, None]
    n = np.arange(128)[None, :]
    decT = np.zeros((128, 4, 128), np.float64)
    for h in range(4):
        decT[:, h, :] = np.where(n >= m, GAM[h] ** np.maximum(n - m, 0), 0.0)
    c["decT"] = decT.astype(f32)
    qdec = np.zeros((128, 2, 128), np.float64)
    for p in range(128):
        for pr in range(2):
            h = 2 * pr + p // 64
            qdec[p, pr, :] = GAM[h] ** (np.arange(128) + 1.0)
    c["qdec"] = qdec.astype(f32)
    kdec = np.zeros((128, 256), np.float64)
    for h in range(4):
        kdec[:, h * 64:(h + 1) * 64] = (GAM[h] ** (127.0 - np.arange(128)))[:, None] * 0.125
    c["kdec"] = kdec.astype(f32)
    jl = np.arange(128)[:, None]
    x = np.arange(TW)[None, :]
    dl = x - jl - 384
    cnt = ((dl <= 128).astype(np.float64) + ((dl % 4 == 0) & (dl <= 512)) + ((dl % 16 == 0) & (dl <= 2048)))
    valid = (dl >= 0) & (dl <= 2048)
    tab = np.zeros((8, 128, TW), np.float64)
    for h in range(8):
        tab[h] = np.where(valid, cnt * np.exp(-SLOPES[h] * np.maximum(dl, 0)), 0.0)
    c["swa_tab"] = tab.astype(ml_dtypes.bfloat16)
    sgn = np.ones((128, 1), f32); sgn[64:] = -1.0
    c["sgn"] = sgn
    c["tau"] = np.tile(np.arange(128, dtype=f32)[None, :], (128, 1))
    sw = np.zeros((128, 128), f32)
    for p in range(64):
        sw[p, p + 64] = 1.0; sw[p + 64, p] = 1.0
    c["swapm"] = sw
    rm = np.zeros((128, 4), f32)
    for p in range(128):
        rm[p, (p % 64) // 16] = 1.0
    c["rowmask"] = rm
    hm = np.zeros((128, 2), f32); hm[:64, 0] = 1.0; hm[64:, 1] = 1.0
    c["hmask"] = hm
    p32 = np.arange(32)
    kdS = np.zeros((32, 256), np.float64)
    for h in range(4):
        kdS[:, h * 64:(h + 1) * 64] = (GAM[h] ** (7.0 - (p32 % 8)))[:, None] * 0.125
    c["kdecS"] = kdS.astype(f32)
    qdS = np.zeros((128, 2, 32), np.float64)
    for p in range(128):
        for pr in range(2):
            qdS[p, pr, :] = GAM[2 * pr + p // 64] ** ((p32 % 8) + 1.0)
    c["qdecS"] = qdS.astype(f32)
    dS = np.zeros((32, 4, 32), np.float64)
    mm, nn = p32[:, None], p32[None, :]
    for h in range(4):
        dS[:, h, :] = np.where((mm // 8 == nn // 8) & (nn >= mm), GAM[h] ** np.maximum(nn - mm, 0), 0.0)
    c["decTS"] = dS.astype(f32)
    bm = np.zeros((32, 4), f32)
    bm[p32, p32 // 8] = 1.0
    c["bmask"] = bm
    r64 = np.arange(64)
    hh, tt = r64 // 8, r64 % 8
    jj = np.arange(2056)[None, :]
    dls = 2048 + tt[:, None] - jj
    cnts = ((dls <= 128).astype(np.float64) + ((dls % 4 == 0) & (dls <= 512)) + ((dls % 16 == 0) & (dls <= 2048)))
    ws = np.where((dls >= 0) & (dls <= 2048), cnts * np.exp(-np.array(SLOPES)[hh][:, None] * np.maximum(dls, 0)), 0.0)
    wsp = np.zeros((64, TW), np.float64); wsp[:, :2056] = ws
    c["WS"] = wsp.astype(ml_dtypes.bfloat16)
    hs = np.zeros((64, 512), f32)
    for r in range(64):
        hs[r, (r // 8) * 64:(r // 8) * 64 + 64] = 1.0
    c["hselB"] = hs
    eo = np.zeros((64, 2), f32); eo[:, 0] = (hh % 2 == 0); eo[:, 1] = (hh % 2 == 1)
    c["eomask"] = eo
    return c


CONST_SHAPES = {"ident_in": ([128, 128], F32), "decT": ([128, 4, 128], F32), "qdec": ([128, 2, 128], F32),
                "kdec": ([128, 256], F32), "swa_tab": ([8, 128, TW], BF16), "sgn": ([128, 1], F32),
                "tau": ([128, 128], F32), "swapm": ([128, 128], F32), "rowmask": ([128, 4], F32), "hmask": ([128, 2], F32),
                "kdecS": ([32, 256], F32), "qdecS": ([128, 2, 32], F32), "decTS": ([32, 4, 32], F32), "bmask": ([32, 4], F32),
                "WS": ([64, TW], BF16), "hselB": ([64, 512], F32), "eomask": ([64, 2], F32)}

WEIGHT_SHAPES = {"norm_mix": [2, D], "norm_ffn": [2, D], "norm_final": [D], "w_in_ab": [D, AB_IN], "ret_gn": [512],
                 "w_out_ab": [D, D], "ssm_lam_re": [64, 64], "ssm_lam_im": [64, 64], "ssm_log_step": [64],
                 "ssm_b_re": [64, 64, 16], "ssm_b_im": [64, 64, 16], "ssm_c_re": [64, 16, 64], "ssm_c_im": [64, 16, 64],
                 "ssm_d": [D], "w_glu": [D, 2 * D], "w_ffn_in": [2, D, 2 * D_FF], "w_ffn_out": [2, D_FF, D]}


def build_program(nblk_run=NBLK, PH=9, sim=False, SUB=9, RUN_SAMPLE=True, KV_FROM=NBLK - 4, blk_lo=0):
    nc = bass.Bass("TRN2", target_bir_lowering=False)
    S = Sched(nc)
    es = ExitStack()

    def din(name, shape, dt=F32):
        return nc.dram_tensor(name, list(shape), dt, kind="ExternalInput").ap()

    def dout(name, shape):
        return nc.dram_tensor(name, list(shape), F32, kind="ExternalOutput").ap()

    xp = din("xp", [SEQ, D])
    W = {k: din(k, v) for k, v in WEIGHT_SHAPES.items()}
    C = {k: din("c_" + k, v[0], v[1]) for k, v in CONST_SHAPES.items()}
    o_yp = dout("o_yp", [SEQ, D])
    o_kp = dout("o_kp", [WB, 512])
    o_vp = dout("o_vp", [WB, 512])
    o_retp = dout("o_retp", [128, 2, 128])
    o_ssp = dout("o_ssp", [128, 64])
    i_Sst = din("i_Sst", [128, 2, 128]); i_Wc = din("i_Wc", [128, 64]); i_Zend = din("i_Zend", [128, 64])
    i_KBT = din("i_KBT", [128, 4, RING * 128], BF16); i_VB = din("i_VB", [128, RING, 512], BF16)
    if len(LAUNCH_RANGES) > 1:
        o_Wc = dout("o_Wc", [128, 64]); o_Zend = dout("o_Zend", [128, 64])
        o_KBT = nc.dram_tensor("o_KBT", [128, 4, RING * 128], BF16, kind="ExternalOutput").ap()
        o_VB = nc.dram_tensor("o_VB", [128, RING, 512], BF16, kind="ExternalOutput").ap()
    xs = din("xs", [TS, D])
    st_ret = din("st_ret", [SB, 4, 64, 128])
    st_k = din("st_k", [SB, WB, 512]); st_v = din("st_v", [SB, WB, 512])
    st_sr = din("st_sr", [SB, 64, 64]); st_si = din("st_si", [SB, 64, 64])
    o_ys = dout("o_ys", [TS, D])
    o_rets = dout("o_rets", [SB, 128, 2, 128])
    o_ks = dout("o_ks", [SB, WB, 512]); o_vs = dout("o_vs", [SB, WB, 512])
    o_sss = dout("o_sss", [SB, 128, 64])

    def dscr(name, shape):
        if sim:
            return din(name, shape, BF16), Buf(name)
        t = nc.dram_tensor(name, list(shape), BF16)
        return t.ap(), Buf(name)
    wb_in, b_wb_in = dscr("wb_in", [6, 128, KC, 512])
    wb_out, b_wb_out = dscr("wb_out", [2, 128, KC, 512])
    wb_glu, b_wb_glu = dscr("wb_glu", [4, 128, KC, 512])
    wb_ffi, b_wb_ffi = dscr("wb_ffi", [2, FC // 2, 128, KC, 512])
    wb_ffo, b_wb_ffo = dscr("wb_ffo", [2, KC, 128, FC, 128])

    def cast_piece(dst_ap, src_ap, b_dst, kcn):
        if sim:
            return
        S.dma("pool", lambda e, a=dst_ap, b=src_ap.rearrange("(k p) n -> p k n", p=128): e.dma_start(out=a, in_=b), b_dst, writes=[b_dst])
    for pi in range(6):
        cast_piece(wb_in[pi], W["w_in_ab"][:, pi * 512:(pi + 1) * 512], b_wb_in, KC)
    for pi in range(2):
        cast_piece(wb_out[pi], W["w_out_ab"][:, pi * 512:(pi + 1) * 512], b_wb_out, KC)
    for l in range(2):
        for p in range(FC // 2):
            cast_piece(wb_ffi[l, p][:, :, 0:256], W["w_ffn_in"][l][:, p * 256:(p + 1) * 256], b_wb_ffi, KC)
            cast_piece(wb_ffi[l, p][:, :, 256:512], W["w_ffn_in"][l][:, D_FF + p * 256:D_FF + (p + 1) * 256], b_wb_ffi, KC)
        for oc in range(KC):
            cast_piece(wb_ffo[l, oc], W["w_ffn_out"][l][:, oc * 128:(oc + 1) * 128], b_wb_ffo, FC)
    for p in range(4):
        cast_piece(wb_glu[p][:, :, 0:256], W["w_glu"][:, p * 256:(p + 1) * 256], b_wb_glu, KC)
        cast_piece(wb_glu[p][:, :, 256:512], W["w_glu"][:, D + p * 256:D + (p + 1) * 256], b_wb_glu, KC)

    def sb(name, shape, dt=F32):
        t = es.enter_context(nc.sbuf_tensor(name, list(shape), dt))
        return t, Buf(name)

    def op(eng, method, reads, writes, *a, **kw):
        return S.op(eng, lambda e, m=method, a=a, kw=kw: getattr(e, m)(*a, **kw), reads=reads, writes=writes)

    def ld(dst_ap, src_ap, b_dst, eng="sp", **kw):
        return S.dma(eng, lambda e, a=dst_ap, b=src_ap, kw=kw: e.dma_start(out=a, in_=b, **kw), b_dst, writes=[b_dst])

    def st(dst_ap, src_ap, b_src, eng="sp", extra_reads=()):
        return S.dma(eng, lambda e, a=dst_ap, b=src_ap: e.dma_start(out=a, in_=b), b_src, reads=[b_src] + list(extra_reads))

    def bcast(t, free_total, off, dims):
        return bass.AP(t, off, [[free_total, 128]] + [[s_, c_] for s_, c_ in dims])

    ident, b_ident = sb("ident", [128, 128])
    ld(ident[:], C["ident_in"], b_ident)
    ones_d, b_ones_d = sb("ones_d", [128, 128], BF16)
    ones_g, b_ones_g = sb("ones_g", [128, 128], BF16)
    ones_1, b_ones_1 = sb("ones_1", [128, 128], BF16)
    op("dve", "memset", [], [b_ones_d], ones_d[:], 1.0 / D)
    op("dve", "memset", [], [b_ones_g], ones_g[:], 1.0 / 128)
    op("dve", "memset", [], [b_ones_1], ones_1[:], 1.0)
    gvec, b_gvec = sb("gvec", [128, 5, KC])
    for i, (nm, l) in enumerate([("norm_mix", 0), ("norm_ffn", 0), ("norm_mix", 1), ("norm_ffn", 1)]):
        ld(gvec[:, i, :], W[nm][l].rearrange("(k p) -> p k", p=128), b_gvec, allow_slow_non_contiguous=True)
    ld(gvec[:, 4, :], W["norm_final"].rearrange("(k p) -> p k", p=128), b_gvec, allow_slow_non_contiguous=True)
    gn, b_gn = sb("gn", [128, 4])
    ld(gn[:], W["ret_gn"].rearrange("(h p) -> p h", p=128), b_gn, allow_slow_non_contiguous=True)
    dvec, b_dvec = sb("dvec", [128, KC])
    ld(dvec[:], W["ssm_d"].rearrange("(k p) -> p k", p=128), b_dvec, allow_slow_non_contiguous=True)
    decT, b_decT = sb("decT", [128, 4, 128]); ld(decT[:], C["decT"], b_decT)
    qdec, b_qdec = sb("qdec", [128, 2, 128]); ld(qdec[:], C["qdec"], b_qdec)
    kdec, b_kdec = sb("kdec", [128, 256]); ld(kdec[:], C["kdec"], b_kdec)
    sgn, b_sgn = sb("sgn", [128, 1]); ld(sgn[:], C["sgn"], b_sgn)
    tau, b_tau = sb("tau", [128, 128]); ld(tau[:], C["tau"], b_tau)
    swapm, b_swapm = sb("swapm", [128, 128]); ld(swapm[:], C["swapm"], b_swapm)
    rowmask, b_rowmask = sb("rowmask", [128, 4]); ld(rowmask[:], C["rowmask"], b_rowmask)
    hmask, b_hmask = sb("hmask", [128, 2]); ld(hmask[:], C["hmask"], b_hmask)

    b_d2d = Buf("d2d")
    if RUN_SAMPLE:
        for b in range(SB):
            S.dma("sp", lambda e, a=o_ks[b, 0:WB - 8, :], c_=st_k[b, 8:WB, :]: e.dma_start(out=a, in_=c_), b_d2d)
            S.dma("sp", lambda e, a=o_vs[b, 0:WB - 8, :], c_=st_v[b, 8:WB, :]: e.dma_start(out=a, in_=c_), b_d2d)
    psb = []
    for i in range(8):
        t = es.enter_context(nc.psum_tensor("ps%d" % i, [128, 512], F32))
        psb.append((t, Buf("ps%d" % i)))
    rr = [0]

    def next_ps():
        i = rr[0] % 5
        rr[0] += 1
        return psb[i]
    PS_A, PS_B, PS_C = psb[5], psb[6], psb[7]

    PANEL_EL = 4096
    panels = [sb("panel%d" % i, [128, PANEL_EL], BF16) for i in range(2)]
    prr = [0]

    def load_panel(src_ap, kcn, w, b_src, pool=None):
        pool = panels if pool is None else pool
        slot_i = prr[0] % len(pool)
        t, b = pool[slot_i]
        prr[0] += 1
        view = t[:, 0:kcn * w].rearrange("p (k n) -> p k n", k=kcn)
        q = "sp" if (slot_i % 2) == 0 else "pool"
        S.dma(q, lambda e, a=t[:, 0:kcn * w], s_=src_ap.rearrange("p k n -> p (k n)"): e.dma_start(out=a, in_=s_), b, reads=[b_src], writes=[b])
        return view, b

    def fm_chunk(view, b_pan, kcn, c0, rhs_t, b_rhs, ntok, extra=None):
        pt, b_pt = next_ps()
        for kc in range(kcn):
            op("pe", "matmul", [b_pan, b_rhs], [b_pt], pt[:, 0:ntok], lhsT=view[:, kc, c0:c0 + 128],
               rhs=rhs_t[:, kc, 0:ntok], start=(kc == 0), stop=(kc == kcn - 1))
        return pt, b_pt

    def tm_tile(view, b_pan, kcn, c0, w, lhs_t, b_lhs, t0, rows):
        pt, b_pt = next_ps()
        for kc in range(kcn):
            op("pe", "matmul", [b_pan, b_lhs], [b_pt], pt[0:rows, 0:w], lhsT=lhs_t[:, kc, t0:t0 + rows],
               rhs=view[:, kc, c0:c0 + w], start=(kc == 0), stop=(kc == kcn - 1))
        return pt, b_pt

    xtok, b_xtok = sb("xtok", [128, D])
    xT, b_xT = sb("xT", [128, KC, 512])
    rstd, b_rstd = sb("rstd", [128, 512])
    hT, b_hT = sb("hT", [128, KC, 512], BF16)
    sq, b_sq = hT, b_hT
    mixT, b_mixT = sb("mixT", [128, KC, 512], BF16)
    act, b_act = sb("act", [128, FC, 512], BF16)
    qaT, b_qaT = act[:, 0:2, :], b_act
    kaT, b_kaT = act[:, 2:4, :], b_act
    vatok, b_vatok = act[:, 4:8, :], b_act
    sgT, b_sgT = act[:, 8:12, :], b_act
    qbT, b_qbT = act[:, 12:16, :], b_act
    katok, b_katok = act[:, 16:18, :].rearrange("p a (b c) -> p (a b) c", c=256), b_act
    KBT, b_KBT = sb("KBT", [128, 4, RING * 128], BF16)
    VB, b_VB = sb("VB", [128, RING, 512], BF16)
    kvo = [(xtok[:, 0:512], b_xtok), (xtok[:, 512:1024], b_xtok)]
    tmpA, b_tmpA = sb("tmpA", [128, 512])
    tmpB, b_tmpB = sb("tmpB", [128, 512])
    tmpC, b_tmpC = sb("tmpC", [128, 512])
    tmpD, b_tmpD = sb("tmpD", [128, 512])
    pT = [sb("pT%d" % i, [128, 512], BF16) for i in range(3)]
    eT = [sb("eT%d" % i, [128, 512], BF16) for i in range(3)]
    thtab = [sb("thtab%d" % i, [128, TW], BF16) for i in range(1)]
    Sst, b_Sst = sb("Sst", [128, 2, 128])
    Sbf, b_Sbf = sb("Sbf", [128, 2, 128], BF16)
    op("dve", "memset", [], [b_Sst], Sst[:], 0.0)
    op("dve", "memset", [], [b_Sbf], Sbf[:], 0.0)
    qz, b_qz = sb("qz", [128, 4, 128], BF16)
    qd, b_qd = sb("qd", [128, 4, 128], BF16)
    osb, b_osb = tmpC, b_tmpC
    obf, b_obf = eT[0]
    osq, b_osq = eT[1]

    lam_abs, b_lam_abs = sb("lam_abs", [128, 64])
    th, b_th = sb("th", [128, 64])
    thS, b_thS = sb("thS", [128, 64])
    CT, b_CT = sb("CT", [128, 64, 128], BF16)
    ST, b_ST = sb("ST", [128, 64, 128], BF16)
    LT1, b_LT1 = sb("LT1", [128, KC, 4, 128], BF16)
    LT2, b_LT2 = sb("LT2", [128, KC, 4, 128], BF16)
    C1, b_C1 = sb("C1", [128, 64, 16], BF16)
    C2, b_C2 = sb("C2", [128, 64, 16], BF16)
    cL, b_cL = sb("cL", [128, 64]); sL, b_sL = sb("sL", [128, 64])
    cE, b_cE = sb("cE", [128, 64]); sE, b_sE = sb("sE", [128, 64])
    gd, b_gd = sb("gd", [128, KC])
    Wc, b_Wc = sb("Wc", [128, 64])
    Zend, b_Zend = sb("Zend", [128, 64])
    op("dve", "memset", [], [b_Wc], Wc[:], 0.0)
    op("dve", "memset", [], [b_Zend], Zend[:], 0.0)
    setup_es = ExitStack()

    def sbt(name, shape, dt=F32):
        t = setup_es.enter_context(nc.sbuf_tensor(name, list(shape), dt))
        return t, Buf(name)

    lamT, b_lamT = sb("lamT", [64, 256])
    ld(lamT[:, 0:64], W["ssm_lam_re"], b_lamT); ld(lamT[:, 64:128], W["ssm_lam_re"], b_lamT)
    ld(lamT[:, 128:192], W["ssm_lam_im"], b_lamT); ld(lamT[:, 192:256], W["ssm_lam_im"], b_lamT)
    lre, b_lre = sb("lre", [128, 64]); lim, b_lim = sb("lim", [128, 64])
    for src0, dst, b_dst in ((0, lre, b_lre), (128, lim, b_lim)):
        pt, b_pt = next_ps()
        op("pe", "transpose", [b_lamT, b_ident], [b_pt], out=pt[:, 0:64], in_=lamT[:, src0:src0 + 128], identity=ident[0:64, 0:64])
        op("dve", "tensor_copy", [b_pt], [b_dst], out=dst[:], in_=pt[:, 0:64])
    dtt, b_dtt = sb("dtt", [128, 64])
    ld(dtt[:], W["ssm_log_step"].partition_broadcast(128), b_dtt)
    op("act", "activation", [b_dtt], [b_dtt], out=dtt[:], in_=dtt[:], func=AF.Exp)
    op("dve", "tensor_mul", [b_lim, b_dtt], [b_th], out=th[:], in0=lim[:], in1=dtt[:])
    op("dve", "tensor_scalar_mul", [b_th, b_sgn], [b_thS], out=thS[:], in0=th[:], scalar1=sgn[:, 0:1])
    rho, b_rho = sb("rho", [128, 64])
    op("dve", "tensor_mul", [b_lre, b_dtt], [b_rho], out=rho[:], in0=lre[:], in1=dtt[:])
    op("act", "activation", [b_rho], [b_lam_abs], out=lam_abs[:], in_=rho[:], func=AF.Exp)

    s5a, b_s5a = tmpA, b_tmpA
    s5b, b_s5b = tmpB, b_tmpB
    s5i, b_s5i = sb("s5i", [128, 512], mybir.dt.int32)

    def sin_of(dst_ap, ang_ap, n, b_dst, reads):
        shp = ang_ap.shape
        kb = s5b[:, 0:n] if len(shp) == 2 else s5b[:, 0:n].rearrange("p (a b) -> p a b", a=shp[1])
        ki = s5i[:, 0:n] if len(shp) == 2 else s5i[:, 0:n].rearrange("p (a b) -> p a b", a=shp[1])
        op("dve", "tensor_scalar_mul", reads, [b_s5b], out=kb, in0=ang_ap, scalar1=1.0 / TWO_PI)
        op("dve", "tensor_copy", [b_s5b], [b_s5i], out=ki, in_=kb)
        op("dve", "tensor_copy", [b_s5i], [b_s5b], out=kb, in_=ki)
        op("dve", "scalar_tensor_tensor", [b_s5b] + reads, [b_s5b], out=kb, in0=kb, scalar=-TWO_PI, in1=ang_ap,
           op0=ALU.mult, op1=ALU.add)
        op("dve", "tensor_scalar", [b_s5b], [b_s5b], out=kb, in0=kb, scalar1=-3.14159, scalar2=3.14159, op0=ALU.max, op1=ALU.min)
        op("act", "activation", [b_s5b], [b_dst], out=dst_ap, in_=kb, func=AF.Sin)

    for g0 in range(0, 64, 4):
        angv = s5a[:, 0:512].rearrange("p (a b) -> p a b", a=4)
        for (thsrc, b_thsrc, dst, b_dst, shift) in ((th, b_th, CT, b_CT, math.pi / 2), (thS, b_thS, ST, b_ST, 0.0)):
            op("dve", "tensor_tensor", [b_thsrc, b_tau], [b_s5a], out=angv, in0=bcast(thsrc, 64, g0, [(1, 4), (0, 128)]),
               in1=bcast(tau, 128, 0, [(0, 4), (1, 128)]), op=ALU.mult)
            if shift:
                op("dve", "tensor_scalar_add", [b_s5a], [b_s5a], out=angv, in0=angv, scalar1=shift)
            sin_of(dst[:, g0:g0 + 4, :], angv, 512, b_dst, [b_s5a])

    def cs_small(mult, cdst, b_c, sdst, b_s, neg_sin):
        a = s5a[:, 0:64]
        op("dve", "tensor_scalar", [b_th], [b_s5a], out=a, in0=th[:], scalar1=float(mult), scalar2=math.pi / 2,
           op0=ALU.mult, op1=ALU.add)
        sin_of(cdst[:], a, 64, b_c, [b_s5a])
        op("dve", "tensor_scalar_mul", [b_thS], [b_s5a], out=a, in0=thS[:], scalar1=(-float(mult) if neg_sin else float(mult)))
        sin_of(sdst[:], a, 64, b_s, [b_s5a])
    cs_small(128.0, cL, b_cL, sL, b_sL, True)
    cs_small(127.0, cE, b_cE, sE, b_sE, True)
    c1t, b_c1t = sb("c1t", [128, 64]); s1t, b_s1t = sb("s1t", [128, 64])
    cs_small(1.0, c1t, b_c1t, s1t, b_s1t, False)
    fre, b_fre = sb("fre", [128, 64]); fim, b_fim = sb("fim", [128, 64])
    abr, b_abr = sb("abr", [128, 64]); abi, b_abi = sb("abi", [128, 64]); den, b_den = sb("den", [128, 64])
    t64, b_t64 = sb("t64", [128, 64])
    op("dve", "tensor_mul", [b_lam_abs, b_c1t], [b_abr], out=abr[:], in0=lam_abs[:], in1=c1t[:])
    op("dve", "tensor_scalar_add", [b_abr], [b_abr], out=abr[:], in0=abr[:], scalar1=-1.0)
    op("dve", "tensor_mul", [b_lam_abs, b_s1t], [b_abi], out=abi[:], in0=lam_abs[:], in1=s1t[:])
    op("dve", "tensor_scalar_mul", [b_abi, b_sgn], [b_abi], out=abi[:], in0=abi[:], scalar1=sgn[:, 0:1])
    op("dve", "tensor_mul", [b_lre], [b_den], out=den[:], in0=lre[:], in1=lre[:])
    op("dve", "tensor_mul", [b_lim], [b_t64], out=t64[:], in0=lim[:], in1=lim[:])
    op("dve", "tensor_add", [b_den, b_t64], [b_den], out=den[:], in0=den[:], in1=t64[:])
    op("dve", "reciprocal", [b_den], [b_den], out=den[:], in_=den[:])
    op("dve", "tensor_mul", [b_abr, b_lre], [b_fre], out=fre[:], in0=abr[:], in1=lre[:])
    op("dve", "tensor_mul", [b_abi, b_lim], [b_t64], out=t64[:], in0=abi[:], in1=lim[:])
    op("dve", "tensor_add", [b_fre, b_t64], [b_fre], out=fre[:], in0=fre[:], in1=t64[:])
    op("dve", "tensor_mul", [b_fre, b_den], [b_fre], out=fre[:], in0=fre[:], in1=den[:])
    op("dve", "tensor_mul", [b_abi, b_lre], [b_fim], out=fim[:], in0=abi[:], in1=lre[:])
    op("dve", "tensor_mul", [b_abr, b_lim], [b_t64], out=t64[:], in0=abr[:], in1=lim[:])
    op("dve", "tensor_sub", [b_fim, b_t64], [b_fim], out=fim[:], in0=fim[:], in1=t64[:])
    op("dve", "tensor_mul", [b_fim, b_den], [b_fim], out=fim[:], in0=fim[:], in1=den[:])
    fiS, b_fiS = sb("fiS", [128, 64])
    op("dve", "tensor_scalar_mul", [b_fim, b_sgn], [b_fiS], out=fiS[:], in0=fim[:], scalar1=sgn[:, 0:1])
    bre = W["ssm_b_re"].rearrange("g n q -> n g q"); bim = W["ssm_b_im"].rearrange("g n q -> n g q")
    v3 = lambda t, c0: t[:, c0:c0 + 128].rearrange("p (a b) -> p a b", a=8)
    for kc in range(KC):
        B1k, B2k = v3(tmpC, 0), v3(tmpD, 0)
        FBak, FBbk, tFk = v3(tmpA, 0), v3(tmpB, 0), v3(tmpA, 128)
        gsl = slice(kc * 8, (kc + 1) * 8)
        ld(B1k[0:64], bre[:, gsl, :], b_tmpC); ld(B1k[64:128], bim[:, gsl, :], b_tmpC)
        ld(B2k[0:64], bim[:, gsl, :], b_tmpD); ld(B2k[64:128], bre[:, gsl, :], b_tmpD)
        frb = bcast(fre, 64, kc * 8, [(1, 8), (0, 16)]); fib = bcast(fiS, 64, kc * 8, [(1, 8), (0, 16)])
        op("dve", "tensor_tensor", [b_tmpC, b_fre], [b_tmpA], out=FBak, in0=B1k, in1=frb, op=ALU.mult)
        op("dve", "tensor_tensor", [b_tmpD, b_fiS], [b_tmpA], out=tFk, in0=B2k, in1=fib, op=ALU.mult)
        op("dve", "tensor_sub", [b_tmpA], [b_tmpA], out=FBak, in0=FBak, in1=tFk)
        op("dve", "tensor_tensor", [b_tmpD, b_fre], [b_tmpB], out=FBbk, in0=B2k, in1=frb, op=ALU.mult)
        op("dve", "tensor_tensor", [b_tmpC, b_fiS], [b_tmpA], out=tFk, in0=B1k, in1=fib, op=ALU.mult)
        op("dve", "tensor_add", [b_tmpB, b_tmpA], [b_tmpB], out=FBbk, in0=FBbk, in1=tFk)
        for (FBt, b_FB, LT, b_LT) in ((tmpA, b_tmpA, LT1, b_LT1), (tmpB, b_tmpB, LT2, b_LT2)):
            pt, b_pt = next_ps()
            op("pe", "transpose", [b_FB, b_ident], [b_pt], out=pt[:, 0:128], in_=FBt[:, 0:128], identity=ident[:])
            for gi in range(4):
                op("dve", "tensor_scalar_mul", [b_pt, b_rowmask], [b_LT], out=LT[:, kc, gi, :], in0=pt[:, 0:128],
                   scalar1=rowmask[:, gi:gi + 1])
    cre = W["ssm_c_re"].rearrange("(k a) p n -> k (a p) n", k=KC); cim = W["ssm_c_im"].rearrange("(k a) p n -> k (a p) n", k=KC)
    for kc in range(KC):
        ld(tmpC[:, 0:64], cre[kc], b_tmpC); ld(tmpC[:, 64:128], cim[kc], b_tmpC)
        for (Cd, b_Cd, first_im) in ((C1, b_C1, False), (C2, b_C2, True)):
            cst = tmpD
            if not first_im:
                op("dve", "tensor_copy", [b_tmpC], [b_tmpD], out=cst[:, 0:64], in_=tmpC[:, 0:64])
                op("dve", "tensor_scalar_mul", [b_tmpC], [b_tmpD], out=cst[:, 64:128], in0=tmpC[:, 64:128], scalar1=-1.0)
            else:
                op("dve", "tensor_scalar_mul", [b_tmpC], [b_tmpD], out=cst[:, 0:64], in0=tmpC[:, 64:128], scalar1=-1.0)
                op("dve", "tensor_copy", [b_tmpC], [b_tmpD], out=cst[:, 64:128], in_=tmpC[:, 0:64])
            pt, b_pt = next_ps()
            op("pe", "transpose", [b_tmpD, b_ident], [b_pt], out=pt[:, 0:128], in_=cst[:, 0:128], identity=ident[:])
            op("dve", "tensor_copy", [b_pt], [b_Cd], out=Cd[:, kc * 8:(kc + 1) * 8, :].rearrange("p a b -> p (a b)"), in_=pt[:, 0:128])
    op("dve", "tensor_mul", [b_gvec, b_dvec], [b_gd], out=gd[:], in0=gvec[:, 2, :], in1=dvec[:])

    def evac(i, pt, b_pt, dst_ap, b_dst, n, scale=None, func=None, rows=128):
        if func is not None:
            kw = {"scale": scale} if scale is not None else {}
            op("act", "activation", [b_pt], [b_dst], out=dst_ap, in_=pt[0:rows, 0:n], func=func, **kw)
        elif scale is not None:
            op("act", "activation", [b_pt], [b_dst], out=dst_ap, in_=pt[0:rows, 0:n], func=AF.Copy, scale=scale)
        elif i % 2 == 0:
            op("act", "copy", [b_pt], [b_dst], out=dst_ap, in_=pt[0:rows, 0:n])
        else:
            op("dve", "tensor_copy", [b_pt], [b_dst], out=dst_ap, in_=pt[0:rows, 0:n])

    def rmsnorm(gi, ntok, dst=None, b_dst=None):
        dst = hT if dst is None else dst
        b_dst = b_hT if b_dst is None else b_dst
        op("act", "activation", [b_xT], [b_sq], out=sq[:, :, 0:ntok], in_=xT[:, :, 0:ntok], func=AF.Square)
        pt, b_pt = next_ps()
        for kc in range(KC):
            op("pe", "matmul", [b_ones_d, b_sq], [b_pt], pt[:, 0:ntok], lhsT=ones_d[:], rhs=sq[:, kc, 0:ntok],
               start=(kc == 0), stop=(kc == KC - 1))
        op("dve", "tensor_scalar_add", [b_pt], [b_rstd], out=rstd[:, 0:ntok], in0=pt[:, 0:ntok], scalar1=EPS)
        op("act", "activation", [b_rstd], [b_rstd], out=rstd[:, 0:ntok], in_=rstd[:, 0:ntok], func=AF.Sqrt)
        op("dve", "reciprocal", [b_rstd], [b_rstd], out=rstd[:, 0:ntok], in_=rstd[:, 0:ntok])
        for kc in range(KC):
            op("dve", "scalar_tensor_tensor", [b_xT, b_rstd, b_gvec], [b_dst], out=dst[:, kc, 0:ntok],
               in0=xT[:, kc, 0:ntok], scalar=gvec[:, gi, kc:kc + 1], in1=rstd[:, 0:ntok], op0=ALU.mult, op1=ALU.mult)

    def ffn(l, ntok):
        rmsnorm(1 + 2 * l, ntok)
        pool_in = panels + [(mixT[:].rearrange("p a b -> p (a b)"), b_mixT)]
        pool_out = pool_in + [(hT[:].rearrange("p a b -> p (a b)"), b_hT)]
        for p in range(FC // 2):
            view, b_pan = load_panel(wb_ffi[l, p], KC, 512, b_wb_ffi, pool_in)
            for j in range(2):
                pg, b_pg = fm_chunk(view, b_pan, KC, j * 128, hT, b_hT, ntok)
                pu, b_pu = fm_chunk(view, b_pan, KC, 256 + j * 128, hT, b_hT, ntok)
                op("act", "activation", [b_pg], [b_tmpA], out=tmpA[:, 0:ntok], in_=pg[:, 0:ntok], func=AF.Silu)
                op("dve", "tensor_tensor", [b_tmpA, b_pu], [b_act], out=act[:, 2 * p + j, 0:ntok], in0=tmpA[:, 0:ntok],
                   in1=pu[:, 0:ntok], op=ALU.mult)
        for oc in range(KC):
            view, b_pan = load_panel(wb_ffo[l, oc], FC, 128, b_wb_ffo, pool_out)
            pt, b_pt = fm_chunk(view, b_pan, FC, 0, act, b_act, ntok)
            op("dve", "tensor_tensor", [b_pt, b_xT], [b_xT], out=xT[:, oc, 0:ntok], in0=pt[:, 0:ntok],
               in1=xT[:, oc, 0:ntok], op=ALU.add)

    def load_x(src_ap, ntok):
        ntile = (ntok + 127) // 128
        for t in range(ntile):
            rows = min(128, ntok - t * 128)
            ld(xtok[0:rows, :], src_ap[t * 128:t * 128 + rows, :], b_xtok)
            for half in range(2):
                pt, b_pt = next_ps()
                for k4 in range(4):
                    kc = half * 4 + k4
                    op("pe", "transpose", [b_xtok, b_ident], [b_pt], out=pt[:, k4 * 128:k4 * 128 + rows],
                       in_=xtok[0:rows, kc * 128:(kc + 1) * 128], identity=ident[0:rows, 0:rows])
                src = pt[:, 0:512].rearrange("p (a b) -> p a b", a=4)[:, :, 0:rows]
                dst = xT[:, half * 4:half * 4 + 4, t * 128:t * 128 + rows]
                if half == 0:
                    op("act", "copy", [b_pt], [b_xT], out=dst, in_=src)
                else:
                    op("dve", "tensor_copy", [b_pt], [b_xT], out=dst, in_=src)

    def final_out(dst_ap, ntok):
        rmsnorm(4, ntok, dst=xT, b_dst=b_xT)
        ntile = (ntok + 127) // 128
        for t in range(ntile):
            rows = min(128, ntok - t * 128)
            for half in range(2):
                pt, b_pt = next_ps()
                for k4 in range(4):
                    kc = half * 4 + k4
                    op("pe", "transpose", [b_xT, b_ident], [b_pt], out=pt[0:rows, k4 * 128:(k4 + 1) * 128],
                       in_=xT[:, kc, t * 128:t * 128 + rows], identity=ident[:])
                evac(half, pt, b_pt, xtok[0:rows, half * 512:(half + 1) * 512], b_xtok, 512, rows=rows)
            st(dst_ap[t * 128:t * 128 + rows, :], xtok[0:rows, :], b_xtok)

    S5BUFS = []
    if blk_lo > 0:
        ld(Sst[:], i_Sst, b_Sst)
        op("dve", "tensor_copy", [b_Sst], [b_Sbf], out=Sbf[:], in_=Sst[:])
        ld(Wc[:], i_Wc, b_Wc); ld(Zend[:], i_Zend, b_Zend)
        ld(KBT[:], i_KBT, b_KBT); ld(VB[:], i_VB, b_VB)
    for blk in range(blk_lo, nblk_run):
        t0 = blk * 512
        load_x(xp[t0:t0 + 512, :], 512)
        rmsnorm(0, 512)
        last_kv = blk >= KV_FROM
        for pi in range(6):
            view, b_pan = load_panel(wb_in[pi], KC, 512, b_wb_in)
            if pi == 0:
                for j in range(2):
                    pt, b_pt = fm_chunk(view, b_pan, KC, j * 128, hT, b_hT, 512)
                    evac(j, pt, b_pt, qaT[:, j, :], b_qaT, 512)
                for j in range(2):
                    pt, b_pt = fm_chunk(view, b_pan, KC, 256 + j * 128, hT, b_hT, 512)
                    evac(j, pt, b_pt, kaT[:, j, :], b_kaT, 512, scale=0.125)
                for t in range(4):
                    pt, b_pt = tm_tile(view, b_pan, KC, 256, 256, hT, b_hT, t * 128, 128)
                    op("dve", "tensor_tensor", [b_pt, b_kdec], [b_katok], out=katok[:, t, :], in0=pt[:, 0:256], in1=kdec[:], op=ALU.mult)
            elif pi == 1:
                for t in range(4):
                    pt, b_pt = tm_tile(view, b_pan, KC, 0, 512, hT, b_hT, t * 128, 128)
                    evac(t, pt, b_pt, vatok[:, t, :], b_vatok, 512)
            elif pi == 2:
                for j in range(4):
                    pt, b_pt = fm_chunk(view, b_pan, KC, j * 128, hT, b_hT, 512)
                    evac(j, pt, b_pt, sgT[:, j, :], b_sgT, 512, func=AF.Silu)
            elif pi == 3:
                for j in range(4):
                    pt, b_pt = fm_chunk(view, b_pan, KC, j * 128, hT, b_hT, 512)
                    evac(j, pt, b_pt, qbT[:, j, :], b_qbT, 512)
            elif pi == 4:
                rs0 = ((blk * 4) % RING) * 128
                for j in range(4):
                    pt, b_pt = fm_chunk(view, b_pan, KC, j * 128, hT, b_hT, 512)
                    evac(j, pt, b_pt, KBT[:, j, rs0:rs0 + 512], b_KBT, 512)
                if last_kv and os.environ.get("NOKVK") is None:
                    for t in range(4):
                        pt, b_pt = tm_tile(view, b_pan, KC, 0, 512, hT, b_hT, t * 128, 128)
                        ko, b_ko = (tmpC, b_tmpC) if t % 2 == 0 else (tmpD, b_tmpD)
                        evac(t, pt, b_pt, ko[:], b_ko, 512)
                        r0 = (blk - KV_FROM) * 512 + t * 128
                        st(o_kp[r0:r0 + 128, :], ko[:], b_ko)
            else:
                for t in range(4):
                    pt, b_pt = tm_tile(view, b_pan, KC, 0, 512, hT, b_hT, t * 128, 128)
                    slot = (blk * 4 + t) % RING
                    if last_kv and os.environ.get("NOKVV") is None:
                        ko, b_ko = (tmpC, b_tmpC) if t % 2 == 0 else (tmpD, b_tmpD)
                        evac(t, pt, b_pt, ko[:], b_ko, 512)
                        op("pool", "tensor_copy", [b_ko], [b_VB], out=VB[:, slot, :], in_=ko[:])
                        r0 = (blk - KV_FROM) * 512 + t * 128
                        st(o_vp[r0:r0 + 128, :], ko[:], b_ko)
                    else:
                        evac(t, pt, b_pt, VB[:, slot, :], b_VB, 512)
        for c in range(4 if PH >= 2 else 0):
            cs = c * 128
            for h in range(4):
                pr = h // 2
                op("dve", "tensor_scalar_mul", [b_qaT, b_hmask], [b_qz], out=qz[:, h, :], in0=qaT[:, pr, cs:cs + 128], scalar1=hmask[:, (h % 2):(h % 2) + 1])
                op("dve", "tensor_tensor", [b_qz, b_qdec], [b_qd], out=qd[:, h, :], in0=qz[:, h, :], in1=qdec[:, pr, :], op=ALU.mult)
            ps_s, b_ps_s = next_ps()
            for h in range(4):
                pr = h // 2
                op("pe", "matmul", [b_kaT, b_qz], [b_ps_s], ps_s[:, h * 128:(h + 1) * 128], lhsT=kaT[:, pr, cs:cs + 128],
                   rhs=qz[:, h, :], start=True, stop=True)
            pTt, b_pTt = pT[c % 3]
            op("dve", "tensor_tensor", [b_ps_s, b_decT], [b_pTt], out=pTt[:], in0=ps_s[:], in1=decT[:].rearrange("p a b -> p (a b)"), op=ALU.mult)
            if SUB < 2:
                continue
            ps_o, b_ps_o = next_ps()
            for h in range(4):
                pr = h // 2
                op("pe", "matmul", [b_vatok, b_pTt], [b_ps_o], ps_o[:, h * 128:(h + 1) * 128], lhsT=vatok[:, c, h * 128:(h + 1) * 128],
                   rhs=pTt[:, h * 128:(h + 1) * 128], start=True, stop=False)
                op("pe", "matmul", [b_Sbf, b_qd], [b_ps_o], ps_o[:, h * 128:(h + 1) * 128], lhsT=Sbf[:, pr, :],
                   rhs=qd[:, h, :], start=False, stop=True)
            if SUB < 3:
                continue
            ps_d, b_ps_d = next_ps()
            for h in range(4):
                pr = h // 2
                op("pe", "matmul", [b_katok, b_vatok], [b_ps_d], ps_d[:, h * 128:(h + 1) * 128], lhsT=katok[:, c, pr * 128:(pr + 1) * 128],
                   rhs=vatok[:, c, h * 128:(h + 1) * 128], start=True, stop=True)
            for h in range(4):
                hp, pr = (h % 2) * 64, h // 2
                op("dve", "scalar_tensor_tensor", [b_Sst, b_ps_d, b_Sbf, b_ps_o], [b_Sst], out=Sst[hp:hp + 64, pr, :], in0=Sst[hp:hp + 64, pr, :],
                   scalar=float(GAM[h] ** 128), in1=ps_d[hp:hp + 64, h * 128:(h + 1) * 128], op0=ALU.mult, op1=ALU.add)
            op("dve", "tensor_copy", [b_Sst], [b_Sbf], out=Sbf[:], in_=Sst[:])
            if SUB < 4:
                continue
            op("act", "copy", [b_ps_o], [b_osb], out=osb[:], in_=ps_o[:])
            op("dve", "tensor_copy", [b_osb], [b_obf], out=obf[:], in_=osb[:])
            ps_m, b_ps_m = next_ps()
            op("pe", "matmul", [b_ones_g, b_obf], [b_ps_m], ps_m[:], lhsT=ones_g[:], rhs=obf[:], start=True, stop=True)
            op("dve", "tensor_tensor", [b_osb, b_ps_m], [b_osb], out=osb[:], in0=osb[:], in1=ps_m[:], op=ALU.subtract)
            op("act", "activation", [b_osb], [b_osq], out=osq[:], in_=osb[:], func=AF.Square)
            ps_q, b_ps_q = next_ps()
            op("pe", "matmul", [b_ones_g, b_osq], [b_ps_q], ps_q[:], lhsT=ones_g[:], rhs=osq[:], start=True, stop=True)
            if SUB < 5:
                continue
            op("dve", "tensor_scalar_add", [b_ps_q], [b_tmpA], out=tmpA[:], in0=ps_q[:], scalar1=EPS)
            op("act", "activation", [b_tmpA], [b_tmpA], out=tmpA[:], in_=tmpA[:], func=AF.Sqrt)
            op("dve", "reciprocal", [b_tmpA], [b_tmpA], out=tmpA[:], in_=tmpA[:])
            op("dve", "tensor_tensor", [b_osb, b_tmpA], [b_osb], out=osb[:], in0=osb[:], in1=tmpA[:], op=ALU.mult)
            if SUB < 6:
                continue
            for h in range(4):
                op("dve", "scalar_tensor_tensor", [b_osb, b_gn, b_sgT], [b_mixT], out=mixT[:, h, cs:cs + 128], in0=osb[:, h * 128:(h + 1) * 128],
                   scalar=gn[:, h:h + 1], in1=sgT[:, h, cs:cs + 128], op0=ALU.mult, op1=ALU.mult)
        kt_hi = blk * 4 + 3
        kt_lo = max(0, blk * 4 - 16)
        for h in range(8 if PH >= 3 else 0):
            hp, pr = (h % 2) * 64, h // 2
            tht, b_tht = thtab[0]
            ld(tht[:], C["swa_tab"][h], b_tht)
            ps_o, b_ps_o = PS_A if h % 2 == 0 else PS_B
            ps_dn, b_ps_dn = PS_C
            nk = kt_hi - kt_lo + 1
            for i, kt in enumerate(range(kt_lo, kt_hi + 1)):
                o = t0 - kt * 128
                slot = kt % RING
                ps_s, b_ps_s = next_ps()
                op("pe", "matmul", [b_KBT, b_qbT], [b_ps_s], ps_s[:], lhsT=KBT[hp:hp + 64, pr, slot * 128:(slot + 1) * 128],
                   rhs=qbT[hp:hp + 64, pr, :], start=True, stop=True)
                et, b_et = eT[i % 3]
                op("act", "activation", [b_ps_s], [b_et], out=et[:], in_=ps_s[:], func=AF.Exp, scale=0.125)
                pt_, b_pt_ = pT[i % 3]
                op("pool" if i % 2 else "dve", "tensor_tensor", [b_et, b_tht], [b_pt_], out=pt_[:], in0=et[:], in1=tht[:, o + 384:o + 384 + 512], op=ALU.mult)
                op("pe", "matmul", [b_VB, b_pt_], [b_ps_o], ps_o[:], lhsT=VB[:, slot, pr * 128:(pr + 1) * 128], rhs=pt_[:],
                   start=(i == 0), stop=(i == nk - 1))
                op("pe", "matmul", [b_ones_1, b_pt_], [b_ps_dn], ps_dn[:], lhsT=ones_1[:], rhs=pt_[:], start=(i == 0), stop=(i == nk - 1))
            op("dve", "reciprocal", [b_ps_dn], [b_tmpB], out=tmpB[hp:hp + 64, :], in_=ps_dn[hp:hp + 64, :])
            op("dve", "tensor_tensor", [b_ps_o, b_tmpB], [b_mixT], out=mixT[hp:hp + 64, 4 + pr, :], in0=ps_o[hp:hp + 64, :], in1=tmpB[hp:hp + 64, :], op=ALU.mult)
        for p in range(2 if PH >= 4 else 0):
            view, b_pan = load_panel(wb_out[p], KC, 512, b_wb_out)
            for j in range(4):
                oc = p * 4 + j
                pt, b_pt = fm_chunk(view, b_pan, KC, j * 128, mixT, b_mixT, 512)
                op("dve", "tensor_tensor", [b_pt, b_xT], [b_xT], out=xT[:, oc, :], in0=pt[:], in1=xT[:, oc, :], op=ALU.add)
        if PH >= 4:
            ffn(0, 512)
        rmsnorm(2, 512)
        for c in range(4 if PH >= 5 else 0):
            cs = c * 128
            ps_y0, b_ps_y0 = PS_A
            ps_y1, b_ps_y1 = PS_B
            actf = act[:].rearrange("p a b -> p (a b)").bitcast(F32)
            if c == 0 and blk == blk_lo:
                s5bufs = [[(actf[:, (k * 4 + j) * 512:(k * 4 + j + 1) * 512], Buf("s5t%d_%d" % (k, j))) for j in range(4)] for k in range(2)]
                S5BUFS.append(s5bufs)
            s5bufs = S5BUFS[0]

            def stage1(qd_i):
                g0 = qd_i * 4
                kc, half = g0 // 8, (g0 % 8) // 4
                hp = half * 64
                (t1, b_t1), (t2, b_t2), (zd, b_zd), _ = s5bufs[qd_i % 2]
                pd1, b_pd1 = next_ps()
                pd2, b_pd2 = next_ps()
                for gi in range(4):
                    op("pe", "matmul", [b_LT1, b_hT], [b_pd1], pd1[:, gi * 128:(gi + 1) * 128], lhsT=LT1[hp:hp + 64, kc, gi, :],
                       rhs=hT[hp:hp + 64, kc, cs:cs + 128], start=True, stop=True)
                for gi in range(4):
                    op("pe", "matmul", [b_LT2, b_hT], [b_pd2], pd2[:, gi * 128:(gi + 1) * 128], lhsT=LT2[hp:hp + 64, kc, gi, :],
                       rhs=hT[hp:hp + 64, kc, cs:cs + 128], start=True, stop=True)
                ctq = CT[:, g0:g0 + 4, :].rearrange("p a b -> p (a b)")
                stq = ST[:, g0:g0 + 4, :].rearrange("p a b -> p (a b)")
                op("dve", "tensor_tensor", [b_pd1, b_CT], [b_t1], out=t1, in0=pd1[:], in1=ctq, op=ALU.mult)
                op("dve", "tensor_tensor", [b_pd2, b_ST], [b_t2], out=t2, in0=pd2[:], in1=stq, op=ALU.mult)
                op("pool", "tensor_tensor", [b_t1, b_t2], [b_zd], out=zd, in0=t1, in1=t2, op=ALU.add)

            def stage2(qd_i):
                g0 = qd_i * 4
                _, _, (zd, b_zd), (zz, b_zz) = s5bufs[qd_i % 2]
                ctq = CT[:, g0:g0 + 4, :].rearrange("p a b -> p (a b)")
                stq = ST[:, g0:g0 + 4, :].rearrange("p a b -> p (a b)")
                for gi in range(4):
                    g = g0 + gi
                    op("dve", "tensor_tensor_scan", [b_zd, b_lam_abs, b_Wc], [b_zz], out=zz[:, gi * 128:(gi + 1) * 128],
                       data0=lam_abs[:, g:g + 1].to_broadcast([128, 128]), data1=zd[:, gi * 128:(gi + 1) * 128],
                       initial=Wc[:, g:g + 1], op0=ALU.mult, op1=ALU.add)
                at, b_at = eT[qd_i % 3]
                bt, b_bt = pT[qd_i % 3]
                op("dve", "tensor_tensor", [b_zz, b_CT], [b_at], out=at[:], in0=zz, in1=ctq, op=ALU.mult)
                op("pool", "tensor_tensor", [b_zz, b_ST], [b_bt], out=bt[:], in0=zz, in1=stq, op=ALU.mult)
                op("pool", "tensor_copy", [b_zz], [b_Zend], out=Zend[:, g0:g0 + 4],
                   in_=zz[:, 0:512].rearrange("p (a b) -> p a b", a=4)[:, :, 127])
                for gi in range(4):
                    g = g0 + gi
                    py, b_py = (ps_y0, b_ps_y0) if g < 32 else (ps_y1, b_ps_y1)
                    col = (g % 32) * 16
                    op("pe", "matmul", [b_at, b_C1], [b_py], py[:, col:col + 16], lhsT=at[:, gi * 128:(gi + 1) * 128], rhs=C1[:, g, :], start=True, stop=False)
                    op("pe", "matmul", [b_bt, b_C2], [b_py], py[:, col:col + 16], lhsT=bt[:, gi * 128:(gi + 1) * 128], rhs=C2[:, g, :], start=False, stop=True)
            for qd_i in range(17):
                if qd_i < 16:
                    stage1(qd_i)
                if qd_i >= 1:
                    stage2(qd_i - 1)
            pw, b_pw = next_ps()
            op("pe", "matmul", [b_swapm, b_Zend], [b_pw], pw[:, 0:64], lhsT=swapm[:], rhs=Zend[:], start=True, stop=True)
            op("dve", "tensor_tensor", [b_pw, b_sL], [b_t64], out=t64[:], in0=pw[:, 0:64], in1=sL[:], op=ALU.mult)
            op("dve", "tensor_tensor", [b_Zend, b_cL], [b_Wc], out=Wc[:], in0=Zend[:], in1=cL[:], op=ALU.mult)
            op("dve", "tensor_add", [b_Wc, b_t64], [b_Wc], out=Wc[:], in0=Wc[:], in1=t64[:])
            ysb, b_ysb = xtok[:, 0:1024].rearrange("p (a b) -> p a b", a=2), b_xtok
            op("act", "copy", [b_ps_y0], [b_ysb], out=ysb[:, 0, :], in_=ps_y0[:])
            op("dve", "tensor_copy", [b_ps_y1], [b_ysb], out=ysb[:, 1, :], in_=ps_y1[:])
            for half in range(2):
                pt, b_pt = next_ps()
                for k4 in range(4):
                    op("pe", "transpose", [b_ysb, b_ident], [b_pt], out=pt[:, k4 * 128:(k4 + 1) * 128],
                       in_=ysb[:, half, k4 * 128:(k4 + 1) * 128], identity=ident[:])
                for k4 in range(4):
                    kc = half * 4 + k4
                    op("dve", "scalar_tensor_tensor", [b_xT, b_gd, b_rstd], [b_tmpA], out=tmpA[:, k4 * 128:(k4 + 1) * 128], in0=xT[:, kc, cs:cs + 128],
                       scalar=gd[:, kc:kc + 1], in1=rstd[:, cs:cs + 128], op0=ALU.mult, op1=ALU.mult)
                op("dve", "tensor_tensor", [b_tmpA, b_pt], [b_tmpA], out=tmpA[:], in0=tmpA[:], in1=pt[:], op=ALU.add)
                op("act", "activation", [b_tmpA], [b_tmpB], out=tmpB[:], in_=tmpA[:], func=AF.Square)
                op("dve", "tensor_scalar", [b_tmpB], [b_tmpB], out=tmpB[:], in0=tmpB[:], scalar1=0.044715, scalar2=1.0, op0=ALU.mult, op1=ALU.add)
                op("dve", "tensor_tensor", [b_tmpB, b_tmpA], [b_tmpB], out=tmpB[:], in0=tmpB[:], in1=tmpA[:], op=ALU.mult)
                op("act", "activation", [b_tmpB], [b_tmpB], out=tmpB[:], in_=tmpB[:], func=AF.Sigmoid, scale=1.5957691216)
                for k4 in range(4):
                    kc = half * 4 + k4
                    op("dve", "tensor_tensor", [b_tmpA, b_tmpB], [b_mixT], out=mixT[:, kc, cs:cs + 128], in0=tmpA[:, k4 * 128:(k4 + 1) * 128],
                       in1=tmpB[:, k4 * 128:(k4 + 1) * 128], op=ALU.mult)
        for p in range(4 if PH >= 6 else 0):
            view, b_pan = load_panel(wb_glu[p], KC, 512, b_wb_glu)
            for j in range(2):
                oc = 2 * p + j
                pv, b_pv = fm_chunk(view, b_pan, KC, j * 128, mixT, b_mixT, 512)
                pg, b_pg = fm_chunk(view, b_pan, KC, 256 + j * 128, mixT, b_mixT, 512)
                op("act", "activation", [b_pg], [b_tmpA], out=tmpA[:], in_=pg[:], func=AF.Sigmoid)
                op("dve", "tensor_tensor", [b_tmpA, b_pv], [b_tmpA], out=tmpA[:], in0=tmpA[:], in1=pv[:], op=ALU.mult)
                op("dve", "tensor_tensor", [b_tmpA, b_xT], [b_xT], out=xT[:, oc, :], in0=tmpA[:], in1=xT[:, oc, :], op=ALU.add)
        if PH >= 6:
            ffn(1, 512)
        final_out(o_yp[t0:t0 + 512, :], 512)

    st(o_retp, Sst[:], b_Sst)
    if len(LAUNCH_RANGES) > 1:
        st(o_Wc, Wc[:], b_Wc); st(o_Zend, Zend[:], b_Zend)
        st(o_KBT, KBT[:], b_KBT); st(o_VB, VB[:], b_VB)
    pw, b_pw = next_ps()
    op("pe", "matmul", [b_swapm, b_Zend], [b_pw], pw[:, 0:64], lhsT=swapm[:], rhs=Zend[:], start=True, stop=True)
    op("dve", "tensor_tensor", [b_pw, b_sE], [b_t64], out=t64[:], in0=pw[:, 0:64], in1=sE[:], op=ALU.mult)
    xfin, b_xfin = sb("xfin", [128, 64])
    op("dve", "tensor_tensor", [b_Zend, b_cE], [b_xfin], out=xfin[:], in0=Zend[:], in1=cE[:], op=ALU.mult)
    op("dve", "tensor_add", [b_xfin, b_t64], [b_xfin], out=xfin[:], in0=xfin[:], in1=t64[:])
    st(o_ssp, xfin[:], b_xfin)

    if RUN_SAMPLE:
        NS = TS
        AX = mybir.AxisListType.X
        bmask, b_bmask = sb("bmask", [32, 4]); ld(bmask[:], C["bmask"], b_bmask)
        eomask, b_eomask = sb("eomask", [64, 2]); ld(eomask[:], C["eomask"], b_eomask)
        kdecS = kdec[0:32, :]
        ld(kdecS, C["kdecS"], b_kdec)
        qdecS = qdec[:, 0, 0:64].rearrange("p (a b) -> p a b", a=2)
        ld(qdecS, C["qdecS"], b_qdec)
        decTS = decT[0:32, 0, :]
        ld(decTS.rearrange("p (a b) -> p a b", a=4), C["decTS"], b_decT)
        load_x(xs, NS)
        rmsnorm(0, NS)
        kvo0, kvo1 = kvo[0][0], kvo[1][0]
        for pi in range(6):
            view, b_pan = load_panel(wb_in[pi], KC, 512, b_wb_in)
            if pi == 0:
                for j in range(2):
                    pt, b_pt = fm_chunk(view, b_pan, KC, j * 128, hT, b_hT, NS)
                    evac(j, pt, b_pt, qaT[:, j, 0:NS], b_qaT, NS)
                for j in range(2):
                    pt, b_pt = fm_chunk(view, b_pan, KC, 256 + j * 128, hT, b_hT, NS)
                    evac(j, pt, b_pt, kaT[:, j, 0:NS], b_kaT, NS, scale=0.125)
                pt, b_pt = tm_tile(view, b_pan, KC, 256, 256, hT, b_hT, 0, NS)
                op("dve", "tensor_tensor", [b_pt, b_kdec], [b_katok], out=katok[0:NS, 0, :], in0=pt[0:NS, 0:256], in1=kdecS, op=ALU.mult)
            elif pi == 1:
                pt, b_pt = tm_tile(view, b_pan, KC, 0, 512, hT, b_hT, 0, NS)
                evac(0, pt, b_pt, vatok[0:NS, 0, :], b_vatok, 512, rows=NS)
            elif pi == 2:
                for j in range(4):
                    pt, b_pt = fm_chunk(view, b_pan, KC, j * 128, hT, b_hT, NS)
                    evac(j, pt, b_pt, sgT[:, j, 0:NS], b_sgT, NS, func=AF.Silu)
            elif pi == 3:
                for j in range(4):
                    pt, b_pt = fm_chunk(view, b_pan, KC, j * 128, hT, b_hT, NS)
                    evac(j, pt, b_pt, qbT[:, j, 0:NS], b_qbT, NS)
            elif pi == 4:
                for j in range(4):
                    pt, b_pt = fm_chunk(view, b_pan, KC, j * 128, hT, b_hT, NS)
                    evac(j, pt, b_pt, KBT[:, j, 2056:2056 + NS], b_KBT, NS)
                pt, b_pt = tm_tile(view, b_pan, KC, 0, 512, hT, b_hT, 0, NS)
                evac(1, pt, b_pt, kvo0[0:NS, :], b_xtok, 512, rows=NS)
                for b in range(SB):
                    st(o_ks[b, WB - 8:WB, :], kvo0[b * 8:(b + 1) * 8, :], b_xtok)
            else:
                pt, b_pt = tm_tile(view, b_pan, KC, 0, 512, hT, b_hT, 0, NS)
                evac(1, pt, b_pt, kvo1[0:NS, :], b_xtok, 512, rows=NS)
                for b in range(SB):
                    st(o_vs[b, WB - 8:WB, :], kvo1[b * 8:(b + 1) * 8, :], b_xtok)
                    S.dma("pool", lambda e, a=VB[0:8, 16 + b, :], c_=kvo1[b * 8:(b + 1) * 8, :]: e.dma_start(out=a, in_=c_),
                          b_VB, reads=[b_xtok], writes=[b_VB])
        def SstS(b):
            t_ = tmpB if b < 2 else tmpD
            return t_[:, (b % 2) * 256:(b % 2) * 256 + 256].rearrange("p (a c) -> p a c", a=2), (b_tmpB if b < 2 else b_tmpD)

        def SbfS(b):
            t_, bb_ = pT[1] if b < 2 else pT[2]
            return t_[:, (b % 2) * 256:(b % 2) * 256 + 256].rearrange("p (a c) -> p a c", a=2), bb_
        for b in range(SB):
            sv, sbuf_ = SstS(b)
            for h in range(4):
                hp, pr = (h % 2) * 64, h // 2
                ld(sv[hp:hp + 64, pr, :], st_ret[b, h], sbuf_)
        for b in range(SB):
            sv, sbuf_ = SstS(b); bv, bbuf_ = SbfS(b)
            op("dve", "tensor_copy", [sbuf_], [bbuf_], out=bv, in_=sv)
        for h in range(4):
            pr = h // 2
            op("dve", "tensor_scalar_mul", [b_qaT, b_hmask], [b_qz], out=qz[:, h, 0:NS], in0=qaT[:, pr, 0:NS], scalar1=hmask[:, (h % 2):(h % 2) + 1])
            op("dve", "tensor_tensor", [b_qz, b_qdec], [b_qd], out=qd[:, h, 0:NS], in0=qz[:, h, 0:NS], in1=qdecS[:, pr, :], op=ALU.mult)
        ps_s, b_ps_s = next_ps()
        for h in range(4):
            pr = h // 2
            op("pe", "matmul", [b_kaT, b_qz], [b_ps_s], ps_s[0:NS, h * NS:(h + 1) * NS], lhsT=kaT[:, pr, 0:NS], rhs=qz[:, h, 0:NS], start=True, stop=True)
        pTt, b_pTt = pT[0]
        op("dve", "tensor_tensor", [b_ps_s, b_decT], [b_pTt], out=pTt[0:NS, 0:4 * NS], in0=ps_s[0:NS, 0:4 * NS], in1=decTS, op=ALU.mult)
        ps_o, b_ps_o = next_ps()
        for h in range(4):
            pr = h // 2
            op("pe", "matmul", [b_vatok, b_pTt], [b_ps_o], ps_o[:, h * NS:(h + 1) * NS], lhsT=vatok[0:NS, 0, h * 128:(h + 1) * 128],
               rhs=pTt[0:NS, h * NS:(h + 1) * NS], start=True, stop=False)
            for b in range(SB):
                bv, bbuf_ = SbfS(b)
                op("pe", "matmul", [bbuf_, b_qd], [b_ps_o], ps_o[:, h * NS + 8 * b:h * NS + 8 * b + 8], lhsT=bv[:, pr, :],
                   rhs=qd[:, h, 8 * b:8 * b + 8], start=False, stop=(b == SB - 1))
        kdm, b_kdm = eT[2]
        for b in range(SB):
            sv, sbuf_ = SstS(b)
            op("dve", "tensor_scalar_mul", [b_katok, b_bmask], [b_kdm], out=kdm[0:NS, 0:256], in0=katok[0:NS, 0, :], scalar1=bmask[0:NS, b:b + 1])
            ps_d, b_ps_d = next_ps()
            for h in range(4):
                pr = h // 2
                op("pe", "matmul", [b_kdm, b_vatok], [b_ps_d], ps_d[:, h * 128:(h + 1) * 128], lhsT=kdm[0:NS, pr * 128:(pr + 1) * 128],
                   rhs=vatok[0:NS, 0, h * 128:(h + 1) * 128], start=True, stop=True)
            for h in range(4):
                hp, pr = (h % 2) * 64, h // 2
                op("dve", "scalar_tensor_tensor", [sbuf_, b_ps_d, b_ps_o], [sbuf_], out=sv[hp:hp + 64, pr, :], in0=sv[hp:hp + 64, pr, :],
                   scalar=float(GAM[h] ** 8), in1=ps_d[hp:hp + 64, h * 128:(h + 1) * 128], op0=ALU.mult, op1=ALU.add)
            st(o_rets[b], sv, sbuf_)
        W4 = 4 * NS
        op("act", "copy", [b_ps_o], [b_osb], out=osb[:, 0:W4], in_=ps_o[:, 0:W4])
        op("dve", "tensor_copy", [b_osb], [b_obf], out=obf[:, 0:W4], in_=osb[:, 0:W4])
        ps_m, b_ps_m = next_ps()
        op("pe", "matmul", [b_ones_g, b_obf], [b_ps_m], ps_m[:, 0:W4], lhsT=ones_g[:], rhs=obf[:, 0:W4], start=True, stop=True)
        op("dve", "tensor_tensor", [b_osb, b_ps_m], [b_osb], out=osb[:, 0:W4], in0=osb[:, 0:W4], in1=ps_m[:, 0:W4], op=ALU.subtract)
        op("act", "activation", [b_osb], [b_osq], out=osq[:, 0:W4], in_=osb[:, 0:W4], func=AF.Square)
        ps_q, b_ps_q = next_ps()
        op("pe", "matmul", [b_ones_g, b_osq], [b_ps_q], ps_q[:, 0:W4], lhsT=ones_g[:], rhs=osq[:, 0:W4], start=True, stop=True)
        op("dve", "tensor_scalar_add", [b_ps_q], [b_tmpA], out=tmpA[:, 0:W4], in0=ps_q[:, 0:W4], scalar1=EPS)
        op("act", "activation", [b_tmpA], [b_tmpA], out=tmpA[:, 0:W4], in_=tmpA[:, 0:W4], func=AF.Sqrt)
        op("dve", "reciprocal", [b_tmpA], [b_tmpA], out=tmpA[:, 0:W4], in_=tmpA[:, 0:W4])
        op("dve", "tensor_tensor", [b_osb, b_tmpA], [b_osb], out=osb[:, 0:W4], in0=osb[:, 0:W4], in1=tmpA[:, 0:W4], op=ALU.mult)
        for h in range(4):
            op("dve", "scalar_tensor_tensor", [b_osb, b_gn, b_sgT], [b_mixT], out=mixT[:, h, 0:NS], in0=osb[:, h * NS:(h + 1) * NS],
               scalar=gn[:, h:h + 1], in1=sgT[:, h, 0:NS], op0=ALU.mult, op1=ALU.mult)
        tht, b_tht = thtab[0]
        ld(tht[0:64, :], C["WS"], b_tht)
        ld(tmpD[0:64, :], C["hselB"], b_tmpD)
        qpad = act[:, 18, 0:256].rearrange("p (a c) -> p a c", a=4)
        PsT = act[:, 19:22, :].rearrange("p a b -> p (a b)")[:, 0:17 * 64].rearrange("p (k c) -> p k c", c=64)
        op("dve", "memset", [], [b_act], qpad, 0.0)
        o64, o2, dparts, dtot = tmpB[0:64, 0:64], tmpB[0:64, 64:192], tmpB[0:64, 192:197], tmpB[0:64, 200:201]
        for b in range(SB):
            for kt in range(16):
                ld(xtok[:, 0:512], st_k[b, kt * 128:(kt + 1) * 128, :], b_xtok)
                pt, b_pt = next_ps()
                for pr in range(4):
                    op("pe", "transpose", [b_xtok, b_ident], [b_pt], out=pt[:, pr * 128:(pr + 1) * 128], in_=xtok[:, pr * 128:(pr + 1) * 128], identity=ident[:])
                op("act" if kt % 2 else "dve", "copy" if kt % 2 else "tensor_copy", [b_pt], [b_KBT], out=KBT[:, :, kt * 128:(kt + 1) * 128],
                   in_=pt[:, 0:512].rearrange("p (a c) -> p a c", a=4))
            op("dve", "tensor_copy", [b_KBT], [b_KBT], out=KBT[:, :, 2048:2056], in_=KBT[:, :, 2056 + 8 * b:2056 + 8 * b + 8])
            S.dma("pool", lambda e, a=VB[:, 0:16, :], c_=st_v[b].rearrange("(t p) f -> p t f", p=128): e.dma_start(out=a, in_=c_),
                  b_VB, writes=[b_VB])
            for pr in range(4):
                op("dve", "tensor_copy", [b_qbT], [b_act], out=qpad[0:64, pr, (2 * pr) * 8:(2 * pr) * 8 + 8], in_=qbT[0:64, pr, 8 * b:8 * b + 8])
                op("dve", "tensor_copy", [b_qbT], [b_act], out=qpad[64:128, pr, (2 * pr + 1) * 8:(2 * pr + 1) * 8 + 8], in_=qbT[64:128, pr, 8 * b:8 * b + 8])
            for grp in range(5):
                k0 = grp * 512
                kw = 512 if grp < 4 else 8
                ps_sc, b_ps_sc = next_ps()
                for pr in range(4):
                    op("pe", "matmul", [b_act, b_KBT], [b_ps_sc], ps_sc[0:64, 0:kw], lhsT=qpad[:, pr, :], rhs=KBT[:, pr, k0:k0 + kw],
                       start=(pr == 0), stop=(pr == 3))
                op("act", "activation", [b_ps_sc], [b_tmpA], out=tmpA[0:64, 0:kw], in_=ps_sc[0:64, 0:kw], func=AF.Exp, scale=0.125)
                op("dve", "tensor_tensor", [b_tmpA, b_tht], [b_tmpA], out=tmpA[0:64, 0:kw], in0=tmpA[0:64, 0:kw], in1=tht[0:64, k0:k0 + kw], op=ALU.mult)
                op("dve", "reduce_sum", [b_tmpA], [b_tmpB], out=dparts[:, grp:grp + 1], in_=tmpA[0:64, 0:kw], axis=AX)
                pt, b_pt = next_ps()
                if grp < 4:
                    for t4 in range(4):
                        op("pe", "transpose", [b_tmpA, b_ident], [b_pt], out=pt[:, t4 * 64:(t4 + 1) * 64], in_=tmpA[0:64, t4 * 128:(t4 + 1) * 128], identity=ident[0:64, 0:64])
                    op("act", "copy", [b_pt], [b_act], out=PsT[:, grp * 4:grp * 4 + 4, :], in_=pt[:, 0:256].rearrange("p (a c) -> p a c", a=4))
                else:
                    op("pe", "transpose", [b_tmpA, b_ident], [b_pt], out=pt[0:8, 0:64], in_=tmpA[0:64, 0:8], identity=ident[0:64, 0:64])
                    op("act", "copy", [b_pt], [b_act], out=PsT[0:8, 16, :], in_=pt[0:8, 0:64])
            ps_pv, b_ps_pv = next_ps()
            for kt in range(16):
                op("pe", "matmul", [b_act, b_VB], [b_ps_pv], ps_pv[0:64, :], lhsT=PsT[:, kt, :], rhs=VB[:, kt, :], start=(kt == 0), stop=False)
            op("pe", "matmul", [b_act, b_VB], [b_ps_pv], ps_pv[0:64, :], lhsT=PsT[0:8, 16, :], rhs=VB[0:8, 16 + b, :], start=False, stop=True)
            op("dve", "reduce_sum", [b_tmpB], [b_tmpB], out=dtot, in_=dparts, axis=AX)
            op("dve", "reciprocal", [b_tmpB], [b_tmpB], out=dtot, in_=dtot)
            op("dve", "tensor_tensor", [b_ps_pv, b_tmpD], [b_tmpA], out=tmpA[0:64, :], in0=ps_pv[0:64, :], in1=tmpD[0:64, :], op=ALU.mult)
            op("dve", "tensor_reduce", [b_tmpA], [b_tmpB], out=o64, in_=tmpA[0:64, :].rearrange("p (h d) -> p d h", h=8), axis=AX, op=ALU.add)
            for e2 in range(2):
                op("dve", "tensor_scalar", [b_tmpB, b_eomask], [b_tmpB], out=o2[:, e2 * 64:(e2 + 1) * 64], in0=o64, scalar1=dtot, scalar2=eomask[:, e2:e2 + 1],
                   op0=ALU.mult, op1=ALU.mult)
            pt, b_pt = next_ps()
            op("pe", "transpose", [b_tmpB, b_ident], [b_pt], out=pt[:, 0:64], in_=o2, identity=ident[0:64, 0:64])
            for pr in range(4):
                op("dve", "tensor_copy", [b_pt], [b_mixT], out=mixT[0:64, 4 + pr, 8 * b:8 * b + 8], in_=pt[0:64, (2 * pr) * 8:(2 * pr) * 8 + 8])
                op("dve", "tensor_copy", [b_pt], [b_mixT], out=mixT[64:128, 4 + pr, 8 * b:8 * b + 8], in_=pt[64:128, (2 * pr + 1) * 8:(2 * pr + 1) * 8 + 8])
        for p in range(2):
            view, b_pan = load_panel(wb_out[p], KC, 512, b_wb_out)
            for j in range(4):
                oc = p * 4 + j
                pt, b_pt = fm_chunk(view, b_pan, KC, j * 128, mixT, b_mixT, NS)
                op("dve", "tensor_tensor", [b_pt, b_xT], [b_xT], out=xT[:, oc, 0:NS], in0=pt[:, 0:NS], in1=xT[:, oc, 0:NS], op=ALU.add)
        ffn(0, NS)
        rmsnorm(2, NS)
        s5f = s5i[:].bitcast(F32)
        WS5 = s5f[:, 0:256].rearrange("p (a c) -> p a c", a=4)
        ZendS = s5f[:, 256:512].rearrange("p (a c) -> p a c", a=4)
        c7, b_c7, s7, b_s7 = den, b_den, abr, b_abr
        cs_small(7.0, c7, b_c7, s7, b_s7, True)
        for b in range(SB):
            ld(tmpC[0:64, 0:64], st_sr[b], b_tmpC); ld(tmpC[0:64, 64:128], st_si[b], b_tmpC)
            ld(tmpC[0:64, 128:192], st_si[b], b_tmpC); ld(tmpC[0:64, 192:256], st_sr[b], b_tmpC)
            pt, b_pt = next_ps()
            op("pe", "transpose", [b_tmpC, b_ident], [b_pt], out=pt[:, 0:64], in_=tmpC[0:64, 0:128], identity=ident[0:64, 0:64])
            op("pe", "transpose", [b_tmpC, b_ident], [b_pt], out=pt[:, 64:128], in_=tmpC[0:64, 128:256], identity=ident[0:64, 0:64])
            op("dve", "tensor_tensor", [b_pt, b_s1t], [b_t64], out=t64[:], in0=pt[:, 64:128], in1=s1t[:], op=ALU.mult)
            op("dve", "tensor_tensor", [b_pt, b_c1t], [b_s5i], out=WS5[:, b, :], in0=pt[:, 0:64], in1=c1t[:], op=ALU.mult)
            op("dve", "tensor_sub", [b_s5i, b_t64], [b_s5i], out=WS5[:, b, :], in0=WS5[:, b, :], in1=t64[:])
        ps_y0, b_ps_y0 = PS_A
        ps_y1, b_ps_y1 = PS_B
        v4 = lambda ap: ap.rearrange("p (a b c) -> p a b c", a=4, b=4)
        for qd_i in range(16):
            g0 = qd_i * 4
            kc, half = g0 // 8, (g0 % 8) // 4
            hp = half * 64
            pd1, b_pd1 = next_ps()
            pd2, b_pd2 = next_ps()
            for gi in range(4):
                op("pe", "matmul", [b_LT1, b_hT], [b_pd1], pd1[:, gi * NS:(gi + 1) * NS], lhsT=LT1[hp:hp + 64, kc, gi, :], rhs=hT[hp:hp + 64, kc, 0:NS], start=True, stop=True)
            for gi in range(4):
                op("pe", "matmul", [b_LT2, b_hT], [b_pd2], pd2[:, gi * NS:(gi + 1) * NS], lhsT=LT2[hp:hp + 64, kc, gi, :], rhs=hT[hp:hp + 64, kc, 0:NS], start=True, stop=True)
            ctq = bcast(CT, 64 * 128, g0 * 128, [(128, 4), (0, 4), (1, 8)])
            stq = bcast(ST, 64 * 128, g0 * 128, [(128, 4), (0, 4), (1, 8)])
            op("dve", "tensor_tensor", [b_pd1, b_CT], [b_tmpA], out=v4(tmpA[:, 0:128]), in0=v4(pd1[:, 0:128]), in1=ctq, op=ALU.mult)
            op("dve", "tensor_tensor", [b_pd2, b_ST], [b_tmpB], out=v4(tmpB[:, 0:128]), in0=v4(pd2[:, 0:128]), in1=stq, op=ALU.mult)
            op("pool", "tensor_tensor", [b_tmpA, b_tmpB], [b_tmpC], out=tmpC[:, 0:128], in0=tmpA[:, 0:128], in1=tmpB[:, 0:128], op=ALU.add)
            for gi in range(4):
                g = g0 + gi
                for b in range(SB):
                    c0 = gi * NS + 8 * b
                    op("dve", "tensor_tensor_scan", [b_tmpC, b_lam_abs, b_s5i], [b_tmpD], out=tmpD[:, c0:c0 + 8],
                       data0=lam_abs[:, g:g + 1].to_broadcast([128, 8]), data1=tmpC[:, c0:c0 + 8], initial=WS5[:, b, g:g + 1], op0=ALU.mult, op1=ALU.add)
            at, b_at = eT[qd_i % 2]
            bt, b_bt = pT[qd_i % 2]
            op("dve", "tensor_tensor", [b_tmpD, b_CT], [b_at], out=v4(at[:, 0:128]), in0=v4(tmpD[:, 0:128]), in1=ctq, op=ALU.mult)
            op("pool", "tensor_tensor", [b_tmpD, b_ST], [b_bt], out=v4(bt[:, 0:128]), in0=v4(tmpD[:, 0:128]), in1=stq, op=ALU.mult)
            for b in range(SB):
                op("pool", "tensor_copy", [b_tmpD], [b_s5i], out=ZendS[:, b, g0:g0 + 4], in_=bcast(tmpD, 512, 8 * b + 7, [(NS, 4)]))
            for gi in range(4):
                g = g0 + gi
                py, b_py = (ps_y0, b_ps_y0) if g < 32 else (ps_y1, b_ps_y1)
                col = (g % 32) * 16
                op("pe", "matmul", [b_at, b_C1], [b_py], py[0:NS, col:col + 16], lhsT=at[:, gi * NS:(gi + 1) * NS], rhs=C1[:, g, :], start=True, stop=False)
                op("pe", "matmul", [b_bt, b_C2], [b_py], py[0:NS, col:col + 16], lhsT=bt[:, gi * NS:(gi + 1) * NS], rhs=C2[:, g, :], start=False, stop=True)
        for b in range(SB):
            pw, b_pw = next_ps()
            op("pe", "matmul", [b_swapm, b_s5i], [b_pw], pw[:, 0:64], lhsT=swapm[:], rhs=ZendS[:, b, :], start=True, stop=True)
            op("dve", "tensor_tensor", [b_pw, b_s7], [b_t64], out=t64[:], in0=pw[:, 0:64], in1=s7[:], op=ALU.mult)
            op("dve", "tensor_tensor", [b_s5i, b_c7], [b_fre], out=fre[:], in0=ZendS[:, b, :], in1=c7[:], op=ALU.mult)
            op("dve", "tensor_add", [b_fre, b_t64], [b_fre], out=fre[:], in0=fre[:], in1=t64[:])
            st(o_sss[b], fre[:], b_fre)
        ysb = xtok[:, 0:1024].rearrange("p (a b) -> p a b", a=2)
        op("act", "copy", [b_ps_y0], [b_xtok], out=ysb[0:NS, 0, :], in_=ps_y0[0:NS, :])
        op("dve", "tensor_copy", [b_ps_y1], [b_xtok], out=ysb[0:NS, 1, :], in_=ps_y1[0:NS, :])
        for half in range(2):
            pt, b_pt = next_ps()
            for k4 in range(4):
                op("pe", "transpose", [b_xtok, b_ident], [b_pt], out=pt[:, k4 * NS:(k4 + 1) * NS], in_=ysb[0:NS, half, k4 * 128:(k4 + 1) * 128], identity=ident[0:NS, 0:NS])
            for k4 in range(4):
                kc = half * 4 + k4
                op("dve", "scalar_tensor_tensor", [b_xT, b_gd, b_rstd], [b_tmpA], out=tmpA[:, k4 * NS:(k4 + 1) * NS], in0=xT[:, kc, 0:NS],
                   scalar=gd[:, kc:kc + 1], in1=rstd[:, 0:NS], op0=ALU.mult, op1=ALU.mult)
            op("dve", "tensor_tensor", [b_tmpA, b_pt], [b_tmpA], out=tmpA[:, 0:W4], in0=tmpA[:, 0:W4], in1=pt[:, 0:W4], op=ALU.add)
            op("act", "activation", [b_tmpA], [b_tmpB], out=tmpB[:, 0:W4], in_=tmpA[:, 0:W4], func=AF.Square)
            op("dve", "tensor_scalar", [b_tmpB], [b_tmpB], out=tmpB[:, 0:W4], in0=tmpB[:, 0:W4], scalar1=0.044715, scalar2=1.0, op0=ALU.mult, op1=ALU.add)
            op("dve", "tensor_tensor", [b_tmpB, b_tmpA], [b_tmpB], out=tmpB[:, 0:W4], in0=tmpB[:, 0:W4], in1=tmpA[:, 0:W4], op=ALU.mult)
            op("act", "activation", [b_tmpB], [b_tmpB], out=tmpB[:, 0:W4], in_=tmpB[:, 0:W4], func=AF.Sigmoid, scale=1.5957691216)
            for k4 in range(4):
                kc = half * 4 + k4
                op("dve", "tensor_tensor", [b_tmpA, b_tmpB], [b_mixT], out=mixT[:, kc, 0:NS], in0=tmpA[:, k4 * NS:(k4 + 1) * NS],
                   in1=tmpB[:, k4 * NS:(k4 + 1) * NS], op=ALU.mult)
        for p in range(4):
            view, b_pan = load_panel(wb_glu[p], KC, 512, b_wb_glu)
            for j in range(2):
                oc = 2 * p + j
                pv, b_pv = fm_chunk(view, b_pan, KC, j * 128, mixT, b_mixT, NS)
                pg, b_pg = fm_chunk(view, b_pan, KC, 256 + j * 128, mixT, b_mixT, NS)
                op("act", "activation", [b_pg], [b_tmpA], out=tmpA[:, 0:NS], in_=pg[:, 0:NS], func=AF.Sigmoid)
                op("dve", "tensor_tensor", [b_tmpA, b_pv], [b_tmpA], out=tmpA[:, 0:NS], in0=tmpA[:, 0:NS], in1=pv[:, 0:NS], op=ALU.mult)
                op("dve", "tensor_tensor", [b_tmpA, b_xT], [b_xT], out=xT[:, oc, 0:NS], in0=tmpA[:, 0:NS], in1=xT[:, oc, 0:NS], op=ALU.add)
        ffn(1, NS)
        final_out(o_ys, NS)

    S.finish()
    es.close()
    return nc


_NC_CACHE = {}
LAUNCH_RANGES = [(0, 16)]


def kernel(**inputs):
    f32 = np.float32
    x_prompt = np.asarray(inputs["x_prompt"], f32)
    consts = host_consts()
    wmap = {}
    for k, shp in WEIGHT_SHAPES.items():
        a = np.asarray(inputs[k], f32)
        if k in ("norm_mix", "norm_ffn", "norm_final", "w_ffn_in", "w_ffn_out"):
            wmap[k] = np.ascontiguousarray(a).reshape(shp)
        else:
            wmap[k] = np.ascontiguousarray(a[0]).reshape(shp)
    bf = ml_dtypes.bfloat16
    state = [{"i_Sst": np.zeros((128, 2, 128), f32), "i_Wc": np.zeros((128, 64), f32), "i_Zend": np.zeros((128, 64), f32),
              "i_KBT": np.zeros((128, 4, RING * 128), bf), "i_VB": np.zeros((128, RING, 512), bf)} for _ in range(NCORES)]
    y_prompt = np.zeros((2, SEQ, D), f32)
    r = None
    for li, (lo, hi) in enumerate(LAUNCH_RANGES):
        last = li == len(LAUNCH_RANGES) - 1
        key = ("nc", lo, hi, last)
        if key not in _NC_CACHE:
            _NC_CACHE[key] = build_program(hi, RUN_SAMPLE=last, blk_lo=lo)
        nc = _NC_CACHE[key]
        in_maps = []
        for c in range(NCORES):
            m = {"xp": np.ascontiguousarray(x_prompt[c % 2])}
            bs = slice(c * SB, (c + 1) * SB)
            m["xs"] = np.ascontiguousarray(np.asarray(inputs["x_sample"], f32)[bs].reshape(TS, D))
            m["st_ret"] = np.ascontiguousarray(np.asarray(inputs["state_ret"], f32)[0, bs])
            m["st_k"] = np.ascontiguousarray(np.asarray(inputs["state_swa_k"], f32)[0, bs].reshape(SB, WB, 512))
            m["st_v"] = np.ascontiguousarray(np.asarray(inputs["state_swa_v"], f32)[0, bs].reshape(SB, WB, 512))
            m["st_sr"] = np.ascontiguousarray(np.asarray(inputs["state_ssm_re"], f32)[0, bs])
            m["st_si"] = np.ascontiguousarray(np.asarray(inputs["state_ssm_im"], f32)[0, bs])
            m.update(state[c])
            m.update(wmap)
            m.update({"c_" + k: v for k, v in consts.items()})
            in_maps.append(m)
        res = run_bass_kernel_spmd(nc, in_maps, core_ids=list(range(NCORES)))
        r = res.results
        for sq_ in range(2):
            y_prompt[sq_, lo * 512:hi * 512] = r[sq_]["o_yp"][lo * 512:hi * 512]
        for c in range(NCORES if len(LAUNCH_RANGES) > 1 else 0):
            state[c] = {"i_Sst": np.asarray(r[c]["o_retp"], f32).reshape(128, 2, 128), "i_Wc": np.asarray(r[c]["o_Wc"], f32),
                        "i_Zend": np.asarray(r[c]["o_Zend"], f32), "i_KBT": np.asarray(r[c]["o_KBT"]).reshape(128, 4, RING * 128),
                        "i_VB": np.asarray(r[c]["o_VB"]).reshape(128, RING, 512)}
    B = 2
    y_sample = np.concatenate([r[c]["o_ys"] for c in range(NCORES)], 0).reshape(32, 8, D)

    def unret(a):
        a = np.asarray(a).reshape(128, 2, 128)
        out = np.zeros((4, 64, 128), f32)
        for h in range(4):
            out[h] = a[(h % 2) * 64:(h % 2) * 64 + 64, h // 2, :]
        return out
    ret_p = np.stack([unret(r[0]["o_retp"]), unret(r[1]["o_retp"])])[None]
    ret_s = np.stack([unret(np.asarray(r[c]["o_rets"]).reshape(SB, 128, 2, 128)[b]) for c in range(NCORES) for b in range(SB)])[None]
    swk_p = np.stack([r[0]["o_kp"], r[1]["o_kp"]]).reshape(1, B, WB, H_B, DH_B)
    swv_p = np.stack([r[0]["o_vp"], r[1]["o_vp"]]).reshape(1, B, WB, H_B, DH_B)
    swk_s = np.concatenate([r[c]["o_ks"] for c in range(NCORES)], 0).reshape(1, 32, WB, H_B, DH_B)
    swv_s = np.concatenate([r[c]["o_vs"] for c in range(NCORES)], 0).reshape(1, 32, WB, H_B, DH_B)
    sr_p = np.stack([r[0]["o_ssp"][0:64].T, r[1]["o_ssp"][0:64].T])[None]
    si_p = np.stack([r[0]["o_ssp"][64:128].T, r[1]["o_ssp"][64:128].T])[None]
    sss = lambda c, b: np.asarray(r[c]["o_sss"]).reshape(SB, 128, 64)[b]
    sr_s = np.stack([sss(c, b)[0:64].T for c in range(NCORES) for b in range(SB)])[None]
    si_s = np.stack([sss(c, b)[64:128].T for c in range(NCORES) for b in range(SB)])[None]
    return (y_prompt, y_sample, ret_p, ret_s, swk_p, swv_p, swk_s, swv_s, sr_p, si_p, sr_s, si_s)
```

```python
import numpy as np
import concourse.bass as bass
import concourse.mybir as mybir
from concourse.bass_utils import run_bass_kernel_spmd

F32 = mybir.dt.float32
BF16 = mybir.dt.bfloat16
ALU = mybir.AluOpType
AF = mybir.ActivationFunctionType

D = 1024
KC = D // 128
NCORES = 8
TP = 2048
TS = 32
SB = 4
WB = 2048
H_A, DK_A, DV_A = 4, 64, 128
H_B, DH_B = 8, 64
AB_IN = 3072
EPS = 1e-6


class Buf:
    __slots__ = ("name", "last_w", "readers")

    def __init__(self, name):
        self.name = name
        self.last_w = None
        self.readers = []


class Sched:
    ENGS = ("pe", "act", "dve", "pool", "sp")

    def __init__(self, nc):
        self.nc = nc
        self.ops = {e: [] for e in self.ENGS}
        self.dma_sems = []
        self.buf_dma = {}

    def _deps(self, reads, writes):
        deps = []
        for b in reads:
            if b.last_w is not None:
                deps.append(b.last_w)
        for b in writes:
            if b.last_w is not None:
                deps.append(b.last_w)
            deps.extend(b.readers)
        return deps

    def _commit(self, tok, reads, writes):
        for b in reads:
            b.readers = [r for r in b.readers if not (r[0] == tok[0] and r[1] == tok[1])]
            b.readers.append(tok)
        for b in writes:
            b.last_w = tok
            b.readers = []

    def op(self, eng, fn, reads=(), writes=()):
        deps = self._deps(reads, writes)
        idx = len(self.ops[eng])
        if eng == "pe":
            deps = [d for d in deps if not (d[0] == "e" and d[1] == "pe")]
        self.ops[eng].append({"fn": fn, "deps": deps, "sig": False, "dma": None})
        tok = ("e", eng, idx)
        self._commit(tok, reads, writes)
        return tok

    def dma(self, eng, fn, key, reads=(), writes=()):
        deps = self._deps(reads, writes)
        if key not in self.buf_dma:
            self.buf_dma[key] = [len(self.buf_dma), 0]
        ent = self.buf_dma[key]
        ent[1] += 16
        tok = ("d", ent[0], ent[1])
        self.ops[eng].append({"fn": fn, "deps": deps, "sig": False, "dma": ent[0]})
        self._commit(tok, reads, writes)
        return tok

    def finish(self, final_waits_eng="sp"):
        nc = self.nc
        for e in self.ENGS:
            for o in self.ops[e]:
                for d in o["deps"]:
                    if d[0] == "e":
                        self.ops[d[1]][d[2]]["sig"] = True
        cnt = {}
        for e in self.ENGS:
            c = 0
            for o in self.ops[e]:
                if o["sig"]:
                    c += 1
                o["cnt"] = c
            cnt[e] = c
        n_dma = len(self.buf_dma)
        from contextlib import ExitStack
        with ExitStack() as st:
            esem = {e: st.enter_context(nc.semaphore("es_" + e)) for e in self.ENGS}
            dsem = [st.enter_context(nc.semaphore("ds_%d" % i)) for i in range(n_dma)]
            block = st.enter_context(nc.Block())
            ops = self.ops
            finals = [(ent[0], ent[1]) for ent in self.buf_dma.values()]

            def emit(e, eng):
                waited_e = {}
                waited_d = {}
                for o in ops[e]:
                    need_e, need_d = {}, {}
                    for d in o["deps"]:
                        if d[0] == "e":
                            v = ops[d[1]][d[2]]["cnt"]
                            if v > need_e.get(d[1], 0):
                                need_e[d[1]] = v
                        else:
                            if d[2] > need_d.get(d[1], 0):
                                need_d[d[1]] = d[2]
                    for pe_, v in need_e.items():
                        if v > waited_e.get(pe_, 0):
                            eng.wait_ge(esem[pe_], v)
                            waited_e[pe_] = v
                    for si, v in need_d.items():
                        if v > waited_d.get(si, 0):
                            eng.wait_ge(dsem[si], v)
                            waited_d[si] = v
                    ins = o["fn"](eng)
                    if o["dma"] is not None:
                        ins.then_inc(dsem[o["dma"]], 16)
                    elif o["sig"]:
                        ins.then_inc(esem[e], 1)
                if e == final_waits_eng:
                    for si, v in finals:
                        eng.wait_ge(dsem[si], v)

            @block.tensor
            def _(eng):
                emit("pe", eng)

            @block.scalar
            def _(eng):
                emit("act", eng)

            @block.vector
            def _(eng):
                emit("dve", eng)

            @block.gpsimd
            def _(eng):
                emit("pool", eng)

            @block.sync
            def _(eng):
                emit("sp", eng)


import math
import os
import ml_dtypes
from contextlib import ExitStack

SEQ = 8192
NBLK = SEQ // 512
D_FF = 2816
FC = D_FF // 128
GAM = [1.0 - 2.0 ** (-5 - h) for h in range(4)]
SLOPES = [2.0 ** (-8.0 * (h + 1) / 8) for h in range(8)]
TW = 2944
RING = 20
TWO_PI = 2.0 * math.pi


def host_consts():
    f32 = np.float32
    c = {}
    c["ident_in"] = np.eye(128, dtype=f32)
    m = np.arange(128)[:, None]
    n = np.arange(128)[None, :]
    decT = np.zeros((128, 4, 128), np.float64)
    for h in range(4):
        decT[:, h, :] = np.where(n >= m, GAM[h] ** np.maximum(n - m, 0), 0.0)
    c["decT"] = decT.astype(f32)
    qdec = np.zeros((128, 2, 128), np.float64)
    for p in range(128):
        for pr in range(2):
            h = 2 * pr + p // 64
            qdec[p, pr, :] = GAM[h] ** (np.arange(128) + 1.0)
    c["qdec"] = qdec.astype(f32)
    kdec = np.zeros((128, 256), np.float64)
    for h in range(4):
        kdec[:, h * 64:(h + 1) * 64] = (GAM[h] ** (127.0 - np.arange(128)))[:, None] * 0.125
    c["kdec"] = kdec.astype(f32)
    jl = np.arange(128)[:, None]
    x = np.arange(TW)[None, :]
    dl = x - jl - 384
    cnt = ((dl <= 128).astype(np.float64) + ((dl % 4 == 0) & (dl <= 512)) + ((dl % 16 == 0) & (dl <= 2048)))
    valid = (dl >= 0) & (dl <= 2048)
    tab = np.zeros((8, 128, TW), np.float64)
    for h in range(8):
        tab[h] = np.where(valid, cnt * np.exp(-SLOPES[h] * np.maximum(dl, 0)), 0.0)
    c["swa_tab"] = tab.astype(ml_dtypes.bfloat16)
    sgn = np.ones((128, 1), f32); sgn[64:] = -1.0
    c["sgn"] = sgn
    c["tau"] = np.tile(np.arange(128, dtype=f32)[None, :], (128, 1))
    sw = np.zeros((128, 128), f32)
    for p in range(64):
        sw[p, p + 64] = 1.0; sw[p + 64, p] = 1.0
    c["swapm"] = sw
    rm = np.zeros((128, 4), f32)
    for p in range(128):
        rm[p, (p % 64) // 16] = 1.0
    c["rowmask"] = rm
    hm = np.zeros((128, 2), f32); hm[:64, 0] = 1.0; hm[64:, 1] = 1.0
    c["hmask"] = hm
    p32 = np.arange(32)
    kdS = np.zeros((32, 256), np.float64)
    for h in range(4):
        kdS[:, h * 64:(h + 1) * 64] = (GAM[h] ** (7.0 - (p32 % 8)))[:, None] * 0.125
    c["kdecS"] = kdS.astype(f32)
    qdS = np.zeros((128, 2, 32), np.float64)
    for p in range(128):
        for pr in range(2):
            qdS[p, pr, :] = GAM[2 * pr + p // 64] ** ((p32 % 8) + 1.0)
    c["qdecS"] = qdS.astype(f32)
    dS = np.zeros((32, 4, 32), np.float64)
    mm, nn = p32[:, None], p32[None, :]
    for h in range(4):
        dS[:, h, :] = np.where((mm // 8 == nn // 8) & (nn >= mm), GAM[h] ** np.maximum(nn - mm, 0), 0.0)
    c["decTS"] = dS.astype(f32)
    bm = np.zeros((32, 4), f32)
    bm[p32, p32 // 8] = 1.0
    c["bmask"] = bm
    r64 = np.arange(64)
    hh, tt = r64 // 8, r64 % 8
    jj = np.arange(2056)[None, :]
    dls = 2048 + tt[:, None] - jj
    cnts = ((dls <= 128).astype(np.float64) + ((dls % 4 == 0) & (dls <= 512)) + ((dls % 16 == 0) & (dls <= 2048)))
    ws = np.where((dls >= 0) & (dls <= 2048), cnts * np.exp(-np.array(SLOPES)[hh][:, None] * np.maximum(dls, 0)), 0.0)
    wsp = np.zeros((64, TW), np.float64); wsp[:, :2056] = ws
    c["WS"] = wsp.astype(ml_dtypes.bfloat16)
    hs = np.zeros((64, 512), f32)
    for r in range(64):
        hs[r, (r // 8) * 64:(r // 8) * 64 + 64] = 1.0
    c["hselB"] = hs
    eo = np.zeros((64, 2), f32); eo[:, 0] = (hh % 2 == 0); eo[:, 1] = (hh % 2 == 1)
    c["eomask"] = eo
    return c


CONST_SHAPES = {"ident_in": ([128, 128], F32), "decT": ([128, 4, 128], F32), "qdec": ([128, 2, 128], F32),
                "kdec": ([128, 256], F32), "swa_tab": ([8, 128, TW], BF16), "sgn": ([128, 1], F32),
                "tau": ([128, 128], F32), "swapm": ([128, 128], F32), "rowmask": ([128, 4], F32), "hmask": ([128, 2], F32),
                "kdecS": ([32, 256], F32), "qdecS": ([128, 2, 32], F32), "decTS": ([32, 4, 32], F32), "bmask": ([32, 4], F32),
                "WS": ([64, TW], BF16), "hselB": ([64, 512], F32), "eomask": ([64, 2], F32)}

WEIGHT_SHAPES = {"norm_mix": [2, D], "norm_ffn": [2, D], "norm_final": [D], "w_in_ab": [D, AB_IN], "ret_gn": [512],
                 "w_out_ab": [D, D], "ssm_lam_re": [64, 64], "ssm_lam_im": [64, 64], "ssm_log_step": [64],
                 "ssm_b_re": [64, 64, 16], "ssm_b_im": [64, 64, 16], "ssm_c_re": [64, 16, 64], "ssm_c_im": [64, 16, 64],
                 "ssm_d": [D], "w_glu": [D, 2 * D], "w_ffn_in": [2, D, 2 * D_FF], "w_ffn_out": [2, D_FF, D]}


def build_program(nblk_run=NBLK, PH=9, sim=False, SUB=9, RUN_SAMPLE=True, KV_FROM=NBLK - 4, blk_lo=0):
    nc = bass.Bass("TRN2", target_bir_lowering=False)
    S = Sched(nc)
    es = ExitStack()

    def din(name, shape, dt=F32):
        return nc.dram_tensor(name, list(shape), dt, kind="ExternalInput").ap()

    def dout(name, shape):
        return nc.dram_tensor(name, list(shape), F32, kind="ExternalOutput").ap()

    xp = din("xp", [SEQ, D])
    W = {k: din(k, v) for k, v in WEIGHT_SHAPES.items()}
    C = {k: din("c_" + k, v[0], v[1]) for k, v in CONST_SHAPES.items()}
    o_yp = dout("o_yp", [SEQ, D])
    o_kp = dout("o_kp", [WB, 512])
    o_vp = dout("o_vp", [WB, 512])
    o_retp = dout("o_retp", [128, 2, 128])
    o_ssp = dout("o_ssp", [128, 64])
    i_Sst = din("i_Sst", [128, 2, 128]); i_Wc = din("i_Wc", [128, 64]); i_Zend = din("i_Zend", [128, 64])
    i_KBT = din("i_KBT", [128, 4, RING * 128], BF16); i_VB = din("i_VB", [128, RING, 512], BF16)
    if len(LAUNCH_RANGES) > 1:
        o_Wc = dout("o_Wc", [128, 64]); o_Zend = dout("o_Zend", [128, 64])
        o_KBT = nc.dram_tensor("o_KBT", [128, 4, RING * 128], BF16, kind="ExternalOutput").ap()
        o_VB = nc.dram_tensor("o_VB", [128, RING, 512], BF16, kind="ExternalOutput").ap()
    xs = din("xs", [TS, D])
    st_ret = din("st_ret", [SB, 4, 64, 128])
    st_k = din("st_k", [SB, WB, 512]); st_v = din("st_v", [SB, WB, 512])
    st_sr = din("st_sr", [SB, 64, 64]); st_si = din("st_si", [SB, 64, 64])
    o_ys = dout("o_ys", [TS, D])
    o_rets = dout("o_rets", [SB, 128, 2, 128])
    o_ks = dout("o_ks", [SB, WB, 512]); o_vs = dout("o_vs", [SB, WB, 512])
    o_sss = dout("o_sss", [SB, 128, 64])

    def dscr(name, shape):
        if sim:
            return din(name, shape, BF16), Buf(name)
        t = nc.dram_tensor(name, list(shape), BF16)
        return t.ap(), Buf(name)
    wb_in, b_wb_in = dscr("wb_in", [6, 128, KC, 512])
    wb_out, b_wb_out = dscr("wb_out", [2, 128, KC, 512])
    wb_glu, b_wb_glu = dscr("wb_glu", [4, 128, KC, 512])
    wb_ffi, b_wb_ffi = dscr("wb_ffi", [2, FC // 2, 128, KC, 512])
    wb_ffo, b_wb_ffo = dscr("wb_ffo", [2, KC, 128, FC, 128])

    def cast_piece(dst_ap, src_ap, b_dst, kcn):
        if sim:
            return
        S.dma("pool", lambda e, a=dst_ap, b=src_ap.rearrange("(k p) n -> p k n", p=128): e.dma_start(out=a, in_=b), b_dst, writes=[b_dst])
    for pi in range(6):
        cast_piece(wb_in[pi], W["w_in_ab"][:, pi * 512:(pi + 1) * 512], b_wb_in, KC)
    for pi in range(2):
        cast_piece(wb_out[pi], W["w_out_ab"][:, pi * 512:(pi + 1) * 512], b_wb_out, KC)
    for l in range(2):
        for p in range(FC // 2):
            cast_piece(wb_ffi[l, p][:, :, 0:256], W["w_ffn_in"][l][:, p * 256:(p + 1) * 256], b_wb_ffi, KC)
            cast_piece(wb_ffi[l, p][:, :, 256:512], W["w_ffn_in"][l][:, D_FF + p * 256:D_FF + (p + 1) * 256], b_wb_ffi, KC)
        for oc in range(KC):
            cast_piece(wb_ffo[l, oc], W["w_ffn_out"][l][:, oc * 128:(oc + 1) * 128], b_wb_ffo, FC)
    for p in range(4):
        cast_piece(wb_glu[p][:, :, 0:256], W["w_glu"][:, p * 256:(p + 1) * 256], b_wb_glu, KC)
        cast_piece(wb_glu[p][:, :, 256:512], W["w_glu"][:, D + p * 256:D + (p + 1) * 256], b_wb_glu, KC)

    def sb(name, shape, dt=F32):
        t = es.enter_context(nc.sbuf_tensor(name, list(shape), dt))
        return t, Buf(name)

    def op(eng, method, reads, writes, *a, **kw):
        return S.op(eng, lambda e, m=method, a=a, kw=kw: getattr(e, m)(*a, **kw), reads=reads, writes=writes)

    def ld(dst_ap, src_ap, b_dst, eng="sp", **kw):
        return S.dma(eng, lambda e, a=dst_ap, b=src_ap, kw=kw: e.dma_start(out=a, in_=b, **kw), b_dst, writes=[b_dst])

    def st(dst_ap, src_ap, b_src, eng="sp", extra_reads=()):
        return S.dma(eng, lambda e, a=dst_ap, b=src_ap: e.dma_start(out=a, in_=b), b_src, reads=[b_src] + list(extra_reads))

    def bcast(t, free_total, off, dims):
        return bass.AP(t, off, [[free_total, 128]] + [[s_, c_] for s_, c_ in dims])

    ident, b_ident = sb("ident", [128, 128])
    ld(ident[:], C["ident_in"], b_ident)
    ones_d, b_ones_d = sb("ones_d", [128, 128], BF16)
    ones_g, b_ones_g = sb("ones_g", [128, 128], BF16)
    ones_1, b_ones_1 = sb("ones_1", [128, 128], BF16)
    op("dve", "memset", [], [b_ones_d], ones_d[:], 1.0 / D)
    op("dve", "memset", [], [b_ones_g], ones_g[:], 1.0 / 128)
    op("dve", "memset", [], [b_ones_1], ones_1[:], 1.0)
    gvec, b_gvec = sb("gvec", [128, 5, KC])
    for i, (nm, l) in enumerate([("norm_mix", 0), ("norm_ffn", 0), ("norm_mix", 1), ("norm_ffn", 1)]):
        ld(gvec[:, i, :], W[nm][l].rearrange("(k p) -> p k", p=128), b_gvec, allow_slow_non_contiguous=True)
    ld(gvec[:, 4, :], W["norm_final"].rearrange("(k p) -> p k", p=128), b_gvec, allow_slow_non_contiguous=True)
    gn, b_gn = sb("gn", [128, 4])
    ld(gn[:], W["ret_gn"].rearrange("(h p) -> p h", p=128), b_gn, allow_slow_non_contiguous=True)
    dvec, b_dvec = sb("dvec", [128, KC])
    ld(dvec[:], W["ssm_d"].rearrange("(k p) -> p k", p=128), b_dvec, allow_slow_non_contiguous=True)
    decT, b_decT = sb("decT", [128, 4, 128]); ld(decT[:], C["decT"], b_decT)
    qdec, b_qdec = sb("qdec", [128, 2, 128]); ld(qdec[:], C["qdec"], b_qdec)
    kdec, b_kdec = sb("kdec", [128, 256]); ld(kdec[:], C["kdec"], b_kdec)
    sgn, b_sgn = sb("sgn", [128, 1]); ld(sgn[:], C["sgn"], b_sgn)
    tau, b_tau = sb("tau", [128, 128]); ld(tau[:], C["tau"], b_tau)
    swapm, b_swapm = sb("swapm", [128, 128]); ld(swapm[:], C["swapm"], b_swapm)
    rowmask, b_rowmask = sb("rowmask", [128, 4]); ld(rowmask[:], C["rowmask"], b_rowmask)
    hmask, b_hmask = sb("hmask", [128, 2]); ld(hmask[:], C["hmask"], b_hmask)

    b_d2d = Buf("d2d")
    if RUN_SAMPLE:
        for b in range(SB):
            S.dma("sp", lambda e, a=o_ks[b, 0:WB - 8, :], c_=st_k[b, 8:WB, :]: e.dma_start(out=a, in_=c_), b_d2d)
            S.dma("sp", lambda e, a=o_vs[b, 0:WB - 8, :], c_=st_v[b, 8:WB, :]: e.dma_start(out=a, in_=c_), b_d2d)
    psb = []
    for i in range(8):
        t = es.enter_context(nc.psum_tensor("ps%d" % i, [128, 512], F32))
        psb.append((t, Buf("ps%d" % i)))
    rr = [0]

    def next_ps():
        i = rr[0] % 5
        rr[0] += 1
        return psb[i]
    PS_A, PS_B, PS_C = psb[5], psb[6], psb[7]

    PANEL_EL = 4096
    panels = [sb("panel%d" % i, [128, PANEL_EL], BF16) for i in range(2)]
    prr = [0]

    def load_panel(src_ap, kcn, w, b_src, pool=None):
        pool = panels if pool is None else pool
        slot_i = prr[0] % len(pool)
        t, b = pool[slot_i]
        prr[0] += 1
        view = t[:, 0:kcn * w].rearrange("p (k n) -> p k n", k=kcn)
        q = "sp" if (slot_i % 2) == 0 else "pool"
        S.dma(q, lambda e, a=t[:, 0:kcn * w], s_=src_ap.rearrange("p k n -> p (k n)"): e.dma_start(out=a, in_=s_), b, reads=[b_src], writes=[b])
        return view, b

    def fm_chunk(view, b_pan, kcn, c0, rhs_t, b_rhs, ntok, extra=None):
        pt, b_pt = next_ps()
        for kc in range(kcn):
            op("pe", "matmul", [b_pan, b_rhs], [b_pt], pt[:, 0:ntok], lhsT=view[:, kc, c0:c0 + 128],
               rhs=rhs_t[:, kc, 0:ntok], start=(kc == 0), stop=(kc == kcn - 1))
        return pt, b_pt

    def tm_tile(view, b_pan, kcn, c0, w, lhs_t, b_lhs, t0, rows):
        pt, b_pt = next_ps()
        for kc in range(kcn):
            op("pe", "matmul", [b_pan, b_lhs], [b_pt], pt[0:rows, 0:w], lhsT=lhs_t[:, kc, t0:t0 + rows],
               rhs=view[:, kc, c0:c0 + w], start=(kc == 0), stop=(kc == kcn - 1))
        return pt, b_pt

    xtok, b_xtok = sb("xtok", [128, D])
    xT, b_xT = sb("xT", [128, KC, 512])
    rstd, b_rstd = sb("rstd", [128, 512])
    hT, b_hT = sb("hT", [128, KC, 512], BF16)
    sq, b_sq = hT, b_hT
    mixT, b_mixT = sb("mixT", [128, KC, 512], BF16)
    act, b_act = sb("act", [128, FC, 512], BF16)
    qaT, b_qaT = act[:, 0:2, :], b_act
    kaT, b_kaT = act[:, 2:4, :], b_act
    vatok, b_vatok = act[:, 4:8, :], b_act
    sgT, b_sgT = act[:, 8:12, :], b_act
    qbT, b_qbT = act[:, 12:16, :], b_act
    katok, b_katok = act[:, 16:18, :].rearrange("p a (b c) -> p (a b) c", c=256), b_act
    KBT, b_KBT = sb("KBT", [128, 4, RING * 128], BF16)
    VB, b_VB = sb("VB", [128, RING, 512], BF16)
    kvo = [(xtok[:, 0:512], b_xtok), (xtok[:, 512:1024], b_xtok)]
    tmpA, b_tmpA = sb("tmpA", [128, 512])
    tmpB, b_tmpB = sb("tmpB", [128, 512])
    tmpC, b_tmpC = sb("tmpC", [128, 512])
    tmpD, b_tmpD = sb("tmpD", [128, 512])
    pT = [sb("pT%d" % i, [128, 512], BF16) for i in range(3)]
    eT = [sb("eT%d" % i, [128, 512], BF16) for i in range(3)]
    thtab = [sb("thtab%d" % i, [128, TW], BF16) for i in range(1)]
    Sst, b_Sst = sb("Sst", [128, 2, 128])
    Sbf, b_Sbf = sb("Sbf", [128, 2, 128], BF16)
    op("dve", "memset", [], [b_Sst], Sst[:], 0.0)
    op("dve", "memset", [], [b_Sbf], Sbf[:], 0.0)
    qz, b_qz = sb("qz", [128, 4, 128], BF16)
    qd, b_qd = sb("qd", [128, 4, 128], BF16)
    osb, b_osb = tmpC, b_tmpC
    obf, b_obf = eT[0]
    osq, b_osq = eT[1]

    lam_abs, b_lam_abs = sb("lam_abs", [128, 64])
    th, b_th = sb("th", [128, 64])
    thS, b_thS = sb("thS", [128, 64])
    CT, b_CT = sb("CT", [128, 64, 128], BF16)
    ST, b_ST = sb("ST", [128, 64, 128], BF16)
    LT1, b_LT1 = sb("LT1", [128, KC, 4, 128], BF16)
    LT2, b_LT2 = sb("LT2", [128, KC, 4, 128], BF16)
    C1, b_C1 = sb("C1", [128, 64, 16], BF16)
    C2, b_C2 = sb("C2", [128, 64, 16], BF16)
    cL, b_cL = sb("cL", [128, 64]); sL, b_sL = sb("sL", [128, 64])
    cE, b_cE = sb("cE", [128, 64]); sE, b_sE = sb("sE", [128, 64])
    gd, b_gd = sb("gd", [128, KC])
    Wc, b_Wc = sb("Wc", [128, 64])
    Zend, b_Zend = sb("Zend", [128, 64])
    op("dve", "memset", [], [b_Wc], Wc[:], 0.0)
    op("dve", "memset", [], [b_Zend], Zend[:], 0.0)
    setup_es = ExitStack()

    def sbt(name, shape, dt=F32):
        t = setup_es.enter_context(nc.sbuf_tensor(name, list(shape), dt))
        return t, Buf(name)

    lamT, b_lamT = sb("lamT", [64, 256])
    ld(lamT[:, 0:64], W["ssm_lam_re"], b_lamT); ld(lamT[:, 64:128], W["ssm_lam_re"], b_lamT)
    ld(lamT[:, 128:192], W["ssm_lam_im"], b_lamT); ld(lamT[:, 192:256], W["ssm_lam_im"], b_lamT)
    lre, b_lre = sb("lre", [128, 64]); lim, b_lim = sb("lim", [128, 64])
    for src0, dst, b_dst in ((0, lre, b_lre), (128, lim, b_lim)):
        pt, b_pt = next_ps()
        op("pe", "transpose", [b_lamT, b_ident], [b_pt], out=pt[:, 0:64], in_=lamT[:, src0:src0 + 128], identity=ident[0:64, 0:64])
        op("dve", "tensor_copy", [b_pt], [b_dst], out=dst[:], in_=pt[:, 0:64])
    dtt, b_dtt = sb("dtt", [128, 64])
    ld(dtt[:], W["ssm_log_step"].partition_broadcast(128), b_dtt)
    op("act", "activation", [b_dtt], [b_dtt], out=dtt[:], in_=dtt[:], func=AF.Exp)
    op("dve", "tensor_mul", [b_lim, b_dtt], [b_th], out=th[:], in0=lim[:], in1=dtt[:])
    op("dve", "tensor_scalar_mul", [b_th, b_sgn], [b_thS], out=thS[:], in0=th[:], scalar1=sgn[:, 0:1])
    rho, b_rho = sb("rho", [128, 64])
    op("dve", "tensor_mul", [b_lre, b_dtt], [b_rho], out=rho[:], in0=lre[:], in1=dtt[:])
    op("act", "activation", [b_rho], [b_lam_abs], out=lam_abs[:], in_=rho[:], func=AF.Exp)

    s5a, b_s5a = tmpA, b_tmpA
    s5b, b_s5b = tmpB, b_tmpB
    s5i, b_s5i = sb("s5i", [128, 512], mybir.dt.int32)

    def sin_of(dst_ap, ang_ap, n, b_dst, reads):
        shp = ang_ap.shape
        kb = s5b[:, 0:n] if len(shp) == 2 else s5b[:, 0:n].rearrange("p (a b) -> p a b", a=shp[1])
        ki = s5i[:, 0:n] if len(shp) == 2 else s5i[:, 0:n].rearrange("p (a b) -> p a b", a=shp[1])
        op("dve", "tensor_scalar_mul", reads, [b_s5b], out=kb, in0=ang_ap, scalar1=1.0 / TWO_PI)
        op("dve", "tensor_copy", [b_s5b], [b_s5i], out=ki, in_=kb)
        op("dve", "tensor_copy", [b_s5i], [b_s5b], out=kb, in_=ki)
        op("dve", "scalar_tensor_tensor", [b_s5b] + reads, [b_s5b], out=kb, in0=kb, scalar=-TWO_PI, in1=ang_ap,
           op0=ALU.mult, op1=ALU.add)
        op("dve", "tensor_scalar", [b_s5b], [b_s5b], out=kb, in0=kb, scalar1=-3.14159, scalar2=3.14159, op0=ALU.max, op1=ALU.min)
        op("act", "activation", [b_s5b], [b_dst], out=dst_ap, in_=kb, func=AF.Sin)

    for g0 in range(0, 64, 4):
        angv = s5a[:, 0:512].rearrange("p (a b) -> p a b", a=4)
        for (thsrc, b_thsrc, dst, b_dst, shift) in ((th, b_th, CT, b_CT, math.pi / 2), (thS, b_thS, ST, b_ST, 0.0)):
            op("dve", "tensor_tensor", [b_thsrc, b_tau], [b_s5a], out=angv, in0=bcast(thsrc, 64, g0, [(1, 4), (0, 128)]),
               in1=bcast(tau, 128, 0, [(0, 4), (1, 128)]), op=ALU.mult)
            if shift:
                op("dve", "tensor_scalar_add", [b_s5a], [b_s5a], out=angv, in0=angv, scalar1=shift)
            sin_of(dst[:, g0:g0 + 4, :], angv, 512, b_dst, [b_s5a])

    def cs_small(mult, cdst, b_c, sdst, b_s, neg_sin):
        a = s5a[:, 0:64]
        op("dve", "tensor_scalar", [b_th], [b_s5a], out=a, in0=th[:], scalar1=float(mult), scalar2=math.pi / 2,
           op0=ALU.mult, op1=ALU.add)
        sin_of(cdst[:], a, 64, b_c, [b_s5a])
        op("dve", "tensor_scalar_mul", [b_thS], [b_s5a], out=a, in0=thS[:], scalar1=(-float(mult) if neg_sin else float(mult)))
        sin_of(sdst[:], a, 64, b_s, [b_s5a])
    cs_small(128.0, cL, b_cL, sL, b_sL, True)
    cs_small(127.0, cE, b_cE, sE, b_sE, True)
    c1t, b_c1t = sb("c1t", [128, 64]); s1t, b_s1t = sb("s1t", [128, 64])
    cs_small(1.0, c1t, b_c1t, s1t, b_s1t, False)
    fre, b_fre = sb("fre", [128, 64]); fim, b_fim = sb("fim", [128, 64])
    abr, b_abr = sb("abr", [128, 64]); abi, b_abi = sb("abi", [128, 64]); den, b_den = sb("den", [128, 64])
    t64, b_t64 = sb("t64", [128, 64])
    op("dve", "tensor_mul", [b_lam_abs, b_c1t], [b_abr], out=abr[:], in0=lam_abs[:], in1=c1t[:])
    op("dve", "tensor_scalar_add", [b_abr], [b_abr], out=abr[:], in0=abr[:], scalar1=-1.0)
    op("dve", "tensor_mul", [b_lam_abs, b_s1t], [b_abi], out=abi[:], in0=lam_abs[:], in1=s1t[:])
    op("dve", "tensor_scalar_mul", [b_abi, b_sgn], [b_abi], out=abi[:], in0=abi[:], scalar1=sgn[:, 0:1])
    op("dve", "tensor_mul", [b_lre], [b_den], out=den[:], in0=lre[:], in1=lre[:])
    op("dve", "tensor_mul", [b_lim], [b_t64], out=t64[:], in0=lim[:], in1=lim[:])
    op("dve", "tensor_add", [b_den, b_t64], [b_den], out=den[:], in0=den[:], in1=t64[:])
    op("dve", "reciprocal", [b_den], [b_den], out=den[:], in_=den[:])
    op("dve", "tensor_mul", [b_abr, b_lre], [b_fre], out=fre[:], in0=abr[:], in1=lre[:])
    op("dve", "tensor_mul", [b_abi, b_lim], [b_t64], out=t64[:], in0=abi[:], in1=lim[:])
    op("dve", "tensor_add", [b_fre, b_t64], [b_fre], out=fre[:], in0=fre[:], in1=t64[:])
    op("dve", "tensor_mul", [b_fre, b_den], [b_fre], out=fre[:], in0=fre[:], in1=den[:])
    op("dve", "tensor_mul", [b_abi, b_lre], [b_fim], out=fim[:], in0=abi[:], in1=lre[:])
    op("dve", "tensor_mul", [b_abr, b_lim], [b_t64], out=t64[:], in0=abr[:], in1=lim[:])
    op("dve", "tensor_sub", [b_fim, b_t64], [b_fim], out=fim[:], in0=fim[:], in1=t64[:])
    op("dve", "tensor_mul", [b_fim, b_den], [b_fim], out=fim[:], in0=fim[:], in1=den[:])
    fiS, b_fiS = sb("fiS", [128, 64])
    op("dve", "tensor_scalar_mul", [b_fim, b_sgn], [b_fiS], out=fiS[:], in0=fim[:], scalar1=sgn[:, 0:1])
    bre = W["ssm_b_re"].rearrange("g n q -> n g q"); bim = W["ssm_b_im"].rearrange("g n q -> n g q")
    v3 = lambda t, c0: t[:, c0:c0 + 128].rearrange("p (a b) -> p a b", a=8)
    for kc in range(KC):
        B1k, B2k = v3(tmpC, 0), v3(tmpD, 0)
        FBak, FBbk, tFk = v3(tmpA, 0), v3(tmpB, 0), v3(tmpA, 128)
        gsl = slice(kc * 8, (kc + 1) * 8)
        ld(B1k[0:64], bre[:, gsl, :], b_tmpC); ld(B1k[64:128], bim[:, gsl, :], b_tmpC)
        ld(B2k[0:64], bim[:, gsl, :], b_tmpD); ld(B2k[64:128], bre[:, gsl, :], b_tmpD)
        frb = bcast(fre, 64, kc * 8, [(1, 8), (0, 16)]); fib = bcast(fiS, 64, kc * 8, [(1, 8), (0, 16)])
        op("dve", "tensor_tensor", [b_tmpC, b_fre], [b_tmpA], out=FBak, in0=B1k, in1=frb, op=ALU.mult)
        op("dve", "tensor_tensor", [b_tmpD, b_fiS], [b_tmpA], out=tFk, in0=B2k, in1=fib, op=ALU.mult)
        op("dve", "tensor_sub", [b_tmpA], [b_tmpA], out=FBak, in0=FBak, in1=tFk)
        op("dve", "tensor_tensor", [b_tmpD, b_fre], [b_tmpB], out=FBbk, in0=B2k, in1=frb, op=ALU.mult)
        op("dve", "tensor_tensor", [b_tmpC, b_fiS], [b_tmpA], out=tFk, in0=B1k, in1=fib, op=ALU.mult)
        op("dve", "tensor_add", [b_tmpB, b_tmpA], [b_tmpB], out=FBbk, in0=FBbk, in1=tFk)
        for (FBt, b_FB, LT, b_LT) in ((tmpA, b_tmpA, LT1, b_LT1), (tmpB, b_tmpB, LT2, b_LT2)):
            pt, b_pt = next_ps()
            op("pe", "transpose", [b_FB, b_ident], [b_pt], out=pt[:, 0:128], in_=FBt[:, 0:128], identity=ident[:])
            for gi in range(4):
                op("dve", "tensor_scalar_mul", [b_pt, b_rowmask], [b_LT], out=LT[:, kc, gi, :], in0=pt[:, 0:128],
                   scalar1=rowmask[:, gi:gi + 1])
    cre = W["ssm_c_re"].rearrange("(k a) p n -> k (a p) n", k=KC); cim = W["ssm_c_im"].rearrange("(k a) p n -> k (a p) n", k=KC)
    for kc in range(KC):
        ld(tmpC[:, 0:64], cre[kc], b_tmpC); ld(tmpC[:, 64:128], cim[kc], b_tmpC)
        for (Cd, b_Cd, first_im) in ((C1, b_C1, False), (C2, b_C2, True)):
            cst = tmpD
            if not first_im:
                op("dve", "tensor_copy", [b_tmpC], [b_tmpD], out=cst[:, 0:64], in_=tmpC[:, 0:64])
                op("dve", "tensor_scalar_mul", [b_tmpC], [b_tmpD], out=cst[:, 64:128], in0=tmpC[:, 64:128], scalar1=-1.0)
            else:
                op("dve", "tensor_scalar_mul", [b_tmpC], [b_tmpD], out=cst[:, 0:64], in0=tmpC[:, 64:128], scalar1=-1.0)
                op("dve", "tensor_copy", [b_tmpC], [b_tmpD], out=cst[:, 64:128], in_=tmpC[:, 0:64])
            pt, b_pt = next_ps()
            op("pe", "transpose", [b_tmpD, b_ident], [b_pt], out=pt[:, 0:128], in_=cst[:, 0:128], identity=ident[:])
            op("dve", "tensor_copy", [b_pt], [b_Cd], out=Cd[:, kc * 8:(kc + 1) * 8, :].rearrange("p a b -> p (a b)"), in_=pt[:, 0:128])
    op("dve", "tensor_mul", [b_gvec, b_dvec], [b_gd], out=gd[:], in0=gvec[:, 2, :], in1=dvec[:])

    def evac(i, pt, b_pt, dst_ap, b_dst, n, scale=None, func=None, rows=128):
        if func is not None:
            kw = {"scale": scale} if scale is not None else {}
            op("act", "activation", [b_pt], [b_dst], out=dst_ap, in_=pt[0:rows, 0:n], func=func, **kw)
        elif scale is not None:
            op("act", "activation", [b_pt], [b_dst], out=dst_ap, in_=pt[0:rows, 0:n], func=AF.Copy, scale=scale)
        elif i % 2 == 0:
            op("act", "copy", [b_pt], [b_dst], out=dst_ap, in_=pt[0:rows, 0:n])
        else:
            op("dve", "tensor_copy", [b_pt], [b_dst], out=dst_ap, in_=pt[0:rows, 0:n])

    def rmsnorm(gi, ntok, dst=None, b_dst=None):
        dst = hT if dst is None else dst
        b_dst = b_hT if b_dst is None else b_dst
        op("act", "activation", [b_xT], [b_sq], out=sq[:, :, 0:ntok], in_=xT[:, :, 0:ntok], func=AF.Square)
        pt, b_pt = next_ps()
        for kc in range(KC):
            op("pe", "matmul", [b_ones_d, b_sq], [b_pt], pt[:, 0:ntok], lhsT=ones_d[:], rhs=sq[:, kc, 0:ntok],
               start=(kc == 0), stop=(kc == KC - 1))
        op("dve", "tensor_scalar_add", [b_pt], [b_rstd], out=rstd[:, 0:ntok], in0=pt[:, 0:ntok], scalar1=EPS)
        op("act", "activation", [b_rstd], [b_rstd], out=rstd[:, 0:ntok], in_=rstd[:, 0:ntok], func=AF.Sqrt)
        op("dve", "reciprocal", [b_rstd], [b_rstd], out=rstd[:, 0:ntok], in_=rstd[:, 0:ntok])
        for kc in range(KC):
            op("dve", "scalar_tensor_tensor", [b_xT, b_rstd, b_gvec], [b_dst], out=dst[:, kc, 0:ntok],
               in0=xT[:, kc, 0:ntok], scalar=gvec[:, gi, kc:kc + 1], in1=rstd[:, 0:ntok], op0=ALU.mult, op1=ALU.mult)

    def ffn(l, ntok):
        rmsnorm(1 + 2 * l, ntok)
        pool_in = panels + [(mixT[:].rearrange("p a b -> p (a b)"), b_mixT)]
        pool_out = pool_in + [(hT[:].rearrange("p a b -> p (a b)"), b_hT)]
        for p in range(FC // 2):
            view, b_pan = load_panel(wb_ffi[l, p], KC, 512, b_wb_ffi, pool_in)
            for j in range(2):
                pg, b_pg = fm_chunk(view, b_pan, KC, j * 128, hT, b_hT, ntok)
                pu, b_pu = fm_chunk(view, b_pan, KC, 256 + j * 128, hT, b_hT, ntok)
                op("act", "activation", [b_pg], [b_tmpA], out=tmpA[:, 0:ntok], in_=pg[:, 0:ntok], func=AF.Silu)
                op("dve", "tensor_tensor", [b_tmpA, b_pu], [b_act], out=act[:, 2 * p + j, 0:ntok], in0=tmpA[:, 0:ntok],
                   in1=pu[:, 0:ntok], op=ALU.mult)
        for oc in range(KC):
            view, b_pan = load_panel(wb_ffo[l, oc], FC, 128, b_wb_ffo, pool_out)
            pt, b_pt = fm_chunk(view, b_pan, FC, 0, act, b_act, ntok)
            op("dve", "tensor_tensor", [b_pt, b_xT], [b_xT], out=xT[:, oc, 0:ntok], in0=pt[:, 0:ntok],
               in1=xT[:, oc, 0:ntok], op=ALU.add)

    def load_x(src_ap, ntok):
        ntile = (ntok + 127) // 128
        for t in range(ntile):
            rows = min(128, ntok - t * 128)
            ld(xtok[0:rows, :], src_ap[t * 128:t * 128 + rows, :], b_xtok)
            for half in range(2):
                pt, b_pt = next_ps()
                for k4 in range(4):
                    kc = half * 4 + k4
                    op("pe", "transpose", [b_xtok, b_ident], [b_pt], out=pt[:, k4 * 128:k4 * 128 + rows],
                       in_=xtok[0:rows, kc * 128:(kc + 1) * 128], identity=ident[0:rows, 0:rows])
                src = pt[:, 0:512].rearrange("p (a b) -> p a b", a=4)[:, :, 0:rows]
                dst = xT[:, half * 4:half * 4 + 4, t * 128:t * 128 + rows]
                if half == 0:
                    op("act", "copy", [b_pt], [b_xT], out=dst, in_=src)
                else:
                    op("dve", "tensor_copy", [b_pt], [b_xT], out=dst, in_=src)

    def final_out(dst_ap, ntok):
        rmsnorm(4, ntok, dst=xT, b_dst=b_xT)
        ntile = (ntok + 127) // 128
        for t in range(ntile):
            rows = min(128, ntok - t * 128)
            for half in range(2):
                pt, b_pt = next_ps()
                for k4 in range(4):
                    kc = half * 4 + k4
                    op("pe", "transpose", [b_xT, b_ident], [b_pt], out=pt[0:rows, k4 * 128:(k4 + 1) * 128],
                       in_=xT[:, kc, t * 128:t * 128 + rows], identity=ident[:])
                evac(half, pt, b_pt, xtok[0:rows, half * 512:(half + 1) * 512], b_xtok, 512, rows=rows)
            st(dst_ap[t * 128:t * 128 + rows, :], xtok[0:rows, :], b_xtok)

    S5BUFS = []
    if blk_lo > 0:
        ld(Sst[:], i_Sst, b_Sst)
        op("dve", "tensor_copy", [b_Sst], [b_Sbf], out=Sbf[:], in_=Sst[:])
        ld(Wc[:], i_Wc, b_Wc); ld(Zend[:], i_Zend, b_Zend)
        ld(KBT[:], i_KBT, b_KBT); ld(VB[:], i_VB, b_VB)
    for blk in range(blk_lo, nblk_run):
        t0 = blk * 512
        load_x(xp[t0:t0 + 512, :], 512)
        rmsnorm(0, 512)
        last_kv = blk >= KV_FROM
        for pi in range(6):
            view, b_pan = load_panel(wb_in[pi], KC, 512, b_wb_in)
            if pi == 0:
                for j in range(2):
                    pt, b_pt = fm_chunk(view, b_pan, KC, j * 128, hT, b_hT, 512)
                    evac(j, pt, b_pt, qaT[:, j, :], b_qaT, 512)
                for j in range(2):
                    pt, b_pt = fm_chunk(view, b_pan, KC, 256 + j * 128, hT, b_hT, 512)
                    evac(j, pt, b_pt, kaT[:, j, :], b_kaT, 512, scale=0.125)
                for t in range(4):
                    pt, b_pt = tm_tile(view, b_pan, KC, 256, 256, hT, b_hT, t * 128, 128)
                    op("dve", "tensor_tensor", [b_pt, b_kdec], [b_katok], out=katok[:, t, :], in0=pt[:, 0:256], in1=kdec[:], op=ALU.mult)
            elif pi == 1:
                for t in range(4):
                    pt, b_pt = tm_tile(view, b_pan, KC, 0, 512, hT, b_hT, t * 128, 128)
                    evac(t, pt, b_pt, vatok[:, t, :], b_vatok, 512)
            elif pi == 2:
                for j in range(4):
                    pt, b_pt = fm_chunk(view, b_pan, KC, j * 128, hT, b_hT, 512)
                    evac(j, pt, b_pt, sgT[:, j, :], b_sgT, 512, func=AF.Silu)
            elif pi == 3:
                for j in range(4):
                    pt, b_pt = fm_chunk(view, b_pan, KC, j * 128, hT, b_hT, 512)
                    evac(j, pt, b_pt, qbT[:, j, :], b_qbT, 512)
            elif pi == 4:
                rs0 = ((blk * 4) % RING) * 128
                for j in range(4):
                    pt, b_pt = fm_chunk(view, b_pan, KC, j * 128, hT, b_hT, 512)
                    evac(j, pt, b_pt, KBT[:, j, rs0:rs0 + 512], b_KBT, 512)
                if last_kv and os.environ.get("NOKVK") is None:
                    for t in range(4):
                        pt, b_pt = tm_tile(view, b_pan, KC, 0, 512, hT, b_hT, t * 128, 128)
                        ko, b_ko = (tmpC, b_tmpC) if t % 2 == 0 else (tmpD, b_tmpD)
                        evac(t, pt, b_pt, ko[:], b_ko, 512)
                        r0 = (blk - KV_FROM) * 512 + t * 128
                        st(o_kp[r0:r0 + 128, :], ko[:], b_ko)
            else:
                for t in range(4):
                    pt, b_pt = tm_tile(view, b_pan, KC, 0, 512, hT, b_hT, t * 128, 128)
                    slot = (blk * 4 + t) % RING
                    if last_kv and os.environ.get("NOKVV") is None:
                        ko, b_ko = (tmpC, b_tmpC) if t % 2 == 0 else (tmpD, b_tmpD)
                        evac(t, pt, b_pt, ko[:], b_ko, 512)
                        op("pool", "tensor_copy", [b_ko], [b_VB], out=VB[:, slot, :], in_=ko[:])
                        r0 = (blk - KV_FROM) * 512 + t * 128
                        st(o_vp[r0:r0 + 128, :], ko[:], b_ko)
                    else:
                        evac(t, pt, b_pt, VB[:, slot, :], b_VB, 512)
        for c in range(4 if PH >= 2 else 0):
            cs = c * 128
            for h in range(4):
                pr = h // 2
                op("dve", "tensor_scalar_mul", [b_qaT, b_hmask], [b_qz], out=qz[:, h, :], in0=qaT[:, pr, cs:cs + 128], scalar1=hmask[:, (h % 2):(h % 2) + 1])
                op("dve", "tensor_tensor", [b_qz, b_qdec], [b_qd], out=qd[:, h, :], in0=qz[:, h, :], in1=qdec[:, pr, :], op=ALU.mult)
            ps_s, b_ps_s = next_ps()
            for h in range(4):
                pr = h // 2
                op("pe", "matmul", [b_kaT, b_qz], [b_ps_s], ps_s[:, h * 128:(h + 1) * 128], lhsT=kaT[:, pr, cs:cs + 128],
                   rhs=qz[:, h, :], start=True, stop=True)
            pTt, b_pTt = pT[c % 3]
            op("dve", "tensor_tensor", [b_ps_s, b_decT], [b_pTt], out=pTt[:], in0=ps_s[:], in1=decT[:].rearrange("p a b -> p (a b)"), op=ALU.mult)
            if SUB < 2:
                continue
            ps_o, b_ps_o = next_ps()
            for h in range(4):
                pr = h // 2
                op("pe", "matmul", [b_vatok, b_pTt], [b_ps_o], ps_o[:, h * 128:(h + 1) * 128], lhsT=vatok[:, c, h * 128:(h + 1) * 128],
                   rhs=pTt[:, h * 128:(h + 1) * 128], start=True, stop=False)
                op("pe", "matmul", [b_Sbf, b_qd], [b_ps_o], ps_o[:, h * 128:(h + 1) * 128], lhsT=Sbf[:, pr, :],
                   rhs=qd[:, h, :], start=False, stop=True)
            if SUB < 3:
                continue
            ps_d, b_ps_d = next_ps()
            for h in range(4):
                pr = h // 2
                op("pe", "matmul", [b_katok, b_vatok], [b_ps_d], ps_d[:, h * 128:(h + 1) * 128], lhsT=katok[:, c, pr * 128:(pr + 1) * 128],
                   rhs=vatok[:, c, h * 128:(h + 1) * 128], start=True, stop=True)
            for h in range(4):
                hp, pr = (h % 2) * 64, h // 2
                op("dve", "scalar_tensor_tensor", [b_Sst, b_ps_d, b_Sbf, b_ps_o], [b_Sst], out=Sst[hp:hp + 64, pr, :], in0=Sst[hp:hp + 64, pr, :],
                   scalar=float(GAM[h] ** 128), in1=ps_d[hp:hp + 64, h * 128:(h + 1) * 128], op0=ALU.mult, op1=ALU.add)
            op("dve", "tensor_copy", [b_Sst], [b_Sbf], out=Sbf[:], in_=Sst[:])
            if SUB < 4:
                continue
            op("act", "copy", [b_ps_o], [b_osb], out=osb[:], in_=ps_o[:])
            op("dve", "tensor_copy", [b_osb], [b_obf], out=obf[:], in_=osb[:])
            ps_m, b_ps_m = next_ps()
            op("pe", "matmul", [b_ones_g, b_obf], [b_ps_m], ps_m[:], lhsT=ones_g[:], rhs=obf[:], start=True, stop=True)
            op("dve", "tensor_tensor", [b_osb, b_ps_m], [b_osb], out=osb[:], in0=osb[:], in1=ps_m[:], op=ALU.subtract)
            op("act", "activation", [b_osb], [b_osq], out=osq[:], in_=osb[:], func=AF.Square)
            ps_q, b_ps_q = next_ps()
            op("pe", "matmul", [b_ones_g, b_osq], [b_ps_q], ps_q[:], lhsT=ones_g[:], rhs=osq[:], start=True, stop=True)
            if SUB < 5:
                continue
            op("dve", "tensor_scalar_add", [b_ps_q], [b_tmpA], out=tmpA[:], in0=ps_q[:], scalar1=EPS)
            op("act", "activation", [b_tmpA], [b_tmpA], out=tmpA[:], in_=tmpA[:], func=AF.Sqrt)
            op("dve", "reciprocal", [b_tmpA], [b_tmpA], out=tmpA[:], in_=tmpA[:])
            op("dve", "tensor_tensor", [b_osb, b_tmpA], [b_osb], out=osb[:], in0=osb[:], in1=tmpA[:], op=ALU.mult)
            if SUB < 6:
                continue
            for h in range(4):
                op("dve", "scalar_tensor_tensor", [b_osb, b_gn, b_sgT], [b_mixT], out=mixT[:, h, cs:cs + 128], in0=osb[:, h * 128:(h + 1) * 128],
                   scalar=gn[:, h:h + 1], in1=sgT[:, h, cs:cs + 128], op0=ALU.mult, op1=ALU.mult)
        kt_hi = blk * 4 + 3
        kt_lo = max(0, blk * 4 - 16)
        for h in range(8 if PH >= 3 else 0):
            hp, pr = (h % 2) * 64, h // 2
            tht, b_tht = thtab[0]
            ld(tht[:], C["swa_tab"][h], b_tht)
            ps_o, b_ps_o = PS_A if h % 2 == 0 else PS_B
            ps_dn, b_ps_dn = PS_C
            nk = kt_hi - kt_lo + 1
            kts = list(range(kt_lo, kt_hi + 1))

            def att_a(i):
                kt = kts[i]
                o = t0 - kt * 128
                slot = kt % RING
                ps_s, b_ps_s = next_ps()
                op("pe", "matmul", [b_KBT, b_qbT], [b_ps_s], ps_s[:], lhsT=KBT[hp:hp + 64, pr, slot * 128:(slot + 1) * 128],
                   rhs=qbT[hp:hp + 64, pr, :], start=True, stop=True)
                et, b_et = eT[i % 3]
                op("act", "activation", [b_ps_s], [b_et], out=et[:], in_=ps_s[:], func=AF.Exp, scale=0.125)
                pt_, b_pt_ = pT[i % 3]
                op("pool" if i % 3 == 2 else "dve", "tensor_tensor", [b_et, b_tht], [b_pt_], out=pt_[:], in0=et[:], in1=tht[:, o + 384:o + 384 + 512], op=ALU.mult)

            def att_b(i):
                slot = kts[i] % RING
                pt_, b_pt_ = pT[i % 3]
                op("pe", "matmul", [b_VB, b_pt_], [b_ps_o], ps_o[:], lhsT=VB[:, slot, pr * 128:(pr + 1) * 128], rhs=pt_[:],
                   start=(i == 0), stop=(i == nk - 1))
                op("pe", "matmul", [b_ones_1, b_pt_], [b_ps_dn], ps_dn[:], lhsT=ones_1[:], rhs=pt_[:], start=(i == 0), stop=(i == nk - 1))
            for i in range(nk + 2):
                if i < nk:
                    att_a(i)
                if i >= 2:
                    att_b(i - 2)
            op("dve", "reciprocal", [b_ps_dn], [b_tmpB], out=tmpB[hp:hp + 64, :], in_=ps_dn[hp:hp + 64, :])
            op("dve", "tensor_tensor", [b_ps_o, b_tmpB], [b_mixT], out=mixT[hp:hp + 64, 4 + pr, :], in0=ps_o[hp:hp + 64, :], in1=tmpB[hp:hp + 64, :], op=ALU.mult)
        for p in range(2 if PH >= 4 else 0):
            view, b_pan = load_panel(wb_out[p], KC, 512, b_wb_out)
            for j in range(4):
                oc = p * 4 + j
                pt, b_pt = fm_chunk(view, b_pan, KC, j * 128, mixT, b_mixT, 512)
                op("dve", "tensor_tensor", [b_pt, b_xT], [b_xT], out=xT[:, oc, :], in0=pt[:], in1=xT[:, oc, :], op=ALU.add)
        if PH >= 4:
            ffn(0, 512)
        rmsnorm(2, 512)
        for c in range(4 if PH >= 5 else 0):
            cs = c * 128
            ps_y0, b_ps_y0 = PS_A
            ps_y1, b_ps_y1 = PS_B
            actf = act[:].rearrange("p a b -> p (a b)").bitcast(F32)
            if c == 0 and blk == blk_lo:
                s5bufs = [[(actf[:, (k * 4 + j) * 512:(k * 4 + j + 1) * 512], Buf("s5t%d_%d" % (k, j))) for j in range(4)] for k in range(2)]
                S5BUFS.append(s5bufs)
            s5bufs = S5BUFS[0]

            def stage1(qd_i):
                g0 = qd_i * 4
                kc, half = g0 // 8, (g0 % 8) // 4
                hp = half * 64
                (t1, b_t1), (t2, b_t2), (zd, b_zd), _ = s5bufs[qd_i % 2]
                pd1, b_pd1 = next_ps()
                pd2, b_pd2 = next_ps()
                for gi in range(4):
                    op("pe", "matmul", [b_LT1, b_hT], [b_pd1], pd1[:, gi * 128:(gi + 1) * 128], lhsT=LT1[hp:hp + 64, kc, gi, :],
                       rhs=hT[hp:hp + 64, kc, cs:cs + 128], start=True, stop=True)
                for gi in range(4):
                    op("pe", "matmul", [b_LT2, b_hT], [b_pd2], pd2[:, gi * 128:(gi + 1) * 128], lhsT=LT2[hp:hp + 64, kc, gi, :],
                       rhs=hT[hp:hp + 64, kc, cs:cs + 128], start=True, stop=True)
                ctq = CT[:, g0:g0 + 4, :].rearrange("p a b -> p (a b)")
                stq = ST[:, g0:g0 + 4, :].rearrange("p a b -> p (a b)")
                op("dve", "tensor_tensor", [b_pd1, b_CT], [b_t1], out=t1, in0=pd1[:], in1=ctq, op=ALU.mult)
                op("dve", "tensor_tensor", [b_pd2, b_ST], [b_t2], out=t2, in0=pd2[:], in1=stq, op=ALU.mult)
                op("pool", "tensor_tensor", [b_t1, b_t2], [b_zd], out=zd, in0=t1, in1=t2, op=ALU.add)

            def stage2(qd_i):
                g0 = qd_i * 4
                _, _, (zd, b_zd), (zz, b_zz) = s5bufs[qd_i % 2]
                ctq = CT[:, g0:g0 + 4, :].rearrange("p a b -> p (a b)")
                stq = ST[:, g0:g0 + 4, :].rearrange("p a b -> p (a b)")
                for gi in range(4):
                    g = g0 + gi
                    op("dve", "tensor_tensor_scan", [b_zd, b_lam_abs, b_Wc], [b_zz], out=zz[:, gi * 128:(gi + 1) * 128],
                       data0=lam_abs[:, g:g + 1].to_broadcast([128, 128]), data1=zd[:, gi * 128:(gi + 1) * 128],
                       initial=Wc[:, g:g + 1], op0=ALU.mult, op1=ALU.add)
                at, b_at = eT[qd_i % 3]
                bt, b_bt = pT[qd_i % 3]
                op("dve", "tensor_tensor", [b_zz, b_CT], [b_at], out=at[:], in0=zz, in1=ctq, op=ALU.mult)
                op("pool", "tensor_tensor", [b_zz, b_ST], [b_bt], out=bt[:], in0=zz, in1=stq, op=ALU.mult)
                op("pool", "tensor_copy", [b_zz], [b_Zend], out=Zend[:, g0:g0 + 4],
                   in_=zz[:, 0:512].rearrange("p (a b) -> p a b", a=4)[:, :, 127])
                for gi in range(4):
                    g = g0 + gi
                    py, b_py = (ps_y0, b_ps_y0) if g < 32 else (ps_y1, b_ps_y1)
                    col = (g % 32) * 16
                    op("pe", "matmul", [b_at, b_C1], [b_py], py[:, col:col + 16], lhsT=at[:, gi * 128:(gi + 1) * 128], rhs=C1[:, g, :], start=True, stop=False)
                    op("pe", "matmul", [b_bt, b_C2], [b_py], py[:, col:col + 16], lhsT=bt[:, gi * 128:(gi + 1) * 128], rhs=C2[:, g, :], start=False, stop=True)
            for qd_i in range(17):
                if qd_i < 16:
                    stage1(qd_i)
                if qd_i >= 1:
                    stage2(qd_i - 1)
            pw, b_pw = next_ps()
            op("pe", "matmul", [b_swapm, b_Zend], [b_pw], pw[:, 0:64], lhsT=swapm[:], rhs=Zend[:], start=True, stop=True)
            op("dve", "tensor_tensor", [b_pw, b_sL], [b_t64], out=t64[:], in0=pw[:, 0:64], in1=sL[:], op=ALU.mult)
            op("dve", "tensor_tensor", [b_Zend, b_cL], [b_Wc], out=Wc[:], in0=Zend[:], in1=cL[:], op=ALU.mult)
            op("dve", "tensor_add", [b_Wc, b_t64], [b_Wc], out=Wc[:], in0=Wc[:], in1=t64[:])
            ysb, b_ysb = xtok[:, 0:1024].rearrange("p (a b) -> p a b", a=2), b_xtok
            op("act", "copy", [b_ps_y0], [b_ysb], out=ysb[:, 0, :], in_=ps_y0[:])
            op("dve", "tensor_copy", [b_ps_y1], [b_ysb], out=ysb[:, 1, :], in_=ps_y1[:])
            for half in range(2):
                pt, b_pt = next_ps()
                for k4 in range(4):
                    op("pe", "transpose", [b_ysb, b_ident], [b_pt], out=pt[:, k4 * 128:(k4 + 1) * 128],
                       in_=ysb[:, half, k4 * 128:(k4 + 1) * 128], identity=ident[:])
                for k4 in range(4):
                    kc = half * 4 + k4
                    op("dve", "scalar_tensor_tensor", [b_xT, b_gd, b_rstd], [b_tmpA], out=tmpA[:, k4 * 128:(k4 + 1) * 128], in0=xT[:, kc, cs:cs + 128],
                       scalar=gd[:, kc:kc + 1], in1=rstd[:, cs:cs + 128], op0=ALU.mult, op1=ALU.mult)
                op("dve", "tensor_tensor", [b_tmpA, b_pt], [b_tmpA], out=tmpA[:], in0=tmpA[:], in1=pt[:], op=ALU.add)
                op("act", "activation", [b_tmpA], [b_tmpB], out=tmpB[:], in_=tmpA[:], func=AF.Square)
                op("dve", "tensor_scalar", [b_tmpB], [b_tmpB], out=tmpB[:], in0=tmpB[:], scalar1=0.044715, scalar2=1.0, op0=ALU.mult, op1=ALU.add)
                op("dve", "tensor_tensor", [b_tmpB, b_tmpA], [b_tmpB], out=tmpB[:], in0=tmpB[:], in1=tmpA[:], op=ALU.mult)
                op("act", "activation", [b_tmpB], [b_tmpB], out=tmpB[:], in_=tmpB[:], func=AF.Sigmoid, scale=1.5957691216)
                for k4 in range(4):
                    kc = half * 4 + k4
                    op("dve", "tensor_tensor", [b_tmpA, b_tmpB], [b_mixT], out=mixT[:, kc, cs:cs + 128], in0=tmpA[:, k4 * 128:(k4 + 1) * 128],
                       in1=tmpB[:, k4 * 128:(k4 + 1) * 128], op=ALU.mult)
        for p in range(4 if PH >= 6 else 0):
            view, b_pan = load_panel(wb_glu[p], KC, 512, b_wb_glu)
            for j in range(2):
                oc = 2 * p + j
                pv, b_pv = fm_chunk(view, b_pan, KC, j * 128, mixT, b_mixT, 512)
                pg, b_pg = fm_chunk(view, b_pan, KC, 256 + j * 128, mixT, b_mixT, 512)
                op("act", "activation", [b_pg], [b_tmpA], out=tmpA[:], in_=pg[:], func=AF.Sigmoid)
                op("dve", "tensor_tensor", [b_tmpA, b_pv], [b_tmpA], out=tmpA[:], in0=tmpA[:], in1=pv[:], op=ALU.mult)
                op("dve", "tensor_tensor", [b_tmpA, b_xT], [b_xT], out=xT[:, oc, :], in0=tmpA[:], in1=xT[:, oc, :], op=ALU.add)
        if PH >= 6:
            ffn(1, 512)
        final_out(o_yp[t0:t0 + 512, :], 512)

    st(o_retp, Sst[:], b_Sst)
    if len(LAUNCH_RANGES) > 1:
        st(o_Wc, Wc[:], b_Wc); st(o_Zend, Zend[:], b_Zend)
        st(o_KBT, KBT[:], b_KBT); st(o_VB, VB[:], b_VB)
    pw, b_pw = next_ps()
    op("pe", "matmul", [b_swapm, b_Zend], [b_pw], pw[:, 0:64], lhsT=swapm[:], rhs=Zend[:], start=True, stop=True)
    op("dve", "tensor_tensor", [b_pw, b_sE], [b_t64], out=t64[:], in0=pw[:, 0:64], in1=sE[:], op=ALU.mult)
    xfin, b_xfin = sb("xfin", [128, 64])
    op("dve", "tensor_tensor", [b_Zend, b_cE], [b_xfin], out=xfin[:], in0=Zend[:], in1=cE[:], op=ALU.mult)
    op("dve", "tensor_add", [b_xfin, b_t64], [b_xfin], out=xfin[:], in0=xfin[:], in1=t64[:])
    st(o_ssp, xfin[:], b_xfin)

    if RUN_SAMPLE:
        NS = TS
        AX = mybir.AxisListType.X
        bmask, b_bmask = sb("bmask", [32, 4]); ld(bmask[:], C["bmask"], b_bmask)
        eomask, b_eomask = sb("eomask", [64, 2]); ld(eomask[:], C["eomask"], b_eomask)
        kdecS = kdec[0:32, :]
        ld(kdecS, C["kdecS"], b_kdec)
        qdecS = qdec[:, 0, 0:64].rearrange("p (a b) -> p a b", a=2)
        ld(qdecS, C["qdecS"], b_qdec)
        decTS = decT[0:32, 0, :]
        ld(decTS.rearrange("p (a b) -> p a b", a=4), C["decTS"], b_decT)
        load_x(xs, NS)
        rmsnorm(0, NS)
        kvo0, kvo1 = kvo[0][0], kvo[1][0]
        for pi in range(6):
            view, b_pan = load_panel(wb_in[pi], KC, 512, b_wb_in)
            if pi == 0:
                for j in range(2):
                    pt, b_pt = fm_chunk(view, b_pan, KC, j * 128, hT, b_hT, NS)
                    evac(j, pt, b_pt, qaT[:, j, 0:NS], b_qaT, NS)
                for j in range(2):
                    pt, b_pt = fm_chunk(view, b_pan, KC, 256 + j * 128, hT, b_hT, NS)
                    evac(j, pt, b_pt, kaT[:, j, 0:NS], b_kaT, NS, scale=0.125)
                pt, b_pt = tm_tile(view, b_pan, KC, 256, 256, hT, b_hT, 0, NS)
                op("dve", "tensor_tensor", [b_pt, b_kdec], [b_katok], out=katok[0:NS, 0, :], in0=pt[0:NS, 0:256], in1=kdecS, op=ALU.mult)
            elif pi == 1:
                pt, b_pt = tm_tile(view, b_pan, KC, 0, 512, hT, b_hT, 0, NS)
                evac(0, pt, b_pt, vatok[0:NS, 0, :], b_vatok, 512, rows=NS)
            elif pi == 2:
                for j in range(4):
                    pt, b_pt = fm_chunk(view, b_pan, KC, j * 128, hT, b_hT, NS)
                    evac(j, pt, b_pt, sgT[:, j, 0:NS], b_sgT, NS, func=AF.Silu)
            elif pi == 3:
                for j in range(4):
                    pt, b_pt = fm_chunk(view, b_pan, KC, j * 128, hT, b_hT, NS)
                    evac(j, pt, b_pt, qbT[:, j, 0:NS], b_qbT, NS)
            elif pi == 4:
                for j in range(4):
                    pt, b_pt = fm_chunk(view, b_pan, KC, j * 128, hT, b_hT, NS)
                    evac(j, pt, b_pt, KBT[:, j, 2056:2056 + NS], b_KBT, NS)
                pt, b_pt = tm_tile(view, b_pan, KC, 0, 512, hT, b_hT, 0, NS)
                evac(1, pt, b_pt, kvo0[0:NS, :], b_xtok, 512, rows=NS)
                for b in range(SB):
                    st(o_ks[b, WB - 8:WB, :], kvo0[b * 8:(b + 1) * 8, :], b_xtok)
            else:
                pt, b_pt = tm_tile(view, b_pan, KC, 0, 512, hT, b_hT, 0, NS)
                evac(1, pt, b_pt, kvo1[0:NS, :], b_xtok, 512, rows=NS)
                for b in range(SB):
                    st(o_vs[b, WB - 8:WB, :], kvo1[b * 8:(b + 1) * 8, :], b_xtok)
                    S.dma("pool", lambda e, a=VB[0:8, 16 + b, :], c_=kvo1[b * 8:(b + 1) * 8, :]: e.dma_start(out=a, in_=c_),
                          b_VB, reads=[b_xtok], writes=[b_VB])
        def SstS(b):
            t_ = tmpB if b < 2 else tmpD
            return t_[:, (b % 2) * 256:(b % 2) * 256 + 256].rearrange("p (a c) -> p a c", a=2), (b_tmpB if b < 2 else b_tmpD)

        def SbfS(b):
            t_, bb_ = pT[1] if b < 2 else pT[2]
            return t_[:, (b % 2) * 256:(b % 2) * 256 + 256].rearrange("p (a c) -> p a c", a=2), bb_
        for b in range(SB):
            sv, sbuf_ = SstS(b)
            for h in range(4):
                hp, pr = (h % 2) * 64, h // 2
                ld(sv[hp:hp + 64, pr, :], st_ret[b, h], sbuf_)
        for b in range(SB):
            sv, sbuf_ = SstS(b); bv, bbuf_ = SbfS(b)
            op("dve", "tensor_copy", [sbuf_], [bbuf_], out=bv, in_=sv)
        for h in range(4):
            pr = h // 2
            op("dve", "tensor_scalar_mul", [b_qaT, b_hmask], [b_qz], out=qz[:, h, 0:NS], in0=qaT[:, pr, 0:NS], scalar1=hmask[:, (h % 2):(h % 2) + 1])
            op("dve", "tensor_tensor", [b_qz, b_qdec], [b_qd], out=qd[:, h, 0:NS], in0=qz[:, h, 0:NS], in1=qdecS[:, pr, :], op=ALU.mult)
        ps_s, b_ps_s = next_ps()
        for h in range(4):
            pr = h // 2
            op("pe", "matmul", [b_kaT, b_qz], [b_ps_s], ps_s[0:NS, h * NS:(h + 1) * NS], lhsT=kaT[:, pr, 0:NS], rhs=qz[:, h, 0:NS], start=True, stop=True)
        pTt, b_pTt = pT[0]
        op("dve", "tensor_tensor", [b_ps_s, b_decT], [b_pTt], out=pTt[0:NS, 0:4 * NS], in0=ps_s[0:NS, 0:4 * NS], in1=decTS, op=ALU.mult)
        ps_o, b_ps_o = next_ps()
        for h in range(4):
            pr = h // 2
            op("pe", "matmul", [b_vatok, b_pTt], [b_ps_o], ps_o[:, h * NS:(h + 1) * NS], lhsT=vatok[0:NS, 0, h * 128:(h + 1) * 128],
               rhs=pTt[0:NS, h * NS:(h + 1) * NS], start=True, stop=False)
            for b in range(SB):
                bv, bbuf_ = SbfS(b)
                op("pe", "matmul", [bbuf_, b_qd], [b_ps_o], ps_o[:, h * NS + 8 * b:h * NS + 8 * b + 8], lhsT=bv[:, pr, :],
                   rhs=qd[:, h, 8 * b:8 * b + 8], start=False, stop=(b == SB - 1))
        kdm, b_kdm = eT[2]
        for b in range(SB):
            sv, sbuf_ = SstS(b)
            op("dve", "tensor_scalar_mul", [b_katok, b_bmask], [b_kdm], out=kdm[0:NS, 0:256], in0=katok[0:NS, 0, :], scalar1=bmask[0:NS, b:b + 1])
            ps_d, b_ps_d = next_ps()
            for h in range(4):
                pr = h // 2
                op("pe", "matmul", [b_kdm, b_vatok], [b_ps_d], ps_d[:, h * 128:(h + 1) * 128], lhsT=kdm[0:NS, pr * 128:(pr + 1) * 128],
                   rhs=vatok[0:NS, 0, h * 128:(h + 1) * 128], start=True, stop=True)
            for h in range(4):
                hp, pr = (h % 2) * 64, h // 2
                op("dve", "scalar_tensor_tensor", [sbuf_, b_ps_d, b_ps_o], [sbuf_], out=sv[hp:hp + 64, pr, :], in0=sv[hp:hp + 64, pr, :],
                   scalar=float(GAM[h] ** 8), in1=ps_d[hp:hp + 64, h * 128:(h + 1) * 128], op0=ALU.mult, op1=ALU.add)
            st(o_rets[b], sv, sbuf_)
        W4 = 4 * NS
        op("act", "copy", [b_ps_o], [b_osb], out=osb[:, 0:W4], in_=ps_o[:, 0:W4])
        op("dve", "tensor_copy", [b_osb], [b_obf], out=obf[:, 0:W4], in_=osb[:, 0:W4])
        ps_m, b_ps_m = next_ps()
        op("pe", "matmul", [b_ones_g, b_obf], [b_ps_m], ps_m[:, 0:W4], lhsT=ones_g[:], rhs=obf[:, 0:W4], start=True, stop=True)
        op("dve", "tensor_tensor", [b_osb, b_ps_m], [b_osb], out=osb[:, 0:W4], in0=osb[:, 0:W4], in1=ps_m[:, 0:W4], op=ALU.subtract)
        op("act", "activation", [b_osb], [b_osq], out=osq[:, 0:W4], in_=osb[:, 0:W4], func=AF.Square)
        ps_q, b_ps_q = next_ps()
        op("pe", "matmul", [b_ones_g, b_osq], [b_ps_q], ps_q[:, 0:W4], lhsT=ones_g[:], rhs=osq[:, 0:W4], start=True, stop=True)
        op("dve", "tensor_scalar_add", [b_ps_q], [b_tmpA], out=tmpA[:, 0:W4], in0=ps_q[:, 0:W4], scalar1=EPS)
        op("act", "activation", [b_tmpA], [b_tmpA], out=tmpA[:, 0:W4], in_=tmpA[:, 0:W4], func=AF.Sqrt)
        op("dve", "reciprocal", [b_tmpA], [b_tmpA], out=tmpA[:, 0:W4], in_=tmpA[:, 0:W4])
        op("dve", "tensor_tensor", [b_osb, b_tmpA], [b_osb], out=osb[:, 0:W4], in0=osb[:, 0:W4], in1=tmpA[:, 0:W4], op=ALU.mult)
        for h in range(4):
            op("dve", "scalar_tensor_tensor", [b_osb, b_gn, b_sgT], [b_mixT], out=mixT[:, h, 0:NS], in0=osb[:, h * NS:(h + 1) * NS],
               scalar=gn[:, h:h + 1], in1=sgT[:, h, 0:NS], op0=ALU.mult, op1=ALU.mult)
        tht, b_tht = thtab[0]
        ld(tht[0:64, :], C["WS"], b_tht)
        ld(tmpD[0:64, :], C["hselB"], b_tmpD)
        qpad = act[:, 18, 0:256].rearrange("p (a c) -> p a c", a=4)
        PsT = act[:, 19:22, :].rearrange("p a b -> p (a b)")[:, 0:17 * 64].rearrange("p (k c) -> p k c", c=64)
        op("dve", "memset", [], [b_act], qpad, 0.0)
        o64, o2, dparts, dtot = tmpB[0:64, 0:64], tmpB[0:64, 64:192], tmpB[0:64, 192:197], tmpB[0:64, 200:201]
        for b in range(SB):
            for kt in range(16):
                ld(xtok[:, 0:512], st_k[b, kt * 128:(kt + 1) * 128, :], b_xtok)
                pt, b_pt = next_ps()
                for pr in range(4):
                    op("pe", "transpose", [b_xtok, b_ident], [b_pt], out=pt[:, pr * 128:(pr + 1) * 128], in_=xtok[:, pr * 128:(pr + 1) * 128], identity=ident[:])
                op("act" if kt % 2 else "dve", "copy" if kt % 2 else "tensor_copy", [b_pt], [b_KBT], out=KBT[:, :, kt * 128:(kt + 1) * 128],
                   in_=pt[:, 0:512].rearrange("p (a c) -> p a c", a=4))
            op("dve", "tensor_copy", [b_KBT], [b_KBT], out=KBT[:, :, 2048:2056], in_=KBT[:, :, 2056 + 8 * b:2056 + 8 * b + 8])
            S.dma("pool", lambda e, a=VB[:, 0:16, :], c_=st_v[b].rearrange("(t p) f -> p t f", p=128): e.dma_start(out=a, in_=c_),
                  b_VB, writes=[b_VB])
            for pr in range(4):
                op("dve", "tensor_copy", [b_qbT], [b_act], out=qpad[0:64, pr, (2 * pr) * 8:(2 * pr) * 8 + 8], in_=qbT[0:64, pr, 8 * b:8 * b + 8])
                op("dve", "tensor_copy", [b_qbT], [b_act], out=qpad[64:128, pr, (2 * pr + 1) * 8:(2 * pr + 1) * 8 + 8], in_=qbT[64:128, pr, 8 * b:8 * b + 8])
            for grp in range(5):
                k0 = grp * 512
                kw = 512 if grp < 4 else 8
                ps_sc, b_ps_sc = next_ps()
                for pr in range(4):
                    op("pe", "matmul", [b_act, b_KBT], [b_ps_sc], ps_sc[0:64, 0:kw], lhsT=qpad[:, pr, :], rhs=KBT[:, pr, k0:k0 + kw],
                       start=(pr == 0), stop=(pr == 3))
                op("act", "activation", [b_ps_sc], [b_tmpA], out=tmpA[0:64, 0:kw], in_=ps_sc[0:64, 0:kw], func=AF.Exp, scale=0.125)
                op("dve", "tensor_tensor", [b_tmpA, b_tht], [b_tmpA], out=tmpA[0:64, 0:kw], in0=tmpA[0:64, 0:kw], in1=tht[0:64, k0:k0 + kw], op=ALU.mult)
                op("dve", "reduce_sum", [b_tmpA], [b_tmpB], out=dparts[:, grp:grp + 1], in_=tmpA[0:64, 0:kw], axis=AX)
                pt, b_pt = next_ps()
                if grp < 4:
                    for t4 in range(4):
                        op("pe", "transpose", [b_tmpA, b_ident], [b_pt], out=pt[:, t4 * 64:(t4 + 1) * 64], in_=tmpA[0:64, t4 * 128:(t4 + 1) * 128], identity=ident[0:64, 0:64])
                    op("act", "copy", [b_pt], [b_act], out=PsT[:, grp * 4:grp * 4 + 4, :], in_=pt[:, 0:256].rearrange("p (a c) -> p a c", a=4))
                else:
                    op("pe", "transpose", [b_tmpA, b_ident], [b_pt], out=pt[0:8, 0:64], in_=tmpA[0:64, 0:8], identity=ident[0:64, 0:64])
                    op("act", "copy", [b_pt], [b_act], out=PsT[0:8, 16, :], in_=pt[0:8, 0:64])
            ps_pv, b_ps_pv = next_ps()
            for kt in range(16):
                op("pe", "matmul", [b_act, b_VB], [b_ps_pv], ps_pv[0:64, :], lhsT=PsT[:, kt, :], rhs=VB[:, kt, :], start=(kt == 0), stop=False)
            op("pe", "matmul", [b_act, b_VB], [b_ps_pv], ps_pv[0:64, :], lhsT=PsT[0:8, 16, :], rhs=VB[0:8, 16 + b, :], start=False, stop=True)
            op("dve", "reduce_sum", [b_tmpB], [b_tmpB], out=dtot, in_=dparts, axis=AX)
            op("dve", "reciprocal", [b_tmpB], [b_tmpB], out=dtot, in_=dtot)
            op("dve", "tensor_tensor", [b_ps_pv, b_tmpD], [b_tmpA], out=tmpA[0:64, :], in0=ps_pv[0:64, :], in1=tmpD[0:64, :], op=ALU.mult)
            op("dve", "tensor_reduce", [b_tmpA], [b_tmpB], out=o64, in_=tmpA[0:64, :].rearrange("p (h d) -> p d h", h=8), axis=AX, op=ALU.add)
            for e2 in range(2):
                op("dve", "tensor_scalar", [b_tmpB, b_eomask], [b_tmpB], out=o2[:, e2 * 64:(e2 + 1) * 64], in0=o64, scalar1=dtot, scalar2=eomask[:, e2:e2 + 1],
                   op0=ALU.mult, op1=ALU.mult)
            pt, b_pt = next_ps()
            op("pe", "transpose", [b_tmpB, b_ident], [b_pt], out=pt[:, 0:64], in_=o2, identity=ident[0:64, 0:64])
            for pr in range(4):
                op("dve", "tensor_copy", [b_pt], [b_mixT], out=mixT[0:64, 4 + pr, 8 * b:8 * b + 8], in_=pt[0:64, (2 * pr) * 8:(2 * pr) * 8 + 8])
                op("dve", "tensor_copy", [b_pt], [b_mixT], out=mixT[64:128, 4 + pr, 8 * b:8 * b + 8], in_=pt[64:128, (2 * pr + 1) * 8:(2 * pr + 1) * 8 + 8])
        for p in range(2):
            view, b_pan = load_panel(wb_out[p], KC, 512, b_wb_out)
            for j in range(4):
                oc = p * 4 + j
                pt, b_pt = fm_chunk(view, b_pan, KC, j * 128, mixT, b_mixT, NS)
                op("dve", "tensor_tensor", [b_pt, b_xT], [b_xT], out=xT[:, oc, 0:NS], in0=pt[:, 0:NS], in1=xT[:, oc, 0:NS], op=ALU.add)
        ffn(0, NS)
        rmsnorm(2, NS)
        s5f = s5i[:].bitcast(F32)
        WS5 = s5f[:, 0:256].rearrange("p (a c) -> p a c", a=4)
        ZendS = s5f[:, 256:512].rearrange("p (a c) -> p a c", a=4)
        c7, b_c7, s7, b_s7 = den, b_den, abr, b_abr
        cs_small(7.0, c7, b_c7, s7, b_s7, True)
        for b in range(SB):
            ld(tmpC[0:64, 0:64], st_sr[b], b_tmpC); ld(tmpC[0:64, 64:128], st_si[b], b_tmpC)
            ld(tmpC[0:64, 128:192], st_si[b], b_tmpC); ld(tmpC[0:64, 192:256], st_sr[b], b_tmpC)
            pt, b_pt = next_ps()
            op("pe", "transpose", [b_tmpC, b_ident], [b_pt], out=pt[:, 0:64], in_=tmpC[0:64, 0:128], identity=ident[0:64, 0:64])
            op("pe", "transpose", [b_tmpC, b_ident], [b_pt], out=pt[:, 64:128], in_=tmpC[0:64, 128:256], identity=ident[0:64, 0:64])
            op("dve", "tensor_tensor", [b_pt, b_s1t], [b_t64], out=t64[:], in0=pt[:, 64:128], in1=s1t[:], op=ALU.mult)
            op("dve", "tensor_tensor", [b_pt, b_c1t], [b_s5i], out=WS5[:, b, :], in0=pt[:, 0:64], in1=c1t[:], op=ALU.mult)
            op("dve", "tensor_sub", [b_s5i, b_t64], [b_s5i], out=WS5[:, b, :], in0=WS5[:, b, :], in1=t64[:])
        ps_y0, b_ps_y0 = PS_A
        ps_y1, b_ps_y1 = PS_B
        v4 = lambda ap: ap.rearrange("p (a b c) -> p a b c", a=4, b=4)
        for qd_i in range(16):
            g0 = qd_i * 4
            kc, half = g0 // 8, (g0 % 8) // 4
            hp = half * 64
            pd1, b_pd1 = next_ps()
            pd2, b_pd2 = next_ps()
            for gi in range(4):
                op("pe", "matmul", [b_LT1, b_hT], [b_pd1], pd1[:, gi * NS:(gi + 1) * NS], lhsT=LT1[hp:hp + 64, kc, gi, :], rhs=hT[hp:hp + 64, kc, 0:NS], start=True, stop=True)
            for gi in range(4):
                op("pe", "matmul", [b_LT2, b_hT], [b_pd2], pd2[:, gi * NS:(gi + 1) * NS], lhsT=LT2[hp:hp + 64, kc, gi, :], rhs=hT[hp:hp + 64, kc, 0:NS], start=True, stop=True)
            ctq = bcast(CT, 64 * 128, g0 * 128, [(128, 4), (0, 4), (1, 8)])
            stq = bcast(ST, 64 * 128, g0 * 128, [(128, 4), (0, 4), (1, 8)])
            op("dve", "tensor_tensor", [b_pd1, b_CT], [b_tmpA], out=v4(tmpA[:, 0:128]), in0=v4(pd1[:, 0:128]), in1=ctq, op=ALU.mult)
            op("dve", "tensor_tensor", [b_pd2, b_ST], [b_tmpB], out=v4(tmpB[:, 0:128]), in0=v4(pd2[:, 0:128]), in1=stq, op=ALU.mult)
            op("pool", "tensor_tensor", [b_tmpA, b_tmpB], [b_tmpC], out=tmpC[:, 0:128], in0=tmpA[:, 0:128], in1=tmpB[:, 0:128], op=ALU.add)
            for gi in range(4):
                g = g0 + gi
                for b in range(SB):
                    c0 = gi * NS + 8 * b
                    op("dve", "tensor_tensor_scan", [b_tmpC, b_lam_abs, b_s5i], [b_tmpD], out=tmpD[:, c0:c0 + 8],
                       data0=lam_abs[:, g:g + 1].to_broadcast([128, 8]), data1=tmpC[:, c0:c0 + 8], initial=WS5[:, b, g:g + 1], op0=ALU.mult, op1=ALU.add)
            at, b_at = eT[qd_i % 2]
            bt, b_bt = pT[qd_i % 2]
            op("dve", "tensor_tensor", [b_tmpD, b_CT], [b_at], out=v4(at[:, 0:128]), in0=v4(tmpD[:, 0:128]), in1=ctq, op=ALU.mult)
            op("pool", "tensor_tensor", [b_tmpD, b_ST], [b_bt], out=v4(bt[:, 0:128]), in0=v4(tmpD[:, 0:128]), in1=stq, op=ALU.mult)
            for b in range(SB):
                op("pool", "tensor_copy", [b_tmpD], [b_s5i], out=ZendS[:, b, g0:g0 + 4], in_=bcast(tmpD, 512, 8 * b + 7, [(NS, 4)]))
            for gi in range(4):
                g = g0 + gi
                py, b_py = (ps_y0, b_ps_y0) if g < 32 else (ps_y1, b_ps_y1)
                col = (g % 32) * 16
                op("pe", "matmul", [b_at, b_C1], [b_py], py[0:NS, col:col + 16], lhsT=at[:, gi * NS:(gi + 1) * NS], rhs=C1[:, g, :], start=True, stop=False)
                op("pe", "matmul", [b_bt, b_C2], [b_py], py[0:NS, col:col + 16], lhsT=bt[:, gi * NS:(gi + 1) * NS], rhs=C2[:, g, :], start=False, stop=True)
        for b in range(SB):
            pw, b_pw = next_ps()
            op("pe", "matmul", [b_swapm, b_s5i], [b_pw], pw[:, 0:64], lhsT=swapm[:], rhs=ZendS[:, b, :], start=True, stop=True)
            op("dve", "tensor_tensor", [b_pw, b_s7], [b_t64], out=t64[:], in0=pw[:, 0:64], in1=s7[:], op=ALU.mult)
            op("dve", "tensor_tensor", [b_s5i, b_c7], [b_fre], out=fre[:], in0=ZendS[:, b, :], in1=c7[:], op=ALU.mult)
            op("dve", "tensor_add", [b_fre, b_t64], [b_fre], out=fre[:], in0=fre[:], in1=t64[:])
            st(o_sss[b], fre[:], b_fre)
        ysb = xtok[:, 0:1024].rearrange("p (a b) -> p a b", a=2)
        op("act", "copy", [b_ps_y0], [b_xtok], out=ysb[0:NS, 0, :], in_=ps_y0[0:NS, :])
        op("dve", "tensor_copy", [b_ps_y1], [b_xtok], out=ysb[0:NS, 1, :], in_=ps_y1[0:NS, :])
        for half in range(2):
            pt, b_pt = next_ps()
            for k4 in range(4):
                op("pe", "transpose", [b_xtok, b_ident], [b_pt], out=pt[:, k4 * NS:(k4 + 1) * NS], in_=ysb[0:NS, half, k4 * 128:(k4 + 1) * 128], identity=ident[0:NS, 0:NS])
            for k4 in range(4):
                kc = half * 4 + k4
                op("dve", "scalar_tensor_tensor", [b_xT, b_gd, b_rstd], [b_tmpA], out=tmpA[:, k4 * NS:(k4 + 1) * NS], in0=xT[:, kc, 0:NS],
                   scalar=gd[:, kc:kc + 1], in1=rstd[:, 0:NS], op0=ALU.mult, op1=ALU.mult)
            op("dve", "tensor_tensor", [b_tmpA, b_pt], [b_tmpA], out=tmpA[:, 0:W4], in0=tmpA[:, 0:W4], in1=pt[:, 0:W4], op=ALU.add)
            op("act", "activation", [b_tmpA], [b_tmpB], out=tmpB[:, 0:W4], in_=tmpA[:, 0:W4], func=AF.Square)
            op("dve", "tensor_scalar", [b_tmpB], [b_tmpB], out=tmpB[:, 0:W4], in0=tmpB[:, 0:W4], scalar1=0.044715, scalar2=1.0, op0=ALU.mult, op1=ALU.add)
            op("dve", "tensor_tensor", [b_tmpB, b_tmpA], [b_tmpB], out=tmpB[:, 0:W4], in0=tmpB[:, 0:W4], in1=tmpA[:, 0:W4], op=ALU.mult)
            op("act", "activation", [b_tmpB], [b_tmpB], out=tmpB[:, 0:W4], in_=tmpB[:, 0:W4], func=AF.Sigmoid, scale=1.5957691216)
            for k4 in range(4):
                kc = half * 4 + k4
                op("dve", "tensor_tensor", [b_tmpA, b_tmpB], [b_mixT], out=mixT[:, kc, 0:NS], in0=tmpA[:, k4 * NS:(k4 + 1) * NS],
                   in1=tmpB[:, k4 * NS:(k4 + 1) * NS], op=ALU.mult)
        for p in range(4):
            view, b_pan = load_panel(wb_glu[p], KC, 512, b_wb_glu)
            for j in range(2):
                oc = 2 * p + j
                pv, b_pv = fm_chunk(view, b_pan, KC, j * 128, mixT, b_mixT, NS)
                pg, b_pg = fm_chunk(view, b_pan, KC, 256 + j * 128, mixT, b_mixT, NS)
                op("act", "activation", [b_pg], [b_tmpA], out=tmpA[:, 0:NS], in_=pg[:, 0:NS], func=AF.Sigmoid)
                op("dve", "tensor_tensor", [b_tmpA, b_pv], [b_tmpA], out=tmpA[:, 0:NS], in0=tmpA[:, 0:NS], in1=pv[:, 0:NS], op=ALU.mult)
                op("dve", "tensor_tensor", [b_tmpA, b_xT], [b_xT], out=xT[:, oc, 0:NS], in0=tmpA[:, 0:NS], in1=xT[:, oc, 0:NS], op=ALU.add)
        ffn(1, NS)
        final_out(o_ys, NS)

    S.finish()
    es.close()
    return nc


_NC_CACHE = {}
LAUNCH_RANGES = [(0, 16)]


def kernel(**inputs):
    f32 = np.float32
    x_prompt = np.asarray(inputs["x_prompt"], f32)
    consts = host_consts()
    wmap = {}
    for k, shp in WEIGHT_SHAPES.items():
        a = np.asarray(inputs[k], f32)
        if k in ("norm_mix", "norm_ffn", "norm_final", "w_ffn_in", "w_ffn_out"):
            wmap[k] = np.ascontiguousarray(a).reshape(shp)
        else:
            wmap[k] = np.ascontiguousarray(a[0]).reshape(shp)
    bf = ml_dtypes.bfloat16
    state = [{"i_Sst": np.zeros((128, 2, 128), f32), "i_Wc": np.zeros((128, 64), f32), "i_Zend": np.zeros((128, 64), f32),
              "i_KBT": np.zeros((128, 4, RING * 128), bf), "i_VB": np.zeros((128, RING, 512), bf)} for _ in range(NCORES)]
    y_prompt = np.zeros((2, SEQ, D), f32)
    r = None
    for li, (lo, hi) in enumerate(LAUNCH_RANGES):
        last = li == len(LAUNCH_RANGES) - 1
        key = ("nc", lo, hi, last)
        if key not in _NC_CACHE:
            _NC_CACHE[key] = build_program(hi, RUN_SAMPLE=last, blk_lo=lo)
        nc = _NC_CACHE[key]
        in_maps = []
        for c in range(NCORES):
            m = {"xp": np.ascontiguousarray(x_prompt[c % 2])}
            bs = slice(c * SB, (c + 1) * SB)
            m["xs"] = np.ascontiguousarray(np.asarray(inputs["x_sample"], f32)[bs].reshape(TS, D))
            m["st_ret"] = np.ascontiguousarray(np.asarray(inputs["state_ret"], f32)[0, bs])
            m["st_k"] = np.ascontiguousarray(np.asarray(inputs["state_swa_k"], f32)[0, bs].reshape(SB, WB, 512))
            m["st_v"] = np.ascontiguousarray(np.asarray(inputs["state_swa_v"], f32)[0, bs].reshape(SB, WB, 512))
            m["st_sr"] = np.ascontiguousarray(np.asarray(inputs["state_ssm_re"], f32)[0, bs])
            m["st_si"] = np.ascontiguousarray(np.asarray(inputs["state_ssm_im"], f32)[0, bs])
            m.update(state[c])
            m.update(wmap)
            m.update({"c_" + k: v for k, v in consts.items()})
            in_maps.append(m)
        res = run_bass_kernel_spmd(nc, in_maps, core_ids=list(range(NCORES)))
        r = res.results
        for sq_ in range(2):
            y_prompt[sq_, lo * 512:hi * 512] = r[sq_]["o_yp"][lo * 512:hi * 512]
        for c in range(NCORES if len(LAUNCH_RANGES) > 1 else 0):
            state[c] = {"i_Sst": np.asarray(r[c]["o_retp"], f32).reshape(128, 2, 128), "i_Wc": np.asarray(r[c]["o_Wc"], f32),
                        "i_Zend": np.asarray(r[c]["o_Zend"], f32), "i_KBT": np.asarray(r[c]["o_KBT"]).reshape(128, 4, RING * 128),
                        "i_VB": np.asarray(r[c]["o_VB"]).reshape(128, RING, 512)}
    B = 2
    y_sample = np.concatenate([r[c]["o_ys"] for c in range(NCORES)], 0).reshape(32, 8, D)

    def unret(a):
        a = np.asarray(a).reshape(128, 2, 128)
        out = np.zeros((4, 64, 128), f32)
        for h in range(4):
            out[h] = a[(h % 2) * 64:(h % 2) * 64 + 64, h // 2, :]
        return out
    ret_p = np.stack([unret(r[0]["o_retp"]), unret(r[1]["o_retp"])])[None]
    ret_s = np.stack([unret(np.asarray(r[c]["o_rets"]).reshape(SB, 128, 2, 128)[b]) for c in range(NCORES) for b in range(SB)])[None]
    swk_p = np.stack([r[0]["o_kp"], r[1]["o_kp"]]).reshape(1, B, WB, H_B, DH_B)
    swv_p = np.stack([r[0]["o_vp"], r[1]["o_vp"]]).reshape(1, B, WB, H_B, DH_B)
    swk_s = np.concatenate([r[c]["o_ks"] for c in range(NCORES)], 0).reshape(1, 32, WB, H_B, DH_B)
    swv_s = np.concatenate([r[c]["o_vs"] for c in range(NCORES)], 0).reshape(1, 32, WB, H_B, DH_B)
    sr_p = np.stack([r[0]["o_ssp"][0:64].T, r[1]["o_ssp"][0:64].T])[None]
    si_p = np.stack([r[0]["o_ssp"][64:128].T, r[1]["o_ssp"][64:128].T])[None]
    sss = lambda c, b: np.asarray(r[c]["o_sss"]).reshape(SB, 128, 64)[b]
    sr_s = np.stack([sss(c, b)[0:64].T for c in range(NCORES) for b in range(SB)])[None]
    si_s = np.stack([sss(c, b)[64:128].T for c in range(NCORES) for b in range(SB)])[None]
    return (y_prompt, y_sample, ret_p, ret_s, swk_p, swv_p, swk_s, swv_s, sr_p, si_p, sr_s, si_s)
```

```python
import numpy as np
import concourse.bass as bass
import concourse.mybir as mybir
from concourse.bass_utils import run_bass_kernel_spmd

F32 = mybir.dt.float32
BF16 = mybir.dt.bfloat16
ALU = mybir.AluOpType
AF = mybir.ActivationFunctionType

D = 1024
KC = D // 128
NCORES = 8
TP = 2048
TS = 32
SB = 4
WB = 2048
H_A, DK_A, DV_A = 4, 64, 128
H_B, DH_B = 8, 64
AB_IN = 3072
EPS = 1e-6


class Buf:
    __slots__ = ("name", "last_w", "readers")

    def __init__(self, name):
        self.name = name
        self.last_w = None
        self.readers = []


class Sched:
    ENGS = ("pe", "act", "dve", "pool", "sp")

    def __init__(self, nc):
        self.nc = nc
        self.ops = {e: [] for e in self.ENGS}
        self.dma_sems = []
        self.buf_dma = {}

    def _deps(self, reads, writes):
        deps = []
        for b in reads:
            if b.last_w is not None:
                deps.append(b.last_w)
        for b in writes:
            if b.last_w is not None:
                deps.append(b.last_w)
            deps.extend(b.readers)
        return deps

    def _commit(self, tok, reads, writes):
        for b in reads:
            b.readers = [r for r in b.readers if not (r[0] == tok[0] and r[1] == tok[1])]
            b.readers.append(tok)
        for b in writes:
            b.last_w = tok
            b.readers = []

    def op(self, eng, fn, reads=(), writes=()):
        deps = self._deps(reads, writes)
        idx = len(self.ops[eng])
        if eng == "pe":
            deps = [d for d in deps if not (d[0] == "e" and d[1] == "pe")]
        self.ops[eng].append({"fn": fn, "deps": deps, "sig": False, "dma": None})
        tok = ("e", eng, idx)
        self._commit(tok, reads, writes)
        return tok

    def dma(self, eng, fn, key, reads=(), writes=()):
        deps = self._deps(reads, writes)
        if key not in self.buf_dma:
            self.buf_dma[key] = [len(self.buf_dma), 0]
        ent = self.buf_dma[key]
        ent[1] += 16
        tok = ("d", ent[0], ent[1])
        self.ops[eng].append({"fn": fn, "deps": deps, "sig": False, "dma": ent[0]})
        self._commit(tok, reads, writes)
        return tok

    def finish(self, final_waits_eng="sp"):
        nc = self.nc
        for e in self.ENGS:
            for o in self.ops[e]:
                for d in o["deps"]:
                    if d[0] == "e":
                        self.ops[d[1]][d[2]]["sig"] = True
        cnt = {}
        for e in self.ENGS:
            c = 0
            for o in self.ops[e]:
                if o["sig"]:
                    c += 1
                o["cnt"] = c
            cnt[e] = c
        n_dma = len(self.buf_dma)
        from contextlib import ExitStack
        with ExitStack() as st:
            esem = {e: st.enter_context(nc.semaphore("es_" + e)) for e in self.ENGS}
            dsem = [st.enter_context(nc.semaphore("ds_%d" % i)) for i in range(n_dma)]
            block = st.enter_context(nc.Block())
            ops = self.ops
            finals = [(ent[0], ent[1]) for ent in self.buf_dma.values()]

            def emit(e, eng):
                waited_e = {}
                waited_d = {}
                for o in ops[e]:
                    need_e, need_d = {}, {}
                    for d in o["deps"]:
                        if d[0] == "e":
                            v = ops[d[1]][d[2]]["cnt"]
                            if v > need_e.get(d[1], 0):
                                need_e[d[1]] = v
                        else:
                            if d[2] > need_d.get(d[1], 0):
                                need_d[d[1]] = d[2]
                    for pe_, v in need_e.items():
                        if v > waited_e.get(pe_, 0):
                            eng.wait_ge(esem[pe_], v)
                            waited_e[pe_] = v
                    for si, v in need_d.items():
                        if v > waited_d.get(si, 0):
                            eng.wait_ge(dsem[si], v)
                            waited_d[si] = v
                    ins = o["fn"](eng)
                    if o["dma"] is not None:
                        ins.then_inc(dsem[o["dma"]], 16)
                    elif o["sig"]:
                        ins.then_inc(esem[e], 1)
                if e == final_waits_eng:
                    for si, v in finals:
                        eng.wait_ge(dsem[si], v)

            @block.tensor
            def _(eng):
                emit("pe", eng)

            @block.scalar
            def _(eng):
                emit("act", eng)

            @block.vector
            def _(eng):
                emit("dve", eng)

            @block.gpsimd
            def _(eng):
                emit("pool", eng)

            @block.sync
            def _(eng):
                emit("sp", eng)


import math
import os
import ml_dtypes
from contextlib import ExitStack

SEQ = 8192
NBLK = SEQ // 512
D_FF = 2816
FC = D_FF // 128
GAM = [1.0 - 2.0 ** (-5 - h) for h in range(4)]
SLOPES = [2.0 ** (-8.0 * (h + 1) / 8) for h in range(8)]
TW = 2944
RING = 20
TWO_PI = 2.0 * math.pi


def host_consts():
    f32 = np.float32
    c = {}
    c["ident_in"] = np.eye(128, dtype=f32)
    m = np.arange(128)[:, None]
    n = np.arange(128)[None, :]
    decT = np.zeros((128, 4, 128), np.float64)
    for h in range(4):
        decT[:, h, :] = np.where(n >= m, GAM[h] ** np.maximum(n - m, 0), 0.0)
    c["decT"] = decT.astype(f32)
    qdec = np.zeros((128, 2, 128), np.float64)
    for p in range(128):
        for pr in range(2):
            h = 2 * pr + p // 64
            qdec[p, pr, :] = GAM[h] ** (np.arange(128) + 1.0)
    c["qdec"] = qdec.astype(f32)
    kdec = np.zeros((128, 256), np.float64)
    for h in range(4):
        kdec[:, h * 64:(h + 1) * 64] = (GAM[h] ** (127.0 - np.arange(128)))[:, None] * 0.125
    c["kdec"] = kdec.astype(f32)
    jl = np.arange(128)[:, None]
    x = np.arange(TW)[None, :]
    dl = x - jl - 384
    cnt = ((dl <= 128).astype(np.float64) + ((dl % 4 == 0) & (dl <= 512)) + ((dl % 16 == 0) & (dl <= 2048)))
    valid = (dl >= 0) & (dl <= 2048)
    tab = np.zeros((8, 128, TW), np.float64)
    for h in range(8):
        tab[h] = np.where(valid, cnt * np.exp(-SLOPES[h] * np.maximum(dl, 0)), 0.0)
    c["swa_tab"] = tab.astype(ml_dtypes.bfloat16)
    sgn = np.ones((128, 1), f32); sgn[64:] = -1.0
    c["sgn"] = sgn
    c["tau"] = np.tile(np.arange(128, dtype=f32)[None, :], (128, 1))
    sw = np.zeros((128, 128), f32)
    for p in range(64):
        sw[p, p + 64] = 1.0; sw[p + 64, p] = 1.0
    c["swapm"] = sw
    rm = np.zeros((128, 4), f32)
    for p in range(128):
        rm[p, (p % 64) // 16] = 1.0
    c["rowmask"] = rm
    hm = np.zeros((128, 2), f32); hm[:64, 0] = 1.0; hm[64:, 1] = 1.0
    c["hmask"] = hm
    p32 = np.arange(32)
    kdS = np.zeros((32, 256), np.float64)
    for h in range(4):
        kdS[:, h * 64:(h + 1) * 64] = (GAM[h] ** (7.0 - (p32 % 8)))[:, None] * 0.125
    c["kdecS"] = kdS.astype(f32)
    qdS = np.zeros((128, 2, 32), np.float64)
    for p in range(128):
        for pr in range(2):
            qdS[p, pr, :] = GAM[2 * pr + p // 64] ** ((p32 % 8) + 1.0)
    c["qdecS"] = qdS.astype(f32)
    dS = np.zeros((32, 4, 32), np.float64)
    mm, nn = p32[:, None], p32[None, :]
    for h in range(4):
        dS[:, h, :] = np.where((mm // 8 == nn // 8) & (nn >= mm), GAM[h] ** np.maximum(nn - mm, 0), 0.0)
    c["decTS"] = dS.astype(f32)
    bm = np.zeros((32, 4), f32)
    bm[p32, p32 // 8] = 1.0
    c["bmask"] = bm
    r64 = np.arange(64)
    hh, tt = r64 // 8, r64 % 8
    jj = np.arange(2056)[None, :]
    dls = 2048 + tt[:, None] - jj
    cnts = ((dls <= 128).astype(np.float64) + ((dls % 4 == 0) & (dls <= 512)) + ((dls % 16 == 0) & (dls <= 2048)))
    ws = np.where((dls >= 0) & (dls <= 2048), cnts * np.exp(-np.array(SLOPES)[hh][:, None] * np.maximum(dls, 0)), 0.0)
    wsp = np.zeros((64, TW), np.float64); wsp[:, :2056] = ws
    c["WS"] = wsp.astype(ml_dtypes.bfloat16)
    hs = np.zeros((64, 512), f32)
    for r in range(64):
        hs[r, (r // 8) * 64:(r // 8) * 64 + 64] = 1.0
    c["hselB"] = hs
    eo = np.zeros((64, 2), f32); eo[:, 0] = (hh % 2 == 0); eo[:, 1] = (hh % 2 == 1)
    c["eomask"] = eo
    return c


CONST_SHAPES = {"ident_in": ([128, 128], F32), "decT": ([128, 4, 128], F32), "qdec": ([128, 2, 128], F32),
                "kdec": ([128, 256], F32), "swa_tab": ([8, 128, TW], BF16), "sgn": ([128, 1], F32),
                "tau": ([128, 128], F32), "swapm": ([128, 128], F32), "rowmask": ([128, 4], F32), "hmask": ([128, 2], F32),
                "kdecS": ([32, 256], F32), "qdecS": ([128, 2, 32], F32), "decTS": ([32, 4, 32], F32), "bmask": ([32, 4], F32),
                "WS": ([64, TW], BF16), "hselB": ([64, 512], F32), "eomask": ([64, 2], F32)}

WEIGHT_SHAPES = {"norm_mix": [2, D], "norm_ffn": [2, D], "norm_final": [D], "w_in_ab": [D, AB_IN], "ret_gn": [512],
                 "w_out_ab": [D, D], "ssm_lam_re": [64, 64], "ssm_lam_im": [64, 64], "ssm_log_step": [64],
                 "ssm_b_re": [64, 64, 16], "ssm_b_im": [64, 64, 16], "ssm_c_re": [64, 16, 64], "ssm_c_im": [64, 16, 64],
                 "ssm_d": [D], "w_glu": [D, 2 * D], "w_ffn_in": [2, D, 2 * D_FF], "w_ffn_out": [2, D_FF, D]}


def build_program(nblk_run=NBLK, PH=9, sim=False, SUB=9, RUN_SAMPLE=True, KV_FROM=NBLK - 4, blk_lo=0):
    nc = bass.Bass("TRN2", target_bir_lowering=False)
    S = Sched(nc)
    es = ExitStack()

    def din(name, shape, dt=F32):
        return nc.dram_tensor(name, list(shape), dt, kind="ExternalInput").ap()

    def dout(name, shape):
        return nc.dram_tensor(name, list(shape), F32, kind="ExternalOutput").ap()

    xp = din("xp", [SEQ, D])
    W = {k: din(k, v) for k, v in WEIGHT_SHAPES.items()}
    C = {k: din("c_" + k, v[0], v[1]) for k, v in CONST_SHAPES.items()}
    o_yp = dout("o_yp", [SEQ, D])
    o_kp = dout("o_kp", [WB, 512])
    o_vp = dout("o_vp", [WB, 512])
    o_retp = dout("o_retp", [128, 2, 128])
    o_ssp = dout("o_ssp", [128, 64])
    i_Sst = din("i_Sst", [128, 2, 128]); i_Wc = din("i_Wc", [128, 64]); i_Zend = din("i_Zend", [128, 64])
    i_KBT = din("i_KBT", [128, 4, RING * 128], BF16); i_VB = din("i_VB", [128, RING, 512], BF16)
    if len(LAUNCH_RANGES) > 1:
        o_Wc = dout("o_Wc", [128, 64]); o_Zend = dout("o_Zend", [128, 64])
        o_KBT = nc.dram_tensor("o_KBT", [128, 4, RING * 128], BF16, kind="ExternalOutput").ap()
        o_VB = nc.dram_tensor("o_VB", [128, RING, 512], BF16, kind="ExternalOutput").ap()
    xs = din("xs", [TS, D])
    st_ret = din("st_ret", [SB, 4, 64, 128])
    st_k = din("st_k", [SB, WB, 512]); st_v = din("st_v", [SB, WB, 512])
    st_sr = din("st_sr", [SB, 64, 64]); st_si = din("st_si", [SB, 64, 64])
    o_ys = dout("o_ys", [TS, D])
    o_rets = dout("o_rets", [SB, 128, 2, 128])
    o_ks = dout("o_ks", [SB, WB, 512]); o_vs = dout("o_vs", [SB, WB, 512])
    o_sss = dout("o_sss", [SB, 128, 64])

    def dscr(name, shape):
        if sim:
            return din(name, shape, BF16), Buf(name)
        t = nc.dram_tensor(name, list(shape), BF16)
        return t.ap(), Buf(name)
    wb_in, b_wb_in = dscr("wb_in", [6, 128, KC, 512])
    wb_out, b_wb_out = dscr("wb_out", [2, 128, KC, 512])
    wb_glu, b_wb_glu = dscr("wb_glu", [4, 128, KC, 512])
    wb_ffi, b_wb_ffi = dscr("wb_ffi", [2, FC // 2, 128, KC, 512])
    wb_ffo, b_wb_ffo = dscr("wb_ffo", [2, KC, 128, FC, 128])

    def cast_piece(dst_ap, src_ap, b_dst, kcn):
        if sim:
            return
        S.dma("pool", lambda e, a=dst_ap, b=src_ap.rearrange("(k p) n -> p k n", p=128): e.dma_start(out=a, in_=b), b_dst, writes=[b_dst])
    for pi in range(6):
        cast_piece(wb_in[pi], W["w_in_ab"][:, pi * 512:(pi + 1) * 512], b_wb_in, KC)
    for pi in range(2):
        cast_piece(wb_out[pi], W["w_out_ab"][:, pi * 512:(pi + 1) * 512], b_wb_out, KC)
    for l in range(2):
        for p in range(FC // 2):
            cast_piece(wb_ffi[l, p][:, :, 0:256], W["w_ffn_in"][l][:, p * 256:(p + 1) * 256], b_wb_ffi, KC)
            cast_piece(wb_ffi[l, p][:, :, 256:512], W["w_ffn_in"][l][:, D_FF + p * 256:D_FF + (p + 1) * 256], b_wb_ffi, KC)
        for oc in range(KC):
            cast_piece(wb_ffo[l, oc], W["w_ffn_out"][l][:, oc * 128:(oc + 1) * 128], b_wb_ffo, FC)
    for p in range(4):
        cast_piece(wb_glu[p][:, :, 0:256], W["w_glu"][:, p * 256:(p + 1) * 256], b_wb_glu, KC)
        cast_piece(wb_glu[p][:, :, 256:512], W["w_glu"][:, D + p * 256:D + (p + 1) * 256], b_wb_glu, KC)

    def sb(name, shape, dt=F32):
        t = es.enter_context(nc.sbuf_tensor(name, list(shape), dt))
        return t, Buf(name)

    def op(eng, method, reads, writes, *a, **kw):
        return S.op(eng, lambda e, m=method, a=a, kw=kw: getattr(e, m)(*a, **kw), reads=reads, writes=writes)

    def ld(dst_ap, src_ap, b_dst, eng="sp", **kw):
        return S.dma(eng, lambda e, a=dst_ap, b=src_ap, kw=kw: e.dma_start(out=a, in_=b, **kw), b_dst, writes=[b_dst])

    def st(dst_ap, src_ap, b_src, eng="sp", extra_reads=()):
        return S.dma(eng, lambda e, a=dst_ap, b=src_ap: e.dma_start(out=a, in_=b), b_src, reads=[b_src] + list(extra_reads))

    def bcast(t, free_total, off, dims):
        return bass.AP(t, off, [[free_total, 128]] + [[s_, c_] for s_, c_ in dims])

    ident, b_ident = sb("ident", [128, 128])
    ld(ident[:], C["ident_in"], b_ident)
    ones_d, b_ones_d = sb("ones_d", [128, 128], BF16)
    ones_g, b_ones_g = sb("ones_g", [128, 128], BF16)
    ones_1, b_ones_1 = sb("ones_1", [128, 128], BF16)
    op("dve", "memset", [], [b_ones_d], ones_d[:], 1.0 / D)
    op("dve", "memset", [], [b_ones_g], ones_g[:], 1.0 / 128)
    op("dve", "memset", [], [b_ones_1], ones_1[:], 1.0)
    gvec, b_gvec = sb("gvec", [128, 5, KC])
    for i, (nm, l) in enumerate([("norm_mix", 0), ("norm_ffn", 0), ("norm_mix", 1), ("norm_ffn", 1)]):
        ld(gvec[:, i, :], W[nm][l].rearrange("(k p) -> p k", p=128), b_gvec, allow_slow_non_contiguous=True)
    ld(gvec[:, 4, :], W["norm_final"].rearrange("(k p) -> p k", p=128), b_gvec, allow_slow_non_contiguous=True)
    gn, b_gn = sb("gn", [128, 4])
    ld(gn[:], W["ret_gn"].rearrange("(h p) -> p h", p=128), b_gn, allow_slow_non_contiguous=True)
    dvec, b_dvec = sb("dvec", [128, KC])
    ld(dvec[:], W["ssm_d"].rearrange("(k p) -> p k", p=128), b_dvec, allow_slow_non_contiguous=True)
    decT, b_decT = sb("decT", [128, 4, 128]); ld(decT[:], C["decT"], b_decT)
    qdec, b_qdec = sb("qdec", [128, 2, 128]); ld(qdec[:], C["qdec"], b_qdec)
    kdec, b_kdec = sb("kdec", [128, 256]); ld(kdec[:], C["kdec"], b_kdec)
    sgn, b_sgn = sb("sgn", [128, 1]); ld(sgn[:], C["sgn"], b_sgn)
    tau, b_tau = sb("tau", [128, 128]); ld(tau[:], C["tau"], b_tau)
    swapm, b_swapm = sb("swapm", [128, 128]); ld(swapm[:], C["swapm"], b_swapm)
    rowmask, b_rowmask = sb("rowmask", [128, 4]); ld(rowmask[:], C["rowmask"], b_rowmask)
    hmask, b_hmask = sb("hmask", [128, 2]); ld(hmask[:], C["hmask"], b_hmask)

    b_d2d = Buf("d2d")
    if RUN_SAMPLE:
        for b in range(SB):
            S.dma("sp", lambda e, a=o_ks[b, 0:WB - 8, :], c_=st_k[b, 8:WB, :]: e.dma_start(out=a, in_=c_), b_d2d)
            S.dma("sp", lambda e, a=o_vs[b, 0:WB - 8, :], c_=st_v[b, 8:WB, :]: e.dma_start(out=a, in_=c_), b_d2d)
    psb = []
    for i in range(8):
        t = es.enter_context(nc.psum_tensor("ps%d" % i, [128, 512], F32))
        psb.append((t, Buf("ps%d" % i)))
    rr = [0]

    def next_ps():
        i = rr[0] % 5
        rr[0] += 1
        return psb[i]
    PS_A, PS_B, PS_C = psb[5], psb[6], psb[7]

    PANEL_EL = 4096
    panels = [sb("panel%d" % i, [128, PANEL_EL], BF16) for i in range(2)]
    prr = [0]

    def load_panel(src_ap, kcn, w, b_src, pool=None):
        pool = panels if pool is None else pool
        slot_i = prr[0] % len(pool)
        t, b = pool[slot_i]
        prr[0] += 1
        view = t[:, 0:kcn * w].rearrange("p (k n) -> p k n", k=kcn)
        q = "sp" if (slot_i % 2) == 0 else "pool"
        S.dma(q, lambda e, a=t[:, 0:kcn * w], s_=src_ap.rearrange("p k n -> p (k n)"): e.dma_start(out=a, in_=s_), b, reads=[b_src], writes=[b])
        return view, b

    def fm_chunk(view, b_pan, kcn, c0, rhs_t, b_rhs, ntok, extra=None):
        pt, b_pt = next_ps()
        for kc in range(kcn):
            op("pe", "matmul", [b_pan, b_rhs], [b_pt], pt[:, 0:ntok], lhsT=view[:, kc, c0:c0 + 128],
               rhs=rhs_t[:, kc, 0:ntok], start=(kc == 0), stop=(kc == kcn - 1))
        return pt, b_pt

    def tm_tile(view, b_pan, kcn, c0, w, lhs_t, b_lhs, t0, rows):
        pt, b_pt = next_ps()
        for kc in range(kcn):
            op("pe", "matmul", [b_pan, b_lhs], [b_pt], pt[0:rows, 0:w], lhsT=lhs_t[:, kc, t0:t0 + rows],
               rhs=view[:, kc, c0:c0 + w], start=(kc == 0), stop=(kc == kcn - 1))
        return pt, b_pt

    xtok, b_xtok = sb("xtok", [128, D])
    xT, b_xT = sb("xT", [128, KC, 512])
    rstd, b_rstd = sb("rstd", [128, 512])
    hT, b_hT = sb("hT", [128, KC, 512], BF16)
    sq, b_sq = hT, b_hT
    mixT, b_mixT = sb("mixT", [128, KC, 512], BF16)
    act, b_act = sb("act", [128, FC, 512], BF16)
    qaT, b_qaT = act[:, 0:2, :], b_act
    kaT, b_kaT = act[:, 2:4, :], b_act
    vatok, b_vatok = act[:, 4:8, :], b_act
    sgT, b_sgT = act[:, 8:12, :], b_act
    qbT, b_qbT = act[:, 12:16, :], b_act
    katok, b_katok = act[:, 16:18, :].rearrange("p a (b c) -> p (a b) c", c=256), b_act
    KBT, b_KBT = sb("KBT", [128, 4, RING * 128], BF16)
    VB, b_VB = sb("VB", [128, RING, 512], BF16)
    kvo = [(xtok[:, 0:512], b_xtok), (xtok[:, 512:1024], b_xtok)]
    tmpA, b_tmpA = sb("tmpA", [128, 512])
    tmpB, b_tmpB = sb("tmpB", [128, 512])
    tmpC, b_tmpC = sb("tmpC", [128, 512])
    tmpD, b_tmpD = sb("tmpD", [128, 512])
    pT = [sb("pT%d" % i, [128, 512], BF16) for i in range(3)]
    eT = [sb("eT%d" % i, [128, 512], BF16) for i in range(3)]
    thtab = [sb("thtab%d" % i, [128, TW], BF16) for i in range(1)]
    Sst, b_Sst = sb("Sst", [128, 2, 128])
    Sbf, b_Sbf = sb("Sbf", [128, 2, 128], BF16)
    op("dve", "memset", [], [b_Sst], Sst[:], 0.0)
    op("dve", "memset", [], [b_Sbf], Sbf[:], 0.0)
    qz, b_qz = sb("qz", [128, 4, 128], BF16)
    qd, b_qd = sb("qd", [128, 4, 128], BF16)
    osb, b_osb = tmpC, b_tmpC
    obf, b_obf = eT[0]
    osq, b_osq = eT[1]

    lam_abs, b_lam_abs = sb("lam_abs", [128, 64])
    th, b_th = sb("th", [128, 64])
    thS, b_thS = sb("thS", [128, 64])
    CT, b_CT = sb("CT", [128, 64, 128], BF16)
    ST, b_ST = sb("ST", [128, 64, 128], BF16)
    LT1, b_LT1 = sb("LT1", [128, KC, 4, 128], BF16)
    LT2, b_LT2 = sb("LT2", [128, KC, 4, 128], BF16)
    C1, b_C1 = sb("C1", [128, 64, 16], BF16)
    C2, b_C2 = sb("C2", [128, 64, 16], BF16)
    cL, b_cL = sb("cL", [128, 64]); sL, b_sL = sb("sL", [128, 64])
    cE, b_cE = sb("cE", [128, 64]); sE, b_sE = sb("sE", [128, 64])
    gd, b_gd = sb("gd", [128, KC])
    Wc, b_Wc = sb("Wc", [128, 64])
    Zend, b_Zend = sb("Zend", [128, 64])
    op("dve", "memset", [], [b_Wc], Wc[:], 0.0)
    op("dve", "memset", [], [b_Zend], Zend[:], 0.0)
    setup_es = ExitStack()

    def sbt(name, shape, dt=F32):
        t = setup_es.enter_context(nc.sbuf_tensor(name, list(shape), dt))
        return t, Buf(name)

    lamT, b_lamT = sb("lamT", [64, 256])
    ld(lamT[:, 0:64], W["ssm_lam_re"], b_lamT); ld(lamT[:, 64:128], W["ssm_lam_re"], b_lamT)
    ld(lamT[:, 128:192], W["ssm_lam_im"], b_lamT); ld(lamT[:, 192:256], W["ssm_lam_im"], b_lamT)
    lre, b_lre = sb("lre", [128, 64]); lim, b_lim = sb("lim", [128, 64])
    for src0, dst, b_dst in ((0, lre, b_lre), (128, lim, b_lim)):
        pt, b_pt = next_ps()
        op("pe", "transpose", [b_lamT, b_ident], [b_pt], out=pt[:, 0:64], in_=lamT[:, src0:src0 + 128], identity=ident[0:64, 0:64])
        op("dve", "tensor_copy", [b_pt], [b_dst], out=dst[:], in_=pt[:, 0:64])
    dtt, b_dtt = sb("dtt", [128, 64])
    ld(dtt[:], W["ssm_log_step"].partition_broadcast(128), b_dtt)
    op("act", "activation", [b_dtt], [b_dtt], out=dtt[:], in_=dtt[:], func=AF.Exp)
    op("dve", "tensor_mul", [b_lim, b_dtt], [b_th], out=th[:], in0=lim[:], in1=dtt[:])
    op("dve", "tensor_scalar_mul", [b_th, b_sgn], [b_thS], out=thS[:], in0=th[:], scalar1=sgn[:, 0:1])
    rho, b_rho = sb("rho", [128, 64])
    op("dve", "tensor_mul", [b_lre, b_dtt], [b_rho], out=rho[:], in0=lre[:], in1=dtt[:])
    op("act", "activation", [b_rho], [b_lam_abs], out=lam_abs[:], in_=rho[:], func=AF.Exp)

    s5a, b_s5a = tmpA, b_tmpA
    s5b, b_s5b = tmpB, b_tmpB
    s5i, b_s5i = sb("s5i", [128, 512], mybir.dt.int32)

    def sin_of(dst_ap, ang_ap, n, b_dst, reads):
        shp = ang_ap.shape
        kb = s5b[:, 0:n] if len(shp) == 2 else s5b[:, 0:n].rearrange("p (a b) -> p a b", a=shp[1])
        ki = s5i[:, 0:n] if len(shp) == 2 else s5i[:, 0:n].rearrange("p (a b) -> p a b", a=shp[1])
        op("dve", "tensor_scalar_mul", reads, [b_s5b], out=kb, in0=ang_ap, scalar1=1.0 / TWO_PI)
        op("dve", "tensor_copy", [b_s5b], [b_s5i], out=ki, in_=kb)
        op("dve", "tensor_copy", [b_s5i], [b_s5b], out=kb, in_=ki)
        op("dve", "scalar_tensor_tensor", [b_s5b] + reads, [b_s5b], out=kb, in0=kb, scalar=-TWO_PI, in1=ang_ap,
           op0=ALU.mult, op1=ALU.add)
        op("dve", "tensor_scalar", [b_s5b], [b_s5b], out=kb, in0=kb, scalar1=-3.14159, scalar2=3.14159, op0=ALU.max, op1=ALU.min)
        op("act", "activation", [b_s5b], [b_dst], out=dst_ap, in_=kb, func=AF.Sin)

    for g0 in range(0, 64, 4):
        angv = s5a[:, 0:512].rearrange("p (a b) -> p a b", a=4)
        for (thsrc, b_thsrc, dst, b_dst, shift) in ((th, b_th, CT, b_CT, math.pi / 2), (thS, b_thS, ST, b_ST, 0.0)):
            op("dve", "tensor_tensor", [b_thsrc, b_tau], [b_s5a], out=angv, in0=bcast(thsrc, 64, g0, [(1, 4), (0, 128)]),
               in1=bcast(tau, 128, 0, [(0, 4), (1, 128)]), op=ALU.mult)
            if shift:
                op("dve", "tensor_scalar_add", [b_s5a], [b_s5a], out=angv, in0=angv, scalar1=shift)
            sin_of(dst[:, g0:g0 + 4, :], angv, 512, b_dst, [b_s5a])

    def cs_small(mult, cdst, b_c, sdst, b_s, neg_sin):
        a = s5a[:, 0:64]
        op("dve", "tensor_scalar", [b_th], [b_s5a], out=a, in0=th[:], scalar1=float(mult), scalar2=math.pi / 2,
           op0=ALU.mult, op1=ALU.add)
        sin_of(cdst[:], a, 64, b_c, [b_s5a])
        op("dve", "tensor_scalar_mul", [b_thS], [b_s5a], out=a, in0=thS[:], scalar1=(-float(mult) if neg_sin else float(mult)))
        sin_of(sdst[:], a, 64, b_s, [b_s5a])
    cs_small(128.0, cL, b_cL, sL, b_sL, True)
    cs_small(127.0, cE, b_cE, sE, b_sE, True)
    c1t, b_c1t = sb("c1t", [128, 64]); s1t, b_s1t = sb("s1t", [128, 64])
    cs_small(1.0, c1t, b_c1t, s1t, b_s1t, False)
    fre, b_fre = sb("fre", [128, 64]); fim, b_fim = sb("fim", [128, 64])
    abr, b_abr = sb("abr", [128, 64]); abi, b_abi = sb("abi", [128, 64]); den, b_den = sb("den", [128, 64])
    t64, b_t64 = sb("t64", [128, 64])
    op("dve", "tensor_mul", [b_lam_abs, b_c1t], [b_abr], out=abr[:], in0=lam_abs[:], in1=c1t[:])
    op("dve", "tensor_scalar_add", [b_abr], [b_abr], out=abr[:], in0=abr[:], scalar1=-1.0)
    op("dve", "tensor_mul", [b_lam_abs, b_s1t], [b_abi], out=abi[:], in0=lam_abs[:], in1=s1t[:])
    op("dve", "tensor_scalar_mul", [b_abi, b_sgn], [b_abi], out=abi[:], in0=abi[:], scalar1=sgn[:, 0:1])
    op("dve", "tensor_mul", [b_lre], [b_den], out=den[:], in0=lre[:], in1=lre[:])
    op("dve", "tensor_mul", [b_lim], [b_t64], out=t64[:], in0=lim[:], in1=lim[:])
    op("dve", "tensor_add", [b_den, b_t64], [b_den], out=den[:], in0=den[:], in1=t64[:])
    op("dve", "reciprocal", [b_den], [b_den], out=den[:], in_=den[:])
    op("dve", "tensor_mul", [b_abr, b_lre], [b_fre], out=fre[:], in0=abr[:], in1=lre[:])
    op("dve", "tensor_mul", [b_abi, b_lim], [b_t64], out=t64[:], in0=abi[:], in1=lim[:])
    op("dve", "tensor_add", [b_fre, b_t64], [b_fre], out=fre[:], in0=fre[:], in1=t64[:])
    op("dve", "tensor_mul", [b_fre, b_den], [b_fre], out=fre[:], in0=fre[:], in1=den[:])
    op("dve", "tensor_mul", [b_abi, b_lre], [b_fim], out=fim[:], in0=abi[:], in1=lre[:])
    op("dve", "tensor_mul", [b_abr, b_lim], [b_t64], out=t64[:], in0=abr[:], in1=lim[:])
    op("dve", "tensor_sub", [b_fim, b_t64], [b_fim], out=fim[:], in0=fim[:], in1=t64[:])
    op("dve", "tensor_mul", [b_fim, b_den], [b_fim], out=fim[:], in0=fim[:], in1=den[:])
    fiS, b_fiS = sb("fiS", [128, 64])
    op("dve", "tensor_scalar_mul", [b_fim, b_sgn], [b_fiS], out=fiS[:], in0=fim[:], scalar1=sgn[:, 0:1])
    bre = W["ssm_b_re"].rearrange("g n q -> n g q"); bim = W["ssm_b_im"].rearrange("g n q -> n g q")
    v3 = lambda t, c0: t[:, c0:c0 + 128].rearrange("p (a b) -> p a b", a=8)
    for kc in range(KC):
        B1k, B2k = v3(tmpC, 0), v3(tmpD, 0)
        FBak, FBbk, tFk = v3(tmpA, 0), v3(tmpB, 0), v3(tmpA, 128)
        gsl = slice(kc * 8, (kc + 1) * 8)
        ld(B1k[0:64], bre[:, gsl, :], b_tmpC); ld(B1k[64:128], bim[:, gsl, :], b_tmpC)
        ld(B2k[0:64], bim[:, gsl, :], b_tmpD); ld(B2k[64:128], bre[:, gsl, :], b_tmpD)
        frb = bcast(fre, 64, kc * 8, [(1, 8), (0, 16)]); fib = bcast(fiS, 64, kc * 8, [(1, 8), (0, 16)])
        op("dve", "tensor_tensor", [b_tmpC, b_fre], [b_tmpA], out=FBak, in0=B1k, in1=frb, op=ALU.mult)
        op("dve", "tensor_tensor", [b_tmpD, b_fiS], [b_tmpA], out=tFk, in0=B2k, in1=fib, op=ALU.mult)
        op("dve", "tensor_sub", [b_tmpA], [b_tmpA], out=FBak, in0=FBak, in1=tFk)
        op("dve", "tensor_tensor", [b_tmpD, b_fre], [b_tmpB], out=FBbk, in0=B2k, in1=frb, op=ALU.mult)
        op("dve", "tensor_tensor", [b_tmpC, b_fiS], [b_tmpA], out=tFk, in0=B1k, in1=fib, op=ALU.mult)
        op("dve", "tensor_add", [b_tmpB, b_tmpA], [b_tmpB], out=FBbk, in0=FBbk, in1=tFk)
        for (FBt, b_FB, LT, b_LT) in ((tmpA, b_tmpA, LT1, b_LT1), (tmpB, b_tmpB, LT2, b_LT2)):
            pt, b_pt = next_ps()
            op("pe", "transpose", [b_FB, b_ident], [b_pt], out=pt[:, 0:128], in_=FBt[:, 0:128], identity=ident[:])
            for gi in range(4):
                op("dve", "tensor_scalar_mul", [b_pt, b_rowmask], [b_LT], out=LT[:, kc, gi, :], in0=pt[:, 0:128],
                   scalar1=rowmask[:, gi:gi + 1])
    cre = W["ssm_c_re"].rearrange("(k a) p n -> k (a p) n", k=KC); cim = W["ssm_c_im"].rearrange("(k a) p n -> k (a p) n", k=KC)
    for kc in range(KC):
        ld(tmpC[:, 0:64], cre[kc], b_tmpC); ld(tmpC[:, 64:128], cim[kc], b_tmpC)
        for (Cd, b_Cd, first_im) in ((C1, b_C1, False), (C2, b_C2, True)):
            cst = tmpD
            if not first_im:
                op("dve", "tensor_copy", [b_tmpC], [b_tmpD], out=cst[:, 0:64], in_=tmpC[:, 0:64])
                op("dve", "tensor_scalar_mul", [b_tmpC], [b_tmpD], out=cst[:, 64:128], in0=tmpC[:, 64:128], scalar1=-1.0)
            else:
                op("dve", "tensor_scalar_mul", [b_tmpC], [b_tmpD], out=cst[:, 0:64], in0=tmpC[:, 64:128], scalar1=-1.0)
                op("dve", "tensor_copy", [b_tmpC], [b_tmpD], out=cst[:, 64:128], in_=tmpC[:, 0:64])
            pt, b_pt = next_ps()
            op("pe", "transpose", [b_tmpD, b_ident], [b_pt], out=pt[:, 0:128], in_=cst[:, 0:128], identity=ident[:])
            op("dve", "tensor_copy", [b_pt], [b_Cd], out=Cd[:, kc * 8:(kc + 1) * 8, :].rearrange("p a b -> p (a b)"), in_=pt[:, 0:128])
    op("dve", "tensor_mul", [b_gvec, b_dvec], [b_gd], out=gd[:], in0=gvec[:, 2, :], in1=dvec[:])

    def evac(i, pt, b_pt, dst_ap, b_dst, n, scale=None, func=None, rows=128):
        if func is not None:
            kw = {"scale": scale} if scale is not None else {}
            op("act", "activation", [b_pt], [b_dst], out=dst_ap, in_=pt[0:rows, 0:n], func=func, **kw)
        elif scale is not None:
            op("act", "activation", [b_pt], [b_dst], out=dst_ap, in_=pt[0:rows, 0:n], func=AF.Copy, scale=scale)
        elif i % 2 == 0:
            op("act", "copy", [b_pt], [b_dst], out=dst_ap, in_=pt[0:rows, 0:n])
        else:
            op("dve", "tensor_copy", [b_pt], [b_dst], out=dst_ap, in_=pt[0:rows, 0:n])

    def rmsnorm(gi, ntok, dst=None, b_dst=None):
        dst = hT if dst is None else dst
        b_dst = b_hT if b_dst is None else b_dst
        op("act", "activation", [b_xT], [b_sq], out=sq[:, :, 0:ntok], in_=xT[:, :, 0:ntok], func=AF.Square)
        pt, b_pt = next_ps()
        for kc in range(KC):
            op("pe", "matmul", [b_ones_d, b_sq], [b_pt], pt[:, 0:ntok], lhsT=ones_d[:], rhs=sq[:, kc, 0:ntok],
               start=(kc == 0), stop=(kc == KC - 1))
        op("dve", "tensor_scalar_add", [b_pt], [b_rstd], out=rstd[:, 0:ntok], in0=pt[:, 0:ntok], scalar1=EPS)
        op("act", "activation", [b_rstd], [b_rstd], out=rstd[:, 0:ntok], in_=rstd[:, 0:ntok], func=AF.Sqrt)
        op("dve", "reciprocal", [b_rstd], [b_rstd], out=rstd[:, 0:ntok], in_=rstd[:, 0:ntok])
        for kc in range(KC):
            op("dve", "scalar_tensor_tensor", [b_xT, b_rstd, b_gvec], [b_dst], out=dst[:, kc, 0:ntok],
               in0=xT[:, kc, 0:ntok], scalar=gvec[:, gi, kc:kc + 1], in1=rstd[:, 0:ntok], op0=ALU.mult, op1=ALU.mult)

    def ffn(l, ntok):
        rmsnorm(1 + 2 * l, ntok)
        pool_in = panels + [(mixT[:].rearrange("p a b -> p (a b)"), b_mixT)]
        pool_out = pool_in + [(hT[:].rearrange("p a b -> p (a b)"), b_hT)]
        for p in range(FC // 2):
            view, b_pan = load_panel(wb_ffi[l, p], KC, 512, b_wb_ffi, pool_in)
            for j in range(2):
                pg, b_pg = fm_chunk(view, b_pan, KC, j * 128, hT, b_hT, ntok)
                pu, b_pu = fm_chunk(view, b_pan, KC, 256 + j * 128, hT, b_hT, ntok)
                op("act", "activation", [b_pg], [b_tmpA], out=tmpA[:, 0:ntok], in_=pg[:, 0:ntok], func=AF.Silu)
                op("dve", "tensor_tensor", [b_tmpA, b_pu], [b_act], out=act[:, 2 * p + j, 0:ntok], in0=tmpA[:, 0:ntok],
                   in1=pu[:, 0:ntok], op=ALU.mult)
        for oc in range(KC):
            view, b_pan = load_panel(wb_ffo[l, oc], FC, 128, b_wb_ffo, pool_out)
            pt, b_pt = fm_chunk(view, b_pan, FC, 0, act, b_act, ntok)
            op("dve", "tensor_tensor", [b_pt, b_xT], [b_xT], out=xT[:, oc, 0:ntok], in0=pt[:, 0:ntok],
               in1=xT[:, oc, 0:ntok], op=ALU.add)

    def load_x(src_ap, ntok):
        ntile = (ntok + 127) // 128
        for t in range(ntile):
            rows = min(128, ntok - t * 128)
            ld(xtok[0:rows, :], src_ap[t * 128:t * 128 + rows, :], b_xtok)
            for half in range(2):
                pt, b_pt = next_ps()
                for k4 in range(4):
                    kc = half * 4 + k4
                    op("pe", "transpose", [b_xtok, b_ident], [b_pt], out=pt[:, k4 * 128:k4 * 128 + rows],
                       in_=xtok[0:rows, kc * 128:(kc + 1) * 128], identity=ident[0:rows, 0:rows])
                src = pt[:, 0:512].rearrange("p (a b) -> p a b", a=4)[:, :, 0:rows]
                dst = xT[:, half * 4:half * 4 + 4, t * 128:t * 128 + rows]
                if half == 0:
                    op("act", "copy", [b_pt], [b_xT], out=dst, in_=src)
                else:
                    op("dve", "tensor_copy", [b_pt], [b_xT], out=dst, in_=src)

    def final_out(dst_ap, ntok):
        rmsnorm(4, ntok, dst=xT, b_dst=b_xT)
        ntile = (ntok + 127) // 128
        for t in range(ntile):
            rows = min(128, ntok - t * 128)
            for half in range(2):
                pt, b_pt = next_ps()
                for k4 in range(4):
                    kc = half * 4 + k4
                    op("pe", "transpose", [b_xT, b_ident], [b_pt], out=pt[0:rows, k4 * 128:(k4 + 1) * 128],
                       in_=xT[:, kc, t * 128:t * 128 + rows], identity=ident[:])
                evac(half, pt, b_pt, xtok[0:rows, half * 512:(half + 1) * 512], b_xtok, 512, rows=rows)
            st(dst_ap[t * 128:t * 128 + rows, :], xtok[0:rows, :], b_xtok)

    S5BUFS = []
    if blk_lo > 0:
        ld(Sst[:], i_Sst, b_Sst)
        op("dve", "tensor_copy", [b_Sst], [b_Sbf], out=Sbf[:], in_=Sst[:])
        ld(Wc[:], i_Wc, b_Wc); ld(Zend[:], i_Zend, b_Zend)
        ld(KBT[:], i_KBT, b_KBT); ld(VB[:], i_VB, b_VB)
    for blk in range(blk_lo, nblk_run):
        t0 = blk * 512
        load_x(xp[t0:t0 + 512, :], 512)
        rmsnorm(0, 512)
        last_kv = blk >= KV_FROM
        for pi in range(6):
            view, b_pan = load_panel(wb_in[pi], KC, 512, b_wb_in)
            if pi == 0:
                for j in range(2):
                    pt, b_pt = fm_chunk(view, b_pan, KC, j * 128, hT, b_hT, 512)
                    evac(j, pt, b_pt, qaT[:, j, :], b_qaT, 512)
                for j in range(2):
                    pt, b_pt = fm_chunk(view, b_pan, KC, 256 + j * 128, hT, b_hT, 512)
                    evac(j, pt, b_pt, kaT[:, j, :], b_kaT, 512, scale=0.125)
                for t in range(4):
                    pt, b_pt = tm_tile(view, b_pan, KC, 256, 256, hT, b_hT, t * 128, 128)
                    op("dve", "tensor_tensor", [b_pt, b_kdec], [b_katok], out=katok[:, t, :], in0=pt[:, 0:256], in1=kdec[:], op=ALU.mult)
            elif pi == 1:
                for t in range(4):
                    pt, b_pt = tm_tile(view, b_pan, KC, 0, 512, hT, b_hT, t * 128, 128)
                    evac(t, pt, b_pt, vatok[:, t, :], b_vatok, 512)
            elif pi == 2:
                for j in range(4):
                    pt, b_pt = fm_chunk(view, b_pan, KC, j * 128, hT, b_hT, 512)
                    evac(j, pt, b_pt, sgT[:, j, :], b_sgT, 512, func=AF.Silu)
            elif pi == 3:
                for j in range(4):
                    pt, b_pt = fm_chunk(view, b_pan, KC, j * 128, hT, b_hT, 512)
                    evac(j, pt, b_pt, qbT[:, j, :], b_qbT, 512)
            elif pi == 4:
                rs0 = ((blk * 4) % RING) * 128
                for j in range(4):
                    pt, b_pt = fm_chunk(view, b_pan, KC, j * 128, hT, b_hT, 512)
                    evac(j, pt, b_pt, KBT[:, j, rs0:rs0 + 512], b_KBT, 512)
                if last_kv and os.environ.get("NOKVK") is None:
                    for t in range(4):
                        pt, b_pt = tm_tile(view, b_pan, KC, 0, 512, hT, b_hT, t * 128, 128)
                        ko, b_ko = (tmpC, b_tmpC) if t % 2 == 0 else (tmpD, b_tmpD)
                        evac(t, pt, b_pt, ko[:], b_ko, 512)
                        r0 = (blk - KV_FROM) * 512 + t * 128
                        st(o_kp[r0:r0 + 128, :], ko[:], b_ko)
            else:
                for t in range(4):
                    pt, b_pt = tm_tile(view, b_pan, KC, 0, 512, hT, b_hT, t * 128, 128)
                    slot = (blk * 4 + t) % RING
                    if last_kv and os.environ.get("NOKVV") is None:
                        ko, b_ko = (tmpC, b_tmpC) if t % 2 == 0 else (tmpD, b_tmpD)
                        evac(t, pt, b_pt, ko[:], b_ko, 512)
                        op("pool", "tensor_copy", [b_ko], [b_VB], out=VB[:, slot, :], in_=ko[:])
                        r0 = (blk - KV_FROM) * 512 + t * 128
                        st(o_vp[r0:r0 + 128, :], ko[:], b_ko)
                    else:
                        evac(t, pt, b_pt, VB[:, slot, :], b_VB, 512)
        for c in range(4 if PH >= 2 else 0):
            cs = c * 128
            for h in range(4):
                pr = h // 2
                op("dve", "tensor_scalar_mul", [b_qaT, b_hmask], [b_qz], out=qz[:, h, :], in0=qaT[:, pr, cs:cs + 128], scalar1=hmask[:, (h % 2):(h % 2) + 1])
                op("dve", "tensor_tensor", [b_qz, b_qdec], [b_qd], out=qd[:, h, :], in0=qz[:, h, :], in1=qdec[:, pr, :], op=ALU.mult)
            ps_s, b_ps_s = next_ps()
            for h in range(4):
                pr = h // 2
                op("pe", "matmul", [b_kaT, b_qz], [b_ps_s], ps_s[:, h * 128:(h + 1) * 128], lhsT=kaT[:, pr, cs:cs + 128],
                   rhs=qz[:, h, :], start=True, stop=True)
            pTt, b_pTt = pT[c % 3]
            op("dve", "tensor_tensor", [b_ps_s, b_decT], [b_pTt], out=pTt[:], in0=ps_s[:], in1=decT[:].rearrange("p a b -> p (a b)"), op=ALU.mult)
            if SUB < 2:
                continue
            ps_o, b_ps_o = next_ps()
            for h in range(4):
                pr = h // 2
                op("pe", "matmul", [b_vatok, b_pTt], [b_ps_o], ps_o[:, h * 128:(h + 1) * 128], lhsT=vatok[:, c, h * 128:(h + 1) * 128],
                   rhs=pTt[:, h * 128:(h + 1) * 128], start=True, stop=False)
                op("pe", "matmul", [b_Sbf, b_qd], [b_ps_o], ps_o[:, h * 128:(h + 1) * 128], lhsT=Sbf[:, pr, :],
                   rhs=qd[:, h, :], start=False, stop=True)
            if SUB < 3:
                continue
            ps_d, b_ps_d = next_ps()
            for h in range(4):
                pr = h // 2
                op("pe", "matmul", [b_katok, b_vatok], [b_ps_d], ps_d[:, h * 128:(h + 1) * 128], lhsT=katok[:, c, pr * 128:(pr + 1) * 128],
                   rhs=vatok[:, c, h * 128:(h + 1) * 128], start=True, stop=True)
            for h in range(4):
                hp, pr = (h % 2) * 64, h // 2
                op("dve", "scalar_tensor_tensor", [b_Sst, b_ps_d, b_Sbf, b_ps_o], [b_Sst], out=Sst[hp:hp + 64, pr, :], in0=Sst[hp:hp + 64, pr, :],
                   scalar=float(GAM[h] ** 128), in1=ps_d[hp:hp + 64, h * 128:(h + 1) * 128], op0=ALU.mult, op1=ALU.add)
            op("dve", "tensor_copy", [b_Sst], [b_Sbf], out=Sbf[:], in_=Sst[:])
            if SUB < 4:
                continue
            op("act", "copy", [b_ps_o], [b_osb], out=osb[:], in_=ps_o[:])
            op("dve", "tensor_copy", [b_osb], [b_obf], out=obf[:], in_=osb[:])
            ps_m, b_ps_m = next_ps()
            op("pe", "matmul", [b_ones_g, b_obf], [b_ps_m], ps_m[:], lhsT=ones_g[:], rhs=obf[:], start=True, stop=True)
            op("dve", "tensor_tensor", [b_osb, b_ps_m], [b_osb], out=osb[:], in0=osb[:], in1=ps_m[:], op=ALU.subtract)
            op("act", "activation", [b_osb], [b_osq], out=osq[:], in_=osb[:], func=AF.Square)
            ps_q, b_ps_q = next_ps()
            op("pe", "matmul", [b_ones_g, b_osq], [b_ps_q], ps_q[:], lhsT=ones_g[:], rhs=osq[:], start=True, stop=True)
            if SUB < 5:
                continue
            op("dve", "tensor_scalar_add", [b_ps_q], [b_tmpA], out=tmpA[:], in0=ps_q[:], scalar1=EPS)
            op("act", "activation", [b_tmpA], [b_tmpA], out=tmpA[:], in_=tmpA[:], func=AF.Sqrt)
            op("dve", "reciprocal", [b_tmpA], [b_tmpA], out=tmpA[:], in_=tmpA[:])
            op("dve", "tensor_tensor", [b_osb, b_tmpA], [b_osb], out=osb[:], in0=osb[:], in1=tmpA[:], op=ALU.mult)
            if SUB < 6:
                continue
            for h in range(4):
                op("dve", "scalar_tensor_tensor", [b_osb, b_gn, b_sgT], [b_mixT], out=mixT[:, h, cs:cs + 128], in0=osb[:, h * 128:(h + 1) * 128],
                   scalar=gn[:, h:h + 1], in1=sgT[:, h, cs:cs + 128], op0=ALU.mult, op1=ALU.mult)
        kt_hi = blk * 4 + 3
        kt_lo = max(0, blk * 4 - 16)
        for h in range(8 if PH >= 3 else 0):
            hp, pr = (h % 2) * 64, h // 2
            tht, b_tht = thtab[0]
            ld(tht[:], C["swa_tab"][h], b_tht)
            ps_o, b_ps_o = PS_A if h % 2 == 0 else PS_B
            ps_dn, b_ps_dn = PS_C
            nk = kt_hi - kt_lo + 1
            kts = list(range(kt_lo, kt_hi + 1))

            def att_a(i):
                kt = kts[i]
                o = t0 - kt * 128
                slot = kt % RING
                ps_s, b_ps_s = next_ps()
                op("pe", "matmul", [b_KBT, b_qbT], [b_ps_s], ps_s[:], lhsT=KBT[hp:hp + 64, pr, slot * 128:(slot + 1) * 128],
                   rhs=qbT[hp:hp + 64, pr, :], start=True, stop=True)
                et, b_et = eT[i % 3]
                op("act", "activation", [b_ps_s], [b_et], out=et[:], in_=ps_s[:], func=AF.Exp, scale=0.125)
                pt_, b_pt_ = pT[i % 3]
                op("dve", "tensor_tensor", [b_et, b_tht], [b_pt_], out=pt_[:], in0=et[:], in1=tht[:, o + 384:o + 384 + 512], op=ALU.mult)

            def att_b(i):
                slot = kts[i] % RING
                pt_, b_pt_ = pT[i % 3]
                op("pe", "matmul", [b_VB, b_pt_], [b_ps_o], ps_o[:], lhsT=VB[:, slot, pr * 128:(pr + 1) * 128], rhs=pt_[:],
                   start=(i == 0), stop=(i == nk - 1))
                op("pe", "matmul", [b_ones_1, b_pt_], [b_ps_dn], ps_dn[:], lhsT=ones_1[:], rhs=pt_[:], start=(i == 0), stop=(i == nk - 1))
            for i in range(nk + 2):
                if i < nk:
                    att_a(i)
                if i >= 2:
                    att_b(i - 2)
            op("dve", "reciprocal", [b_ps_dn], [b_tmpB], out=tmpB[hp:hp + 64, :], in_=ps_dn[hp:hp + 64, :])
            op("dve", "tensor_tensor", [b_ps_o, b_tmpB], [b_mixT], out=mixT[hp:hp + 64, 4 + pr, :], in0=ps_o[hp:hp + 64, :], in1=tmpB[hp:hp + 64, :], op=ALU.mult)
        for p in range(2 if PH >= 4 else 0):
            view, b_pan = load_panel(wb_out[p], KC, 512, b_wb_out)
            for j in range(4):
                oc = p * 4 + j
                pt, b_pt = fm_chunk(view, b_pan, KC, j * 128, mixT, b_mixT, 512)
                op("dve", "tensor_tensor", [b_pt, b_xT], [b_xT], out=xT[:, oc, :], in0=pt[:], in1=xT[:, oc, :], op=ALU.add)
        if PH >= 4:
            ffn(0, 512)
        rmsnorm(2, 512)
        for c in range(4 if PH >= 5 else 0):
            cs = c * 128
            ps_y0, b_ps_y0 = PS_A
            ps_y1, b_ps_y1 = PS_B
            actf = act[:].rearrange("p a b -> p (a b)").bitcast(F32)
            if c == 0 and blk == blk_lo:
                s5bufs = [[(actf[:, (k * 3 + j) * 512:(k * 3 + j + 1) * 512], Buf("s5t%d_%d" % (k, j))) for j in range(3)] for k in range(3)]
                S5BUFS.append(s5bufs)
            s5bufs = S5BUFS[0]

            def stage1(qd_i):
                g0 = qd_i * 4
                kc, half = g0 // 8, (g0 % 8) // 4
                hp = half * 64
                (t1, b_t1), (t2, b_t2), _ = s5bufs[qd_i % 3]
                pd1, b_pd1 = next_ps()
                pd2, b_pd2 = next_ps()
                for gi in range(4):
                    op("pe", "matmul", [b_LT1, b_hT], [b_pd1], pd1[:, gi * 128:(gi + 1) * 128], lhsT=LT1[hp:hp + 64, kc, gi, :],
                       rhs=hT[hp:hp + 64, kc, cs:cs + 128], start=True, stop=True)
                for gi in range(4):
                    op("pe", "matmul", [b_LT2, b_hT], [b_pd2], pd2[:, gi * 128:(gi + 1) * 128], lhsT=LT2[hp:hp + 64, kc, gi, :],
                       rhs=hT[hp:hp + 64, kc, cs:cs + 128], start=True, stop=True)
                ctq = CT[:, g0:g0 + 4, :].rearrange("p a b -> p (a b)")
                stq = ST[:, g0:g0 + 4, :].rearrange("p a b -> p (a b)")
                op("dve", "tensor_tensor", [b_pd1, b_CT], [b_t1], out=t1, in0=pd1[:], in1=ctq, op=ALU.mult)
                op("dve", "tensor_tensor", [b_pd2, b_ST], [b_t2], out=t2, in0=pd2[:], in1=stq, op=ALU.mult)
                op("pool", "tensor_tensor", [b_t1, b_t2], [b_t1], out=t1, in0=t1, in1=t2, op=ALU.add)

            def stage2(qd_i):
                g0 = qd_i * 4
                (zd, b_zd), _, (zz, b_zz) = s5bufs[qd_i % 3]
                ctq = CT[:, g0:g0 + 4, :].rearrange("p a b -> p (a b)")
                stq = ST[:, g0:g0 + 4, :].rearrange("p a b -> p (a b)")
                for gi in range(4):
                    g = g0 + gi
                    op("dve", "tensor_tensor_scan", [b_zd, b_lam_abs, b_Wc], [b_zz], out=zz[:, gi * 128:(gi + 1) * 128],
                       data0=lam_abs[:, g:g + 1].to_broadcast([128, 128]), data1=zd[:, gi * 128:(gi + 1) * 128],
                       initial=Wc[:, g:g + 1], op0=ALU.mult, op1=ALU.add)
                at, b_at = eT[qd_i % 3]
                bt, b_bt = pT[qd_i % 3]
                op("dve", "tensor_tensor", [b_zz, b_CT], [b_at], out=at[:], in0=zz, in1=ctq, op=ALU.mult)
                op("pool", "tensor_tensor", [b_zz, b_ST], [b_bt], out=bt[:], in0=zz, in1=stq, op=ALU.mult)
                op("pool", "tensor_copy", [b_zz], [b_Zend], out=Zend[:, g0:g0 + 4],
                   in_=zz[:, 0:512].rearrange("p (a b) -> p a b", a=4)[:, :, 127])
                for gi in range(4):
                    g = g0 + gi
                    py, b_py = (ps_y0, b_ps_y0) if g < 32 else (ps_y1, b_ps_y1)
                    col = (g % 32) * 16
                    op("pe", "matmul", [b_at, b_C1], [b_py], py[:, col:col + 16], lhsT=at[:, gi * 128:(gi + 1) * 128], rhs=C1[:, g, :], start=True, stop=False)
                    op("pe", "matmul", [b_bt, b_C2], [b_py], py[:, col:col + 16], lhsT=bt[:, gi * 128:(gi + 1) * 128], rhs=C2[:, g, :], start=False, stop=True)
            for qd_i in range(18):
                if qd_i < 16:
                    stage1(qd_i)
                if qd_i >= 2:
                    stage2(qd_i - 2)
            pw, b_pw = next_ps()
            op("pe", "matmul", [b_swapm, b_Zend], [b_pw], pw[:, 0:64], lhsT=swapm[:], rhs=Zend[:], start=True, stop=True)
            op("dve", "tensor_tensor", [b_pw, b_sL], [b_t64], out=t64[:], in0=pw[:, 0:64], in1=sL[:], op=ALU.mult)
            op("dve", "tensor_tensor", [b_Zend, b_cL], [b_Wc], out=Wc[:], in0=Zend[:], in1=cL[:], op=ALU.mult)
            op("dve", "tensor_add", [b_Wc, b_t64], [b_Wc], out=Wc[:], in0=Wc[:], in1=t64[:])
            ysb, b_ysb = xtok[:, 0:1024].rearrange("p (a b) -> p a b", a=2), b_xtok
            op("act", "copy", [b_ps_y0], [b_ysb], out=ysb[:, 0, :], in_=ps_y0[:])
            op("dve", "tensor_copy", [b_ps_y1], [b_ysb], out=ysb[:, 1, :], in_=ps_y1[:])
            for half in range(2):
                pt, b_pt = next_ps()
                for k4 in range(4):
                    op("pe", "transpose", [b_ysb, b_ident], [b_pt], out=pt[:, k4 * 128:(k4 + 1) * 128],
                       in_=ysb[:, half, k4 * 128:(k4 + 1) * 128], identity=ident[:])
                for k4 in range(4):
                    kc = half * 4 + k4
                    op("dve", "scalar_tensor_tensor", [b_xT, b_gd, b_rstd], [b_tmpA], out=tmpA[:, k4 * 128:(k4 + 1) * 128], in0=xT[:, kc, cs:cs + 128],
                       scalar=gd[:, kc:kc + 1], in1=rstd[:, cs:cs + 128], op0=ALU.mult, op1=ALU.mult)
                op("dve", "tensor_tensor", [b_tmpA, b_pt], [b_tmpA], out=tmpA[:], in0=tmpA[:], in1=pt[:], op=ALU.add)
                op("act", "activation", [b_tmpA], [b_tmpB], out=tmpB[:], in_=tmpA[:], func=AF.Square)
                op("dve", "tensor_scalar", [b_tmpB], [b_tmpB], out=tmpB[:], in0=tmpB[:], scalar1=0.044715, scalar2=1.0, op0=ALU.mult, op1=ALU.add)
                op("dve", "tensor_tensor", [b_tmpB, b_tmpA], [b_tmpB], out=tmpB[:], in0=tmpB[:], in1=tmpA[:], op=ALU.mult)
                op("act", "activation", [b_tmpB], [b_tmpB], out=tmpB[:], in_=tmpB[:], func=AF.Sigmoid, scale=1.5957691216)
                for k4 in range(4):
                    kc = half * 4 + k4
                    op("dve", "tensor_tensor", [b_tmpA, b_tmpB], [b_mixT], out=mixT[:, kc, cs:cs + 128], in0=tmpA[:, k4 * 128:(k4 + 1) * 128],
                       in1=tmpB[:, k4 * 128:(k4 + 1) * 128], op=ALU.mult)
        for p in range(4 if PH >= 6 else 0):
            view, b_pan = load_panel(wb_glu[p], KC, 512, b_wb_glu)
            for j in range(2):
                oc = 2 * p + j
                pv, b_pv = fm_chunk(view, b_pan, KC, j * 128, mixT, b_mixT, 512)
                pg, b_pg = fm_chunk(view, b_pan, KC, 256 + j * 128, mixT, b_mixT, 512)
                op("act", "activation", [b_pg], [b_tmpA], out=tmpA[:], in_=pg[:], func=AF.Sigmoid)
                op("dve", "tensor_tensor", [b_tmpA, b_pv], [b_tmpA], out=tmpA[:], in0=tmpA[:], in1=pv[:], op=ALU.mult)
                op("dve", "tensor_tensor", [b_tmpA, b_xT], [b_xT], out=xT[:, oc, :], in0=tmpA[:], in1=xT[:, oc, :], op=ALU.add)
        if PH >= 6:
            ffn(1, 512)
        final_out(o_yp[t0:t0 + 512, :], 512)

    st(o_retp, Sst[:], b_Sst)
    if len(LAUNCH_RANGES) > 1:
        st(o_Wc, Wc[:], b_Wc); st(o_Zend, Zend[:], b_Zend)
        st(o_KBT, KBT[:], b_KBT); st(o_VB, VB[:], b_VB)
    pw, b_pw = next_ps()
    op("pe", "matmul", [b_swapm, b_Zend], [b_pw], pw[:, 0:64], lhsT=swapm[:], rhs=Zend[:], start=True, stop=True)
    op("dve", "tensor_tensor", [b_pw, b_sE], [b_t64], out=t64[:], in0=pw[:, 0:64], in1=sE[:], op=ALU.mult)
    xfin, b_xfin = sb("xfin", [128, 64])
    op("dve", "tensor_tensor", [b_Zend, b_cE], [b_xfin], out=xfin[:], in0=Zend[:], in1=cE[:], op=ALU.mult)
    op("dve", "tensor_add", [b_xfin, b_t64], [b_xfin], out=xfin[:], in0=xfin[:], in1=t64[:])
    st(o_ssp, xfin[:], b_xfin)

    if RUN_SAMPLE:
        NS = TS
        AX = mybir.AxisListType.X
        bmask, b_bmask = sb("bmask", [32, 4]); ld(bmask[:], C["bmask"], b_bmask)
        eomask, b_eomask = sb("eomask", [64, 2]); ld(eomask[:], C["eomask"], b_eomask)
        kdecS = kdec[0:32, :]
        ld(kdecS, C["kdecS"], b_kdec)
        qdecS = qdec[:, 0, 0:64].rearrange("p (a b) -> p a b", a=2)
        ld(qdecS, C["qdecS"], b_qdec)
        decTS = decT[0:32, 0, :]
        ld(decTS.rearrange("p (a b) -> p a b", a=4), C["decTS"], b_decT)
        load_x(xs, NS)
        rmsnorm(0, NS)
        kvo0, kvo1 = kvo[0][0], kvo[1][0]
        for pi in range(6):
            view, b_pan = load_panel(wb_in[pi], KC, 512, b_wb_in)
            if pi == 0:
                for j in range(2):
                    pt, b_pt = fm_chunk(view, b_pan, KC, j * 128, hT, b_hT, NS)
                    evac(j, pt, b_pt, qaT[:, j, 0:NS], b_qaT, NS)
                for j in range(2):
                    pt, b_pt = fm_chunk(view, b_pan, KC, 256 + j * 128, hT, b_hT, NS)
                    evac(j, pt, b_pt, kaT[:, j, 0:NS], b_kaT, NS, scale=0.125)
                pt, b_pt = tm_tile(view, b_pan, KC, 256, 256, hT, b_hT, 0, NS)
                op("dve", "tensor_tensor", [b_pt, b_kdec], [b_katok], out=katok[0:NS, 0, :], in0=pt[0:NS, 0:256], in1=kdecS, op=ALU.mult)
            elif pi == 1:
                pt, b_pt = tm_tile(view, b_pan, KC, 0, 512, hT, b_hT, 0, NS)
                evac(0, pt, b_pt, vatok[0:NS, 0, :], b_vatok, 512, rows=NS)
            elif pi == 2:
                for j in range(4):
                    pt, b_pt = fm_chunk(view, b_pan, KC, j * 128, hT, b_hT, NS)
                    evac(j, pt, b_pt, sgT[:, j, 0:NS], b_sgT, NS, func=AF.Silu)
            elif pi == 3:
                for j in range(4):
                    pt, b_pt = fm_chunk(view, b_pan, KC, j * 128, hT, b_hT, NS)
                    evac(j, pt, b_pt, qbT[:, j, 0:NS], b_qbT, NS)
            elif pi == 4:
                for j in range(4):
                    pt, b_pt = fm_chunk(view, b_pan, KC, j * 128, hT, b_hT, NS)
                    evac(j, pt, b_pt, KBT[:, j, 2056:2056 + NS], b_KBT, NS)
                pt, b_pt = tm_tile(view, b_pan, KC, 0, 512, hT, b_hT, 0, NS)
                evac(1, pt, b_pt, kvo0[0:NS, :], b_xtok, 512, rows=NS)
                for b in range(SB):
                    st(o_ks[b, WB - 8:WB, :], kvo0[b * 8:(b + 1) * 8, :], b_xtok)
            else:
                pt, b_pt = tm_tile(view, b_pan, KC, 0, 512, hT, b_hT, 0, NS)
                evac(1, pt, b_pt, kvo1[0:NS, :], b_xtok, 512, rows=NS)
                for b in range(SB):
                    st(o_vs[b, WB - 8:WB, :], kvo1[b * 8:(b + 1) * 8, :], b_xtok)
                    S.dma("pool", lambda e, a=VB[0:8, 16 + b, :], c_=kvo1[b * 8:(b + 1) * 8, :]: e.dma_start(out=a, in_=c_),
                          b_VB, reads=[b_xtok], writes=[b_VB])
        def SstS(b):
            t_ = tmpB if b < 2 else tmpD
            return t_[:, (b % 2) * 256:(b % 2) * 256 + 256].rearrange("p (a c) -> p a c", a=2), (b_tmpB if b < 2 else b_tmpD)

        def SbfS(b):
            t_, bb_ = pT[1] if b < 2 else pT[2]
            return t_[:, (b % 2) * 256:(b % 2) * 256 + 256].rearrange("p (a c) -> p a c", a=2), bb_
        for b in range(SB):
            sv, sbuf_ = SstS(b)
            for h in range(4):
                hp, pr = (h % 2) * 64, h // 2
                ld(sv[hp:hp + 64, pr, :], st_ret[b, h], sbuf_)
        for b in range(SB):
            sv, sbuf_ = SstS(b); bv, bbuf_ = SbfS(b)
            op("dve", "tensor_copy", [sbuf_], [bbuf_], out=bv, in_=sv)
        for h in range(4):
            pr = h // 2
            op("dve", "tensor_scalar_mul", [b_qaT, b_hmask], [b_qz], out=qz[:, h, 0:NS], in0=qaT[:, pr, 0:NS], scalar1=hmask[:, (h % 2):(h % 2) + 1])
            op("dve", "tensor_tensor", [b_qz, b_qdec], [b_qd], out=qd[:, h, 0:NS], in0=qz[:, h, 0:NS], in1=qdecS[:, pr, :], op=ALU.mult)
        ps_s, b_ps_s = next_ps()
        for h in range(4):
            pr = h // 2
            op("pe", "matmul", [b_kaT, b_qz], [b_ps_s], ps_s[0:NS, h * NS:(h + 1) * NS], lhsT=kaT[:, pr, 0:NS], rhs=qz[:, h, 0:NS], start=True, stop=True)
        pTt, b_pTt = pT[0]
        op("dve", "tensor_tensor", [b_ps_s, b_decT], [b_pTt], out=pTt[0:NS, 0:4 * NS], in0=ps_s[0:NS, 0:4 * NS], in1=decTS, op=ALU.mult)
        ps_o, b_ps_o = next_ps()
        for h in range(4):
            pr = h // 2
            op("pe", "matmul", [b_vatok, b_pTt], [b_ps_o], ps_o[:, h * NS:(h + 1) * NS], lhsT=vatok[0:NS, 0, h * 128:(h + 1) * 128],
               rhs=pTt[0:NS, h * NS:(h + 1) * NS], start=True, stop=False)
            for b in range(SB):
                bv, bbuf_ = SbfS(b)
                op("pe", "matmul", [bbuf_, b_qd], [b_ps_o], ps_o[:, h * NS + 8 * b:h * NS + 8 * b + 8], lhsT=bv[:, pr, :],
                   rhs=qd[:, h, 8 * b:8 * b + 8], start=False, stop=(b == SB - 1))
        kdm, b_kdm = eT[2]
        for b in range(SB):
            sv, sbuf_ = SstS(b)
            op("dve", "tensor_scalar_mul", [b_katok, b_bmask], [b_kdm], out=kdm[0:NS, 0:256], in0=katok[0:NS, 0, :], scalar1=bmask[0:NS, b:b + 1])
            ps_d, b_ps_d = next_ps()
            for h in range(4):
                pr = h // 2
                op("pe", "matmul", [b_kdm, b_vatok], [b_ps_d], ps_d[:, h * 128:(h + 1) * 128], lhsT=kdm[0:NS, pr * 128:(pr + 1) * 128],
                   rhs=vatok[0:NS, 0, h * 128:(h + 1) * 128], start=True, stop=True)
            for h in range(4):
                hp, pr = (h % 2) * 64, h // 2
                op("dve", "scalar_tensor_tensor", [sbuf_, b_ps_d, b_ps_o], [sbuf_], out=sv[hp:hp + 64, pr, :], in0=sv[hp:hp + 64, pr, :],
                   scalar=float(GAM[h] ** 8), in1=ps_d[hp:hp + 64, h * 128:(h + 1) * 128], op0=ALU.mult, op1=ALU.add)
            st(o_rets[b], sv, sbuf_)
        W4 = 4 * NS
        op("act", "copy", [b_ps_o], [b_osb], out=osb[:, 0:W4], in_=ps_o[:, 0:W4])
        op("dve", "tensor_copy", [b_osb], [b_obf], out=obf[:, 0:W4], in_=osb[:, 0:W4])
        ps_m, b_ps_m = next_ps()
        op("pe", "matmul", [b_ones_g, b_obf], [b_ps_m], ps_m[:, 0:W4], lhsT=ones_g[:], rhs=obf[:, 0:W4], start=True, stop=True)
        op("dve", "tensor_tensor", [b_osb, b_ps_m], [b_osb], out=osb[:, 0:W4], in0=osb[:, 0:W4], in1=ps_m[:, 0:W4], op=ALU.subtract)
        op("act", "activation", [b_osb], [b_osq], out=osq[:, 0:W4], in_=osb[:, 0:W4], func=AF.Square)
        ps_q, b_ps_q = next_ps()
        op("pe", "matmul", [b_ones_g, b_osq], [b_ps_q], ps_q[:, 0:W4], lhsT=ones_g[:], rhs=osq[:, 0:W4], start=True, stop=True)
        op("dve", "tensor_scalar_add", [b_ps_q], [b_tmpA], out=tmpA[:, 0:W4], in0=ps_q[:, 0:W4], scalar1=EPS)
        op("act", "activation", [b_tmpA], [b_tmpA], out=tmpA[:, 0:W4], in_=tmpA[:, 0:W4], func=AF.Sqrt)
        op("dve", "reciprocal", [b_tmpA], [b_tmpA], out=tmpA[:, 0:W4], in_=tmpA[:, 0:W4])
        op("dve", "tensor_tensor", [b_osb, b_tmpA], [b_osb], out=osb[:, 0:W4], in0=osb[:, 0:W4], in1=tmpA[:, 0:W4], op=ALU.mult)
        for h in range(4):
            op("dve", "scalar_tensor_tensor", [b_osb, b_gn, b_sgT], [b_mixT], out=mixT[:, h, 0:NS], in0=osb[:, h * NS:(h + 1) * NS],
               scalar=gn[:, h:h + 1], in1=sgT[:, h, 0:NS], op0=ALU.mult, op1=ALU.mult)
        tht, b_tht = thtab[0]
        ld(tht[0:64, :], C["WS"], b_tht)
        ld(tmpD[0:64, :], C["hselB"], b_tmpD)
        qpad = act[:, 18, 0:256].rearrange("p (a c) -> p a c", a=4)
        PsT = act[:, 19:22, :].rearrange("p a b -> p (a b)")[:, 0:17 * 64].rearrange("p (k c) -> p k c", c=64)
        op("dve", "memset", [], [b_act], qpad, 0.0)
        o64, o2, dparts, dtot = tmpB[0:64, 0:64], tmpB[0:64, 64:192], tmpB[0:64, 192:197], tmpB[0:64, 200:201]
        for b in range(SB):
            for kt in range(16):
                ld(xtok[:, 0:512], st_k[b, kt * 128:(kt + 1) * 128, :], b_xtok)
                pt, b_pt = next_ps()
                for pr in range(4):
                    op("pe", "transpose", [b_xtok, b_ident], [b_pt], out=pt[:, pr * 128:(pr + 1) * 128], in_=xtok[:, pr * 128:(pr + 1) * 128], identity=ident[:])
                op("act" if kt % 2 else "dve", "copy" if kt % 2 else "tensor_copy", [b_pt], [b_KBT], out=KBT[:, :, kt * 128:(kt + 1) * 128],
                   in_=pt[:, 0:512].rearrange("p (a c) -> p a c", a=4))
            op("dve", "tensor_copy", [b_KBT], [b_KBT], out=KBT[:, :, 2048:2056], in_=KBT[:, :, 2056 + 8 * b:2056 + 8 * b + 8])
            S.dma("pool", lambda e, a=VB[:, 0:16, :], c_=st_v[b].rearrange("(t p) f -> p t f", p=128): e.dma_start(out=a, in_=c_),
                  b_VB, writes=[b_VB])
            for pr in range(4):
                op("dve", "tensor_copy", [b_qbT], [b_act], out=qpad[0:64, pr, (2 * pr) * 8:(2 * pr) * 8 + 8], in_=qbT[0:64, pr, 8 * b:8 * b + 8])
                op("dve", "tensor_copy", [b_qbT], [b_act], out=qpad[64:128, pr, (2 * pr + 1) * 8:(2 * pr + 1) * 8 + 8], in_=qbT[64:128, pr, 8 * b:8 * b + 8])
            for grp in range(5):
                k0 = grp * 512
                kw = 512 if grp < 4 else 8
                ps_sc, b_ps_sc = next_ps()
                for pr in range(4):
                    op("pe", "matmul", [b_act, b_KBT], [b_ps_sc], ps_sc[0:64, 0:kw], lhsT=qpad[:, pr, :], rhs=KBT[:, pr, k0:k0 + kw],
                       start=(pr == 0), stop=(pr == 3))
                op("act", "activation", [b_ps_sc], [b_tmpA], out=tmpA[0:64, 0:kw], in_=ps_sc[0:64, 0:kw], func=AF.Exp, scale=0.125)
                op("dve", "tensor_tensor", [b_tmpA, b_tht], [b_tmpA], out=tmpA[0:64, 0:kw], in0=tmpA[0:64, 0:kw], in1=tht[0:64, k0:k0 + kw], op=ALU.mult)
                op("dve", "reduce_sum", [b_tmpA], [b_tmpB], out=dparts[:, grp:grp + 1], in_=tmpA[0:64, 0:kw], axis=AX)
                pt, b_pt = next_ps()
                if grp < 4:
                    for t4 in range(4):
                        op("pe", "transpose", [b_tmpA, b_ident], [b_pt], out=pt[:, t4 * 64:(t4 + 1) * 64], in_=tmpA[0:64, t4 * 128:(t4 + 1) * 128], identity=ident[0:64, 0:64])
                    op("act", "copy", [b_pt], [b_act], out=PsT[:, grp * 4:grp * 4 + 4, :], in_=pt[:, 0:256].rearrange("p (a c) -> p a c", a=4))
                else:
                    op("pe", "transpose", [b_tmpA, b_ident], [b_pt], out=pt[0:8, 0:64], in_=tmpA[0:64, 0:8], identity=ident[0:64, 0:64])
                    op("act", "copy", [b_pt], [b_act], out=PsT[0:8, 16, :], in_=pt[0:8, 0:64])
            ps_pv, b_ps_pv = next_ps()
            for kt in range(16):
                op("pe", "matmul", [b_act, b_VB], [b_ps_pv], ps_pv[0:64, :], lhsT=PsT[:, kt, :], rhs=VB[:, kt, :], start=(kt == 0), stop=False)
            op("pe", "matmul", [b_act, b_VB], [b_ps_pv], ps_pv[0:64, :], lhsT=PsT[0:8, 16, :], rhs=VB[0:8, 16 + b, :], start=False, stop=True)
            op("dve", "reduce_sum", [b_tmpB], [b_tmpB], out=dtot, in_=dparts, axis=AX)
            op("dve", "reciprocal", [b_tmpB], [b_tmpB], out=dtot, in_=dtot)
            op("dve", "tensor_tensor", [b_ps_pv, b_tmpD], [b_tmpA], out=tmpA[0:64, :], in0=ps_pv[0:64, :], in1=tmpD[0:64, :], op=ALU.mult)
            op("dve", "tensor_reduce", [b_tmpA], [b_tmpB], out=o64, in_=tmpA[0:64, :].rearrange("p (h d) -> p d h", h=8), axis=AX, op=ALU.add)
            for e2 in range(2):
                op("dve", "tensor_scalar", [b_tmpB, b_eomask], [b_tmpB], out=o2[:, e2 * 64:(e2 + 1) * 64], in0=o64, scalar1=dtot, scalar2=eomask[:, e2:e2 + 1],
                   op0=ALU.mult, op1=ALU.mult)
            pt, b_pt = next_ps()
            op("pe", "transpose", [b_tmpB, b_ident], [b_pt], out=pt[:, 0:64], in_=o2, identity=ident[0:64, 0:64])
            for pr in range(4):
                op("dve", "tensor_copy", [b_pt], [b_mixT], out=mixT[0:64, 4 + pr, 8 * b:8 * b + 8], in_=pt[0:64, (2 * pr) * 8:(2 * pr) * 8 + 8])
                op("dve", "tensor_copy", [b_pt], [b_mixT], out=mixT[64:128, 4 + pr, 8 * b:8 * b + 8], in_=pt[64:128, (2 * pr + 1) * 8:(2 * pr + 1) * 8 + 8])
        for p in range(2):
            view, b_pan = load_panel(wb_out[p], KC, 512, b_wb_out)
            for j in range(4):
                oc = p * 4 + j
                pt, b_pt = fm_chunk(view, b_pan, KC, j * 128, mixT, b_mixT, NS)
                op("dve", "tensor_tensor", [b_pt, b_xT], [b_xT], out=xT[:, oc, 0:NS], in0=pt[:, 0:NS], in1=xT[:, oc, 0:NS], op=ALU.add)
        ffn(0, NS)
        rmsnorm(2, NS)
        s5f = s5i[:].bitcast(F32)
        WS5 = s5f[:, 0:256].rearrange("p (a c) -> p a c", a=4)
        ZendS = s5f[:, 256:512].rearrange("p (a c) -> p a c", a=4)
        c7, b_c7, s7, b_s7 = den, b_den, abr, b_abr
        cs_small(7.0, c7, b_c7, s7, b_s7, True)
        for b in range(SB):
            ld(tmpC[0:64, 0:64], st_sr[b], b_tmpC); ld(tmpC[0:64, 64:128], st_si[b], b_tmpC)
            ld(tmpC[0:64, 128:192], st_si[b], b_tmpC); ld(tmpC[0:64, 192:256], st_sr[b], b_tmpC)
            pt, b_pt = next_ps()
            op("pe", "transpose", [b_tmpC, b_ident], [b_pt], out=pt[:, 0:64], in_=tmpC[0:64, 0:128], identity=ident[0:64, 0:64])
            op("pe", "transpose", [b_tmpC, b_ident], [b_pt], out=pt[:, 64:128], in_=tmpC[0:64, 128:256], identity=ident[0:64, 0:64])
            op("dve", "tensor_tensor", [b_pt, b_s1t], [b_t64], out=t64[:], in0=pt[:, 64:128], in1=s1t[:], op=ALU.mult)
            op("dve", "tensor_tensor", [b_pt, b_c1t], [b_s5i], out=WS5[:, b, :], in0=pt[:, 0:64], in1=c1t[:], op=ALU.mult)
            op("dve", "tensor_sub", [b_s5i, b_t64], [b_s5i], out=WS5[:, b, :], in0=WS5[:, b, :], in1=t64[:])
        ps_y0, b_ps_y0 = PS_A
        ps_y1, b_ps_y1 = PS_B
        v4 = lambda ap: ap.rearrange("p (a b c) -> p a b c", a=4, b=4)
        for qd_i in range(16):
            g0 = qd_i * 4
            kc, half = g0 // 8, (g0 % 8) // 4
            hp = half * 64
            pd1, b_pd1 = next_ps()
            pd2, b_pd2 = next_ps()
            for gi in range(4):
                op("pe", "matmul", [b_LT1, b_hT], [b_pd1], pd1[:, gi * NS:(gi + 1) * NS], lhsT=LT1[hp:hp + 64, kc, gi, :], rhs=hT[hp:hp + 64, kc, 0:NS], start=True, stop=True)
            for gi in range(4):
                op("pe", "matmul", [b_LT2, b_hT], [b_pd2], pd2[:, gi * NS:(gi + 1) * NS], lhsT=LT2[hp:hp + 64, kc, gi, :], rhs=hT[hp:hp + 64, kc, 0:NS], start=True, stop=True)
            ctq = bcast(CT, 64 * 128, g0 * 128, [(128, 4), (0, 4), (1, 8)])
            stq = bcast(ST, 64 * 128, g0 * 128, [(128, 4), (0, 4), (1, 8)])
            op("dve", "tensor_tensor", [b_pd1, b_CT], [b_tmpA], out=v4(tmpA[:, 0:128]), in0=v4(pd1[:, 0:128]), in1=ctq, op=ALU.mult)
            op("dve", "tensor_tensor", [b_pd2, b_ST], [b_tmpB], out=v4(tmpB[:, 0:128]), in0=v4(pd2[:, 0:128]), in1=stq, op=ALU.mult)
            op("pool", "tensor_tensor", [b_tmpA, b_tmpB], [b_tmpC], out=tmpC[:, 0:128], in0=tmpA[:, 0:128], in1=tmpB[:, 0:128], op=ALU.add)
            for gi in range(4):
                g = g0 + gi
                for b in range(SB):
                    c0 = gi * NS + 8 * b
                    op("dve", "tensor_tensor_scan", [b_tmpC, b_lam_abs, b_s5i], [b_tmpD], out=tmpD[:, c0:c0 + 8],
                       data0=lam_abs[:, g:g + 1].to_broadcast([128, 8]), data1=tmpC[:, c0:c0 + 8], initial=WS5[:, b, g:g + 1], op0=ALU.mult, op1=ALU.add)
            at, b_at = eT[qd_i % 2]
            bt, b_bt = pT[qd_i % 2]
            op("dve", "tensor_tensor", [b_tmpD, b_CT], [b_at], out=v4(at[:, 0:128]), in0=v4(tmpD[:, 0:128]), in1=ctq, op=ALU.mult)
            op("pool", "tensor_tensor", [b_tmpD, b_ST], [b_bt], out=v4(bt[:, 0:128]), in0=v4(tmpD[:, 0:128]), in1=stq, op=ALU.mult)
            for b in range(SB):
                op("pool", "tensor_copy", [b_tmpD], [b_s5i], out=ZendS[:, b, g0:g0 + 4], in_=bcast(tmpD, 512, 8 * b + 7, [(NS, 4)]))
            for gi in range(4):
                g = g0 + gi
                py, b_py = (ps_y0, b_ps_y0) if g < 32 else (ps_y1, b_ps_y1)
                col = (g % 32) * 16
                op("pe", "matmul", [b_at, b_C1], [b_py], py[0:NS, col:col + 16], lhsT=at[:, gi * NS:(gi + 1) * NS], rhs=C1[:, g, :], start=True, stop=False)
                op("pe", "matmul", [b_bt, b_C2], [b_py], py[0:NS, col:col + 16], lhsT=bt[:, gi * NS:(gi + 1) * NS], rhs=C2[:, g, :], start=False, stop=True)
        for b in range(SB):
            pw, b_pw = next_ps()
            op("pe", "matmul", [b_swapm, b_s5i], [b_pw], pw[:, 0:64], lhsT=swapm[:], rhs=ZendS[:, b, :], start=True, stop=True)
            op("dve", "tensor_tensor", [b_pw, b_s7], [b_t64], out=t64[:], in0=pw[:, 0:64], in1=s7[:], op=ALU.mult)
            op("dve", "tensor_tensor", [b_s5i, b_c7], [b_fre], out=fre[:], in0=ZendS[:, b, :], in1=c7[:], op=ALU.mult)
            op("dve", "tensor_add", [b_fre, b_t64], [b_fre], out=fre[:], in0=fre[:], in1=t64[:])
            st(o_sss[b], fre[:], b_fre)
        ysb = xtok[:, 0:1024].rearrange("p (a b) -> p a b", a=2)
        op("act", "copy", [b_ps_y0], [b_xtok], out=ysb[0:NS, 0, :], in_=ps_y0[0:NS, :])
        op("dve", "tensor_copy", [b_ps_y1], [b_xtok], out=ysb[0:NS, 1, :], in_=ps_y1[0:NS, :])
        for half in range(2):
            pt, b_pt = next_ps()
            for k4 in range(4):
                op("pe", "transpose", [b_xtok, b_ident], [b_pt], out=pt[:, k4 * NS:(k4 + 1) * NS], in_=ysb[0:NS, half, k4 * 128:(k4 + 1) * 128], identity=ident[0:NS, 0:NS])
            for k4 in range(4):
                kc = half * 4 + k4
                op("dve", "scalar_tensor_tensor", [b_xT, b_gd, b_rstd], [b_tmpA], out=tmpA[:, k4 * NS:(k4 + 1) * NS], in0=xT[:, kc, 0:NS],
                   scalar=gd[:, kc:kc + 1], in1=rstd[:, 0:NS], op0=ALU.mult, op1=ALU.mult)
            op("dve", "tensor_tensor", [b_tmpA, b_pt], [b_tmpA], out=tmpA[:, 0:W4], in0=tmpA[:, 0:W4], in1=pt[:, 0:W4], op=ALU.add)
            op("act", "activation", [b_tmpA], [b_tmpB], out=tmpB[:, 0:W4], in_=tmpA[:, 0:W4], func=AF.Square)
            op("dve", "tensor_scalar", [b_tmpB], [b_tmpB], out=tmpB[:, 0:W4], in0=tmpB[:, 0:W4], scalar1=0.044715, scalar2=1.0, op0=ALU.mult, op1=ALU.add)
            op("dve", "tensor_tensor", [b_tmpB, b_tmpA], [b_tmpB], out=tmpB[:, 0:W4], in0=tmpB[:, 0:W4], in1=tmpA[:, 0:W4], op=ALU.mult)
            op("act", "activation", [b_tmpB], [b_tmpB], out=tmpB[:, 0:W4], in_=tmpB[:, 0:W4], func=AF.Sigmoid, scale=1.5957691216)
            for k4 in range(4):
                kc = half * 4 + k4
                op("dve", "tensor_tensor", [b_tmpA, b_tmpB], [b_mixT], out=mixT[:, kc, 0:NS], in0=tmpA[:, k4 * NS:(k4 + 1) * NS],
                   in1=tmpB[:, k4 * NS:(k4 + 1) * NS], op=ALU.mult)
        for p in range(4):
            view, b_pan = load_panel(wb_glu[p], KC, 512, b_wb_glu)
            for j in range(2):
                oc = 2 * p + j
                pv, b_pv = fm_chunk(view, b_pan, KC, j * 128, mixT, b_mixT, NS)
                pg, b_pg = fm_chunk(view, b_pan, KC, 256 + j * 128, mixT, b_mixT, NS)
                op("act", "activation", [b_pg], [b_tmpA], out=tmpA[:, 0:NS], in_=pg[:, 0:NS], func=AF.Sigmoid)
                op("dve", "tensor_tensor", [b_tmpA, b_pv], [b_tmpA], out=tmpA[:, 0:NS], in0=tmpA[:, 0:NS], in1=pv[:, 0:NS], op=ALU.mult)
                op("dve", "tensor_tensor", [b_tmpA, b_xT], [b_xT], out=xT[:, oc, 0:NS], in0=tmpA[:, 0:NS], in1=xT[:, oc, 0:NS], op=ALU.add)
        ffn(1, NS)
        final_out(o_ys, NS)

    S.finish()
    es.close()
    return nc


_NC_CACHE = {}
LAUNCH_RANGES = [(0, 16)]


def kernel(**inputs):
    f32 = np.float32
    x_prompt = np.asarray(inputs["x_prompt"], f32)
    consts = host_consts()
    wmap = {}
    for k, shp in WEIGHT_SHAPES.items():
        a = np.asarray(inputs[k], f32)
        if k in ("norm_mix", "norm_ffn", "norm_final", "w_ffn_in", "w_ffn_out"):
            wmap[k] = np.ascontiguousarray(a).reshape(shp)
        else:
            wmap[k] = np.ascontiguousarray(a[0]).reshape(shp)
    bf = ml_dtypes.bfloat16
    state = [{"i_Sst": np.zeros((128, 2, 128), f32), "i_Wc": np.zeros((128, 64), f32), "i_Zend": np.zeros((128, 64), f32),
              "i_KBT": np.zeros((128, 4, RING * 128), bf), "i_VB": np.zeros((128, RING, 512), bf)} for _ in range(NCORES)]
    y_prompt = np.zeros((2, SEQ, D), f32)
    r = None
    for li, (lo, hi) in enumerate(LAUNCH_RANGES):
        last = li == len(LAUNCH_RANGES) - 1
        key = ("nc", lo, hi, last)
        if key not in _NC_CACHE:
            _NC_CACHE[key] = build_program(hi, RUN_SAMPLE=last, blk_lo=lo)
        nc = _NC_CACHE[key]
        in_maps = []
        for c in range(NCORES):
            m = {"xp": np.ascontiguousarray(x_prompt[c % 2])}
            bs = slice(c * SB, (c + 1) * SB)
            m["xs"] = np.ascontiguousarray(np.asarray(inputs["x_sample"], f32)[bs].reshape(TS, D))
            m["st_ret"] = np.ascontiguousarray(np.asarray(inputs["state_ret"], f32)[0, bs])
            m["st_k"] = np.ascontiguousarray(np.asarray(inputs["state_swa_k"], f32)[0, bs].reshape(SB, WB, 512))
            m["st_v"] = np.ascontiguousarray(np.asarray(inputs["state_swa_v"], f32)[0, bs].reshape(SB, WB, 512))
            m["st_sr"] = np.ascontiguousarray(np.asarray(inputs["state_ssm_re"], f32)[0, bs])
            m["st_si"] = np.ascontiguousarray(np.asarray(inputs["state_ssm_im"], f32)[0, bs])
            m.update(state[c])
            m.update(wmap)
            m.update({"c_" + k: v for k, v in consts.items()})
            in_maps.append(m)
        res = run_bass_kernel_spmd(nc, in_maps, core_ids=list(range(NCORES)))
        r = res.results
        for sq_ in range(2):
            y_prompt[sq_, lo * 512:hi * 512] = r[sq_]["o_yp"][lo * 512:hi * 512]
        for c in range(NCORES if len(LAUNCH_RANGES) > 1 else 0):
            state[c] = {"i_Sst": np.asarray(r[c]["o_retp"], f32).reshape(128, 2, 128), "i_Wc": np.asarray(r[c]["o_Wc"], f32),
                        "i_Zend": np.asarray(r[c]["o_Zend"], f32), "i_KBT": np.asarray(r[c]["o_KBT"]).reshape(128, 4, RING * 128),
                        "i_VB": np.asarray(r[c]["o_VB"]).reshape(128, RING, 512)}
    B = 2
    y_sample = np.concatenate([r[c]["o_ys"] for c in range(NCORES)], 0).reshape(32, 8, D)

    def unret(a):
        a = np.asarray(a).reshape(128, 2, 128)
        out = np.zeros((4, 64, 128), f32)
        for h in range(4):
            out[h] = a[(h % 2) * 64:(h % 2) * 64 + 64, h // 2, :]
        return out
    ret_p = np.stack([unret(r[0]["o_retp"]), unret(r[1]["o_retp"])])[None]
    ret_s = np.stack([unret(np.asarray(r[c]["o_rets"]).reshape(SB, 128, 2, 128)[b]) for c in range(NCORES) for b in range(SB)])[None]
    swk_p = np.stack([r[0]["o_kp"], r[1]["o_kp"]]).reshape(1, B, WB, H_B, DH_B)
    swv_p = np.stack([r[0]["o_vp"], r[1]["o_vp"]]).reshape(1, B, WB, H_B, DH_B)
    swk_s = np.concatenate([r[c]["o_ks"] for c in range(NCORES)], 0).reshape(1, 32, WB, H_B, DH_B)
    swv_s = np.concatenate([r[c]["o_vs"] for c in range(NCORES)], 0).reshape(1, 32, WB, H_B, DH_B)
    sr_p = np.stack([r[0]["o_ssp"][0:64].T, r[1]["o_ssp"][0:64].T])[None]
    si_p = np.stack([r[0]["o_ssp"][64:128].T, r[1]["o_ssp"][64:128].T])[None]
    sss = lambda c, b: np.asarray(r[c]["o_sss"]).reshape(SB, 128, 64)[b]
    sr_s = np.stack([sss(c, b)[0:64].T for c in range(NCORES) for b in range(SB)])[None]
    si_s = np.stack([sss(c, b)[64:128].T for c in range(NCORES) for b in range(SB)])[None]
    return (y_prompt, y_sample, ret_p, ret_s, swk_p, swv_p, swk_s, swv_s, sr_p, si_p, sr_s, si_s)
```

```python
import numpy as np
import concourse.bass as bass
import concourse.mybir as mybir
from concourse.bass_utils import run_bass_kernel_spmd

F32 = mybir.dt.float32
BF16 = mybir.dt.bfloat16
ALU = mybir.AluOpType
AF = mybir.ActivationFunctionType

D = 1024
KC = D // 128
NCORES = 8
TP = 2048
TS = 32
SB = 4
WB = 2048
H_A, DK_A, DV_A = 4, 64, 128
H_B, DH_B = 8, 64
AB_IN = 3072
EPS = 1e-6


class Buf:
    __slots__ = ("name", "last_w", "readers")

    def __init__(self, name):
        self.name = name
        self.last_w = None
        self.readers = []


class Sched:
    ENGS = ("pe", "act", "dve", "pool", "sp")

    def __init__(self, nc):
        self.nc = nc
        self.ops = {e: [] for e in self.ENGS}
        self.dma_sems = []
        self.buf_dma = {}

    def _deps(self, reads, writes):
        deps = []
        for b in reads:
            if b.last_w is not None:
                deps.append(b.last_w)
        for b in writes:
            if b.last_w is not None:
                deps.append(b.last_w)
            deps.extend(b.readers)
        return deps

    def _commit(self, tok, reads, writes):
        for b in reads:
            b.readers = [r for r in b.readers if not (r[0] == tok[0] and r[1] == tok[1])]
            b.readers.append(tok)
        for b in writes:
            b.last_w = tok
            b.readers = []

    def op(self, eng, fn, reads=(), writes=()):
        deps = self._deps(reads, writes)
        idx = len(self.ops[eng])
        if eng == "pe":
            deps = [d for d in deps if not (d[0] == "e" and d[1] == "pe")]
        self.ops[eng].append({"fn": fn, "deps": deps, "sig": False, "dma": None})
        tok = ("e", eng, idx)
        self._commit(tok, reads, writes)
        return tok

    def dma(self, eng, fn, key, reads=(), writes=()):
        deps = self._deps(reads, writes)
        if key not in self.buf_dma:
            self.buf_dma[key] = [len(self.buf_dma), 0]
        ent = self.buf_dma[key]
        ent[1] += 16
        tok = ("d", ent[0], ent[1])
        self.ops[eng].append({"fn": fn, "deps": deps, "sig": False, "dma": ent[0]})
        self._commit(tok, reads, writes)
        return tok

    def finish(self, final_waits_eng="sp"):
        nc = self.nc
        for e in self.ENGS:
            for o in self.ops[e]:
                for d in o["deps"]:
                    if d[0] == "e":
                        self.ops[d[1]][d[2]]["sig"] = True
        cnt = {}
        for e in self.ENGS:
            c = 0
            for o in self.ops[e]:
                if o["sig"]:
                    c += 1
                o["cnt"] = c
            cnt[e] = c
        n_dma = len(self.buf_dma)
        from contextlib import ExitStack
        with ExitStack() as st:
            esem = {e: st.enter_context(nc.semaphore("es_" + e)) for e in self.ENGS}
            dsem = [st.enter_context(nc.semaphore("ds_%d" % i)) for i in range(n_dma)]
            block = st.enter_context(nc.Block())
            ops = self.ops
            finals = [(ent[0], ent[1]) for ent in self.buf_dma.values()]

            def emit(e, eng):
                waited_e = {}
                waited_d = {}
                for o in ops[e]:
                    need_e, need_d = {}, {}
                    for d in o["deps"]:
                        if d[0] == "e":
                            v = ops[d[1]][d[2]]["cnt"]
                            if v > need_e.get(d[1], 0):
                                need_e[d[1]] = v
                        else:
                            if d[2] > need_d.get(d[1], 0):
                                need_d[d[1]] = d[2]
                    for pe_, v in need_e.items():
                        if v > waited_e.get(pe_, 0):
                            eng.wait_ge(esem[pe_], v)
                            waited_e[pe_] = v
                    for si, v in need_d.items():
                        if v > waited_d.get(si, 0):
                            eng.wait_ge(dsem[si], v)
                            waited_d[si] = v
                    ins = o["fn"](eng)
                    if o["dma"] is not None:
                        ins.then_inc(dsem[o["dma"]], 16)
                    elif o["sig"]:
                        ins.then_inc(esem[e], 1)
                if e == final_waits_eng:
                    for si, v in finals:
                        eng.wait_ge(dsem[si], v)

            @block.tensor
            def _(eng):
                emit("pe", eng)

            @block.scalar
            def _(eng):
                emit("act", eng)

            @block.vector
            def _(eng):
                emit("dve", eng)

            @block.gpsimd
            def _(eng):
                emit("pool", eng)

            @block.sync
            def _(eng):
                emit("sp", eng)


import math
import os
import ml_dtypes
from contextlib import ExitStack

SEQ = 8192
NBLK = SEQ // 512
D_FF = 2816
FC = D_FF // 128
GAM = [1.0 - 2.0 ** (-5 - h) for h in range(4)]
SLOPES = [2.0 ** (-8.0 * (h + 1) / 8) for h in range(8)]
TW = 2944
RING = 20
TWO_PI = 2.0 * math.pi


def host_consts():
    f32 = np.float32
    c = {}
    c["ident_in"] = np.eye(128, dtype=f32)
    m = np.arange(128)[:, None]
    n = np.arange(128)[None, :]
    decT = np.zeros((128, 4, 128), np.float64)
    for h in range(4):
        decT[:, h, :] = np.where(n >= m, GAM[h] ** np.maximum(n - m, 0), 0.0)
    c["decT"] = decT.astype(f32)
    qdec = np.zeros((128, 2, 128), np.float64)
    for p in range(128):
        for pr in range(2):
            h = 2 * pr + p // 64
            qdec[p, pr, :] = GAM[h] ** (np.arange(128) + 1.0)
    c["qdec"] = qdec.astype(f32)
    kdec = np.zeros((128, 256), np.float64)
    for h in range(4):
        kdec[:, h * 64:(h + 1) * 64] = (GAM[h] ** (127.0 - np.arange(128)))[:, None] * 0.125
    c["kdec"] = kdec.astype(f32)
    jl = np.arange(128)[:, None]
    x = np.arange(TW)[None, :]
    dl = x - jl - 384
    cnt = ((dl <= 128).astype(np.float64) + ((dl % 4 == 0) & (dl <= 512)) + ((dl % 16 == 0) & (dl <= 2048)))
    valid = (dl >= 0) & (dl <= 2048)
    tab = np.zeros((8, 128, TW), np.float64)
    for h in range(8):
        tab[h] = np.where(valid, cnt * np.exp(-SLOPES[h] * np.maximum(dl, 0)), 0.0)
    c["swa_tab"] = tab.astype(ml_dtypes.bfloat16)
    sgn = np.ones((128, 1), f32); sgn[64:] = -1.0
    c["sgn"] = sgn
    c["tau"] = np.tile(np.arange(128, dtype=f32)[None, :], (128, 1))
    sw = np.zeros((128, 128), f32)
    for p in range(64):
        sw[p, p + 64] = 1.0; sw[p + 64, p] = 1.0
    c["swapm"] = sw
    rm = np.zeros((128, 4), f32)
    for p in range(128):
        rm[p, (p % 64) // 16] = 1.0
    c["rowmask"] = rm
    hm = np.zeros((128, 2), f32); hm[:64, 0] = 1.0; hm[64:, 1] = 1.0
    c["hmask"] = hm
    p32 = np.arange(32)
    kdS = np.zeros((32, 256), np.float64)
    for h in range(4):
        kdS[:, h * 64:(h + 1) * 64] = (GAM[h] ** (7.0 - (p32 % 8)))[:, None] * 0.125
    c["kdecS"] = kdS.astype(f32)
    qdS = np.zeros((128, 2, 32), np.float64)
    for p in range(128):
        for pr in range(2):
            qdS[p, pr, :] = GAM[2 * pr + p // 64] ** ((p32 % 8) + 1.0)
    c["qdecS"] = qdS.astype(f32)
    dS = np.zeros((32, 4, 32), np.float64)
    mm, nn = p32[:, None], p32[None, :]
    for h in range(4):
        dS[:, h, :] = np.where((mm // 8 == nn // 8) & (nn >= mm), GAM[h] ** np.maximum(nn - mm, 0), 0.0)
    c["decTS"] = dS.astype(f32)
    bm = np.zeros((32, 4), f32)
    bm[p32, p32 // 8] = 1.0
    c["bmask"] = bm
    r64 = np.arange(64)
    hh, tt = r64 // 8, r64 % 8
    jj = np.arange(2056)[None, :]
    dls = 2048 + tt[:, None] - jj
    cnts = ((dls <= 128).astype(np.float64) + ((dls % 4 == 0) & (dls <= 512)) + ((dls % 16 == 0) & (dls <= 2048)))
    ws = np.where((dls >= 0) & (dls <= 2048), cnts * np.exp(-np.array(SLOPES)[hh][:, None] * np.maximum(dls, 0)), 0.0)
    wsp = np.zeros((64, TW), np.float64); wsp[:, :2056] = ws
    c["WS"] = wsp.astype(ml_dtypes.bfloat16)
    hs = np.zeros((64, 512), f32)
    for r in range(64):
        hs[r, (r // 8) * 64:(r // 8) * 64 + 64] = 1.0
    c["hselB"] = hs
    eo = np.zeros((64, 2), f32); eo[:, 0] = (hh % 2 == 0); eo[:, 1] = (hh % 2 == 1)
    c["eomask"] = eo
    return c


CONST_SHAPES = {"ident_in": ([128, 128], F32), "decT": ([128, 4, 128], F32), "qdec": ([128, 2, 128], F32),
                "kdec": ([128, 256], F32), "swa_tab": ([8, 128, TW], BF16), "sgn": ([128, 1], F32),
                "tau": ([128, 128], F32), "swapm": ([128, 128], F32), "rowmask": ([128, 4], F32), "hmask": ([128, 2], F32),
                "kdecS": ([32, 256], F32), "qdecS": ([128, 2, 32], F32), "decTS": ([32, 4, 32], F32), "bmask": ([32, 4], F32),
                "WS": ([64, TW], BF16), "hselB": ([64, 512], F32), "eomask": ([64, 2], F32)}

WEIGHT_SHAPES = {"norm_mix": [2, D], "norm_ffn": [2, D], "norm_final": [D], "w_in_ab": [D, AB_IN], "ret_gn": [512],
                 "w_out_ab": [D, D], "ssm_lam_re": [64, 64], "ssm_lam_im": [64, 64], "ssm_log_step": [64],
                 "ssm_b_re": [64, 64, 16], "ssm_b_im": [64, 64, 16], "ssm_c_re": [64, 16, 64], "ssm_c_im": [64, 16, 64],
                 "ssm_d": [D], "w_glu": [D, 2 * D], "w_ffn_in": [2, D, 2 * D_FF], "w_ffn_out": [2, D_FF, D]}


def build_program(nblk_run=NBLK, PH=9, sim=False, SUB=9, RUN_SAMPLE=True, KV_FROM=NBLK - 4, blk_lo=0):
    nc = bass.Bass("TRN2", target_bir_lowering=False)
    S = Sched(nc)
    es = ExitStack()

    def din(name, shape, dt=F32):
        return nc.dram_tensor(name, list(shape), dt, kind="ExternalInput").ap()

    def dout(name, shape):
        return nc.dram_tensor(name, list(shape), F32, kind="ExternalOutput").ap()

    xp = din("xp", [SEQ, D])
    W = {k: din(k, v) for k, v in WEIGHT_SHAPES.items()}
    C = {k: din("c_" + k, v[0], v[1]) for k, v in CONST_SHAPES.items()}
    o_yp = dout("o_yp", [SEQ, D])
    o_kp = dout("o_kp", [WB, 512])
    o_vp = dout("o_vp", [WB, 512])
    o_retp = dout("o_retp", [128, 2, 128])
    o_ssp = dout("o_ssp", [128, 64])
    i_Sst = din("i_Sst", [128, 2, 128]); i_Wc = din("i_Wc", [128, 64]); i_Zend = din("i_Zend", [128, 64])
    i_KBT = din("i_KBT", [128, 4, RING * 128], BF16); i_VB = din("i_VB", [128, RING, 512], BF16)
    if len(LAUNCH_RANGES) > 1:
        o_Wc = dout("o_Wc", [128, 64]); o_Zend = dout("o_Zend", [128, 64])
        o_KBT = nc.dram_tensor("o_KBT", [128, 4, RING * 128], BF16, kind="ExternalOutput").ap()
        o_VB = nc.dram_tensor("o_VB", [128, RING, 512], BF16, kind="ExternalOutput").ap()
    xs = din("xs", [TS, D])
    st_ret = din("st_ret", [SB, 4, 64, 128])
    st_k = din("st_k", [SB, WB, 512]); st_v = din("st_v", [SB, WB, 512])
    st_sr = din("st_sr", [SB, 64, 64]); st_si = din("st_si", [SB, 64, 64])
    o_ys = dout("o_ys", [TS, D])
    o_rets = dout("o_rets", [SB, 128, 2, 128])
    o_ks = dout("o_ks", [SB, WB, 512]); o_vs = dout("o_vs", [SB, WB, 512])
    o_sss = dout("o_sss", [SB, 128, 64])

    def dscr(name, shape):
        if sim:
            return din(name, shape, BF16), Buf(name)
        t = nc.dram_tensor(name, list(shape), BF16)
        return t.ap(), Buf(name)
    wb_in, b_wb_in = dscr("wb_in", [6, 128, KC, 512])
    wb_out, b_wb_out = dscr("wb_out", [2, 128, KC, 512])
    wb_glu, b_wb_glu = dscr("wb_glu", [4, 128, KC, 512])
    wb_ffi, b_wb_ffi = dscr("wb_ffi", [2, FC // 2, 128, KC, 512])
    wb_ffo, b_wb_ffo = dscr("wb_ffo", [2, KC, 128, FC, 128])

    def cast_piece(dst_ap, src_ap, b_dst, kcn):
        if sim:
            return
        S.dma("pool", lambda e, a=dst_ap, b=src_ap.rearrange("(k p) n -> p k n", p=128): e.dma_start(out=a, in_=b), b_dst, writes=[b_dst])
    for pi in range(6):
        cast_piece(wb_in[pi], W["w_in_ab"][:, pi * 512:(pi + 1) * 512], b_wb_in, KC)
    for pi in range(2):
        cast_piece(wb_out[pi], W["w_out_ab"][:, pi * 512:(pi + 1) * 512], b_wb_out, KC)
    for l in range(2):
        for p in range(FC // 2):
            cast_piece(wb_ffi[l, p][:, :, 0:256], W["w_ffn_in"][l][:, p * 256:(p + 1) * 256], b_wb_ffi, KC)
            cast_piece(wb_ffi[l, p][:, :, 256:512], W["w_ffn_in"][l][:, D_FF + p * 256:D_FF + (p + 1) * 256], b_wb_ffi, KC)
        for oc in range(KC):
            cast_piece(wb_ffo[l, oc], W["w_ffn_out"][l][:, oc * 128:(oc + 1) * 128], b_wb_ffo, FC)
    for p in range(4):
        cast_piece(wb_glu[p][:, :, 0:256], W["w_glu"][:, p * 256:(p + 1) * 256], b_wb_glu, KC)
        cast_piece(wb_glu[p][:, :, 256:512], W["w_glu"][:, D + p * 256:D + (p + 1) * 256], b_wb_glu, KC)

    def sb(name, shape, dt=F32):
        t = es.enter_context(nc.sbuf_tensor(name, list(shape), dt))
        return t, Buf(name)

    def op(eng, method, reads, writes, *a, **kw):
        return S.op(eng, lambda e, m=method, a=a, kw=kw: getattr(e, m)(*a, **kw), reads=reads, writes=writes)

    def ld(dst_ap, src_ap, b_dst, eng="sp", **kw):
        return S.dma(eng, lambda e, a=dst_ap, b=src_ap, kw=kw: e.dma_start(out=a, in_=b, **kw), b_dst, writes=[b_dst])

    def st(dst_ap, src_ap, b_src, eng="sp", extra_reads=()):
        return S.dma(eng, lambda e, a=dst_ap, b=src_ap: e.dma_start(out=a, in_=b), b_src, reads=[b_src] + list(extra_reads))

    def bcast(t, free_total, off, dims):
        return bass.AP(t, off, [[free_total, 128]] + [[s_, c_] for s_, c_ in dims])

    ident, b_ident = sb("ident", [128, 128])
    ld(ident[:], C["ident_in"], b_ident)
    ones_d, b_ones_d = sb("ones_d", [128, 128], BF16)
    ones_g, b_ones_g = sb("ones_g", [128, 128], BF16)
    ones_1, b_ones_1 = sb("ones_1", [128, 128], BF16)
    op("dve", "memset", [], [b_ones_d], ones_d[:], 1.0 / D)
    op("dve", "memset", [], [b_ones_g], ones_g[:], 1.0 / 128)
    op("dve", "memset", [], [b_ones_1], ones_1[:], 1.0)
    gvec, b_gvec = sb("gvec", [128, 5, KC])
    for i, (nm, l) in enumerate([("norm_mix", 0), ("norm_ffn", 0), ("norm_mix", 1), ("norm_ffn", 1)]):
        ld(gvec[:, i, :], W[nm][l].rearrange("(k p) -> p k", p=128), b_gvec, allow_slow_non_contiguous=True)
    ld(gvec[:, 4, :], W["norm_final"].rearrange("(k p) -> p k", p=128), b_gvec, allow_slow_non_contiguous=True)
    gn, b_gn = sb("gn", [128, 4])
    ld(gn[:], W["ret_gn"].rearrange("(h p) -> p h", p=128), b_gn, allow_slow_non_contiguous=True)
    dvec, b_dvec = sb("dvec", [128, KC])
    ld(dvec[:], W["ssm_d"].rearrange("(k p) -> p k", p=128), b_dvec, allow_slow_non_contiguous=True)
    decT, b_decT = sb("decT", [128, 4, 128]); ld(decT[:], C["decT"], b_decT)
    qdec, b_qdec = sb("qdec", [128, 2, 128]); ld(qdec[:], C["qdec"], b_qdec)
    kdec, b_kdec = sb("kdec", [128, 256]); ld(kdec[:], C["kdec"], b_kdec)
    sgn, b_sgn = sb("sgn", [128, 1]); ld(sgn[:], C["sgn"], b_sgn)
    tau, b_tau = sb("tau", [128, 128]); ld(tau[:], C["tau"], b_tau)
    swapm, b_swapm = sb("swapm", [128, 128]); ld(swapm[:], C["swapm"], b_swapm)
    rowmask, b_rowmask = sb("rowmask", [128, 4]); ld(rowmask[:], C["rowmask"], b_rowmask)
    hmask, b_hmask = sb("hmask", [128, 2]); ld(hmask[:], C["hmask"], b_hmask)

    b_d2d = Buf("d2d")
    if RUN_SAMPLE:
        for b in range(SB):
            S.dma("sp", lambda e, a=o_ks[b, 0:WB - 8, :], c_=st_k[b, 8:WB, :]: e.dma_start(out=a, in_=c_), b_d2d)
            S.dma("sp", lambda e, a=o_vs[b, 0:WB - 8, :], c_=st_v[b, 8:WB, :]: e.dma_start(out=a, in_=c_), b_d2d)
    psb = []
    for i in range(8):
        t = es.enter_context(nc.psum_tensor("ps%d" % i, [128, 512], F32))
        psb.append((t, Buf("ps%d" % i)))
    rr = [0]

    def next_ps():
        i = rr[0] % 5
        rr[0] += 1
        return psb[i]
    PS_A, PS_B, PS_C = psb[5], psb[6], psb[7]

    PANEL_EL = 4096
    panels = [sb("panel%d" % i, [128, PANEL_EL], BF16) for i in range(2)]
    prr = [0]

    def load_panel(src_ap, kcn, w, b_src, pool=None):
        pool = panels if pool is None else pool
        slot_i = prr[0] % len(pool)
        t, b = pool[slot_i]
        prr[0] += 1
        view = t[:, 0:kcn * w].rearrange("p (k n) -> p k n", k=kcn)
        q = "sp" if (slot_i % 2) == 0 else "pool"
        S.dma(q, lambda e, a=t[:, 0:kcn * w], s_=src_ap.rearrange("p k n -> p (k n)"): e.dma_start(out=a, in_=s_), b, reads=[b_src], writes=[b])
        return view, b

    def fm_chunk(view, b_pan, kcn, c0, rhs_t, b_rhs, ntok, extra=None):
        pt, b_pt = next_ps()
        for kc in range(kcn):
            op("pe", "matmul", [b_pan, b_rhs], [b_pt], pt[:, 0:ntok], lhsT=view[:, kc, c0:c0 + 128],
               rhs=rhs_t[:, kc, 0:ntok], start=(kc == 0), stop=(kc == kcn - 1))
        return pt, b_pt

    def tm_tile(view, b_pan, kcn, c0, w, lhs_t, b_lhs, t0, rows):
        pt, b_pt = next_ps()
        for kc in range(kcn):
            op("pe", "matmul", [b_pan, b_lhs], [b_pt], pt[0:rows, 0:w], lhsT=lhs_t[:, kc, t0:t0 + rows],
               rhs=view[:, kc, c0:c0 + w], start=(kc == 0), stop=(kc == kcn - 1))
        return pt, b_pt

    xtok, b_xtok = sb("xtok", [128, D])
    xT, b_xT = sb("xT", [128, KC, 512])
    rstd, b_rstd = sb("rstd", [128, 512])
    hT, b_hT = sb("hT", [128, KC, 512], BF16)
    sq, b_sq = hT, b_hT
    mixT, b_mixT = sb("mixT", [128, KC, 512], BF16)
    act, b_act = sb("act", [128, FC, 512], BF16)
    qaT, b_qaT = act[:, 0:2, :], b_act
    kaT, b_kaT = act[:, 2:4, :], b_act
    vatok, b_vatok = act[:, 4:8, :], b_act
    sgT, b_sgT = act[:, 8:12, :], b_act
    qbT, b_qbT = act[:, 12:16, :], b_act
    katok, b_katok = act[:, 16:18, :].rearrange("p a (b c) -> p (a b) c", c=256), b_act
    KBT, b_KBT = sb("KBT", [128, 4, RING * 128], BF16)
    VB, b_VB = sb("VB", [128, RING, 512], BF16)
    kvo = [(xtok[:, 0:512], b_xtok), (xtok[:, 512:1024], b_xtok)]
    tmpA, b_tmpA = sb("tmpA", [128, 512])
    tmpB, b_tmpB = sb("tmpB", [128, 512])
    tmpC, b_tmpC = sb("tmpC", [128, 512])
    tmpD, b_tmpD = sb("tmpD", [128, 512])
    pT = [sb("pT%d" % i, [128, 512], BF16) for i in range(3)]
    eT = [sb("eT%d" % i, [128, 512], BF16) for i in range(3)]
    thtab = [sb("thtab%d" % i, [128, TW], BF16) for i in range(1)]
    Sst, b_Sst = sb("Sst", [128, 2, 128])
    Sbf, b_Sbf = sb("Sbf", [128, 2, 128], BF16)
    op("dve", "memset", [], [b_Sst], Sst[:], 0.0)
    op("dve", "memset", [], [b_Sbf], Sbf[:], 0.0)
    qz, b_qz = sb("qz", [128, 4, 128], BF16)
    qd, b_qd = sb("qd", [128, 4, 128], BF16)
    osb, b_osb = tmpC, b_tmpC
    obf, b_obf = eT[0]
    osq, b_osq = eT[1]

    lam_abs, b_lam_abs = sb("lam_abs", [128, 64])
    th, b_th = sb("th", [128, 64])
    thS, b_thS = sb("thS", [128, 64])
    CT, b_CT = sb("CT", [128, 64, 128], BF16)
    ST, b_ST = sb("ST", [128, 64, 128], BF16)
    LT1, b_LT1 = sb("LT1", [128, KC, 4, 128], BF16)
    LT2, b_LT2 = sb("LT2", [128, KC, 4, 128], BF16)
    C1, b_C1 = sb("C1", [128, 64, 16], BF16)
    C2, b_C2 = sb("C2", [128, 64, 16], BF16)
    cL, b_cL = sb("cL", [128, 64]); sL, b_sL = sb("sL", [128, 64])
    cE, b_cE = sb("cE", [128, 64]); sE, b_sE = sb("sE", [128, 64])
    gd, b_gd = sb("gd", [128, KC])
    Wc, b_Wc = sb("Wc", [128, 64])
    Zend, b_Zend = sb("Zend", [128, 64])
    op("dve", "memset", [], [b_Wc], Wc[:], 0.0)
    op("dve", "memset", [], [b_Zend], Zend[:], 0.0)
    setup_es = ExitStack()

    def sbt(name, shape, dt=F32):
        t = setup_es.enter_context(nc.sbuf_tensor(name, list(shape), dt))
        return t, Buf(name)

    lamT, b_lamT = sb("lamT", [64, 256])
    ld(lamT[:, 0:64], W["ssm_lam_re"], b_lamT); ld(lamT[:, 64:128], W["ssm_lam_re"], b_lamT)
    ld(lamT[:, 128:192], W["ssm_lam_im"], b_lamT); ld(lamT[:, 192:256], W["ssm_lam_im"], b_lamT)
    lre, b_lre = sb("lre", [128, 64]); lim, b_lim = sb("lim", [128, 64])
    for src0, dst, b_dst in ((0, lre, b_lre), (128, lim, b_lim)):
        pt, b_pt = next_ps()
        op("pe", "transpose", [b_lamT, b_ident], [b_pt], out=pt[:, 0:64], in_=lamT[:, src0:src0 + 128], identity=ident[0:64, 0:64])
        op("dve", "tensor_copy", [b_pt], [b_dst], out=dst[:], in_=pt[:, 0:64])
    dtt, b_dtt = sb("dtt", [128, 64])
    ld(dtt[:], W["ssm_log_step"].partition_broadcast(128), b_dtt)
    op("act", "activation", [b_dtt], [b_dtt], out=dtt[:], in_=dtt[:], func=AF.Exp)
    op("dve", "tensor_mul", [b_lim, b_dtt], [b_th], out=th[:], in0=lim[:], in1=dtt[:])
    op("dve", "tensor_scalar_mul", [b_th, b_sgn], [b_thS], out=thS[:], in0=th[:], scalar1=sgn[:, 0:1])
    rho, b_rho = sb("rho", [128, 64])
    op("dve", "tensor_mul", [b_lre, b_dtt], [b_rho], out=rho[:], in0=lre[:], in1=dtt[:])
    op("act", "activation", [b_rho], [b_lam_abs], out=lam_abs[:], in_=rho[:], func=AF.Exp)

    s5a, b_s5a = tmpA, b_tmpA
    s5b, b_s5b = tmpB, b_tmpB
    s5i, b_s5i = sb("s5i", [128, 512], mybir.dt.int32)

    def sin_of(dst_ap, ang_ap, n, b_dst, reads):
        shp = ang_ap.shape
        kb = s5b[:, 0:n] if len(shp) == 2 else s5b[:, 0:n].rearrange("p (a b) -> p a b", a=shp[1])
        ki = s5i[:, 0:n] if len(shp) == 2 else s5i[:, 0:n].rearrange("p (a b) -> p a b", a=shp[1])
        op("dve", "tensor_scalar_mul", reads, [b_s5b], out=kb, in0=ang_ap, scalar1=1.0 / TWO_PI)
        op("dve", "tensor_copy", [b_s5b], [b_s5i], out=ki, in_=kb)
        op("dve", "tensor_copy", [b_s5i], [b_s5b], out=kb, in_=ki)
        op("dve", "scalar_tensor_tensor", [b_s5b] + reads, [b_s5b], out=kb, in0=kb, scalar=-TWO_PI, in1=ang_ap,
           op0=ALU.mult, op1=ALU.add)
        op("dve", "tensor_scalar", [b_s5b], [b_s5b], out=kb, in0=kb, scalar1=-3.14159, scalar2=3.14159, op0=ALU.max, op1=ALU.min)
        op("act", "activation", [b_s5b], [b_dst], out=dst_ap, in_=kb, func=AF.Sin)

    for g0 in range(0, 64, 4):
        angv = s5a[:, 0:512].rearrange("p (a b) -> p a b", a=4)
        for (thsrc, b_thsrc, dst, b_dst, shift) in ((th, b_th, CT, b_CT, math.pi / 2), (thS, b_thS, ST, b_ST, 0.0)):
            op("dve", "tensor_tensor", [b_thsrc, b_tau], [b_s5a], out=angv, in0=bcast(thsrc, 64, g0, [(1, 4), (0, 128)]),
               in1=bcast(tau, 128, 0, [(0, 4), (1, 128)]), op=ALU.mult)
            if shift:
                op("dve", "tensor_scalar_add", [b_s5a], [b_s5a], out=angv, in0=angv, scalar1=shift)
            sin_of(dst[:, g0:g0 + 4, :], angv, 512, b_dst, [b_s5a])

    def cs_small(mult, cdst, b_c, sdst, b_s, neg_sin):
        a = s5a[:, 0:64]
        op("dve", "tensor_scalar", [b_th], [b_s5a], out=a, in0=th[:], scalar1=float(mult), scalar2=math.pi / 2,
           op0=ALU.mult, op1=ALU.add)
        sin_of(cdst[:], a, 64, b_c, [b_s5a])
        op("dve", "tensor_scalar_mul", [b_thS], [b_s5a], out=a, in0=thS[:], scalar1=(-float(mult) if neg_sin else float(mult)))
        sin_of(sdst[:], a, 64, b_s, [b_s5a])
    cs_small(128.0, cL, b_cL, sL, b_sL, True)
    cs_small(127.0, cE, b_cE, sE, b_sE, True)
    c1t, b_c1t = sb("c1t", [128, 64]); s1t, b_s1t = sb("s1t", [128, 64])
    cs_small(1.0, c1t, b_c1t, s1t, b_s1t, False)
    fre, b_fre = sb("fre", [128, 64]); fim, b_fim = sb("fim", [128, 64])
    abr, b_abr = sb("abr", [128, 64]); abi, b_abi = sb("abi", [128, 64]); den, b_den = sb("den", [128, 64])
    t64, b_t64 = sb("t64", [128, 64])
    op("dve", "tensor_mul", [b_lam_abs, b_c1t], [b_abr], out=abr[:], in0=lam_abs[:], in1=c1t[:])
    op("dve", "tensor_scalar_add", [b_abr], [b_abr], out=abr[:], in0=abr[:], scalar1=-1.0)
    op("dve", "tensor_mul", [b_lam_abs, b_s1t], [b_abi], out=abi[:], in0=lam_abs[:], in1=s1t[:])
    op("dve", "tensor_scalar_mul", [b_abi, b_sgn], [b_abi], out=abi[:], in0=abi[:], scalar1=sgn[:, 0:1])
    op("dve", "tensor_mul", [b_lre], [b_den], out=den[:], in0=lre[:], in1=lre[:])
    op("dve", "tensor_mul", [b_lim], [b_t64], out=t64[:], in0=lim[:], in1=lim[:])
    op("dve", "tensor_add", [b_den, b_t64], [b_den], out=den[:], in0=den[:], in1=t64[:])
    op("dve", "reciprocal", [b_den], [b_den], out=den[:], in_=den[:])
    op("dve", "tensor_mul", [b_abr, b_lre], [b_fre], out=fre[:], in0=abr[:], in1=lre[:])
    op("dve", "tensor_mul", [b_abi, b_lim], [b_t64], out=t64[:], in0=abi[:], in1=lim[:])
    op("dve", "tensor_add", [b_fre, b_t64], [b_fre], out=fre[:], in0=fre[:], in1=t64[:])
    op("dve", "tensor_mul", [b_fre, b_den], [b_fre], out=fre[:], in0=fre[:], in1=den[:])
    op("dve", "tensor_mul", [b_abi, b_lre], [b_fim], out=fim[:], in0=abi[:], in1=lre[:])
    op("dve", "tensor_mul", [b_abr, b_lim], [b_t64], out=t64[:], in0=abr[:], in1=lim[:])
    op("dve", "tensor_sub", [b_fim, b_t64], [b_fim], out=fim[:], in0=fim[:], in1=t64[:])
    op("dve", "tensor_mul", [b_fim, b_den], [b_fim], out=fim[:], in0=fim[:], in1=den[:])
    fiS, b_fiS = sb("fiS", [128, 64])
    op("dve", "tensor_scalar_mul", [b_fim, b_sgn], [b_fiS], out=fiS[:], in0=fim[:], scalar1=sgn[:, 0:1])
    bre = W["ssm_b_re"].rearrange("g n q -> n g q"); bim = W["ssm_b_im"].rearrange("g n q -> n g q")
    v3 = lambda t, c0: t[:, c0:c0 + 128].rearrange("p (a b) -> p a b", a=8)
    for kc in range(KC):
        B1k, B2k = v3(tmpC, 0), v3(tmpD, 0)
        FBak, FBbk, tFk = v3(tmpA, 0), v3(tmpB, 0), v3(tmpA, 128)
        gsl = slice(kc * 8, (kc + 1) * 8)
        ld(B1k[0:64], bre[:, gsl, :], b_tmpC); ld(B1k[64:128], bim[:, gsl, :], b_tmpC)
        ld(B2k[0:64], bim[:, gsl, :], b_tmpD); ld(B2k[64:128], bre[:, gsl, :], b_tmpD)
        frb = bcast(fre, 64, kc * 8, [(1, 8), (0, 16)]); fib = bcast(fiS, 64, kc * 8, [(1, 8), (0, 16)])
        op("dve", "tensor_tensor", [b_tmpC, b_fre], [b_tmpA], out=FBak, in0=B1k, in1=frb, op=ALU.mult)
        op("dve", "tensor_tensor", [b_tmpD, b_fiS], [b_tmpA], out=tFk, in0=B2k, in1=fib, op=ALU.mult)
        op("dve", "tensor_sub", [b_tmpA], [b_tmpA], out=FBak, in0=FBak, in1=tFk)
        op("dve", "tensor_tensor", [b_tmpD, b_fre], [b_tmpB], out=FBbk, in0=B2k, in1=frb, op=ALU.mult)
        op("dve", "tensor_tensor", [b_tmpC, b_fiS], [b_tmpA], out=tFk, in0=B1k, in1=fib, op=ALU.mult)
        op("dve", "tensor_add", [b_tmpB, b_tmpA], [b_tmpB], out=FBbk, in0=FBbk, in1=tFk)
        for (FBt, b_FB, LT, b_LT) in ((tmpA, b_tmpA, LT1, b_LT1), (tmpB, b_tmpB, LT2, b_LT2)):
            pt, b_pt = next_ps()
            op("pe", "transpose", [b_FB, b_ident], [b_pt], out=pt[:, 0:128], in_=FBt[:, 0:128], identity=ident[:])
            for gi in range(4):
                op("dve", "tensor_scalar_mul", [b_pt, b_rowmask], [b_LT], out=LT[:, kc, gi, :], in0=pt[:, 0:128],
                   scalar1=rowmask[:, gi:gi + 1])
    cre = W["ssm_c_re"].rearrange("(k a) p n -> k (a p) n", k=KC); cim = W["ssm_c_im"].rearrange("(k a) p n -> k (a p) n", k=KC)
    for kc in range(KC):
        ld(tmpC[:, 0:64], cre[kc], b_tmpC); ld(tmpC[:, 64:128], cim[kc], b_tmpC)
        for (Cd, b_Cd, first_im) in ((C1, b_C1, False), (C2, b_C2, True)):
            cst = tmpD
            if not first_im:
                op("dve", "tensor_copy", [b_tmpC], [b_tmpD], out=cst[:, 0:64], in_=tmpC[:, 0:64])
                op("dve", "tensor_scalar_mul", [b_tmpC], [b_tmpD], out=cst[:, 64:128], in0=tmpC[:, 64:128], scalar1=-1.0)
            else:
                op("dve", "tensor_scalar_mul", [b_tmpC], [b_tmpD], out=cst[:, 0:64], in0=tmpC[:, 64:128], scalar1=-1.0)
                op("dve", "tensor_copy", [b_tmpC], [b_tmpD], out=cst[:, 64:128], in_=tmpC[:, 0:64])
            pt, b_pt = next_ps()
            op("pe", "transpose", [b_tmpD, b_ident], [b_pt], out=pt[:, 0:128], in_=cst[:, 0:128], identity=ident[:])
            op("dve", "tensor_copy", [b_pt], [b_Cd], out=Cd[:, kc * 8:(kc + 1) * 8, :].rearrange("p a b -> p (a b)"), in_=pt[:, 0:128])
    op("dve", "tensor_mul", [b_gvec, b_dvec], [b_gd], out=gd[:], in0=gvec[:, 2, :], in1=dvec[:])

    def evac(i, pt, b_pt, dst_ap, b_dst, n, scale=None, func=None, rows=128):
        if func is not None:
            kw = {"scale": scale} if scale is not None else {}
            op("act", "activation", [b_pt], [b_dst], out=dst_ap, in_=pt[0:rows, 0:n], func=func, **kw)
        elif scale is not None:
            op("act", "activation", [b_pt], [b_dst], out=dst_ap, in_=pt[0:rows, 0:n], func=AF.Copy, scale=scale)
        elif i % 2 == 0:
            op("act", "copy", [b_pt], [b_dst], out=dst_ap, in_=pt[0:rows, 0:n])
        else:
            op("dve", "tensor_copy", [b_pt], [b_dst], out=dst_ap, in_=pt[0:rows, 0:n])

    def rmsnorm(gi, ntok, dst=None, b_dst=None):
        dst = hT if dst is None else dst
        b_dst = b_hT if b_dst is None else b_dst
        op("act", "activation", [b_xT], [b_sq], out=sq[:, :, 0:ntok], in_=xT[:, :, 0:ntok], func=AF.Square)
        pt, b_pt = next_ps()
        for kc in range(KC):
            op("pe", "matmul", [b_ones_d, b_sq], [b_pt], pt[:, 0:ntok], lhsT=ones_d[:], rhs=sq[:, kc, 0:ntok],
               start=(kc == 0), stop=(kc == KC - 1))
        op("dve", "tensor_scalar_add", [b_pt], [b_rstd], out=rstd[:, 0:ntok], in0=pt[:, 0:ntok], scalar1=EPS)
        op("act", "activation", [b_rstd], [b_rstd], out=rstd[:, 0:ntok], in_=rstd[:, 0:ntok], func=AF.Sqrt)
        op("dve", "reciprocal", [b_rstd], [b_rstd], out=rstd[:, 0:ntok], in_=rstd[:, 0:ntok])
        for kc in range(KC):
            op("dve", "scalar_tensor_tensor", [b_xT, b_rstd, b_gvec], [b_dst], out=dst[:, kc, 0:ntok],
               in0=xT[:, kc, 0:ntok], scalar=gvec[:, gi, kc:kc + 1], in1=rstd[:, 0:ntok], op0=ALU.mult, op1=ALU.mult)

    def ffn(l, ntok):
        rmsnorm(1 + 2 * l, ntok)
        pool_in = panels + [(mixT[:].rearrange("p a b -> p (a b)"), b_mixT)]
        pool_out = pool_in + [(hT[:].rearrange("p a b -> p (a b)"), b_hT)]
        for p in range(FC // 2):
            view, b_pan = load_panel(wb_ffi[l, p], KC, 512, b_wb_ffi, pool_in)
            for j in range(2):
                pg, b_pg = fm_chunk(view, b_pan, KC, j * 128, hT, b_hT, ntok)
                pu, b_pu = fm_chunk(view, b_pan, KC, 256 + j * 128, hT, b_hT, ntok)
                op("act", "activation", [b_pg], [b_tmpA], out=tmpA[:, 0:ntok], in_=pg[:, 0:ntok], func=AF.Silu)
                op("dve", "tensor_tensor", [b_tmpA, b_pu], [b_act], out=act[:, 2 * p + j, 0:ntok], in0=tmpA[:, 0:ntok],
                   in1=pu[:, 0:ntok], op=ALU.mult)
        for oc in range(KC):
            view, b_pan = load_panel(wb_ffo[l, oc], FC, 128, b_wb_ffo, pool_out)
            pt, b_pt = fm_chunk(view, b_pan, FC, 0, act, b_act, ntok)
            op("dve", "tensor_tensor", [b_pt, b_xT], [b_xT], out=xT[:, oc, 0:ntok], in0=pt[:, 0:ntok],
               in1=xT[:, oc, 0:ntok], op=ALU.add)

    def load_x(src_ap, ntok):
        ntile = (ntok + 127) // 128
        for t in range(ntile):
            rows = min(128, ntok - t * 128)
            ld(xtok[0:rows, :], src_ap[t * 128:t * 128 + rows, :], b_xtok)
            for half in range(2):
                pt, b_pt = next_ps()
                for k4 in range(4):
                    kc = half * 4 + k4
                    op("pe", "transpose", [b_xtok, b_ident], [b_pt], out=pt[:, k4 * 128:k4 * 128 + rows],
                       in_=xtok[0:rows, kc * 128:(kc + 1) * 128], identity=ident[0:rows, 0:rows])
                src = pt[:, 0:512].rearrange("p (a b) -> p a b", a=4)[:, :, 0:rows]
                dst = xT[:, half * 4:half * 4 + 4, t * 128:t * 128 + rows]
                if half == 0:
                    op("act", "copy", [b_pt], [b_xT], out=dst, in_=src)
                else:
                    op("dve", "tensor_copy", [b_pt], [b_xT], out=dst, in_=src)

    def final_out(dst_ap, ntok):
        rmsnorm(4, ntok, dst=xT, b_dst=b_xT)
        ntile = (ntok + 127) // 128
        for t in range(ntile):
            rows = min(128, ntok - t * 128)
            for half in range(2):
                pt, b_pt = next_ps()
                for k4 in range(4):
                    kc = half * 4 + k4
                    op("pe", "transpose", [b_xT, b_ident], [b_pt], out=pt[0:rows, k4 * 128:(k4 + 1) * 128],
                       in_=xT[:, kc, t * 128:t * 128 + rows], identity=ident[:])
                evac(half, pt, b_pt, xtok[0:rows, half * 512:(half + 1) * 512], b_xtok, 512, rows=rows)
            st(dst_ap[t * 128:t * 128 + rows, :], xtok[0:rows, :], b_xtok)

    S5BUFS = []
    if blk_lo > 0:
        ld(Sst[:], i_Sst, b_Sst)
        op("dve", "tensor_copy", [b_Sst], [b_Sbf], out=Sbf[:], in_=Sst[:])
        ld(Wc[:], i_Wc, b_Wc); ld(Zend[:], i_Zend, b_Zend)
        ld(KBT[:], i_KBT, b_KBT); ld(VB[:], i_VB, b_VB)
    for blk in range(blk_lo, nblk_run):
        t0 = blk * 512
        load_x(xp[t0:t0 + 512, :], 512)
        rmsnorm(0, 512)
        last_kv = blk >= KV_FROM
        for pi in range(6):
            view, b_pan = load_panel(wb_in[pi], KC, 512, b_wb_in)
            if pi == 0:
                for j in range(2):
                    pt, b_pt = fm_chunk(view, b_pan, KC, j * 128, hT, b_hT, 512)
                    evac(j, pt, b_pt, qaT[:, j, :], b_qaT, 512)
                for j in range(2):
                    pt, b_pt = fm_chunk(view, b_pan, KC, 256 + j * 128, hT, b_hT, 512)
                    evac(j, pt, b_pt, kaT[:, j, :], b_kaT, 512, scale=0.125)
                for t in range(4):
                    pt, b_pt = tm_tile(view, b_pan, KC, 256, 256, hT, b_hT, t * 128, 128)
                    op("dve", "tensor_tensor", [b_pt, b_kdec], [b_katok], out=katok[:, t, :], in0=pt[:, 0:256], in1=kdec[:], op=ALU.mult)
            elif pi == 1:
                for t in range(4):
                    pt, b_pt = tm_tile(view, b_pan, KC, 0, 512, hT, b_hT, t * 128, 128)
                    evac(t, pt, b_pt, vatok[:, t, :], b_vatok, 512)
            elif pi == 2:
                for j in range(4):
                    pt, b_pt = fm_chunk(view, b_pan, KC, j * 128, hT, b_hT, 512)
                    evac(j, pt, b_pt, sgT[:, j, :], b_sgT, 512, func=AF.Silu)
            elif pi == 3:
                for j in range(4):
                    pt, b_pt = fm_chunk(view, b_pan, KC, j * 128, hT, b_hT, 512)
                    evac(j, pt, b_pt, qbT[:, j, :], b_qbT, 512)
            elif pi == 4:
                rs0 = ((blk * 4) % RING) * 128
                for j in range(4):
                    pt, b_pt = fm_chunk(view, b_pan, KC, j * 128, hT, b_hT, 512)
                    evac(j, pt, b_pt, KBT[:, j, rs0:rs0 + 512], b_KBT, 512)
                if last_kv and os.environ.get("NOKVK") is None:
                    for t in range(4):
                        pt, b_pt = tm_tile(view, b_pan, KC, 0, 512, hT, b_hT, t * 128, 128)
                        ko, b_ko = (tmpC, b_tmpC) if t % 2 == 0 else (tmpD, b_tmpD)
                        evac(t, pt, b_pt, ko[:], b_ko, 512)
                        r0 = (blk - KV_FROM) * 512 + t * 128
                        st(o_kp[r0:r0 + 128, :], ko[:], b_ko)
            else:
                for t in range(4):
                    pt, b_pt = tm_tile(view, b_pan, KC, 0, 512, hT, b_hT, t * 128, 128)
                    slot = (blk * 4 + t) % RING
                    if last_kv and os.environ.get("NOKVV") is None:
                        ko, b_ko = (tmpC, b_tmpC) if t % 2 == 0 else (tmpD, b_tmpD)
                        evac(t, pt, b_pt, ko[:], b_ko, 512)
                        op("pool", "tensor_copy", [b_ko], [b_VB], out=VB[:, slot, :], in_=ko[:])
                        r0 = (blk - KV_FROM) * 512 + t * 128
                        st(o_vp[r0:r0 + 128, :], ko[:], b_ko)
                    else:
                        evac(t, pt, b_pt, VB[:, slot, :], b_VB, 512)
        for c in range(4 if PH >= 2 else 0):
            cs = c * 128
            for h in range(4):
                pr = h // 2
                op("dve", "tensor_scalar_mul", [b_qaT, b_hmask], [b_qz], out=qz[:, h, :], in0=qaT[:, pr, cs:cs + 128], scalar1=hmask[:, (h % 2):(h % 2) + 1])
                op("dve", "tensor_tensor", [b_qz, b_qdec], [b_qd], out=qd[:, h, :], in0=qz[:, h, :], in1=qdec[:, pr, :], op=ALU.mult)
            ps_s, b_ps_s = next_ps()
            for h in range(4):
                pr = h // 2
                op("pe", "matmul", [b_kaT, b_qz], [b_ps_s], ps_s[:, h * 128:(h + 1) * 128], lhsT=kaT[:, pr, cs:cs + 128],
                   rhs=qz[:, h, :], start=True, stop=True)
            pTt, b_pTt = pT[c % 3]
            op("dve", "tensor_tensor", [b_ps_s, b_decT], [b_pTt], out=pTt[:], in0=ps_s[:], in1=decT[:].rearrange("p a b -> p (a b)"), op=ALU.mult)
            if SUB < 2:
                continue
            ps_o, b_ps_o = next_ps()
            for h in range(4):
                pr = h // 2
                op("pe", "matmul", [b_vatok, b_pTt], [b_ps_o], ps_o[:, h * 128:(h + 1) * 128], lhsT=vatok[:, c, h * 128:(h + 1) * 128],
                   rhs=pTt[:, h * 128:(h + 1) * 128], start=True, stop=False)
                op("pe", "matmul", [b_Sbf, b_qd], [b_ps_o], ps_o[:, h * 128:(h + 1) * 128], lhsT=Sbf[:, pr, :],
                   rhs=qd[:, h, :], start=False, stop=True)
            if SUB < 3:
                continue
            ps_d, b_ps_d = next_ps()
            for h in range(4):
                pr = h // 2
                op("pe", "matmul", [b_katok, b_vatok], [b_ps_d], ps_d[:, h * 128:(h + 1) * 128], lhsT=katok[:, c, pr * 128:(pr + 1) * 128],
                   rhs=vatok[:, c, h * 128:(h + 1) * 128], start=True, stop=True)
            for h in range(4):
                hp, pr = (h % 2) * 64, h // 2
                op("dve", "scalar_tensor_tensor", [b_Sst, b_ps_d, b_Sbf, b_ps_o], [b_Sst], out=Sst[hp:hp + 64, pr, :], in0=Sst[hp:hp + 64, pr, :],
                   scalar=float(GAM[h] ** 128), in1=ps_d[hp:hp + 64, h * 128:(h + 1) * 128], op0=ALU.mult, op1=ALU.add)
            op("dve", "tensor_copy", [b_Sst], [b_Sbf], out=Sbf[:], in_=Sst[:])
            if SUB < 4:
                continue
            op("act", "copy", [b_ps_o], [b_osb], out=osb[:], in_=ps_o[:])
            op("dve", "tensor_copy", [b_osb], [b_obf], out=obf[:], in_=osb[:])
            ps_m, b_ps_m = next_ps()
            op("pe", "matmul", [b_ones_g, b_obf], [b_ps_m], ps_m[:], lhsT=ones_g[:], rhs=obf[:], start=True, stop=True)
            op("dve", "tensor_tensor", [b_osb, b_ps_m], [b_osb], out=osb[:], in0=osb[:], in1=ps_m[:], op=ALU.subtract)
            op("act", "activation", [b_osb], [b_osq], out=osq[:], in_=osb[:], func=AF.Square)
            ps_q, b_ps_q = next_ps()
            op("pe", "matmul", [b_ones_g, b_osq], [b_ps_q], ps_q[:], lhsT=ones_g[:], rhs=osq[:], start=True, stop=True)
            if SUB < 5:
                continue
            op("dve", "tensor_scalar_add", [b_ps_q], [b_tmpA], out=tmpA[:], in0=ps_q[:], scalar1=EPS)
            op("act", "activation", [b_tmpA], [b_tmpA], out=tmpA[:], in_=tmpA[:], func=AF.Sqrt)
            op("dve", "reciprocal", [b_tmpA], [b_tmpA], out=tmpA[:], in_=tmpA[:])
            op("dve", "tensor_tensor", [b_osb, b_tmpA], [b_osb], out=osb[:], in0=osb[:], in1=tmpA[:], op=ALU.mult)
            if SUB < 6:
                continue
            for h in range(4):
                op("dve", "scalar_tensor_tensor", [b_osb, b_gn, b_sgT], [b_mixT], out=mixT[:, h, cs:cs + 128], in0=osb[:, h * 128:(h + 1) * 128],
                   scalar=gn[:, h:h + 1], in1=sgT[:, h, cs:cs + 128], op0=ALU.mult, op1=ALU.mult)
        kt_hi = blk * 4 + 3
        kt_lo = max(0, blk * 4 - 16)
        for h in range(8 if PH >= 3 else 0):
            hp, pr = (h % 2) * 64, h // 2
            tht, b_tht = thtab[0]
            ld(tht[:], C["swa_tab"][h], b_tht)
            ps_o, b_ps_o = PS_A if h % 2 == 0 else PS_B
            ps_dn, b_ps_dn = PS_C
            nk = kt_hi - kt_lo + 1
            kts = list(range(kt_lo, kt_hi + 1))

            def att_a(i):
                kt = kts[i]
                o = t0 - kt * 128
                slot = kt % RING
                ps_s, b_ps_s = next_ps()
                op("pe", "matmul", [b_KBT, b_qbT], [b_ps_s], ps_s[:], lhsT=KBT[hp:hp + 64, pr, slot * 128:(slot + 1) * 128],
                   rhs=qbT[hp:hp + 64, pr, :], start=True, stop=True)
                et, b_et = eT[i % 3]
                op("act", "activation", [b_ps_s], [b_et], out=et[:], in_=ps_s[:], func=AF.Exp, scale=0.125)
                pt_, b_pt_ = pT[i % 3]
                op("dve", "tensor_tensor", [b_et, b_tht], [b_pt_], out=pt_[:], in0=et[:], in1=tht[:, o + 384:o + 384 + 512], op=ALU.mult)

            def att_b(i):
                slot = kts[i] % RING
                pt_, b_pt_ = pT[i % 3]
                op("pe", "matmul", [b_VB, b_pt_], [b_ps_o], ps_o[:], lhsT=VB[:, slot, pr * 128:(pr + 1) * 128], rhs=pt_[:],
                   start=(i == 0), stop=(i == nk - 1))
                op("pe", "matmul", [b_ones_1, b_pt_], [b_ps_dn], ps_dn[:], lhsT=ones_1[:], rhs=pt_[:], start=(i == 0), stop=(i == nk - 1))
            for i in range(nk + 2):
                if i < nk:
                    att_a(i)
                if i >= 2:
                    att_b(i - 2)
            op("dve", "reciprocal", [b_ps_dn], [b_tmpB], out=tmpB[hp:hp + 64, :], in_=ps_dn[hp:hp + 64, :])
            op("dve", "tensor_tensor", [b_ps_o, b_tmpB], [b_mixT], out=mixT[hp:hp + 64, 4 + pr, :], in0=ps_o[hp:hp + 64, :], in1=tmpB[hp:hp + 64, :], op=ALU.mult)
        for p in range(2 if PH >= 4 else 0):
            view, b_pan = load_panel(wb_out[p], KC, 512, b_wb_out)
            for j in range(4):
                oc = p * 4 + j
                pt, b_pt = fm_chunk(view, b_pan, KC, j * 128, mixT, b_mixT, 512)
                op("dve", "tensor_tensor", [b_pt, b_xT], [b_xT], out=xT[:, oc, :], in0=pt[:], in1=xT[:, oc, :], op=ALU.add)
        if PH >= 4:
            ffn(0, 512)
        rmsnorm(2, 512)
        for c in range(4 if PH >= 5 else 0):
            cs = c * 128
            ps_y0, b_ps_y0 = PS_A
            ps_y1, b_ps_y1 = PS_B
            actf = act[:].rearrange("p a b -> p (a b)").bitcast(F32)
            if c == 0 and blk == blk_lo:
                s5bufs = [[(actf[:, (k * 3 + j) * 512:(k * 3 + j + 1) * 512], Buf("s5t%d_%d" % (k, j))) for j in range(3)] for k in range(3)]
                S5BUFS.append(s5bufs)
            s5bufs = S5BUFS[0]

            def stage1(qd_i):
                g0 = qd_i * 4
                kc, half = g0 // 8, (g0 % 8) // 4
                hp = half * 64
                (t1, b_t1), (t2, b_t2), _ = s5bufs[qd_i % 3]
                pd1, b_pd1 = next_ps()
                pd2, b_pd2 = next_ps()
                for gi in range(4):
                    op("pe", "matmul", [b_LT1, b_hT], [b_pd1], pd1[:, gi * 128:(gi + 1) * 128], lhsT=LT1[hp:hp + 64, kc, gi, :],
                       rhs=hT[hp:hp + 64, kc, cs:cs + 128], start=True, stop=True)
                for gi in range(4):
                    op("pe", "matmul", [b_LT2, b_hT], [b_pd2], pd2[:, gi * 128:(gi + 1) * 128], lhsT=LT2[hp:hp + 64, kc, gi, :],
                       rhs=hT[hp:hp + 64, kc, cs:cs + 128], start=True, stop=True)
                ctq = CT[:, g0:g0 + 4, :].rearrange("p a b -> p (a b)")
                stq = ST[:, g0:g0 + 4, :].rearrange("p a b -> p (a b)")
                op("dve", "tensor_tensor", [b_pd1, b_CT], [b_t1], out=t1, in0=pd1[:], in1=ctq, op=ALU.mult)
                op("dve", "tensor_tensor", [b_pd2, b_ST], [b_t2], out=t2, in0=pd2[:], in1=stq, op=ALU.mult)
                op("pool", "tensor_tensor", [b_t1, b_t2], [b_t1], out=t1, in0=t1, in1=t2, op=ALU.add)

            def stage2(qd_i):
                g0 = qd_i * 4
                (zd, b_zd), _, (zz, b_zz) = s5bufs[qd_i % 3]
                ctq = CT[:, g0:g0 + 4, :].rearrange("p a b -> p (a b)")
                stq = ST[:, g0:g0 + 4, :].rearrange("p a b -> p (a b)")
                for gi in range(4):
                    g = g0 + gi
                    op("dve", "tensor_tensor_scan", [b_zd, b_lam_abs, b_Wc], [b_zz], out=zz[:, gi * 128:(gi + 1) * 128],
                       data0=lam_abs[:, g:g + 1].to_broadcast([128, 128]), data1=zd[:, gi * 128:(gi + 1) * 128],
                       initial=Wc[:, g:g + 1], op0=ALU.mult, op1=ALU.add)
                at, b_at = eT[qd_i % 3]
                bt, b_bt = pT[qd_i % 3]
                op("dve", "tensor_tensor", [b_zz, b_CT], [b_at], out=at[:], in0=zz, in1=ctq, op=ALU.mult)
                op("pool", "tensor_tensor", [b_zz, b_ST], [b_bt], out=bt[:], in0=zz, in1=stq, op=ALU.mult)
                op("pool", "tensor_copy", [b_zz], [b_Zend], out=Zend[:, g0:g0 + 4],
                   in_=zz[:, 0:512].rearrange("p (a b) -> p a b", a=4)[:, :, 127])

            def stage2b(qd_i):
                g0 = qd_i * 4
                at, b_at = eT[qd_i % 3]
                bt, b_bt = pT[qd_i % 3]
                for gi in range(4):
                    g = g0 + gi
                    py, b_py = (ps_y0, b_ps_y0) if g < 32 else (ps_y1, b_ps_y1)
                    col = (g % 32) * 16
                    op("pe", "matmul", [b_at, b_C1], [b_py], py[:, col:col + 16], lhsT=at[:, gi * 128:(gi + 1) * 128], rhs=C1[:, g, :], start=True, stop=False)
                    op("pe", "matmul", [b_bt, b_C2], [b_py], py[:, col:col + 16], lhsT=bt[:, gi * 128:(gi + 1) * 128], rhs=C2[:, g, :], start=False, stop=True)
            for qd_i in range(19):
                if qd_i < 16:
                    stage1(qd_i)
                if 2 <= qd_i < 18:
                    stage2(qd_i - 2)
                if qd_i >= 3:
                    stage2b(qd_i - 3)
            pw, b_pw = next_ps()
            op("pe", "matmul", [b_swapm, b_Zend], [b_pw], pw[:, 0:64], lhsT=swapm[:], rhs=Zend[:], start=True, stop=True)
            op("dve", "tensor_tensor", [b_pw, b_sL], [b_t64], out=t64[:], in0=pw[:, 0:64], in1=sL[:], op=ALU.mult)
            op("dve", "tensor_tensor", [b_Zend, b_cL], [b_Wc], out=Wc[:], in0=Zend[:], in1=cL[:], op=ALU.mult)
            op("dve", "tensor_add", [b_Wc, b_t64], [b_Wc], out=Wc[:], in0=Wc[:], in1=t64[:])
            ysb, b_ysb = xtok[:, 0:1024].rearrange("p (a b) -> p a b", a=2), b_xtok
            op("act", "copy", [b_ps_y0], [b_ysb], out=ysb[:, 0, :], in_=ps_y0[:])
            op("dve", "tensor_copy", [b_ps_y1], [b_ysb], out=ysb[:, 1, :], in_=ps_y1[:])
            for half in range(2):
                pt, b_pt = next_ps()
                for k4 in range(4):
                    op("pe", "transpose", [b_ysb, b_ident], [b_pt], out=pt[:, k4 * 128:(k4 + 1) * 128],
                       in_=ysb[:, half, k4 * 128:(k4 + 1) * 128], identity=ident[:])
                for k4 in range(4):
                    kc = half * 4 + k4
                    op("dve", "scalar_tensor_tensor", [b_xT, b_gd, b_rstd], [b_tmpA], out=tmpA[:, k4 * 128:(k4 + 1) * 128], in0=xT[:, kc, cs:cs + 128],
                       scalar=gd[:, kc:kc + 1], in1=rstd[:, cs:cs + 128], op0=ALU.mult, op1=ALU.mult)
                op("dve", "tensor_tensor", [b_tmpA, b_pt], [b_tmpA], out=tmpA[:], in0=tmpA[:], in1=pt[:], op=ALU.add)
                op("act", "activation", [b_tmpA], [b_tmpB], out=tmpB[:], in_=tmpA[:], func=AF.Square)
                op("dve", "tensor_scalar", [b_tmpB], [b_tmpB], out=tmpB[:], in0=tmpB[:], scalar1=0.044715, scalar2=1.0, op0=ALU.mult, op1=ALU.add)
                op("dve", "tensor_tensor", [b_tmpB, b_tmpA], [b_tmpB], out=tmpB[:], in0=tmpB[:], in1=tmpA[:], op=ALU.mult)
                op("act", "activation", [b_tmpB], [b_tmpB], out=tmpB[:], in_=tmpB[:], func=AF.Sigmoid, scale=1.5957691216)
                for k4 in range(4):
                    kc = half * 4 + k4
                    op("dve", "tensor_tensor", [b_tmpA, b_tmpB], [b_mixT], out=mixT[:, kc, cs:cs + 128], in0=tmpA[:, k4 * 128:(k4 + 1) * 128],
                       in1=tmpB[:, k4 * 128:(k4 + 1) * 128], op=ALU.mult)
        for p in range(4 if PH >= 6 else 0):
            view, b_pan = load_panel(wb_glu[p], KC, 512, b_wb_glu)
            for j in range(2):
                oc = 2 * p + j
                pv, b_pv = fm_chunk(view, b_pan, KC, j * 128, mixT, b_mixT, 512)
                pg, b_pg = fm_chunk(view, b_pan, KC, 256 + j * 128, mixT, b_mixT, 512)
                op("act", "activation", [b_pg], [b_tmpA], out=tmpA[:], in_=pg[:], func=AF.Sigmoid)
                op("dve", "tensor_tensor", [b_tmpA, b_pv], [b_tmpA], out=tmpA[:], in0=tmpA[:], in1=pv[:], op=ALU.mult)
                op("dve", "tensor_tensor", [b_tmpA, b_xT], [b_xT], out=xT[:, oc, :], in0=tmpA[:], in1=xT[:, oc, :], op=ALU.add)
        if PH >= 6:
            ffn(1, 512)
        final_out(o_yp[t0:t0 + 512, :], 512)

    st(o_retp, Sst[:], b_Sst)
    if len(LAUNCH_RANGES) > 1:
        st(o_Wc, Wc[:], b_Wc); st(o_Zend, Zend[:], b_Zend)
        st(o_KBT, KBT[:], b_KBT); st(o_VB, VB[:], b_VB)
    pw, b_pw = next_ps()
    op("pe", "matmul", [b_swapm, b_Zend], [b_pw], pw[:, 0:64], lhsT=swapm[:], rhs=Zend[:], start=True, stop=True)
    op("dve", "tensor_tensor", [b_pw, b_sE], [b_t64], out=t64[:], in0=pw[:, 0:64], in1=sE[:], op=ALU.mult)
    xfin, b_xfin = sb("xfin", [128, 64])
    op("dve", "tensor_tensor", [b_Zend, b_cE], [b_xfin], out=xfin[:], in0=Zend[:], in1=cE[:], op=ALU.mult)
    op("dve", "tensor_add", [b_xfin, b_t64], [b_xfin], out=xfin[:], in0=xfin[:], in1=t64[:])
    st(o_ssp, xfin[:], b_xfin)

    if RUN_SAMPLE:
        NS = TS
        AX = mybir.AxisListType.X
        bmask, b_bmask = sb("bmask", [32, 4]); ld(bmask[:], C["bmask"], b_bmask)
        eomask, b_eomask = sb("eomask", [64, 2]); ld(eomask[:], C["eomask"], b_eomask)
        kdecS = kdec[0:32, :]
        ld(kdecS, C["kdecS"], b_kdec)
        qdecS = qdec[:, 0, 0:64].rearrange("p (a b) -> p a b", a=2)
        ld(qdecS, C["qdecS"], b_qdec)
        decTS = decT[0:32, 0, :]
        ld(decTS.rearrange("p (a b) -> p a b", a=4), C["decTS"], b_decT)
        load_x(xs, NS)
        rmsnorm(0, NS)
        kvo0, kvo1 = kvo[0][0], kvo[1][0]
        for pi in range(6):
            view, b_pan = load_panel(wb_in[pi], KC, 512, b_wb_in)
            if pi == 0:
                for j in range(2):
                    pt, b_pt = fm_chunk(view, b_pan, KC, j * 128, hT, b_hT, NS)
                    evac(j, pt, b_pt, qaT[:, j, 0:NS], b_qaT, NS)
                for j in range(2):
                    pt, b_pt = fm_chunk(view, b_pan, KC, 256 + j * 128, hT, b_hT, NS)
                    evac(j, pt, b_pt, kaT[:, j, 0:NS], b_kaT, NS, scale=0.125)
                pt, b_pt = tm_tile(view, b_pan, KC, 256, 256, hT, b_hT, 0, NS)
                op("dve", "tensor_tensor", [b_pt, b_kdec], [b_katok], out=katok[0:NS, 0, :], in0=pt[0:NS, 0:256], in1=kdecS, op=ALU.mult)
            elif pi == 1:
                pt, b_pt = tm_tile(view, b_pan, KC, 0, 512, hT, b_hT, 0, NS)
                evac(0, pt, b_pt, vatok[0:NS, 0, :], b_vatok, 512, rows=NS)
            elif pi == 2:
                for j in range(4):
                    pt, b_pt = fm_chunk(view, b_pan, KC, j * 128, hT, b_hT, NS)
                    evac(j, pt, b_pt, sgT[:, j, 0:NS], b_sgT, NS, func=AF.Silu)
            elif pi == 3:
                for j in range(4):
                    pt, b_pt = fm_chunk(view, b_pan, KC, j * 128, hT, b_hT, NS)
                    evac(j, pt, b_pt, qbT[:, j, 0:NS], b_qbT, NS)
            elif pi == 4:
                for j in range(4):
                    pt, b_pt = fm_chunk(view, b_pan, KC, j * 128, hT, b_hT, NS)
                    evac(j, pt, b_pt, KBT[:, j, 2056:2056 + NS], b_KBT, NS)
                pt, b_pt = tm_tile(view, b_pan, KC, 0, 512, hT, b_hT, 0, NS)
                evac(1, pt, b_pt, kvo0[0:NS, :], b_xtok, 512, rows=NS)
                for b in range(SB):
                    st(o_ks[b, WB - 8:WB, :], kvo0[b * 8:(b + 1) * 8, :], b_xtok)
            else:
                pt, b_pt = tm_tile(view, b_pan, KC, 0, 512, hT, b_hT, 0, NS)
                evac(1, pt, b_pt, kvo1[0:NS, :], b_xtok, 512, rows=NS)
                for b in range(SB):
                    st(o_vs[b, WB - 8:WB, :], kvo1[b * 8:(b + 1) * 8, :], b_xtok)
                    S.dma("pool", lambda e, a=VB[0:8, 16 + b, :], c_=kvo1[b * 8:(b + 1) * 8, :]: e.dma_start(out=a, in_=c_),
                          b_VB, reads=[b_xtok], writes=[b_VB])
        def SstS(b):
            t_ = tmpB if b < 2 else tmpD
            return t_[:, (b % 2) * 256:(b % 2) * 256 + 256].rearrange("p (a c) -> p a c", a=2), (b_tmpB if b < 2 else b_tmpD)

        def SbfS(b):
            t_, bb_ = pT[1] if b < 2 else pT[2]
            return t_[:, (b % 2) * 256:(b % 2) * 256 + 256].rearrange("p (a c) -> p a c", a=2), bb_
        for b in range(SB):
            sv, sbuf_ = SstS(b)
            for h in range(4):
                hp, pr = (h % 2) * 64, h // 2
                ld(sv[hp:hp + 64, pr, :], st_ret[b, h], sbuf_)
        for b in range(SB):
            sv, sbuf_ = SstS(b); bv, bbuf_ = SbfS(b)
            op("dve", "tensor_copy", [sbuf_], [bbuf_], out=bv, in_=sv)
        for h in range(4):
            pr = h // 2
            op("dve", "tensor_scalar_mul", [b_qaT, b_hmask], [b_qz], out=qz[:, h, 0:NS], in0=qaT[:, pr, 0:NS], scalar1=hmask[:, (h % 2):(h % 2) + 1])
            op("dve", "tensor_tensor", [b_qz, b_qdec], [b_qd], out=qd[:, h, 0:NS], in0=qz[:, h, 0:NS], in1=qdecS[:, pr, :], op=ALU.mult)
        ps_s, b_ps_s = next_ps()
        for h in range(4):
            pr = h // 2
            op("pe", "matmul", [b_kaT, b_qz], [b_ps_s], ps_s[0:NS, h * NS:(h + 1) * NS], lhsT=kaT[:, pr, 0:NS], rhs=qz[:, h, 0:NS], start=True, stop=True)
        pTt, b_pTt = pT[0]
        op("dve", "tensor_tensor", [b_ps_s, b_decT], [b_pTt], out=pTt[0:NS, 0:4 * NS], in0=ps_s[0:NS, 0:4 * NS], in1=decTS, op=ALU.mult)
        ps_o, b_ps_o = next_ps()
        for h in range(4):
            pr = h // 2
            op("pe", "matmul", [b_vatok, b_pTt], [b_ps_o], ps_o[:, h * NS:(h + 1) * NS], lhsT=vatok[0:NS, 0, h * 128:(h + 1) * 128],
               rhs=pTt[0:NS, h * NS:(h + 1) * NS], start=True, stop=False)
            for b in range(SB):
                bv, bbuf_ = SbfS(b)
                op("pe", "matmul", [bbuf_, b_qd], [b_ps_o], ps_o[:, h * NS + 8 * b:h * NS + 8 * b + 8], lhsT=bv[:, pr, :],
                   rhs=qd[:, h, 8 * b:8 * b + 8], start=False, stop=(b == SB - 1))
        kdm, b_kdm = eT[2]
        for b in range(SB):
            sv, sbuf_ = SstS(b)
            op("dve", "tensor_scalar_mul", [b_katok, b_bmask], [b_kdm], out=kdm[0:NS, 0:256], in0=katok[0:NS, 0, :], scalar1=bmask[0:NS, b:b + 1])
            ps_d, b_ps_d = next_ps()
            for h in range(4):
                pr = h // 2
                op("pe", "matmul", [b_kdm, b_vatok], [b_ps_d], ps_d[:, h * 128:(h + 1) * 128], lhsT=kdm[0:NS, pr * 128:(pr + 1) * 128],
                   rhs=vatok[0:NS, 0, h * 128:(h + 1) * 128], start=True, stop=True)
            for h in range(4):
                hp, pr = (h % 2) * 64, h // 2
                op("dve", "scalar_tensor_tensor", [sbuf_, b_ps_d, b_ps_o], [sbuf_], out=sv[hp:hp + 64, pr, :], in0=sv[hp:hp + 64, pr, :],
                   scalar=float(GAM[h] ** 8), in1=ps_d[hp:hp + 64, h * 128:(h + 1) * 128], op0=ALU.mult, op1=ALU.add)
            st(o_rets[b], sv, sbuf_)
        W4 = 4 * NS
        op("act", "copy", [b_ps_o], [b_osb], out=osb[:, 0:W4], in_=ps_o[:, 0:W4])
        op("dve", "tensor_copy", [b_osb], [b_obf], out=obf[:, 0:W4], in_=osb[:, 0:W4])
        ps_m, b_ps_m = next_ps()
        op("pe", "matmul", [b_ones_g, b_obf], [b_ps_m], ps_m[:, 0:W4], lhsT=ones_g[:], rhs=obf[:, 0:W4], start=True, stop=True)
        op("dve", "tensor_tensor", [b_osb, b_ps_m], [b_osb], out=osb[:, 0:W4], in0=osb[:, 0:W4], in1=ps_m[:, 0:W4], op=ALU.subtract)
        op("act", "activation", [b_osb], [b_osq], out=osq[:, 0:W4], in_=osb[:, 0:W4], func=AF.Square)
        ps_q, b_ps_q = next_ps()
        op("pe", "matmul", [b_ones_g, b_osq], [b_ps_q], ps_q[:, 0:W4], lhsT=ones_g[:], rhs=osq[:, 0:W4], start=True, stop=True)
        op("dve", "tensor_scalar_add", [b_ps_q], [b_tmpA], out=tmpA[:, 0:W4], in0=ps_q[:, 0:W4], scalar1=EPS)
        op("act", "activation", [b_tmpA], [b_tmpA], out=tmpA[:, 0:W4], in_=tmpA[:, 0:W4], func=AF.Sqrt)
        op("dve", "reciprocal", [b_tmpA], [b_tmpA], out=tmpA[:, 0:W4], in_=tmpA[:, 0:W4])
        op("dve", "tensor_tensor", [b_osb, b_tmpA], [b_osb], out=osb[:, 0:W4], in0=osb[:, 0:W4], in1=tmpA[:, 0:W4], op=ALU.mult)
        for h in range(4):
            op("dve", "scalar_tensor_tensor", [b_osb, b_gn, b_sgT], [b_mixT], out=mixT[:, h, 0:NS], in0=osb[:, h * NS:(h + 1) * NS],
               scalar=gn[:, h:h + 1], in1=sgT[:, h, 0:NS], op0=ALU.mult, op1=ALU.mult)
        tht, b_tht = thtab[0]
        ld(tht[0:64, :], C["WS"], b_tht)
        ld(tmpD[0:64, :], C["hselB"], b_tmpD)
        qpad = act[:, 18, 0:256].rearrange("p (a c) -> p a c", a=4)
        PsT = act[:, 19:22, :].rearrange("p a b -> p (a b)")[:, 0:17 * 64].rearrange("p (k c) -> p k c", c=64)
        op("dve", "memset", [], [b_act], qpad, 0.0)
        o64, o2, dparts, dtot = tmpB[0:64, 0:64], tmpB[0:64, 64:192], tmpB[0:64, 192:197], tmpB[0:64, 200:201]
        for b in range(SB):
            for kt in range(16):
                ld(xtok[:, 0:512], st_k[b, kt * 128:(kt + 1) * 128, :], b_xtok)
                pt, b_pt = next_ps()
                for pr in range(4):
                    op("pe", "transpose", [b_xtok, b_ident], [b_pt], out=pt[:, pr * 128:(pr + 1) * 128], in_=xtok[:, pr * 128:(pr + 1) * 128], identity=ident[:])
                op("act" if kt % 2 else "dve", "copy" if kt % 2 else "tensor_copy", [b_pt], [b_KBT], out=KBT[:, :, kt * 128:(kt + 1) * 128],
                   in_=pt[:, 0:512].rearrange("p (a c) -> p a c", a=4))
            op("dve", "tensor_copy", [b_KBT], [b_KBT], out=KBT[:, :, 2048:2056], in_=KBT[:, :, 2056 + 8 * b:2056 + 8 * b + 8])
            S.dma("pool", lambda e, a=VB[:, 0:16, :], c_=st_v[b].rearrange("(t p) f -> p t f", p=128): e.dma_start(out=a, in_=c_),
                  b_VB, writes=[b_VB])
            for pr in range(4):
                op("dve", "tensor_copy", [b_qbT], [b_act], out=qpad[0:64, pr, (2 * pr) * 8:(2 * pr) * 8 + 8], in_=qbT[0:64, pr, 8 * b:8 * b + 8])
                op("dve", "tensor_copy", [b_qbT], [b_act], out=qpad[64:128, pr, (2 * pr + 1) * 8:(2 * pr + 1) * 8 + 8], in_=qbT[64:128, pr, 8 * b:8 * b + 8])
            for grp in range(5):
                k0 = grp * 512
                kw = 512 if grp < 4 else 8
                ps_sc, b_ps_sc = next_ps()
                for pr in range(4):
                    op("pe", "matmul", [b_act, b_KBT], [b_ps_sc], ps_sc[0:64, 0:kw], lhsT=qpad[:, pr, :], rhs=KBT[:, pr, k0:k0 + kw],
                       start=(pr == 0), stop=(pr == 3))
                op("act", "activation", [b_ps_sc], [b_tmpA], out=tmpA[0:64, 0:kw], in_=ps_sc[0:64, 0:kw], func=AF.Exp, scale=0.125)
                op("dve", "tensor_tensor", [b_tmpA, b_tht], [b_tmpA], out=tmpA[0:64, 0:kw], in0=tmpA[0:64, 0:kw], in1=tht[0:64, k0:k0 + kw], op=ALU.mult)
                op("dve", "reduce_sum", [b_tmpA], [b_tmpB], out=dparts[:, grp:grp + 1], in_=tmpA[0:64, 0:kw], axis=AX)
                pt, b_pt = next_ps()
                if grp < 4:
                    for t4 in range(4):
                        op("pe", "transpose", [b_tmpA, b_ident], [b_pt], out=pt[:, t4 * 64:(t4 + 1) * 64], in_=tmpA[0:64, t4 * 128:(t4 + 1) * 128], identity=ident[0:64, 0:64])
                    op("act", "copy", [b_pt], [b_act], out=PsT[:, grp * 4:grp * 4 + 4, :], in_=pt[:, 0:256].rearrange("p (a c) -> p a c", a=4))
                else:
                    op("pe", "transpose", [b_tmpA, b_ident], [b_pt], out=pt[0:8, 0:64], in_=tmpA[0:64, 0:8], identity=ident[0:64, 0:64])
                    op("act", "copy", [b_pt], [b_act], out=PsT[0:8, 16, :], in_=pt[0:8, 0:64])
            ps_pv, b_ps_pv = next_ps()
            for kt in range(16):
                op("pe", "matmul", [b_act, b_VB], [b_ps_pv], ps_pv[0:64, :], lhsT=PsT[:, kt, :], rhs=VB[:, kt, :], start=(kt == 0), stop=False)
            op("pe", "matmul", [b_act, b_VB], [b_ps_pv], ps_pv[0:64, :], lhsT=PsT[0:8, 16, :], rhs=VB[0:8, 16 + b, :], start=False, stop=True)
            op("dve", "reduce_sum", [b_tmpB], [b_tmpB], out=dtot, in_=dparts, axis=AX)
            op("dve", "reciprocal", [b_tmpB], [b_tmpB], out=dtot, in_=dtot)
            op("dve", "tensor_tensor", [b_ps_pv, b_tmpD], [b_tmpA], out=tmpA[0:64, :], in0=ps_pv[0:64, :], in1=tmpD[0:64, :], op=ALU.mult)
            op("dve", "tensor_reduce", [b_tmpA], [b_tmpB], out=o64, in_=tmpA[0:64, :].rearrange("p (h d) -> p d h", h=8), axis=AX, op=ALU.add)
            for e2 in range(2):
                op("dve", "tensor_scalar", [b_tmpB, b_eomask], [b_tmpB], out=o2[:, e2 * 64:(e2 + 1) * 64], in0=o64, scalar1=dtot, scalar2=eomask[:, e2:e2 + 1],
                   op0=ALU.mult, op1=ALU.mult)
            pt, b_pt = next_ps()
            op("pe", "transpose", [b_tmpB, b_ident], [b_pt], out=pt[:, 0:64], in_=o2, identity=ident[0:64, 0:64])
            for pr in range(4):
                op("dve", "tensor_copy", [b_pt], [b_mixT], out=mixT[0:64, 4 + pr, 8 * b:8 * b + 8], in_=pt[0:64, (2 * pr) * 8:(2 * pr) * 8 + 8])
                op("dve", "tensor_copy", [b_pt], [b_mixT], out=mixT[64:128, 4 + pr, 8 * b:8 * b + 8], in_=pt[64:128, (2 * pr + 1) * 8:(2 * pr + 1) * 8 + 8])
        for p in range(2):
            view, b_pan = load_panel(wb_out[p], KC, 512, b_wb_out)
            for j in range(4):
                oc = p * 4 + j
                pt, b_pt = fm_chunk(view, b_pan, KC, j * 128, mixT, b_mixT, NS)
                op("dve", "tensor_tensor", [b_pt, b_xT], [b_xT], out=xT[:, oc, 0:NS], in0=pt[:, 0:NS], in1=xT[:, oc, 0:NS], op=ALU.add)
        ffn(0, NS)
        rmsnorm(2, NS)
        s5f = s5i[:].bitcast(F32)
        WS5 = s5f[:, 0:256].rearrange("p (a c) -> p a c", a=4)
        ZendS = s5f[:, 256:512].rearrange("p (a c) -> p a c", a=4)
        c7, b_c7, s7, b_s7 = den, b_den, abr, b_abr
        cs_small(7.0, c7, b_c7, s7, b_s7, True)
        for b in range(SB):
            ld(tmpC[0:64, 0:64], st_sr[b], b_tmpC); ld(tmpC[0:64, 64:128], st_si[b], b_tmpC)
            ld(tmpC[0:64, 128:192], st_si[b], b_tmpC); ld(tmpC[0:64, 192:256], st_sr[b], b_tmpC)
            pt, b_pt = next_ps()
            op("pe", "transpose", [b_tmpC, b_ident], [b_pt], out=pt[:, 0:64], in_=tmpC[0:64, 0:128], identity=ident[0:64, 0:64])
            op("pe", "transpose", [b_tmpC, b_ident], [b_pt], out=pt[:, 64:128], in_=tmpC[0:64, 128:256], identity=ident[0:64, 0:64])
            op("dve", "tensor_tensor", [b_pt, b_s1t], [b_t64], out=t64[:], in0=pt[:, 64:128], in1=s1t[:], op=ALU.mult)
            op("dve", "tensor_tensor", [b_pt, b_c1t], [b_s5i], out=WS5[:, b, :], in0=pt[:, 0:64], in1=c1t[:], op=ALU.mult)
            op("dve", "tensor_sub", [b_s5i, b_t64], [b_s5i], out=WS5[:, b, :], in0=WS5[:, b, :], in1=t64[:])
        ps_y0, b_ps_y0 = PS_A
        ps_y1, b_ps_y1 = PS_B
        v4 = lambda ap: ap.rearrange("p (a b c) -> p a b c", a=4, b=4)
        for qd_i in range(16):
            g0 = qd_i * 4
            kc, half = g0 // 8, (g0 % 8) // 4
            hp = half * 64
            pd1, b_pd1 = next_ps()
            pd2, b_pd2 = next_ps()
            for gi in range(4):
                op("pe", "matmul", [b_LT1, b_hT], [b_pd1], pd1[:, gi * NS:(gi + 1) * NS], lhsT=LT1[hp:hp + 64, kc, gi, :], rhs=hT[hp:hp + 64, kc, 0:NS], start=True, stop=True)
            for gi in range(4):
                op("pe", "matmul", [b_LT2, b_hT], [b_pd2], pd2[:, gi * NS:(gi + 1) * NS], lhsT=LT2[hp:hp + 64, kc, gi, :], rhs=hT[hp:hp + 64, kc, 0:NS], start=True, stop=True)
            ctq = bcast(CT, 64 * 128, g0 * 128, [(128, 4), (0, 4), (1, 8)])
            stq = bcast(ST, 64 * 128, g0 * 128, [(128, 4), (0, 4), (1, 8)])
            op("dve", "tensor_tensor", [b_pd1, b_CT], [b_tmpA], out=v4(tmpA[:, 0:128]), in0=v4(pd1[:, 0:128]), in1=ctq, op=ALU.mult)
            op("dve", "tensor_tensor", [b_pd2, b_ST], [b_tmpB], out=v4(tmpB[:, 0:128]), in0=v4(pd2[:, 0:128]), in1=stq, op=ALU.mult)
            op("pool", "tensor_tensor", [b_tmpA, b_tmpB], [b_tmpC], out=tmpC[:, 0:128], in0=tmpA[:, 0:128], in1=tmpB[:, 0:128], op=ALU.add)
            for gi in range(4):
                g = g0 + gi
                for b in range(SB):
                    c0 = gi * NS + 8 * b
                    op("dve", "tensor_tensor_scan", [b_tmpC, b_lam_abs, b_s5i], [b_tmpD], out=tmpD[:, c0:c0 + 8],
                       data0=lam_abs[:, g:g + 1].to_broadcast([128, 8]), data1=tmpC[:, c0:c0 + 8], initial=WS5[:, b, g:g + 1], op0=ALU.mult, op1=ALU.add)
            at, b_at = eT[qd_i % 2]
            bt, b_bt = pT[qd_i % 2]
            op("dve", "tensor_tensor", [b_tmpD, b_CT], [b_at], out=v4(at[:, 0:128]), in0=v4(tmpD[:, 0:128]), in1=ctq, op=ALU.mult)
            op("pool", "tensor_tensor", [b_tmpD, b_ST], [b_bt], out=v4(bt[:, 0:128]), in0=v4(tmpD[:, 0:128]), in1=stq, op=ALU.mult)
            for b in range(SB):
                op("pool", "tensor_copy", [b_tmpD], [b_s5i], out=ZendS[:, b, g0:g0 + 4], in_=bcast(tmpD, 512, 8 * b + 7, [(NS, 4)]))
            for gi in range(4):
                g = g0 + gi
                py, b_py = (ps_y0, b_ps_y0) if g < 32 else (ps_y1, b_ps_y1)
                col = (g % 32) * 16
                op("pe", "matmul", [b_at, b_C1], [b_py], py[0:NS, col:col + 16], lhsT=at[:, gi * NS:(gi + 1) * NS], rhs=C1[:, g, :], start=True, stop=False)
                op("pe", "matmul", [b_bt, b_C2], [b_py], py[0:NS, col:col + 16], lhsT=bt[:, gi * NS:(gi + 1) * NS], rhs=C2[:, g, :], start=False, stop=True)
        for b in range(SB):
            pw, b_pw = next_ps()
            op("pe", "matmul", [b_swapm, b_s5i], [b_pw], pw[:, 0:64], lhsT=swapm[:], rhs=ZendS[:, b, :], start=True, stop=True)
            op("dve", "tensor_tensor", [b_pw, b_s7], [b_t64], out=t64[:], in0=pw[:, 0:64], in1=s7[:], op=ALU.mult)
            op("dve", "tensor_tensor", [b_s5i, b_c7], [b_fre], out=fre[:], in0=ZendS[:, b, :], in1=c7[:], op=ALU.mult)
            op("dve", "tensor_add", [b_fre, b_t64], [b_fre], out=fre[:], in0=fre[:], in1=t64[:])
            st(o_sss[b], fre[:], b_fre)
        ysb = xtok[:, 0:1024].rearrange("p (a b) -> p a b", a=2)
        op("act", "copy", [b_ps_y0], [b_xtok], out=ysb[0:NS, 0, :], in_=ps_y0[0:NS, :])
        op("dve", "tensor_copy", [b_ps_y1], [b_xtok], out=ysb[0:NS, 1, :], in_=ps_y1[0:NS, :])
        for half in range(2):
            pt, b_pt = next_ps()
            for k4 in range(4):
                op("pe", "transpose", [b_xtok, b_ident], [b_pt], out=pt[:, k4 * NS:(k4 + 1) * NS], in_=ysb[0:NS, half, k4 * 128:(k4 + 1) * 128], identity=ident[0:NS, 0:NS])
            for k4 in range(4):
                kc = half * 4 + k4
                op("dve", "scalar_tensor_tensor", [b_xT, b_gd, b_rstd], [b_tmpA], out=tmpA[:, k4 * NS:(k4 + 1) * NS], in0=xT[:, kc, 0:NS],
                   scalar=gd[:, kc:kc + 1], in1=rstd[:, 0:NS], op0=ALU.mult, op1=ALU.mult)
            op("dve", "tensor_tensor", [b_tmpA, b_pt], [b_tmpA], out=tmpA[:, 0:W4], in0=tmpA[:, 0:W4], in1=pt[:, 0:W4], op=ALU.add)
            op("act", "activation", [b_tmpA], [b_tmpB], out=tmpB[:, 0:W4], in_=tmpA[:, 0:W4], func=AF.Square)
            op("dve", "tensor_scalar", [b_tmpB], [b_tmpB], out=tmpB[:, 0:W4], in0=tmpB[:, 0:W4], scalar1=0.044715, scalar2=1.0, op0=ALU.mult, op1=ALU.add)
            op("dve", "tensor_tensor", [b_tmpB, b_tmpA], [b_tmpB], out=tmpB[:, 0:W4], in0=tmpB[:, 0:W4], in1=tmpA[:, 0:W4], op=ALU.mult)
            op("act", "activation", [b_tmpB], [b_tmpB], out=tmpB[:, 0:W4], in_=tmpB[:, 0:W4], func=AF.Sigmoid, scale=1.5957691216)
            for k4 in range(4):
                kc = half * 4 + k4
                op("dve", "tensor_tensor", [b_tmpA, b_tmpB], [b_mixT], out=mixT[:, kc, 0:NS], in0=tmpA[:, k4 * NS:(k4 + 1) * NS],
                   in1=tmpB[:, k4 * NS:(k4 + 1) * NS], op=ALU.mult)
        for p in range(4):
            view, b_pan = load_panel(wb_glu[p], KC, 512, b_wb_glu)
            for j in range(2):
                oc = 2 * p + j
                pv, b_pv = fm_chunk(view, b_pan, KC, j * 128, mixT, b_mixT, NS)
                pg, b_pg = fm_chunk(view, b_pan, KC, 256 + j * 128, mixT, b_mixT, NS)
                op("act", "activation", [b_pg], [b_tmpA], out=tmpA[:, 0:NS], in_=pg[:, 0:NS], func=AF.Sigmoid)
                op("dve", "tensor_tensor", [b_tmpA, b_pv], [b_tmpA], out=tmpA[:, 0:NS], in0=tmpA[:, 0:NS], in1=pv[:, 0:NS], op=ALU.mult)
                op("dve", "tensor_tensor", [b_tmpA, b_xT], [b_xT], out=xT[:, oc, 0:NS], in0=tmpA[:, 0:NS], in1=xT[:, oc, 0:NS], op=ALU.add)
        ffn(1, NS)
        final_out(o_ys, NS)

    S.finish()
    es.close()
    return nc


_NC_CACHE = {}
LAUNCH_RANGES = [(0, 16)]


def kernel(**inputs):
    f32 = np.float32
    x_prompt = np.asarray(inputs["x_prompt"], f32)
    consts = host_consts()
    wmap = {}
    for k, shp in WEIGHT_SHAPES.items():
        a = np.asarray(inputs[k], f32)
        if k in ("norm_mix", "norm_ffn", "norm_final", "w_ffn_in", "w_ffn_out"):
            wmap[k] = np.ascontiguousarray(a).reshape(shp)
        else:
            wmap[k] = np.ascontiguousarray(a[0]).reshape(shp)
    bf = ml_dtypes.bfloat16
    state = [{"i_Sst": np.zeros((128, 2, 128), f32), "i_Wc": np.zeros((128, 64), f32), "i_Zend": np.zeros((128, 64), f32),
              "i_KBT": np.zeros((128, 4, RING * 128), bf), "i_VB": np.zeros((128, RING, 512), bf)} for _ in range(NCORES)]
    y_prompt = np.zeros((2, SEQ, D), f32)
    r = None
    for li, (lo, hi) in enumerate(LAUNCH_RANGES):
        last = li == len(LAUNCH_RANGES) - 1
        key = ("nc", lo, hi, last)
        if key not in _NC_CACHE:
            _NC_CACHE[key] = build_program(hi, RUN_SAMPLE=last, blk_lo=lo)
        nc = _NC_CACHE[key]
        in_maps = []
        for c in range(NCORES):
            m = {"xp": np.ascontiguousarray(x_prompt[c % 2])}
            bs = slice(c * SB, (c + 1) * SB)
            m["xs"] = np.ascontiguousarray(np.asarray(inputs["x_sample"], f32)[bs].reshape(TS, D))
            m["st_ret"] = np.ascontiguousarray(np.asarray(inputs["state_ret"], f32)[0, bs])
            m["st_k"] = np.ascontiguousarray(np.asarray(inputs["state_swa_k"], f32)[0, bs].reshape(SB, WB, 512))
            m["st_v"] = np.ascontiguousarray(np.asarray(inputs["state_swa_v"], f32)[0, bs].reshape(SB, WB, 512))
            m["st_sr"] = np.ascontiguousarray(np.asarray(inputs["state_ssm_re"], f32)[0, bs])
            m["st_si"] = np.ascontiguousarray(np.asarray(inputs["state_ssm_im"], f32)[0, bs])
            m.update(state[c])
            m.update(wmap)
            m.update({"c_" + k: v for k, v in consts.items()})
            in_maps.append(m)
        res = run_bass_kernel_spmd(nc, in_maps, core_ids=list(range(NCORES)))
        r = res.results
        for sq_ in range(2):
            y_prompt[sq_, lo * 512:hi * 512] = r[sq_]["o_yp"][lo * 512:hi * 512]
        for c in range(NCORES if len(LAUNCH_RANGES) > 1 else 0):
            state[c] = {"i_Sst": np.asarray(r[c]["o_retp"], f32).reshape(128, 2, 128), "i_Wc": np.asarray(r[c]["o_Wc"], f32),
                        "i_Zend": np.asarray(r[c]["o_Zend"], f32), "i_KBT": np.asarray(r[c]["o_KBT"]).reshape(128, 4, RING * 128),
                        "i_VB": np.asarray(r[c]["o_VB"]).reshape(128, RING, 512)}
    B = 2
    y_sample = np.concatenate([r[c]["o_ys"] for c in range(NCORES)], 0).reshape(32, 8, D)

    def unret(a):
        a = np.asarray(a).reshape(128, 2, 128)
        out = np.zeros((4, 64, 128), f32)
        for h in range(4):
            out[h] = a[(h % 2) * 64:(h % 2) * 64 + 64, h // 2, :]
        return out
    ret_p = np.stack([unret(r[0]["o_retp"]), unret(r[1]["o_retp"])])[None]
    ret_s = np.stack([unret(np.asarray(r[c]["o_rets"]).reshape(SB, 128, 2, 128)[b]) for c in range(NCORES) for b in range(SB)])[None]
    swk_p = np.stack([r[0]["o_kp"], r[1]["o_kp"]]).reshape(1, B, WB, H_B, DH_B)
    swv_p = np.stack([r[0]["o_vp"], r[1]["o_vp"]]).reshape(1, B, WB, H_B, DH_B)
    swk_s = np.concatenate([r[c]["o_ks"] for c in range(NCORES)], 0).reshape(1, 32, WB, H_B, DH_B)
    swv_s = np.concatenate([r[c]["o_vs"] for c in range(NCORES)], 0).reshape(1, 32, WB, H_B, DH_B)
    sr_p = np.stack([r[0]["o_ssp"][0:64].T, r[1]["o_ssp"][0:64].T])[None]
    si_p = np.stack([r[0]["o_ssp"][64:128].T, r[1]["o_ssp"][64:128].T])[None]
    sss = lambda c, b: np.asarray(r[c]["o_sss"]).reshape(SB, 128, 64)[b]
    sr_s = np.stack([sss(c, b)[0:64].T for c in range(NCORES) for b in range(SB)])[None]
    si_s = np.stack([sss(c, b)[64:128].T for c in range(NCORES) for b in range(SB)])[None]
    return (y_prompt, y_sample, ret_p, ret_s, swk_p, swv_p, swk_s, swv_s, sr_p, si_p, sr_s, si_s)
```
